# Optimizing a Trainium2 kernel written in Bass

```python
import math
import jax, jax.numpy as jnp
from jax import lax
import numpy as np

D_MODEL = 1024
BATCH = 8
SEQ = 2048
DEPTH = 1
DEC_BATCH = 128
DEC_SEQ = 1
PAST_LEN = 16384
PAGE_SIZE = 128

GDN_HEADS = 8
GDN_DK = 128
GDN_DV = 128
GDN_CONV = 4
CHUNK = 64
QK_W = GDN_HEADS * GDN_DK
V_W = GDN_HEADS * GDN_DV
QKV_W = 2 * QK_W + V_W
SC_W = 1024
SC_GROUPS = 8
SC_CONV = 3
FFN_HIDDEN = 2816
PLE_DIM = 256
IN_SPLITS = (QKV_W, V_W, GDN_HEADS, GDN_HEADS, SC_W, SC_W, SC_W, D_MODEL, D_MODEL)
IN_W = QKV_W + V_W + 2 * GDN_HEADS + 3 * SC_W + 2 * D_MODEL
DEEPNORM_ALPHA = (2.0 * DEPTH) ** 0.25
DEEPNORM_BETA = (8.0 * DEPTH) ** -0.25
LN_EPS = 1e-5
RMS_EPS = 1e-6
L2_EPS = 1e-6

kernel_name = "hybrid_gdn_shortconv_macaron_deepnorm_step"


def split_cols(a, widths):
    out = []
    s = 0
    for wd in widths:
        out.append(a[..., s:s + wd])
        s += wd
    return out


def layer_norm(x, g, b):
    xf = x.astype(jnp.float32)
    mu = jnp.mean(xf, -1, keepdims=True)
    var = jnp.mean(jnp.square(xf - mu), -1, keepdims=True)
    return ((xf - mu) * lax.rsqrt(var + LN_EPS) * g.astype(jnp.float32) + b.astype(jnp.float32)).astype(x.dtype)


def swiglu(x, w_gate, w_up, w_down):
    return (jax.nn.silu(x @ w_gate) * (x @ w_up)) @ w_down


def causal_dwconv(x, buf, w):
    width = w.shape[0]
    t = x.shape[1]
    xp = jnp.concatenate([buf.astype(x.dtype), x], axis=1)
    y = sum(xp[:, j:j + t] * w[j] for j in range(width))
    return y, xp[:, -(width - 1):]


def l2norm(x):
    return x * lax.rsqrt(jnp.sum(jnp.square(x), -1, keepdims=True) + L2_EPS)


def gdn_chunked(q, k, v, g, beta, s0):
    b, t, h, _ = q.shape
    dv = v.shape[-1]
    n = t // CHUNK
    c = CHUNK

    def blk(a):
        return a.reshape(b, n, c, h, -1).transpose(1, 0, 3, 2, 4)

    q, k, v = blk(q), blk(k), blk(v)
    g = g.reshape(b, n, c, h).transpose(1, 0, 3, 2)
    beta = beta.reshape(b, n, c, h).transpose(1, 0, 3, 2)
    gc = jnp.cumsum(g, axis=-1)
    tril = jnp.tril(jnp.ones((c, c), bool))
    strict = jnp.tril(jnp.ones((c, c), bool), -1)
    decay = jnp.where(tril, jnp.exp(jnp.where(tril, gc[..., :, None] - gc[..., None, :], 0.0)), 0.0)
    kb = k * beta[..., None]
    m = jnp.where(strict, jnp.einsum("nbhik,nbhjk->nbhij", kb, k) * decay, 0.0)
    a = m + jnp.eye(c, dtype=m.dtype)
    u = lax.linalg.triangular_solve(a, v * beta[..., None], left_side=True, lower=True, unit_diagonal=True)
    w = lax.linalg.triangular_solve(a, kb * jnp.exp(gc)[..., None], left_side=True, lower=True, unit_diagonal=True)
    qk = jnp.einsum("nbhik,nbhjk->nbhij", q, k) * decay
    g_last = gc[..., -1]
    q_dec = q * jnp.exp(gc)[..., None]
    k_dec = k * jnp.exp(g_last[..., None] - gc)[..., None]

    def step(s, inp):
        u_n, w_n, qk_n, q_n, k_n, gl_n = inp
        v_new = u_n - jnp.einsum("bhck,bhkv->bhcv", w_n, s)
        o = jnp.einsum("bhck,bhkv->bhcv", q_n, s) + jnp.einsum("bhij,bhjv->bhiv", qk_n, v_new)
        s = s * jnp.exp(gl_n)[..., None, None] + jnp.einsum("bhck,bhcv->bhkv", k_n, v_new)
        return s, o

    s, o = lax.scan(step, s0, (u, w, qk, q_dec, k_dec, g_last))
    o = o.transpose(1, 0, 3, 2, 4).reshape(b, t, h, dv)
    return o, s


def gdn_recurrent(q, k, v, g, beta, s0):
    def step(s, inp):
        q_t, k_t, v_t, g_t, b_t = inp
        s = s * jnp.exp(g_t)[..., None, None]
        kv = jnp.einsum("bhk,bhkv->bhv", k_t, s)
        d = (v_t - kv) * b_t[..., None]
        s = s + jnp.einsum("bhk,bhv->bhkv", k_t, d)
        o = jnp.einsum("bhk,bhkv->bhv", q_t, s)
        return s, o

    xs = (jnp.swapaxes(q, 0, 1), jnp.swapaxes(k, 0, 1), jnp.swapaxes(v, 0, 1), jnp.swapaxes(g, 0, 1), jnp.swapaxes(beta, 0, 1))
    s, o = lax.scan(step, s0, xs)
    return jnp.swapaxes(o, 0, 1), s


def token_mixers(x, s0, buf_qkv, buf_sc, w, chunked):
    bsz, t, _ = x.shape
    proj = x @ w["w_in"]
    qkv, z, b_raw, a_raw, gate_b, gate_c, hsc, gate_gdn, gate_sc = split_cols(proj, IN_SPLITS)
    qkv_c, new_buf_qkv = causal_dwconv(qkv, buf_qkv, w["w_conv_qkv"])
    qkv_c = jax.nn.silu(qkv_c).astype(jnp.float32)
    q, k, v = split_cols(qkv_c, (QK_W, QK_W, V_W))
    q = l2norm(q.reshape(bsz, t, GDN_HEADS, GDN_DK)) * (GDN_DK ** -0.5)
    k = l2norm(k.reshape(bsz, t, GDN_HEADS, GDN_DK))
    v = v.reshape(bsz, t, GDN_HEADS, GDN_DV)
    beta = jax.nn.sigmoid(b_raw.astype(jnp.float32))
    g = -jnp.exp(w["A_log"].astype(jnp.float32)) * jax.nn.softplus(a_raw.astype(jnp.float32) + w["dt_bias"].astype(jnp.float32))
    s0 = s0.astype(jnp.float32)
    if chunked:
        o, s_new = gdn_chunked(q, k, v, g, beta, s0)
    else:
        o, s_new = gdn_recurrent(q, k, v, g, beta, s0)
    zf = z.astype(jnp.float32).reshape(bsz, t, GDN_HEADS, GDN_DV)
    o = o * lax.rsqrt(jnp.mean(jnp.square(o), -1, keepdims=True) + RMS_EPS) * w["w_onorm"].astype(jnp.float32) * jax.nn.silu(zf)
    branch_gdn = o.reshape(bsz, t, V_W).astype(x.dtype) @ w["w_p_gdn"]
    u_c, new_buf_sc = causal_dwconv(gate_c * hsc, buf_sc, w["w_conv_sc"])
    branch_sc = (gate_b * u_c) @ w["w_p_sc"]
    merged = jax.nn.sigmoid(gate_gdn) * branch_gdn + jax.nn.sigmoid(gate_sc) * branch_sc
    return merged @ w["w_o"], s_new, new_buf_qkv, new_buf_sc


def decoder_layer(x, p, s0, buf_qkv, buf_sc, w, chunked):
    x = layer_norm(DEEPNORM_ALPHA * x + 0.5 * swiglu(x, w["ffn1_w_gate"], w["ffn1_w_up"], w["ffn1_w_down"]), w["ln1_g"], w["ln1_b"])
    mix, s_new, nb_qkv, nb_sc = token_mixers(x, s0, buf_qkv, buf_sc, w, chunked)
    x = layer_norm(DEEPNORM_ALPHA * x + mix, w["ln2_g"], w["ln2_b"])
    x = layer_norm(DEEPNORM_ALPHA * x + 0.5 * swiglu(x, w["ffn2_w_gate"], w["ffn2_w_up"], w["ffn2_w_down"]), w["ln3_g"], w["ln3_b"])
    ple = jax.nn.sigmoid(x @ w["w_ple_gate"]) * (p.astype(x.dtype) @ w["w_ple_proj"])
    x = layer_norm(DEEPNORM_ALPHA * x + ple, w["ln4_g"], w["ln4_b"])
    return x, s_new, nb_qkv, nb_sc


def setup_inputs(seed: int = 0) -> dict:
    key = jax.random.key(seed)
    ks = iter(jax.random.split(key, 48))

    def nrm(shape, scale):
        return jax.random.normal(next(ks), shape, jnp.float32) * scale

    def gain():
        return 1.0 + nrm((DEPTH, D_MODEL), 0.02)

    def bias():
        return nrm((DEPTH, D_MODEL), 0.02)

    A_log = jnp.log(jax.random.uniform(next(ks), (DEPTH, GDN_HEADS), jnp.float32, 1.0, 16.0))
    dt = jnp.exp(jax.random.uniform(next(ks), (DEPTH, GDN_HEADS), jnp.float32, math.log(1e-3), math.log(1e-1)))
    dt_bias = dt + jnp.log(-jnp.expm1(-dt))
    return {
        "x_prompt": nrm((BATCH, SEQ, D_MODEL), 1.0),
        "x_sample": nrm((DEC_BATCH, DEC_SEQ, D_MODEL), 1.0),
        "p_prompt": nrm((DEPTH, BATCH, SEQ, PLE_DIM), 1.0),
        "p_sample": nrm((DEPTH, DEC_BATCH, DEC_SEQ, PLE_DIM), 1.0),
        "state_gdn": nrm((DEPTH, DEC_BATCH, GDN_HEADS, GDN_DK, GDN_DV), 0.1),
        "state_qkv_conv": nrm((DEPTH, DEC_BATCH, GDN_CONV - 1, QKV_W), 1.0),
        "state_sc_conv": nrm((DEPTH, DEC_BATCH, SC_CONV - 1, SC_W), 1.0),
        "ffn1_w_gate": nrm((DEPTH, D_MODEL, FFN_HIDDEN), D_MODEL ** -0.5),
        "ffn1_w_up": nrm((DEPTH, D_MODEL, FFN_HIDDEN), D_MODEL ** -0.5),
        "ffn1_w_down": nrm((DEPTH, FFN_HIDDEN, D_MODEL), FFN_HIDDEN ** -0.5 * DEEPNORM_BETA),
        "ln1_g": gain(),
        "ln1_b": bias(),
        "w_in": nrm((DEPTH, D_MODEL, IN_W), D_MODEL ** -0.5),
        "w_conv_qkv": nrm((DEPTH, GDN_CONV, QKV_W), GDN_CONV ** -0.5),
        "A_log": A_log,
        "dt_bias": dt_bias,
        "w_onorm": 1.0 + nrm((DEPTH, GDN_DV), 0.02),
        "w_p_gdn": nrm((DEPTH, V_W, D_MODEL), V_W ** -0.5 * DEEPNORM_BETA),
        "w_conv_sc": nrm((DEPTH, SC_CONV, SC_W), SC_CONV ** -0.5),
        "w_p_sc": nrm((DEPTH, SC_W, D_MODEL), SC_W ** -0.5 * DEEPNORM_BETA),
        "w_o": nrm((DEPTH, D_MODEL, D_MODEL), D_MODEL ** -0.5 * DEEPNORM_BETA),
        "ln2_g": gain(),
        "ln2_b": bias(),
        "ffn2_w_gate": nrm((DEPTH, D_MODEL, FFN_HIDDEN), D_MODEL ** -0.5),
        "ffn2_w_up": nrm((DEPTH, D_MODEL, FFN_HIDDEN), D_MODEL ** -0.5),
        "ffn2_w_down": nrm((DEPTH, FFN_HIDDEN, D_MODEL), FFN_HIDDEN ** -0.5 * DEEPNORM_BETA),
        "ln3_g": gain(),
        "ln3_b": bias(),
        "w_ple_gate": nrm((DEPTH, D_MODEL, D_MODEL), D_MODEL ** -0.5),
        "w_ple_proj": nrm((DEPTH, PLE_DIM, D_MODEL), PLE_DIM ** -0.5 * DEEPNORM_BETA),
        "ln4_g": gain(),
        "ln4_b": bias(),
    }


def reference(x_prompt, x_sample, p_prompt, p_sample, state_gdn, state_qkv_conv, state_sc_conv,
              ffn1_w_gate, ffn1_w_up, ffn1_w_down, ln1_g, ln1_b,
              w_in, w_conv_qkv, A_log, dt_bias, w_onorm, w_p_gdn, w_conv_sc, w_p_sc, w_o, ln2_g, ln2_b,
              ffn2_w_gate, ffn2_w_up, ffn2_w_down, ln3_g, ln3_b,
              w_ple_gate, w_ple_proj, ln4_g, ln4_b):
    xp = x_prompt
    xs = x_sample
    sp_gdn, sp_qkv, sp_sc = [], [], []
    ss_gdn, ss_qkv, ss_sc = [], [], []
    for i in range(DEPTH):
        w = {
            "ffn1_w_gate": ffn1_w_gate[i], "ffn1_w_up": ffn1_w_up[i], "ffn1_w_down": ffn1_w_down[i],
            "ln1_g": ln1_g[i], "ln1_b": ln1_b[i],
            "w_in": w_in[i], "w_conv_qkv": w_conv_qkv[i], "A_log": A_log[i], "dt_bias": dt_bias[i],
            "w_onorm": w_onorm[i], "w_p_gdn": w_p_gdn[i], "w_conv_sc": w_conv_sc[i], "w_p_sc": w_p_sc[i],
            "w_o": w_o[i], "ln2_g": ln2_g[i], "ln2_b": ln2_b[i],
            "ffn2_w_gate": ffn2_w_gate[i], "ffn2_w_up": ffn2_w_up[i], "ffn2_w_down": ffn2_w_down[i],
            "ln3_g": ln3_g[i], "ln3_b": ln3_b[i],
            "w_ple_gate": w_ple_gate[i], "w_ple_proj": w_ple_proj[i], "ln4_g": ln4_g[i], "ln4_b": ln4_b[i],
        }
        s0 = jnp.zeros((BATCH, GDN_HEADS, GDN_DK, GDN_DV), jnp.float32)
        b_qkv = jnp.zeros((BATCH, GDN_CONV - 1, QKV_W), xp.dtype)
        b_sc = jnp.zeros((BATCH, SC_CONV - 1, SC_W), xp.dtype)
        xp, s_n, q_n, c_n = decoder_layer(xp, p_prompt[i], s0, b_qkv, b_sc, w, True)
        sp_gdn.append(s_n)
        sp_qkv.append(q_n)
        sp_sc.append(c_n)
        xs, s_n, q_n, c_n = decoder_layer(xs, p_sample[i], state_gdn[i], state_qkv_conv[i], state_sc_conv[i], w, False)
        ss_gdn.append(s_n)
        ss_qkv.append(q_n)
        ss_sc.append(c_n)
    return (xp, xs, jnp.stack(sp_gdn), jnp.stack(sp_qkv), jnp.stack(sp_sc), jnp.stack(ss_gdn), jnp.stack(ss_qkv), jnp.stack(ss_sc))
```

```python
import contextlib
import numpy as np
import concourse.bass as bass
import concourse.mybir as mybir
from concourse.bass_utils import run_bass_kernel_spmd

F32 = mybir.dt.float32
BF16 = mybir.dt.bfloat16
AF = mybir.ActivationFunctionType
ALU = mybir.AluOpType
AX = mybir.AxisListType

D = 1024
SEQ = 2048
NSAMP = 16
HID = 2816
NJ = HID // 128
H = 8
QKV_W = 3072
IN_W = 9232
Z0, BETA0, A0, B0, C0, H0, GG0, GS0 = 3072, 4096, 4104, 4112, 5136, 6160, 7184, 8208
ALPHA = 2.0 ** 0.25
LN_EPS = 1e-5
RMS_EPS = 1e-6
L2_EPS = 1e-6
NTP = 512
SLOT = 4096
NSLOT = 4
NBK = 16

COMPUTE = ("pe", "act", "dve", "pool")


class _Op:
    __slots__ = ("eng", "fn", "r", "w", "key", "eidx", "kn", "waits", "done", "inc")

    def __init__(self, eng, fn, r, w, key):
        self.eng = eng
        self.fn = fn
        self.r = r
        self.w = w
        self.key = key
        self.eidx = -1
        self.kn = 0
        self.waits = []
        self.done = None
        self.inc = False


class Prog:
    def __init__(self, nc):
        self.nc = nc
        self.ops = []

    def add(self, eng, fn, r=(), w=(), key=None):
        self.ops.append(_Op(eng, fn, tuple(r), tuple(w), key))

    def finalize(self):
        last_w = {}
        readers = {}
        issue = {e: {} for e in ("pe", "act", "dve", "pool", "sp")}
        ecount = {e: 0 for e in issue}
        kcount = {}
        kops = {}
        eops = {e: [] for e in issue}
        for op in self.ops:
            e = op.eng
            deps = set()
            for res in op.r:
                lw = last_w.get(res)
                if lw is not None:
                    deps.add(lw)
            for res in op.w:
                lw = last_w.get(res)
                if lw is not None:
                    deps.add(lw)
                for rd in readers.get(res, ()):
                    deps.add(rd)
            deps.discard(op)
            clock = issue[e]
            if op.key is None:
                op.eidx = ecount[e]
                ecount[e] += 1
                eops[e].append(op)
            else:
                n = kcount.get(op.key, 0) + 1
                kcount[op.key] = n
                op.kn = n
                kops.setdefault(op.key, []).append(op)
                if n > 1:
                    deps.add(kops[op.key][n - 2])
            best = {}
            dma_deps = []
            for d in deps:
                if d.key is None:
                    b = best.get(d.eng)
                    if b is None or d.eidx > b.eidx:
                        best[d.eng] = d
                else:
                    dma_deps.append(d)
            newclock = None
            for f, d in best.items():
                if clock.get(f, -1) >= d.eidx:
                    continue
                if f == e and op.key is None:
                    if e == "pe":
                        continue
                    if e != "pool" and (op.eidx - d.eidx) > 2:
                        continue
                op.waits.append(("c", f, d.eidx))
                d.inc = True
                if newclock is None:
                    newclock = dict(clock)
                for k, v in d.done.items():
                    if newclock.get(k, -1) < v:
                        newclock[k] = v
            for d in dma_deps:
                kk = ("dma", d.key)
                cur = clock if newclock is None else newclock
                if cur.get(kk, 0) >= d.kn:
                    continue
                op.waits.append(("d", d.key, d.kn))
                if newclock is None:
                    newclock = dict(clock)
                for k, v in d.done.items():
                    if newclock.get(k, -1) < v:
                        newclock[k] = v
            if newclock is not None:
                issue[e] = newclock
                clock = newclock
            done = dict(clock)
            if op.key is None:
                done[e] = op.eidx
            else:
                done[("dma", op.key)] = op.kn
            op.done = done
            for res in op.r:
                readers.setdefault(res, []).append(op)
            for res in op.w:
                last_w[res] = op
                readers[res] = []
        self.rank = {}
        for e, lst in eops.items():
            k = 0
            for op in lst:
                if op.inc:
                    k += 1
                    self.rank[(e, op.eidx)] = k
        self.keys = list(kcount.keys())
        for op in self.ops:
            op.done = None
            if len(op.waits) > 1:
                m = {}
                for t, a, b in op.waits:
                    if (t, a) not in m or m[(t, a)] < b:
                        m[(t, a)] = b
                op.waits = [(t, a, b) for (t, a), b in m.items()]

    def emit(self, es):
        nc = self.nc
        sems = {}
        for e in COMPUTE:
            sems[e] = es.enter_context(nc.semaphore("s_" + e))
        ksem = {}
        for k in self.keys:
            ksem[k] = es.enter_context(nc.semaphore("k_" + str(k)))
        block = es.enter_context(nc.Block())
        rank = self.rank

        def run(ename, eng):
            for op in self.ops:
                if op.eng != ename:
                    continue
                for t, a, b in op.waits:
                    if t == "c":
                        eng.wait_ge(sems[a], rank[(a, b)])
                    else:
                        eng.wait_ge(ksem[a], 16 * b)
                if op.fn is None:
                    continue
                ins = op.fn(eng)
                if op.key is not None:
                    ins.then_inc(ksem[op.key], 16)
                elif op.inc:
                    ins.then_inc(sems[ename], 1)

        @block.tensor
        def _(eng):
            run("pe", eng)

        @block.scalar
        def _(eng):
            run("act", eng)

        @block.vector
        def _(eng):
            run("dve", eng)

        @block.gpsimd
        def _(eng):
            run("pool", eng)

        @block.sync
        def _(eng):
            run("sp", eng)


CSTF_NAMES = ["ident", "ltri", "su", "muincl", "ones"]
CSTB_NAMES = ["ident", "ones", "mndn", "mndtn", "me16", "me16t", "me32", "me32t", "me64", "me64t"]


def make_consts():
    i = np.arange(128)[:, None]
    j = np.arange(128)[None, :]
    c = {}
    c["ident"] = (i == j)
    c["ltri"] = (i <= j)
    c["su"] = (i > j)
    c["muincl"] = (j >= i)
    c["ones"] = np.ones((128, 128), bool)
    nd = (i // NBK == j // NBK) & (i > j)
    c["mndn"] = -1.0 * nd
    c["mndtn"] = -1.0 * nd.T
    for b in (16, 32, 64):
        e = (i // (2 * b) == j // (2 * b)) & ((i % (2 * b)) >= b) & ((j % (2 * b)) < b)
        c["me%d" % b] = e
        c["me%dt" % b] = e.T
    arrf = np.stack([np.asarray(c[n], np.float32) for n in CSTF_NAMES], axis=1)
    arrb = np.stack([np.asarray(c[n], np.float32) for n in CSTB_NAMES], axis=1)
    return np.ascontiguousarray(arrf), np.ascontiguousarray(arrb)


class Builder:
    def __init__(self, debug=None):
        self.debug = debug or {}
        self.slab_specs = []
        self.slab_off = []
        self.slab_tot = 0
        self.nslab_pass = None

    def mm(self, out, lhsT, rhs, r, w, start=True, stop=True):
        self.P.add("pe", lambda e: e.matmul(out, lhsT, rhs, start=start, stop=stop), r, w)

    def tr(self, out, in_, ident, r, w):
        self.P.add("pe", lambda e: e.transpose(out, in_, ident), r, w)

    def act(self, out, in_, func, r, w, bias=None, scale=None):
        kw = {}
        if bias is not None:
            kw["bias"] = bias
        if scale is not None:
            kw["scale"] = scale
        self.P.add("act", lambda e: e.activation(out, in_, func, **kw), r, w)

    def tt(self, eng, out, in0, in1, op, r, w):
        self.P.add(eng, lambda e: e.tensor_tensor(out, in0, in1, op), r, w)

    def ts(self, eng, out, in0, s1, op0, r, w, s2=None, op1=None):
        if op1 is None:
            self.P.add(eng, lambda e: e.tensor_scalar(out, in0, s1, None, op0), r, w)
        else:
            self.P.add(eng, lambda e: e.tensor_scalar(out, in0, s1, s2, op0, op1), r, w)

    def stt(self, out, in0, scalar, in1, op0, op1, r, w):
        self.P.add("dve", lambda e: e.scalar_tensor_tensor(out, in0, scalar, in1, op0, op1), r, w)

    def cp(self, eng, out, in_, r, w):
        if eng == "act":
            self.P.add("act", lambda e: e.activation(out, in_, AF.Copy), r, w)
        else:
            self.P.add(eng, lambda e: e.tensor_copy(out, in_), r, w)

    def dq(self):
        return "sp" if self.recording else "pool"

    def dma(self, eng, out, in_, key, r, w, slow=False):
        if eng == "aux":
            eng = self.dq()
        if slow:
            self.P.add(eng, lambda e: e.dma_start(out=out, in_=in_, allow_slow_non_contiguous=True), r, w, key=key)
        else:
            self.P.add(eng, lambda e: e.dma_start(out=out, in_=in_), r, w, key=key)

    def slab(self, spec):
        if self.recording:
            self.slab_specs.append(spec)
            n = sum(((r1 - r0) // 128) * (c1 - c0) for (_, r0, r1, c0, c1) in spec)
            assert n <= SLOT, n
            self.slab_off.append((self.slab_tot, n))
            self.slab_tot += 128 * n
        si = self.slab_i % self.nslab_pass if self.nslab_pass else self.slab_i
        off, n = self.slab_off[si]
        slot = self.slab_i % NSLOT
        self.slab_i += 1
        t = self.wring[slot]
        key = "w%d" % slot
        scr = self.wscr[off:off + 128 * n].rearrange("(p n) -> p n", p=128)
        if self.recording:
            src = self.wbig[off:off + 128 * n].rearrange("(p n) -> p n", p=128)
            self.dma("pool", t[:, 0:n], src, key, r=[], w=[key])
            if self.debug.get("npass", 4) > 1 or not self.debug.get("nosample", False):
                self.dma("sp", scr, t[:, 0:n], "wb%d" % slot, r=[key], w=["wscr%d" % si])
        else:
            self.dma("sp", t[:, 0:n], scr, key, r=["wscr%d" % si], w=[key])
        views = []
        o = 0
        for (_, r0, r1, c0, c1) in spec:
            kc = (r1 - r0) // 128
            nc_ = c1 - c0
            views.append(t[:, o:o + kc * nc_].rearrange("p (k n) -> p k n", k=kc))
            o += kc * nc_
        return views, key

    def build(self):
        nc = bass.Bass("TRN2", target_bir_lowering=False)
        self.nc = nc
        self.es = contextlib.ExitStack()
        with self.es:
            self._build_inner()
        return nc

    def dram_in(self, name, shape, dt=F32):
        return self.nc.dram_tensor(name, list(shape), dt, kind="ExternalInput").ap()

    def dram_out(self, name, shape, dt=F32):
        return self.nc.dram_tensor(name, list(shape), dt, kind="ExternalOutput").ap()

    def sb(self, name, shape, dt):
        return self.es.enter_context(self.nc.sbuf_tensor(name, list(shape), dt))

    def _build_inner(self):
        nc = self.nc
        self.P = Prog(nc)
        P = self.P
        self.x = self.dram_in("x", [SEQ, D])
        self.pp = self.dram_in("pp", [SEQ, 256])
        self.xs = self.dram_in("xs", [NSAMP, D])
        self.psm = self.dram_in("psm", [NSAMP, 256])
        self.sg = self.dram_in("sg", [NSAMP, H, 128, 128])
        self.sq = self.dram_in("sq", [NSAMP, 3, QKV_W])
        self.ssc = self.dram_in("ssc", [NSAMP, 2, D])
        self.wbig = self.dram_in("wbig", [self.wbig_len])
        self.wscr = self.nc.dram_tensor("wscr", [self.wbig_len], BF16, kind="Internal").ap()
        self.lnp = self.dram_in("lnp", [8, D])
        self.wcq_d = self.dram_in("wcq", [128, 24 * 4])
        self.wcs_d = self.dram_in("wcs", [128, 8 * 3])
        self.smallp = self.dram_in("smallp", [2, 8])
        self.won_d = self.dram_in("won", [128])
        self.cst_d = self.dram_in("cst", [128, len(CSTF_NAMES), 128])
        self.cst2_d = self.dram_in("cst2", [128, len(CSTB_NAMES) * 128])
        self.i16_d = self.dram_in("i16", [128, 256])
        self.y = self.dram_out("y", [SEQ, D])
        self.ys = self.dram_out("ys", [NSAMP, D])
        self.sgp = self.dram_out("sgp", [H, 128, 128])
        self.sqp = self.dram_out("sqp", [3, QKV_W])
        self.ssp = self.dram_out("ssp", [2, D])
        self.sgs = self.dram_out("sgs", [NSAMP, H, 128, 128])
        self.sqs = self.dram_out("sqs", [NSAMP, 3, QKV_W])
        self.sss = self.dram_out("sss", [NSAMP, 2, D])
        self.outkeys = []

        sb = self.sb
        self.wring = [sb("wr%d" % i, [128, SLOT], BF16) for i in range(NSLOT)]
        self.xres = sb("xres", [128, 4, D], F32)
        self.xT = sb("xT", [128, 8, NTP], BF16)
        self.gbt = sb("gbt", [128, 2, D], F32)
        self.cstf = sb("cstf", [128, len(CSTF_NAMES), 128], F32)
        self.cstb = sb("cstb", [128, len(CSTB_NAMES), 128], BF16)
        self.i16b = sb("i16b", [128, 256], BF16)
        self.wcq = sb("wcq_s", [128, 24, 4], F32)
        self.wcs = sb("wcs_s", [128, 8, 3], F32)
        self.wonb = sb("wonb", [128, 128], F32)
        self.smallb = sb("smallb", [128, 16], F32)
        self.negA = sb("negA", [128, 8], F32)
        self.histq = sb("histq", [128, 24, 3], F32)
        self.hists = sb("hists", [128, 8, 2], F32)
        self.S = sb("S", [128, H, 128], F32)
        self.Sbf = sb("Sbf", [128, H, 128], BF16)
        self.A1 = sb("A1", [128, 24, NTP], BF16)
        self.ztok = sb("ztok", [128, 4, D], BF16)
        self.ktok = sb("ktok", [128, H, 128], BF16)
        self.vtok = sb("vtok", [128, H, 128], BF16)
        self.batok = sb("batok", [128, 4, 16], F32)
        self.beta = sb("beta", [128, 4, 8], F32)
        self.gtok = sb("gtok", [128, 4, 8], F32)
        self.tf = [sb("tf%d" % i, [128, NTP + 4], F32) for i in range(6)]
        self.tfi = 0
        self.tb16 = [sb("tb%d" % i, [128, NTP], BF16) for i in range(4)]
        self.tbi = 0
        self.t1 = sb("t1", [128, D], F32)
        self.t2 = sb("t2", [128, D], F32)
        self.otok = sb("otok", [128, D], F32)
        self.xb16 = sb("xb16", [128, D], BF16)
        self.stat = sb("stat", [128, 32], F32)
        self.pT = sb("pT", [128, 2, NTP], BF16)
        self.pb = sb("pb", [128, 256], BF16)
        GQ = [("decTm", F32), ("Lg", F32), ("qkTm", BF16), ("MT", BF16), ("M", BF16),
              ("Na", BF16), ("Nb", BF16), ("Nc", BF16), ("Nd", BF16), ("Pa", BF16), ("Pb", BF16),
              ("Pc", BF16), ("Pd", BF16)]
        self.gqs = []
        gA = {}
        for nm, dt in GQ:
            gA[nm] = sb("g_" + nm, [128, 4, 128], dt)[:]
        self.gqs.append((gA, "gA_"))
        self.arenaB = sb("arenaB", [128, 15 * 512], BF16)
        gB = {}
        o = 0
        for nm, dt in GQ:
            n = 1024 if dt == F32 else 512
            v = self.arenaB[:, o:o + n]
            if dt == F32:
                v = v.bitcast(F32)
            gB[nm] = v.rearrange("p (h d) -> p h d", h=4)
            o += n
        self.gqs.append((gB, "gB_"))
        self.kdec = sb("kdec", [128, H, 128], BF16)
        self.gsm = sb("gsm", [128, 64], F32)
        self.hsq = sb("hsq", [128, 24, 3, NSAMP], F32)
        self.hss = sb("hss", [128, 8, 2, NSAMP], F32)
        self.Sin = [sb("Sin%d" % i, [128, H, 128], F32) for i in range(2)]
        self.Sinb = [sb("Sinb%d" % i, [128, H, 128], BF16) for i in range(2)]
        self.kTm = self.arenaB[:, 0:2048].rearrange("p (h a b) -> p h a b", h=H, a=NSAMP)
        self.qTm = self.arenaB[:, 2048:4096].rearrange("p (h a b) -> p h a b", h=H, a=NSAMP)
        self.kmask = [sb("kmask%d" % i, [NSAMP, D], BF16) for i in range(2)]
        self.qtok = sb("qtok", [NSAMP, D], BF16)
        self.abc = sb("abc", [128, 128], F32)
        self.ps = [self.es.enter_context(nc.psum_tensor("ps%d" % i, [128, 512], F32)) for i in range(8)]
        self.psb = [p.bitcast(BF16) for p in self.ps]

        self.recording = True
        self.slab_i = 0
        self.setup()
        npass = self.debug.get("npass", 4)
        for pi in range(npass):
            self.layer_pass(pi, NTP, sample=False, last=(pi == npass - 1))
            if pi == 0:
                self.recording = False
                self.nslab_pass = len(self.slab_off)
        if not self.debug.get("nosample", False):
            gbk = ["gB_" + nm for nm in ("decTm", "Lg", "qkTm", "MT", "M", "Na", "Nb", "Nc", "Nd", "Pa", "Pb", "Pc", "Pd")]
            P.add("dve", lambda e: e.memset(self.gsm[:, 60:64], 0.0), gbk, gbk + ["kTm", "qTm"])
            self.layer_pass(0, NSAMP, sample=True, last=True)
        P.add("sp", None, r=self.outkeys)
        P.finalize()
        P.emit(self.es)

    def cf(self, name):
        return self.cstf[:, CSTF_NAMES.index(name), :]

    def cb(self, name):
        return self.cstb[:, CSTB_NAMES.index(name), :]

    def cb4(self, name):
        i = CSTB_NAMES.index(name)
        return self.cstb[:, i:i + 1, :].to_broadcast([128, 4, 128])

    def cf4(self, name):
        i = CSTF_NAMES.index(name)
        return self.cstf[:, i:i + 1, :].to_broadcast([128, 4, 128])

    def tmpf(self):
        i = self.tfi % len(self.tf)
        self.tfi += 1
        return self.tf[i], "tf%d" % i

    def tmpb(self):
        i = self.tbi % len(self.tb16)
        self.tbi += 1
        return self.tb16[i], "tb%d" % i

    def setup(self):
        d = self.dma
        d("sp", self.cstf[:], self.cst_d, "c0", [], ["cstf"])
        nb = len(CSTB_NAMES) * 128
        for k in range(0, nb, 1024):
            n = min(1024, nb - k)
            d("sp", self.t1[:, 0:n], self.cst2_d[:, k:k + n], "c1", [], ["t1"])
            self.cp("dve", self.cstb[:].rearrange("p c d -> p (c d)")[:, k:k + n], self.t1[:, 0:n], ["t1"], ["cstb"])
        d("sp", self.t2[:, 0:256], self.i16_d, "c1", [], ["t2"])
        self.cp("dve", self.i16b[:], self.t2[:, 0:256], ["t2"], ["i16b"])
        d("sp", self.wcq[:].rearrange("p c j -> p (c j)"), self.wcq_d, "c2", [], ["wcq"])
        d("sp", self.wcs[:].rearrange("p c j -> p (c j)"), self.wcs_d, "c3", [], ["wcs"])
        d("sp", self.wonb[:], self.won_d.partition_broadcast(128), "c4", [], ["wonb"])
        d("sp", self.smallb[:], self.smallp.rearrange("a b -> (a b)").partition_broadcast(128), "c5", [], ["smallb"])
        self.act(self.negA[:], self.smallb[:, 0:8], AF.Exp, ["smallb"], ["negA"])
        self.ts("dve", self.negA[:], self.negA[:], -1.0, ALU.mult, ["negA"], ["negA"])
        self.P.add("dve", lambda e: e.memset(self.S[:], 0.0), [], ["S0", "S1"])
        self.P.add("dve", lambda e: e.memset(self.Sbf[:], 0.0), [], ["Sbf0", "Sbf1"])
        self.P.add("dve", lambda e: e.memset(self.histq[:], 0.0), [], ["histq"])
        self.P.add("dve", lambda e: e.memset(self.hists[:], 0.0), [], ["hists"])

    def make_xT(self, tb, TB, bank):
        ps, psk = self.psb[bank], "ps%d" % bank
        self.cp("act", self.xb16[:TB, :], self.xres[:TB, tb, :], ["xres%d" % tb], ["xb16"])
        for c in range(8):
            self.tr(ps[:, c * TB:(c + 1) * TB], self.xb16[:TB, c * 128:(c + 1) * 128], self.cb("ident")[:TB, :TB],
                    ["xb16", "cstb"], [psk])
        self.cp("dve", self.xT[:, :, tb * TB:(tb + 1) * TB],
                ps[:, 0:8 * TB].rearrange("p (c t) -> p c t", c=8), [psk], ["xT"])

    def layer_norm(self, idx, NB, TB, final_out=None):
        self.dma("aux", self.gbt[:, 0, :], self.lnp[2 * idx, :].partition_broadcast(128), "gb0", [], ["gbt0"])
        self.dma("aux", self.gbt[:, 1, :], self.lnp[2 * idx + 1, :].partition_broadcast(128), "gb1", [], ["gbt1"])
        eps = LN_EPS / (ALPHA * ALPHA)
        for tb in range(NB):
            xr = self.xres[:TB, tb, :]
            xk = "xres%d" % tb
            st = self.stat
            self.P.add("dve", lambda e, xr=xr: e.bn_stats(st[:TB, 0:6], xr[:, 0:512]), [xk], ["stat"])
            self.P.add("dve", lambda e, xr=xr: e.bn_stats(st[:TB, 6:12], xr[:, 512:1024]), [xk], ["stat"])
            self.P.add("dve", lambda e: e.bn_aggr(st[:TB, 12:14], st[:TB, 0:12]), ["stat"], ["stat"])
            self.act(st[:TB, 14:15], st[:TB, 13:14], AF.Sqrt, ["stat"], ["stat"], bias=eps)
            self.P.add("dve", lambda e: e.reciprocal(st[:TB, 15:16], st[:TB, 14:15]), ["stat"], ["stat"])
            self.stt(self.t1[:TB, :], xr, st[:TB, 12:13], self.gbt[:TB, 0, :], ALU.subtract, ALU.mult,
                     [xk, "stat", "gbt0"], ["t1"])
            self.stt(xr, self.t1[:TB, :], st[:TB, 15:16], self.gbt[:TB, 1, :], ALU.mult, ALU.add,
                     ["t1", "stat", "gbt1"], [xk])
            if final_out is not None:
                ok = final_out[1] + str(tb)
                self.dma("aux", final_out[0][tb * TB:(tb + 1) * TB, :], xr, "yo%d" % tb, [xk], [ok])
                self.outkeys.append(ok)
            else:
                self.make_xT(tb, TB, (2 * tb) % 8)

    def ffn(self, pfx, NB, TB):
        NT = NB * TB
        for j0 in range(0, NJ, 2):
            (wg, wu), wk = self.slab([(pfx + "_w_gate", 0, D, j0 * 128, j0 * 128 + 256),
                                      (pfx + "_w_up", 0, D, j0 * 128, j0 * 128 + 256)])
            for jj in range(2):
                j = j0 + jj
                bg, bu = 2 * (j % 2), 2 * (j % 2) + 1
                for kc in range(8):
                    self.mm(self.ps[bg][:, :NT], wg[:, kc, jj * 128:(jj + 1) * 128], self.xT[:, kc, :NT],
                            [wk, "xT"], ["ps%d" % bg], start=(kc == 0), stop=(kc == 7))
                for kc in range(8):
                    self.mm(self.ps[bu][:, :NT], wu[:, kc, jj * 128:(jj + 1) * 128], self.xT[:, kc, :NT],
                            [wk, "xT"], ["ps%d" % bu], start=(kc == 0), stop=(kc == 7))
                t, tk = self.tmpf()
                self.act(t[:, :NT], self.ps[bg][:, :NT], AF.Silu, ["ps%d" % bg], [tk])
                self.tt("dve", self.A1[:, j, :NT], t[:, :NT], self.ps[bu][:, :NT], ALU.mult,
                        [tk, "ps%d" % bu], ["A1.%d" % j])
        for j0 in range(0, NJ, 4):
            j1 = min(NJ, j0 + 4)
            (wd,), wk = self.slab([(pfx + "_w_down", j0 * 128, j1 * 128, 0, D)])
            for jj in range(j1 - j0):
                j = j0 + jj
                for tb in range(NB):
                    for nh in range(2):
                        b = tb * 2 + nh
                        self.mm(self.ps[b][:TB, :], self.A1[:, j, tb * TB:(tb + 1) * TB], wd[:, jj, nh * 512:(nh + 1) * 512],
                                [wk, "A1.%d" % j], ["ps%d" % b], start=(j == 0), stop=(j == NJ - 1))
        c = 0.5 / ALPHA
        for tb in range(NB):
            for nh in range(2):
                b = tb * 2 + nh
                xr = self.xres[:TB, tb, nh * 512:(nh + 1) * 512]
                self.stt(xr, self.ps[b][:TB, :], c, xr, ALU.mult, ALU.add, ["ps%d" % b, "xres%d" % tb], ["xres%d" % tb])

    def layer_pass(self, pi, NT, sample, last):
        TB = min(128, NT)
        NB = NT // TB
        self.slab_i = 0 if self.recording else self.slab_i
        if not sample:
            src, psrc = self.x[pi * NT:(pi + 1) * NT, :], self.pp[pi * NT:(pi + 1) * NT, :]
            yout = (self.y[pi * NT:(pi + 1) * NT, :], "y%d_" % pi)
        else:
            src, psrc = self.xs, self.psm
            yout = (self.ys, "ys_")
        for tb in range(NB):
            self.dma("aux", self.xres[:TB, tb, :], src[tb * TB:(tb + 1) * TB, :], "x%d" % tb, [], ["xres%d" % tb])
            self.make_xT(tb, TB, tb % 8)
        stop = self.debug.get("stop")
        self.ffn("ffn1", NB, TB)
        if stop == "ffn1":
            return self.dump(yout, NB, TB)
        self.layer_norm(0, NB, TB)
        if stop == "ln1":
            return self.dump(yout, NB, TB)
        self.mixers(pi, NB, TB, sample, last)
        if stop == "mix":
            return self.dump(yout, NB, TB)
        self.layer_norm(1, NB, TB)
        self.ffn("ffn2", NB, TB)
        self.layer_norm(2, NB, TB)
        if stop == "ln3":
            return self.dump(yout, NB, TB)
        self.ple(psrc, NB, TB)
        self.layer_norm(3, NB, TB, final_out=yout)

    def dump(self, yout, NB, TB):
        for tb in range(NB):
            ok = yout[1] + str(tb)
            self.dma("aux", yout[0][tb * TB:(tb + 1) * TB, :], self.xres[:TB, tb, :], "yo%d" % tb, ["xres%d" % tb], [ok])
            self.outkeys.append(ok)

    def ple(self, psrc, NB, TB):
        NT = NB * TB
        for tb in range(NB):
            pf, pfk = self.tmpf()
            self.dma("aux", pf[:TB, 0:256], psrc[tb * TB:(tb + 1) * TB, :], "pf", [], [pfk])
            self.cp("act", self.pb[:TB, :], pf[:TB, 0:256], [pfk], ["pb"])
            for c in range(2):
                self.tr(self.psb[7][:, c * TB:(c + 1) * TB], self.pb[:TB, c * 128:(c + 1) * 128],
                        self.cb("ident")[:TB, :TB], ["pb", "cstb"], ["ps7"])
            self.cp("dve", self.pT[:, :, tb * TB:(tb + 1) * TB],
                    self.psb[7][:, 0:2 * TB].rearrange("p (c t) -> p c t", c=2), ["ps7"], ["pT"])
        for nh in range(2):
            (wg,), wgk = self.slab([("w_ple_gate", 0, D, nh * 512, (nh + 1) * 512)])
            (wp,), wpk = self.slab([("w_ple_proj", 0, 256, nh * 512, (nh + 1) * 512)])
            for tb in range(NB):
                bg, bp = 2 * (tb % 2), 2 * (tb % 2) + 1
                for kc in range(8):
                    self.mm(self.ps[bg][:TB, :], self.xT[:, kc, tb * TB:(tb + 1) * TB], wg[:, kc, :],
                            [wgk, "xT"], ["ps%d" % bg], start=(kc == 0), stop=(kc == 7))
                for kc in range(2):
                    self.mm(self.ps[bp][:TB, :], self.pT[:, kc, tb * TB:(tb + 1) * TB], wp[:, kc, :],
                            [wpk, "pT"], ["ps%d" % bp], start=(kc == 0), stop=(kc == 1))
                t, tk = self.tmpf()
                self.act(t[:TB, :512], self.ps[bg][:TB, :], AF.Sigmoid, ["ps%d" % bg], [tk])
                self.tt("dve", t[:TB, :512], t[:TB, :512], self.ps[bp][:TB, :], ALU.mult, [tk, "ps%d" % bp], [tk])
                xr = self.xres[:TB, tb, nh * 512:(nh + 1) * 512]
                self.stt(xr, t[:TB, :512], 1.0 / ALPHA, xr, ALU.mult, ALU.add, [tk, "xres%d" % tb], ["xres%d" % tb])

    def conv_chunk(self, psbank, NT, taps_hist, wts, ntap, hist_tile, hist_key, sample, src_is_psum=True, src=None):
        H_ = ntap - 1
        cbt, cbk = self.tmpf()
        if src_is_psum:
            self.cp("act", cbt[:, H_:H_ + NT], self.ps[psbank][:, :NT], ["ps%d" % psbank], [cbk])
        else:
            src(cbt[:, H_:H_ + NT], cbk)
        if not sample:
            self.cp("act", cbt[:, 0:H_], hist_tile, [hist_key], [cbk])
            self.cp("act", hist_tile, cbt[:, NT:NT + H_], [cbk], [hist_key])
            taps = [cbt[:, j:j + NT] for j in range(ntap)]
            tr_ = [cbk]
        else:
            taps = [taps_hist[j] for j in range(H_)] + [cbt[:, H_:H_ + NT]]
            tr_ = [cbk, "hsamp"]
        acc, ak = self.tmpf()
        self.ts("dve", acc[:, :NT], taps[0], wts[0], ALU.mult, tr_ + ["wc"], [ak])
        for j in range(1, ntap):
            self.stt(acc[:, :NT], taps[j], wts[j], acc[:, :NT], ALU.mult, ALU.add, tr_ + ["wc", ak], [ak])
        return acc, ak, cbt, cbk

    def mixers(self, pi, NB, TB, sample, last):
        NT = NB * TB
        A1 = self.A1
        if sample:
            self.load_sample_hist()
        for g in range(6):
            (wq,), wk = self.slab([("w_in", 0, D, g * 512, (g + 1) * 512)])
            for jj in range(4):
                c = g * 4 + jj
                bank = c % 2
                for kc in range(8):
                    self.mm(self.ps[bank][:, :NT], wq[:, kc, jj * 128:(jj + 1) * 128], self.xT[:, kc, :NT],
                            [wk, "xT"], ["ps%d" % bank], start=(kc == 0), stop=(kc == 7))
                th = [self.hsq[:, c, j, :] for j in range(3)] if sample else None
                wts = [self.wcq[:, c, j:j + 1] for j in range(4)]
                acc, ak, cbt, cbk = self.conv_chunk(bank, NT, th, wts, 4, self.histq[:, c, :], "histq%d" % c, sample)
                if sample:
                    self.tr(self.ps[6][:NT, (c % 4) * 128:(c % 4 + 1) * 128], cbt[:, 3:3 + NT], self.cf("ident"),
                            [cbk, "cstf"], ["ps6"])
                    if c % 4 == 3:
                        stg, stk = self.stage(c // 8)
                        self.cp("act", stg[:NT, (c % 8 - 3) * 128:(c % 8 + 1) * 128], self.ps[6][:NT, :], ["ps6"], [stk])
                if c >= 16:
                    self.act(A1[:, c, :NT], acc[:, :NT], AF.Silu, [ak], ["A1.%d" % c])
                else:
                    so, sk = self.tmpf()
                    self.act(so[:, :NT], acc[:, :NT], AF.Silu, [ak], [sk])
                    sq, sqk = self.tmpb()
                    self.act(sq[:, :NT], so[:, :NT], AF.Square, [sk], [sqk])
                    b2 = 2 + c % 2
                    self.mm(self.ps[b2][:, :NT], self.cb("ones"), sq[:, :NT], [sqk, "cstb"], ["ps%d" % b2])
                    sd, sdk = self.tmpf()
                    self.act(sd[:, :NT], self.ps[b2][:, :NT], AF.Sqrt, ["ps%d" % b2], [sdk], bias=L2_EPS)
                    self.P.add("dve", lambda e, sd=sd: e.reciprocal(sd[:, :NT], sd[:, :NT]), [sdk], [sdk])
                    const = 128.0 ** -0.5 if c < 8 else 1.0
                    self.stt(A1[:, c, :NT], so[:, :NT], const, sd[:, :NT], ALU.mult, ALU.mult, [sk, sdk], ["A1.%d" % c])
        if sample:
            for k in range(3):
                stg, stk = self.stage(k)
                self.dma("aux", self.sqs[:, 2, k * 1024:(k + 1) * 1024], stg[:NSAMP, :], "so0", [stk], ["sqs2_%d" % k])
                self.outkeys.append("sqs2_%d" % k)
            self.dma("aux", self.sqs[:, 0:2, :], self.sq[:, 1:3, :], "so1", [], ["sqs01"])
            self.outkeys += ["sqs01"]
        elif last:
            for j in range(3):
                self.dma("aux", self.sqp[j, :].rearrange("(c p) -> p c", p=128), self.histq[:, :, j], "so0",
                         ["histq%d" % c for c in range(24)], ["sqp%d" % j], slow=True)
                self.outkeys.append("sqp%d" % j)
        if self.debug.get("mstop") == "A":
            return
        for nh in range(2):
            (wz,), wk = self.slab([("w_in", 0, D, Z0 + nh * 512, Z0 + (nh + 1) * 512)])
            for tb in range(NB):
                b = 4 + tb % 2
                for kc in range(8):
                    self.mm(self.ps[b][:TB, :], self.xT[:, kc, tb * TB:(tb + 1) * TB], wz[:, kc, :],
                            [wk, "xT"], ["ps%d" % b], start=(kc == 0), stop=(kc == 7))
                self.act(self.ztok[:TB, tb, nh * 512:(nh + 1) * 512], self.ps[b][:TB, :], AF.Silu, ["ps%d" % b], ["ztok%d" % tb])
        (wba,), wk = self.slab([("w_in", 0, D, BETA0, BETA0 + 16)])
        for tb in range(NB):
            for kc in range(8):
                self.mm(self.ps[6][:TB, 0:16], self.xT[:, kc, tb * TB:(tb + 1) * TB], wba[:, kc, :],
                        [wk, "xT"], ["ps6"], start=(kc == 0), stop=(kc == 7))
            self.act(self.beta[:TB, tb, :], self.ps[6][:TB, 0:8], AF.Sigmoid, ["ps6"], ["beta"])
            self.tt("dve", self.batok[:TB, tb, 8:16], self.ps[6][:TB, 8:16], self.smallb[:TB, 8:16], ALU.add,
                    ["ps6", "smallb"], ["batok"])
        for tb in range(NB):
            self.act(self.batok[:TB, tb, 0:8], self.batok[:TB, tb, 8:16], AF.Exp, ["batok"], ["batok"])
        for tb in range(NB):
            self.act(self.batok[:TB, tb, 0:8], self.batok[:TB, tb, 0:8], AF.Ln, ["batok"], ["batok"], bias=1.0)
            self.tt("dve", self.gtok[:TB, tb, :], self.batok[:TB, tb, 0:8], self.negA[:TB, :], ALU.mult,
                    ["batok", "negA"], ["gtok"])
        if self.debug.get("mstop") == "B":
            return
        if sample:
            self.gdn_sample()
        else:
            for tb in range(NB):
                self.gdn_block(tb)
            if last:
                self.dma("aux", self.sgp.rearrange("h k v -> k h v"), self.S[:], "so1", ["S0", "S1"], ["sgp"])
                self.outkeys.append("sgp")
        if self.debug.get("mstop") == "C":
            return
        for c in range(8):
            (wB, wC, wH), wk = self.slab([("w_in", 0, D, B0 + c * 128, B0 + (c + 1) * 128),
                                          ("w_in", 0, D, C0 + c * 128, C0 + (c + 1) * 128),
                                          ("w_in", 0, D, H0 + c * 128, H0 + (c + 1) * 128)])
            bB, bC, bH = 0 + 3 * (c % 2), 1 + 3 * (c % 2), 2 + 3 * (c % 2)
            for (w_, b_) in ((wC, bC), (wH, bH), (wB, bB)):
                for kc in range(8):
                    self.mm(self.ps[b_][:, :NT], w_[:, kc, :], self.xT[:, kc, :NT], [wk, "xT"], ["ps%d" % b_],
                            start=(kc == 0), stop=(kc == 7))
            ct, ck = self.tmpf()
            self.cp("act", ct[:, :NT], self.ps[bC][:, :NT], ["ps%d" % bC], [ck])

            def src(dst, dk, ct=ct, ck=ck, bH=bH):
                self.tt("dve", dst, ct[:, :NT], self.ps[bH][:, :NT], ALU.mult, [ck, "ps%d" % bH], [dk])
            th = [self.hss[:, c, j, :] for j in range(2)] if sample else None
            wts = [self.wcs[:, c, j:j + 1] for j in range(3)]
            acc, ak, cbt, cbk = self.conv_chunk(None, NT, th, wts, 3, self.hists[:, c, :], "hists%d" % c, sample,
                                                src_is_psum=False, src=src)
            if sample:
                self.tr(self.ps[6][:NT, (c % 4) * 128:(c % 4 + 1) * 128], cbt[:, 2:2 + NT], self.cf("ident"),
                        [cbk, "cstf"], ["ps6"])
                if c % 4 == 3:
                    stg, stk = self.stage(0)
                    self.cp("act", stg[:NT, (c - 3) * 128:(c + 1) * 128], self.ps[6][:NT, :], ["ps6"], [stk])
            self.tt("dve", A1[:, c, :NT], acc[:, :NT], self.ps[bB][:, :NT], ALU.mult, [ak, "ps%d" % bB], ["A1.%d" % c])
        if sample:
            stg, stk = self.stage(0)
            self.dma("aux", self.sss[:, 1, :], stg[:NSAMP, 0:D], "so2", [stk], ["sss1"])
            self.dma("aux", self.sss[:, 0:1, :], self.ssc[:, 1:2, :], "so3", [], ["sss0"])
            self.outkeys += ["sss1", "sss0"]
        elif last:
            for j in range(2):
                self.dma("aux", self.ssp[j, :].rearrange("(c p) -> p c", p=128), self.hists[:, :, j], "so2",
                         ["hists%d" % c for c in range(8)], ["ssp%d" % j], slow=True)
                self.outkeys.append("ssp%d" % j)
        if self.debug.get("mstop") == "D":
            return
        for c in range(8):
            (wpg, wgg, wps, wgs), wk = self.slab([("w_p_gdn", 0, D, c * 128, (c + 1) * 128),
                                                  ("w_in", 0, D, GG0 + c * 128, GG0 + (c + 1) * 128),
                                                  ("w_p_sc", 0, D, c * 128, (c + 1) * 128),
                                                  ("w_in", 0, D, GS0 + c * 128, GS0 + (c + 1) * 128)])
            o = 4 * (c % 2)
            for (w_, b_, rhs_, rk) in ((wpg, o, A1[:, 16:24, :], ["A1.%d" % k for k in range(16, 24)]),
                                       (wgg, o + 1, self.xT, ["xT"]),
                                       (wps, o + 2, A1[:, 0:8, :], ["A1.%d" % k for k in range(8)]),
                                       (wgs, o + 3, self.xT, ["xT"])):
                for kc in range(8):
                    self.mm(self.ps[b_][:, :NT], w_[:, kc, :], rhs_[:, kc, :NT], [wk] + rk, ["ps%d" % b_],
                            start=(kc == 0), stop=(kc == 7))
            s1, s1k = self.tmpf()
            self.act(s1[:, :NT], self.ps[o + 1][:, :NT], AF.Sigmoid, ["ps%d" % (o + 1)], [s1k])
            self.tt("dve", s1[:, :NT], s1[:, :NT], self.ps[o][:, :NT], ALU.mult, [s1k, "ps%d" % o], [s1k])
            s2, s2k = self.tmpf()
            self.act(s2[:, :NT], self.ps[o + 3][:, :NT], AF.Sigmoid, ["ps%d" % (o + 3)], [s2k])
            self.tt("dve", s2[:, :NT], s2[:, :NT], self.ps[o + 2][:, :NT], ALU.mult, [s2k, "ps%d" % (o + 2)], [s2k])
            self.tt("dve", A1[:, 8 + c, :NT], s1[:, :NT], s2[:, :NT], ALU.add, [s1k, s2k], ["A1.%d" % (8 + c)])
        for nh in range(2):
            (wo,), wk = self.slab([("w_o", 0, D, nh * 512, (nh + 1) * 512)])
            for tb in range(NB):
                b = tb % 2
                for kc in range(8):
                    self.mm(self.ps[b][:TB, :], A1[:, 8 + kc, tb * TB:(tb + 1) * TB], wo[:, kc, :],
                            [wk, "A1.%d" % (8 + kc)], ["ps%d" % b], start=(kc == 0), stop=(kc == 7))
                xr = self.xres[:TB, tb, nh * 512:(nh + 1) * 512]
                self.stt(xr, self.ps[b][:TB, :], 1.0 / ALPHA, xr, ALU.mult, ALU.add, ["ps%d" % b, "xres%d" % tb], ["xres%d" % tb])

    def onorm_and_T(self, tb, TB):
        o3 = self.otok[:TB, :].rearrange("p (h d) -> p h d", h=H)
        t13 = self.t1[:TB, :].rearrange("p (h d) -> p h d", h=H)
        t23 = self.t2[:TB, :].rearrange("p (h d) -> p h d", h=H)
        st = self.stat
        self.act(self.t1[:TB, :], self.otok[:TB, :], AF.Square, ["otok"], ["t1"])
        self.P.add("dve", lambda e: e.tensor_reduce(st[:TB, 16:24], t13, AX.X, ALU.add), ["t1"], ["stat"])
        self.act(st[:TB, 16:24], st[:TB, 16:24], AF.Sqrt, ["stat"], ["stat"], bias=RMS_EPS, scale=1.0 / 128.0)
        self.P.add("dve", lambda e: e.reciprocal(st[:TB, 24:32], st[:TB, 16:24]), ["stat"], ["stat"])
        self.tt("dve", t13, o3, st[:TB, 24:32].unsqueeze(2).to_broadcast([TB, H, 128]), ALU.mult, ["otok", "stat"], ["t1"])
        z3 = self.ztok[:TB, tb, :].rearrange("p (h d) -> p h d", h=H)
        self.tt("dve", t23, z3, self.wonb[:TB, :].unsqueeze(1).to_broadcast([TB, H, 128]), ALU.mult,
                ["ztok%d" % tb, "wonb"], ["t2"])
        self.tt("dve", self.xb16[:TB, :], self.t1[:TB, :], self.t2[:TB, :], ALU.mult, ["t1", "t2"], ["xb16"])
        for c in range(8):
            self.tr(self.psb[7][:, c * TB:(c + 1) * TB], self.xb16[:TB, c * 128:(c + 1) * 128], self.cb("ident")[:TB, :TB],
                    ["xb16", "cstb"], ["ps7"])
        self.cp("act", self.A1[:, 16:24, tb * TB:(tb + 1) * TB],
                self.psb[7][:, 0:8 * TB].rearrange("p (c t) -> p c t", c=8), ["ps7"],
                ["A1.%d" % k for k in range(16, 24)])

    def inv_chain(self, tb, hg, G, gp, pb):
        A1 = self.A1
        blk = slice(tb * 128, (tb + 1) * 128)
        g8 = self.gtok[:, tb, :]
        hs = [hg * 4 + hh for hh in range(4)]
        K = lambda nm: gp + nm
        pk = ["ps%d" % x for x in pb]
        ps = [self.ps[x] for x in pb]
        psb = [self.psb[x] for x in pb]
        f4 = lambda t: t.rearrange("p h d -> p (h d)")
        kq_r = ["A1.%d" % (8 + h) for h in hs] + ["A1.%d" % h for h in hs]
        for hh, h in enumerate(hs):
            self.ts("dve", G["Lg"][:, hh, :], self.cf("ltri"), g8[:, h:h + 1], ALU.mult, ["cstf", "gtok"], [K("Lg")])
        yield
        for hh, h in enumerate(hs):
            cs = slice(hh * 128, (hh + 1) * 128)
            self.mm(ps[0][:, cs], self.cf("su"), G["Lg"][:, hh, :], ["cstf", K("Lg")], [pk[0]])
            self.mm(ps[1][:, cs], A1[:, 8 + h, blk], A1[:, 8 + h, blk], kq_r, [pk[1]])
            self.mm(ps[2][:, cs], A1[:, 8 + h, blk], A1[:, h, blk], kq_r, [pk[2]])
        yield
        self.act(f4(G["decTm"]), ps[0][:, :], AF.Exp, [pk[0]], [K("decTm")])
        yield
        self.tt("dve", G["decTm"], G["decTm"], self.cf4("muincl"), ALU.mult, [K("decTm"), "cstf"], [K("decTm")])
        yield
        self.tt("dve", f4(G["qkTm"]), ps[2][:, :], f4(G["decTm"]), ALU.mult, [pk[2], K("decTm")], [K("qkTm")])
        self.tt("dve", f4(G["Lg"]), ps[1][:, :], f4(G["decTm"]), ALU.mult, [pk[1], K("decTm")], [K("Lg")])
        yield
        self.tt("dve", G["MT"], G["Lg"],
                self.beta[:, tb, hg * 4:hg * 4 + 4].unsqueeze(2).to_broadcast([128, 4, 128]), ALU.mult,
                [K("Lg"), "beta"], [K("MT")])
        yield
        for hh in range(4):
            self.tr(psb[3][:, hh * 128:(hh + 1) * 128], G["MT"][:, hh, :], self.cb("ident"), [K("MT"), "cstb"], [pk[3]])
        yield
        self.cp("act", f4(G["M"]), psb[3][:, 0:512], [pk[3]], [K("M")])
        yield
        Nn, Nt, N2, N2t = "Na", "Nb", "Nc", "Nd"
        Pn, Pt, Pn2, Pt2 = "Pa", "Pb", "Pc", "Pd"
        self.tt("dve", G[Nn], G["M"], self.cb4("mndn"), ALU.mult, [K("M"), "cstb"], [K(Nn)])
        self.tt("dve", G[Nt], G["MT"], self.cb4("mndtn"), ALU.mult, [K("MT"), "cstb"], [K(Nt)])
        yield
        self.tt("dve", G[Pn], G[Nn], self.cb4("ident"), ALU.add, [K(Nn), "cstb"], [K(Pn)])
        self.tt("dve", G[Pt], G[Nt], self.cb4("ident"), ALU.add, [K(Nt), "cstb"], [K(Pt)])
        nstep = int(np.log2(NBK)) - 1
        for s_ in range(nstep):
            for hh in range(4):
                cs = slice(hh * 128, (hh + 1) * 128)
                self.mm(ps[0][:, cs], G[Nt][:, hh, :], G[Nn][:, hh, :], [K(Nt), K(Nn)], [pk[0]])
                self.mm(ps[1][:, cs], G[Nn][:, hh, :], G[Nt][:, hh, :], [K(Nt), K(Nn)], [pk[1]])
            yield
            self.cp("act", f4(G[N2]), ps[0][:, :], [pk[0]], [K(N2)])
            self.cp("act", f4(G[N2t]), ps[1][:, :], [pk[1]], [K(N2t)])
            yield
            for hh in range(4):
                cs = slice(hh * 128, (hh + 1) * 128)
                self.mm(ps[2][:, cs], G[N2t][:, hh, :], G[Pn][:, hh, :], [K(N2t), K(Pn)], [pk[2]])
                self.mm(ps[3][:, cs], G[N2][:, hh, :], G[Pt][:, hh, :], [K(N2), K(Pt)], [pk[3]])
            yield
            self.tt("dve", f4(G[Pn2]), f4(G[Pn]), ps[2][:, :], ALU.add, [K(Pn), pk[2]], [K(Pn2)])
            self.tt("dve", f4(G[Pt2]), f4(G[Pt]), ps[3][:, :], ALU.add, [K(Pt), pk[3]], [K(Pt2)])
            yield
            Nn, Nt, N2, N2t = N2, N2t, Nn, Nt
            Pn, Pt, Pn2, Pt2 = Pn2, Pt2, Pn, Pt
        T, U, T2, U2 = Pn, Pt, Pn2, Pt2
        E_, F_, X_, Y_ = Nn, Nt, N2, N2t
        b = NBK
        while b < 128:
            lastlvl = (b == 64)
            self.tt("dve", G[E_], G["M"], self.cb4("me%d" % b), ALU.mult, [K("M"), "cstb"], [K(E_)])
            if not lastlvl:
                self.tt("dve", G[F_], G["MT"], self.cb4("me%dt" % b), ALU.mult, [K("MT"), "cstb"], [K(F_)])
            yield
            for hh in range(4):
                cs = slice(hh * 128, (hh + 1) * 128)
                self.mm(ps[0][:, cs], G[E_][:, hh, :], G[U][:, hh, :], [K(E_), K(U)], [pk[0]])
                if not lastlvl:
                    self.mm(ps[1][:, cs], G[F_][:, hh, :], G[T][:, hh, :], [K(F_), K(T)], [pk[1]])
            yield
            self.cp("act", f4(G[Y_]), ps[0][:, :], [pk[0]], [K(Y_)])
            if not lastlvl:
                self.cp("act", f4(G[X_]), ps[1][:, :], [pk[1]], [K(X_)])
            yield
            for hh in range(4):
                cs = slice(hh * 128, (hh + 1) * 128)
                self.mm(ps[2][:, cs], G[T][:, hh, :], G[Y_][:, hh, :], [K(T), K(Y_)], [pk[2]])
                if not lastlvl:
                    self.mm(ps[3][:, cs], G[U][:, hh, :], G[X_][:, hh, :], [K(U), K(X_)], [pk[3]])
            yield
            self.tt("dve", f4(G[U2]), f4(G[U]), ps[2][:, :], ALU.subtract, [K(U), pk[2]], [K(U2)])
            if not lastlvl:
                self.tt("dve", f4(G[T2]), f4(G[T]), ps[3][:, :], ALU.subtract, [K(T), pk[3]], [K(T2)])
            yield
            T, U, T2, U2 = T2, U2, T, U
            b *= 2
        self.inv_result[hg] = U

    def scan_chain(self, tb, hg, G, gp, U, bx, by):
        A1 = self.A1
        blk = slice(tb * 128, (tb + 1) * 128)
        sm = self.gsm
        hs = [hg * 4 + hh for hh in range(4)]
        hsl = slice(hg * 4, hg * 4 + 4)
        X, Y = self.ps[bx], self.ps[by]
        xk, yk = "ps%d" % bx, "ps%d" % by
        X3 = X[:, :].rearrange("p (h d) -> p h d", h=4)
        Y3 = Y[:, :].rearrange("p (h d) -> p h d", h=4)
        bc = lambda ap: ap.unsqueeze(2).to_broadcast([128, 4, 128])
        Sk, Sbk = "S%d" % hg, "Sbf%d" % hg
        for hh, h in enumerate(hs):
            cs = slice(hh * 128, (hh + 1) * 128)
            self.mm(X[:, cs], A1[:, 8 + h, blk], self.Sbf[:, h, :], ["A1.%d" % (8 + h), Sbk], [xk])
            self.mm(Y[:, cs], A1[:, h, blk], self.Sbf[:, h, :], ["A1.%d" % h, Sbk], [yk])
        yield
        tS, tSk = self.tmpf()
        tS3 = tS[:, 0:512].rearrange("p (h d) -> p h d", h=4)
        self.tt("dve", tS3, X3, bc(sm[:, 24 + hg * 4:28 + hg * 4]), ALU.mult, [xk, "gsm"], [tSk])
        o1, o1k = self.tmpf()
        o13 = o1[:, 0:512].rearrange("p (h d) -> p h d", h=4)
        self.tt("dve", o13, Y3, bc(sm[:, 16 + hg * 4:20 + hg * 4]), ALU.mult, [yk, "gsm"], [o1k])
        yield
        r, rk = self.tmpb()
        r3 = r[:, :].rearrange("p (h d) -> p h d", h=4)
        self.tt("dve", r3, tS3, self.vtok[:, hsl, :], ALU.add, [tSk, "vtok"], [rk])
        yield
        for hh in range(4):
            cs = slice(hh * 128, (hh + 1) * 128)
            self.mm(X[:, cs], G[U][:, hh, :], r3[:, hh, :], [gp + U, rk], [xk])
        yield
        vn, vk = self.tmpb()
        vn3 = vn[:, :].rearrange("p (h d) -> p h d", h=4)
        self.tt("dve", vn3, X3, bc(self.beta[:, tb, hsl]), ALU.mult, [xk, "beta"], [vk])
        yield
        for hh, h in enumerate(hs):
            cs = slice(hh * 128, (hh + 1) * 128)
            self.mm(Y[:, cs], G["qkTm"][:, hh, :], vn3[:, hh, :], [gp + "qkTm", vk], [yk])
            self.mm(X[:, cs], self.kdec[:, h, :], vn3[:, hh, :], ["kdec", vk], [xk])
        yield
        self.tt("dve", self.otok[:, hg * 512:(hg + 1) * 512], o1[:, 0:512], Y[:, :], ALU.add, [o1k, yk], ["otok"])
        self.tt("dve", self.S[:, hsl, :], self.S[:, hsl, :], bc(sm[:, 40 + hg * 4:44 + hg * 4]), ALU.mult, [Sk, "gsm"], [Sk])
        yield
        self.tt("dve", self.S[:, hsl, :], self.S[:, hsl, :], X3, ALU.add, [Sk, xk], [Sk])
        yield
        self.cp("act", self.Sbf[:, hsl, :], self.S[:, hsl, :], [Sk], [Sbk])

    def lockstep(self, gens):
        gens = list(gens)
        while gens:
            nxt = []
            for g in gens:
                try:
                    next(g)
                    nxt.append(g)
                except StopIteration:
                    pass
            gens = nxt

    def gdn_block(self, tb):
        A1 = self.A1
        blk = slice(tb * 128, (tb + 1) * 128)
        sm = self.gsm
        g8 = self.gtok[:, tb, :]
        for (dst, dk, u0, bank) in ((self.ktok, "ktok", 8, 5), (self.vtok, "vtok", 16, 6)):
            for h in range(H):
                self.tr(self.psb[bank][:, h * 128:(h + 1) * 128], A1[:, u0 + h, blk], self.cb("ident"),
                        ["A1.%d" % (u0 + h), "cstb"], ["ps%d" % bank])
            self.cp("act", dst[:].rearrange("p h d -> p (h d)"), self.psb[bank][:, 0:1024], ["ps%d" % bank], [dk])
        self.mm(self.ps[7][:, 0:8], self.cf("ltri"), g8, ["cstf", "gtok"], ["ps7"])
        self.mm(self.ps[7][:, 8:16], self.cf("ones"), g8, ["cstf", "gtok"], ["ps7"])
        self.cp("dve", sm[:, 0:16], self.ps[7][:, 0:16], ["ps7"], ["gsm"])
        self.act(sm[:, 16:24], sm[:, 0:8], AF.Exp, ["gsm"], ["gsm"])
        self.ts("dve", sm[:, 24:32], sm[:, 16:24], -1.0, ALU.mult, ["gsm"], ["gsm"])
        self.tt("dve", sm[:, 32:40], sm[:, 8:16], sm[:, 0:8], ALU.subtract, ["gsm"], ["gsm"])
        self.act(sm[:, 32:40], sm[:, 32:40], AF.Exp, ["gsm"], ["gsm"])
        self.act(sm[:, 40:48], sm[:, 8:16], AF.Exp, ["gsm"], ["gsm"])
        self.tt("dve", self.kdec[:], self.ktok[:], sm[:, 32:40].unsqueeze(2).to_broadcast([128, H, 128]), ALU.mult,
                ["ktok", "gsm"], ["kdec"])
        self.inv_result = {}
        self.lockstep([self.inv_chain(tb, hg, self.gqs[hg][0], self.gqs[hg][1], [4 * hg + i for i in range(4)])
                       for hg in range(2)])
        self.lockstep([self.scan_chain(tb, hg, self.gqs[hg][0], self.gqs[hg][1], self.inv_result[hg], 2 * hg, 2 * hg + 1)
                       for hg in range(2)])
        self.onorm_and_T(tb, 128)

    def stage(self, k):
        return [(self.t1, "t1"), (self.t2, "t2"), (self.otok, "otok")][k]

    def load_sample_hist(self):
        for (srcd, dst, nch, nj) in ((self.sq, self.hsq, 24, 3), (self.ssc, self.hss, 8, 2)):
            for j in range(nj):
                for k in range(nch // 8):
                    t, tk = self.stage(k)
                    self.dma("aux", t[:NSAMP, :], srcd[:, j, k * 1024:(k + 1) * 1024], "hl", [], [tk])
                    for cc in range(8):
                        self.tr(self.ps[6][:, cc * NSAMP:(cc + 1) * NSAMP], t[:NSAMP, cc * 128:(cc + 1) * 128],
                                self.cf("ident")[:NSAMP, :NSAMP], [tk, "cstf"], ["ps6"])
                    self.cp("dve", dst[:, k * 8:(k + 1) * 8, j, :],
                            self.ps[6][:, 0:8 * NSAMP].rearrange("p (c b) -> p c b", c=8), ["ps6"], ["hsamp"])

    def gdn_sample(self):
        A1 = self.A1
        NS = NSAMP
        sm = self.gsm
        st = self.stat
        for (dst, dk, u0, bank) in ((self.qtok[:NS, :], "qtok", 0, 4), (self.ktok[:NS].rearrange("p h d -> p (h d)"), "ktok", 8, 5),
                                    (self.vtok[:NS].rearrange("p h d -> p (h d)"), "vtok", 16, 6)):
            for h in range(H):
                self.tr(self.psb[bank][:NS, h * 128:(h + 1) * 128], A1[:, u0 + h, 0:NS], self.cb("ident"),
                        ["A1.%d" % (u0 + h), "cstb"], ["ps%d" % bank])
            self.cp("act", dst, self.psb[bank][:NS, 0:1024], ["ps%d" % bank], [dk])
        a = sm[:NS, 0:8]
        self.act(a, self.gtok[:NS, 0, :], AF.Exp, ["gtok"], ["gsm"])
        q3 = self.qtok[:NS, :].rearrange("p (h d) -> p h d", h=H)
        t13 = self.t1[:NS, :].rearrange("p (h d) -> p h d", h=H)
        t23 = self.t2[:NS, :].rearrange("p (h d) -> p h d", h=H)
        o3 = self.otok[:NS, :].rearrange("p (h d) -> p h d", h=H)
        self.tt("dve", t13, q3, self.ktok[:NS], ALU.mult, ["qtok", "ktok"], ["t1"])
        self.P.add("dve", lambda e: e.tensor_reduce(sm[:NS, 8:16], t13, AX.X, ALU.add), ["t1"], ["gsm"])
        i16 = self.i16b[:].rearrange("p (a b) -> p a b", a=NS)
        for h in range(H):
            self.tt("dve", self.kTm[:, h, :, :], A1[:, 8 + h:9 + h, 0:NS].to_broadcast([128, NS, NS]), i16, ALU.mult,
                    ["A1.%d" % (8 + h), "i16b"], ["kTm"])
            self.tt("dve", self.qTm[:, h, :, :], A1[:, h:h + 1, 0:NS].to_broadcast([128, NS, NS]), i16, ALU.mult,
                    ["A1.%d" % h, "i16b"], ["qTm"])
        for b in range(NS):
            i2 = b % 2
            self.dma("aux", self.Sin[i2][:], self.sg[b].rearrange("h k v -> k h v"), "sin%d" % i2, [], ["Sin%d" % i2])
            self.cp("act" if b % 2 == 0 else "dve", self.Sinb[i2][:], self.Sin[i2][:], ["Sin%d" % i2], ["Sinb%d" % i2])
            for h in range(H):
                bk, bq = h // 4, 2 + h // 4
                cs = slice((h % 4) * 128, (h % 4 + 1) * 128)
                first = (b == 0 and h % 4 == 0)
                self.mm(self.ps[bk][:NS, cs], self.kTm[:, h, b, :], self.Sinb[i2][:, h, :], ["kTm", "Sinb%d" % i2],
                        ["ps%d" % bk], start=first, stop=(b == NS - 1))
                self.mm(self.ps[bq][:NS, cs], self.qTm[:, h, b, :], self.Sinb[i2][:, h, :], ["qTm", "Sinb%d" % i2],
                        ["ps%d" % bq], start=first, stop=(b == NS - 1))
        a_b = a.unsqueeze(2).to_broadcast([NS, H, 128])
        for half in range(2):
            hsl = slice(half * 4, half * 4 + 4)
            k3 = self.ps[half][:NS, :].rearrange("p (h d) -> p h d", h=4)
            qs3 = self.ps[2 + half][:NS, :].rearrange("p (h d) -> p h d", h=4)
            ab = a[:, hsl].unsqueeze(2).to_broadcast([NS, 4, 128])
            self.tt("dve", t13[:, hsl, :], k3, ab, ALU.mult, ["ps%d" % half, "gsm"], ["t1"])
            self.tt("dve", t13[:, hsl, :], self.vtok[:NS, hsl, :], t13[:, hsl, :], ALU.subtract, ["vtok", "t1"], ["t1"])
            self.tt("dve", t13[:, hsl, :], t13[:, hsl, :],
                    self.beta[:NS, 0, hsl].unsqueeze(2).to_broadcast([NS, 4, 128]), ALU.mult, ["t1", "beta"], ["t1"])
            self.tt("dve", t23[:, hsl, :], qs3, ab, ALU.mult, ["ps%d" % (2 + half), "gsm"], ["t2"])
            self.tt("dve", o3[:, hsl, :], t13[:, hsl, :], sm[:NS, 8 + half * 4:12 + half * 4].unsqueeze(2).to_broadcast([NS, 4, 128]),
                    ALU.mult, ["t1", "gsm"], ["otok"])
            self.tt("dve", o3[:, hsl, :], o3[:, hsl, :], t23[:, hsl, :], ALU.add, ["otok", "t2"], ["otok"])
        dbf = self.xb16
        self.cp("act", dbf[:NS, :], self.t1[:NS, :], ["t1"], ["xb16"])
        ad = self.t2[:NS, 0:128].rearrange("p (b h) -> p b h", b=NS)
        idr = self.cf("ident")[:NS, 0:NS].unsqueeze(2).to_broadcast([NS, NS, H])
        self.tt("dve", ad, a.unsqueeze(1).to_broadcast([NS, NS, H]), idr, ALU.mult, ["gsm", "cstf"], ["t2"])
        self.mm(self.ps[4][:, 0:128], self.cf("ones")[:NS, :], self.t2[:NS, 0:128], ["cstf", "t2"], ["ps4"])
        self.cp("dve", self.abc[:], self.ps[4][:, 0:128], ["ps4"], ["abc"])
        kflat = self.ktok[:NS].rearrange("p h d -> p (h d)")
        for b in range(NS):
            i2 = b % 2
            self.dma("aux", self.Sin[i2][:], self.sg[b].rearrange("h k v -> k h v"), "sin%d" % i2, [], ["Sin%d" % i2])
            self.ts("dve", self.kmask[i2][:NS, :], kflat, self.cf("ident")[:NS, b:b + 1], ALU.mult,
                    ["ktok", "cstf"], ["kmask%d" % i2])
            for h in range(H):
                pb_ = 5 + h // 4
                cs = slice((h % 4) * 128, (h % 4 + 1) * 128)
                self.mm(self.ps[pb_][:, cs], self.kmask[i2][:NS, h * 128:(h + 1) * 128], dbf[:NS, h * 128:(h + 1) * 128],
                        ["kmask%d" % i2, "xb16"], ["ps%d" % pb_])
                self.stt(self.Sin[i2][:, h, :], self.Sin[i2][:, h, :], self.abc[:, b * 8 + h:b * 8 + h + 1],
                         self.ps[pb_][:, cs], ALU.mult, ALU.add, ["Sin%d" % i2, "abc", "ps%d" % pb_], ["Sin%d" % i2])
            self.dma("aux", self.sgs[b].rearrange("h k v -> k h v"), self.Sin[i2][:], "sout%d" % i2, ["Sin%d" % i2], ["sgs%d" % b])
            self.outkeys.append("sgs%d" % b)
        self.onorm_and_T(0, NS)


_CACHE = {}


WBIG_LEN = 2 * (3 * D * HID) + D * IN_W + 4 * D * D + 256 * D


def pack_wbig(weights, specs, offs, tot):
    out = np.empty((tot,), np.float32)
    for spec, (off, n) in zip(specs, offs):
        parts = []
        for (name, r0, r1, c0, c1) in spec:
            w = weights[name][r0:r1, c0:c1]
            kc = (r1 - r0) // 128
            parts.append(w.reshape(kc, 128, c1 - c0).transpose(1, 0, 2).reshape(128, kc * (c1 - c0)))
        out[off:off + 128 * n] = np.concatenate(parts, axis=1).reshape(-1)
    return out


def kernel(x_prompt, x_sample, p_prompt, p_sample, state_gdn, state_qkv_conv, state_sc_conv,
           ffn1_w_gate, ffn1_w_up, ffn1_w_down, ln1_g, ln1_b,
           w_in, w_conv_qkv, A_log, dt_bias, w_onorm, w_p_gdn, w_conv_sc, w_p_sc, w_o, ln2_g, ln2_b,
           ffn2_w_gate, ffn2_w_up, ffn2_w_down, ln3_g, ln3_b,
           w_ple_gate, w_ple_proj, ln4_g, ln4_b, _debug=None):
    f = lambda a: np.ascontiguousarray(np.asarray(a, dtype=np.float32))
    weights = {"ffn1_w_gate": f(ffn1_w_gate)[0], "ffn1_w_up": f(ffn1_w_up)[0], "ffn1_w_down": f(ffn1_w_down)[0],
               "w_in": f(w_in)[0], "w_p_gdn": f(w_p_gdn)[0], "w_p_sc": f(w_p_sc)[0], "w_o": f(w_o)[0],
               "ffn2_w_gate": f(ffn2_w_gate)[0], "ffn2_w_up": f(ffn2_w_up)[0], "ffn2_w_down": f(ffn2_w_down)[0],
               "w_ple_gate": f(w_ple_gate)[0], "w_ple_proj": f(w_ple_proj)[0]}
    bld = Builder(debug=_debug)
    bld.wbig_len = WBIG_LEN
    nc = bld.build()
    assert bld.slab_tot == WBIG_LEN or _debug, (bld.slab_tot, WBIG_LEN)
    assert bld.slab_tot <= WBIG_LEN
    wbig = np.zeros((WBIG_LEN,), np.float32)
    wbig[:bld.slab_tot] = pack_wbig(weights, bld.slab_specs, bld.slab_off, bld.slab_tot)
    lnp = np.stack([f(ln1_g)[0], f(ln1_b)[0], f(ln2_g)[0], f(ln2_b)[0], f(ln3_g)[0], f(ln3_b)[0], f(ln4_g)[0], f(ln4_b)[0]])
    wcq = np.ascontiguousarray(f(w_conv_qkv)[0].reshape(4, 24, 128).transpose(2, 1, 0).reshape(128, 96))
    wcs = np.ascontiguousarray(f(w_conv_sc)[0].reshape(3, 8, 128).transpose(2, 1, 0).reshape(128, 24))
    smallp = np.stack([f(A_log)[0], f(dt_bias)[0]])
    cst, cst2 = make_consts()
    cst2 = np.ascontiguousarray(cst2.reshape(128, -1))
    i16 = np.ascontiguousarray(np.broadcast_to(np.eye(16, dtype=np.float32).reshape(1, 256), (128, 256)))
    xp = f(x_prompt)
    xsm = f(x_sample)[:, 0, :]
    ppr = f(p_prompt)[0]
    psm = f(p_sample)[0, :, 0, :]
    sg = f(state_gdn)[0]
    sq = f(state_qkv_conv)[0]
    ssc = f(state_sc_conv)[0]
    in_maps = []
    for c in range(8):
        sl = slice(c * NSAMP, (c + 1) * NSAMP)
        in_maps.append({"x": xp[c], "pp": ppr[c], "xs": xsm[sl], "psm": psm[sl], "sg": sg[sl], "sq": sq[sl], "ssc": ssc[sl],
                        "wbig": wbig, "lnp": lnp, "wcq": wcq, "wcs": wcs, "smallp": smallp, "won": f(w_onorm)[0],
                        "cst": cst, "cst2": cst2, "i16": i16})
    ncores = (_debug or {}).get("ncores", 8)
    res = run_bass_kernel_spmd(nc, in_maps[:ncores], core_ids=list(range(ncores)))
    R = list(res.results)
    while len(R) < 8:
        R.append({k: np.zeros_like(v) for k, v in R[0].items()})
    y = np.stack([R[c]["y"] for c in range(8)])
    ys = np.concatenate([R[c]["ys"] for c in range(8)])[:, None, :]
    sgp = np.stack([R[c]["sgp"] for c in range(8)])[None]
    sqp = np.stack([R[c]["sqp"] for c in range(8)])[None]
    ssp = np.stack([R[c]["ssp"] for c in range(8)])[None]
    sgs = np.concatenate([R[c]["sgs"] for c in range(8)])[None]
    sqs = np.concatenate([R[c]["sqs"] for c in range(8)])[None]
    sss = np.concatenate([R[c]["sss"] for c in range(8)])[None]
    return (y.astype(np.float32), ys.astype(np.float32), sgp.astype(np.float32), sqp.astype(np.float32),
            ssp.astype(np.float32), sgs.astype(np.float32), sqs.astype(np.float32), sss.astype(np.float32))
```

```python
import contextlib
import numpy as np
import concourse.bass as bass
import concourse.mybir as mybir
from concourse.bass_utils import run_bass_kernel_spmd

F32 = mybir.dt.float32
BF16 = mybir.dt.bfloat16
AF = mybir.ActivationFunctionType
ALU = mybir.AluOpType
AX = mybir.AxisListType

D = 1024
SEQ = 2048
NSAMP = 16
HID = 2816
NJ = HID // 128
H = 8
QKV_W = 3072
IN_W = 9232
Z0, BETA0, A0, B0, C0, H0, GG0, GS0 = 3072, 4096, 4104, 4112, 5136, 6160, 7184, 8208
ALPHA = 2.0 ** 0.25
LN_EPS = 1e-5
RMS_EPS = 1e-6
L2_EPS = 1e-6
NTP = 512
SLOT = 4096
NSLOT = 4
NBK = 16

COMPUTE = ("pe", "act", "dve", "pool")


class _Op:
    __slots__ = ("eng", "fn", "r", "w", "key", "eidx", "kn", "waits", "done", "inc")

    def __init__(self, eng, fn, r, w, key):
        self.eng = eng
        self.fn = fn
        self.r = r
        self.w = w
        self.key = key
        self.eidx = -1
        self.kn = 0
        self.waits = []
        self.done = None
        self.inc = False


class Prog:
    def __init__(self, nc):
        self.nc = nc
        self.ops = []

    def add(self, eng, fn, r=(), w=(), key=None):
        self.ops.append(_Op(eng, fn, tuple(r), tuple(w), key))

    def finalize(self):
        last_w = {}
        readers = {}
        issue = {e: {} for e in ("pe", "act", "dve", "pool", "sp")}
        ecount = {e: 0 for e in issue}
        kcount = {}
        kops = {}
        eops = {e: [] for e in issue}
        for op in self.ops:
            e = op.eng
            deps = set()
            for res in op.r:
                lw = last_w.get(res)
                if lw is not None:
                    deps.add(lw)
            for res in op.w:
                lw = last_w.get(res)
                if lw is not None:
                    deps.add(lw)
                for rd in readers.get(res, ()):
                    deps.add(rd)
            deps.discard(op)
            clock = issue[e]
            if op.key is None:
                op.eidx = ecount[e]
                ecount[e] += 1
                eops[e].append(op)
            else:
                n = kcount.get(op.key, 0) + 1
                kcount[op.key] = n
                op.kn = n
                kops.setdefault(op.key, []).append(op)
                if n > 1:
                    deps.add(kops[op.key][n - 2])
            best = {}
            dma_deps = []
            for d in deps:
                if d.key is None:
                    b = best.get(d.eng)
                    if b is None or d.eidx > b.eidx:
                        best[d.eng] = d
                else:
                    dma_deps.append(d)
            newclock = None
            for f, d in best.items():
                if clock.get(f, -1) >= d.eidx:
                    continue
                if f == e and op.key is None:
                    if e == "pe":
                        continue
                    if e != "pool" and (op.eidx - d.eidx) > 2:
                        continue
                op.waits.append(("c", f, d.eidx))
                d.inc = True
                if newclock is None:
                    newclock = dict(clock)
                for k, v in d.done.items():
                    if newclock.get(k, -1) < v:
                        newclock[k] = v
            for d in dma_deps:
                kk = ("dma", d.key)
                cur = clock if newclock is None else newclock
                if cur.get(kk, 0) >= d.kn:
                    continue
                op.waits.append(("d", d.key, d.kn))
                if newclock is None:
                    newclock = dict(clock)
                for k, v in d.done.items():
                    if newclock.get(k, -1) < v:
                        newclock[k] = v
            if newclock is not None:
                issue[e] = newclock
                clock = newclock
            done = dict(clock)
            if op.key is None:
                done[e] = op.eidx
            else:
                done[("dma", op.key)] = op.kn
            op.done = done
            for res in op.r:
                readers.setdefault(res, []).append(op)
            for res in op.w:
                last_w[res] = op
                readers[res] = []
        self.rank = {}
        for e, lst in eops.items():
            k = 0
            for op in lst:
                if op.inc:
                    k += 1
                    self.rank[(e, op.eidx)] = k
        self.keys = list(kcount.keys())
        for op in self.ops:
            op.done = None
            if len(op.waits) > 1:
                m = {}
                for t, a, b in op.waits:
                    if (t, a) not in m or m[(t, a)] < b:
                        m[(t, a)] = b
                op.waits = [(t, a, b) for (t, a), b in m.items()]

    def emit(self, es):
        nc = self.nc
        sems = {}
        for e in COMPUTE:
            sems[e] = es.enter_context(nc.semaphore("s_" + e))
        ksem = {}
        for k in self.keys:
            ksem[k] = es.enter_context(nc.semaphore("k_" + str(k)))
        block = es.enter_context(nc.Block())
        rank = self.rank

        def run(ename, eng):
            for op in self.ops:
                if op.eng != ename:
                    continue
                for t, a, b in op.waits:
                    if t == "c":
                        eng.wait_ge(sems[a], rank[(a, b)])
                    else:
                        eng.wait_ge(ksem[a], 16 * b)
                if op.fn is None:
                    continue
                ins = op.fn(eng)
                if op.key is not None:
                    ins.then_inc(ksem[op.key], 16)
                elif op.inc:
                    ins.then_inc(sems[ename], 1)

        @block.tensor
        def _(eng):
            run("pe", eng)

        @block.scalar
        def _(eng):
            run("act", eng)

        @block.vector
        def _(eng):
            run("dve", eng)

        @block.gpsimd
        def _(eng):
            run("pool", eng)

        @block.sync
        def _(eng):
            run("sp", eng)


CSTF_NAMES = ["ident", "ltri", "su", "muincl", "ones"]
CSTB_NAMES = ["ident", "ones", "mndn", "mndtn", "me16", "me16t", "me32", "me32t", "me64", "me64t"]


def make_consts():
    i = np.arange(128)[:, None]
    j = np.arange(128)[None, :]
    c = {}
    c["ident"] = (i == j)
    c["ltri"] = (i <= j)
    c["su"] = (i > j)
    c["muincl"] = (j >= i)
    c["ones"] = np.ones((128, 128), bool)
    nd = (i // NBK == j // NBK) & (i > j)
    c["mndn"] = -1.0 * nd
    c["mndtn"] = -1.0 * nd.T
    for b in (16, 32, 64):
        e = (i // (2 * b) == j // (2 * b)) & ((i % (2 * b)) >= b) & ((j % (2 * b)) < b)
        c["me%d" % b] = e
        c["me%dt" % b] = e.T
    arrf = np.stack([np.asarray(c[n], np.float32) for n in CSTF_NAMES], axis=1)
    arrb = np.stack([np.asarray(c[n], np.float32) for n in CSTB_NAMES], axis=1)
    return np.ascontiguousarray(arrf), np.ascontiguousarray(arrb)


class Builder:
    def __init__(self, debug=None):
        self.debug = debug or {}
        self.slab_specs = []
        self.slab_off = []
        self.slab_tot = 0
        self.nslab_pass = None

    def mm(self, out, lhsT, rhs, r, w, start=True, stop=True):
        self.P.add("pe", lambda e: e.matmul(out, lhsT, rhs, start=start, stop=stop), r, w)

    def tr(self, out, in_, ident, r, w):
        self.P.add("pe", lambda e: e.transpose(out, in_, ident), r, w)

    def act(self, out, in_, func, r, w, bias=None, scale=None):
        kw = {}
        if bias is not None:
            kw["bias"] = bias
        if scale is not None:
            kw["scale"] = scale
        self.P.add("act", lambda e: e.activation(out, in_, func, **kw), r, w)

    def tt(self, eng, out, in0, in1, op, r, w):
        self.P.add(eng, lambda e: e.tensor_tensor(out, in0, in1, op), r, w)

    def ts(self, eng, out, in0, s1, op0, r, w, s2=None, op1=None):
        if op1 is None:
            self.P.add(eng, lambda e: e.tensor_scalar(out, in0, s1, None, op0), r, w)
        else:
            self.P.add(eng, lambda e: e.tensor_scalar(out, in0, s1, s2, op0, op1), r, w)

    def stt(self, out, in0, scalar, in1, op0, op1, r, w):
        self.P.add("dve", lambda e: e.scalar_tensor_tensor(out, in0, scalar, in1, op0, op1), r, w)

    def cp(self, eng, out, in_, r, w):
        if eng == "act":
            self.P.add("act", lambda e: e.activation(out, in_, AF.Copy), r, w)
        else:
            self.P.add(eng, lambda e: e.tensor_copy(out, in_), r, w)

    def dq(self):
        return "sp" if self.recording else "pool"

    def dma(self, eng, out, in_, key, r, w, slow=False):
        if eng == "aux":
            eng = self.dq()
        if slow:
            self.P.add(eng, lambda e: e.dma_start(out=out, in_=in_, allow_slow_non_contiguous=True), r, w, key=key)
        else:
            self.P.add(eng, lambda e: e.dma_start(out=out, in_=in_), r, w, key=key)

    def slab(self, spec):
        if self.recording:
            self.slab_specs.append(spec)
            n = sum(((r1 - r0) // 128) * (c1 - c0) for (_, r0, r1, c0, c1) in spec)
            assert n <= SLOT, n
            self.slab_off.append((self.slab_tot, n))
            self.slab_tot += 128 * n
        si = self.slab_i % self.nslab_pass if self.nslab_pass else self.slab_i
        off, n = self.slab_off[si]
        slot = self.slab_i % NSLOT
        self.slab_i += 1
        t = self.wring[slot]
        key = "w%d" % slot
        scr = self.wscr[off:off + 128 * n].rearrange("(p n) -> p n", p=128)
        if self.recording:
            src = self.wbig[off:off + 128 * n].rearrange("(p n) -> p n", p=128)
            self.dma("pool", t[:, 0:n], src, key, r=[], w=[key])
            if self.debug.get("npass", 4) > 1 or not self.debug.get("nosample", False):
                self.dma("sp", scr, t[:, 0:n], "wb%d" % slot, r=[key], w=["wscr%d" % si])
        else:
            self.dma("sp", t[:, 0:n], scr, key, r=["wscr%d" % si], w=[key])
        views = []
        o = 0
        for (_, r0, r1, c0, c1) in spec:
            kc = (r1 - r0) // 128
            nc_ = c1 - c0
            views.append(t[:, o:o + kc * nc_].rearrange("p (k n) -> p k n", k=kc))
            o += kc * nc_
        return views, key

    def build(self):
        nc = bass.Bass("TRN2", target_bir_lowering=False)
        self.nc = nc
        self.es = contextlib.ExitStack()
        with self.es:
            self._build_inner()
        return nc

    def dram_in(self, name, shape, dt=F32):
        return self.nc.dram_tensor(name, list(shape), dt, kind="ExternalInput").ap()

    def dram_out(self, name, shape, dt=F32):
        return self.nc.dram_tensor(name, list(shape), dt, kind="ExternalOutput").ap()

    def sb(self, name, shape, dt):
        return self.es.enter_context(self.nc.sbuf_tensor(name, list(shape), dt))

    def _build_inner(self):
        nc = self.nc
        self.P = Prog(nc)
        P = self.P
        self.x = self.dram_in("x", [SEQ, D])
        self.pp = self.dram_in("pp", [SEQ, 256])
        self.xs = self.dram_in("xs", [NSAMP, D])
        self.psm = self.dram_in("psm", [NSAMP, 256])
        self.sg = self.dram_in("sg", [NSAMP, H, 128, 128])
        self.sq = self.dram_in("sq", [NSAMP, 3, QKV_W])
        self.ssc = self.dram_in("ssc", [NSAMP, 2, D])
        self.wbig = self.dram_in("wbig", [self.wbig_len])
        self.wscr = self.nc.dram_tensor("wscr", [self.wbig_len], BF16, kind="Internal").ap()
        self.lnp = self.dram_in("lnp", [8, D])
        self.wcq_d = self.dram_in("wcq", [128, 24 * 4])
        self.wcs_d = self.dram_in("wcs", [128, 8 * 3])
        self.smallp = self.dram_in("smallp", [2, 8])
        self.won_d = self.dram_in("won", [128])
        self.cst_d = self.dram_in("cst", [128, len(CSTF_NAMES), 128])
        self.cst2_d = self.dram_in("cst2", [128, len(CSTB_NAMES) * 128])
        self.i16_d = self.dram_in("i16", [128, 256])
        self.y = self.dram_out("y", [SEQ, D])
        self.ys = self.dram_out("ys", [NSAMP, D])
        self.sgp = self.dram_out("sgp", [H, 128, 128])
        self.sqp = self.dram_out("sqp", [3, QKV_W])
        self.ssp = self.dram_out("ssp", [2, D])
        self.sgs = self.dram_out("sgs", [NSAMP, H, 128, 128])
        self.sqs = self.dram_out("sqs", [NSAMP, 3, QKV_W])
        self.sss = self.dram_out("sss", [NSAMP, 2, D])
        self.outkeys = []

        sb = self.sb
        self.wring = [sb("wr%d" % i, [128, SLOT], BF16) for i in range(NSLOT)]
        self.xres = sb("xres", [128, 4, D], F32)
        self.xT = sb("xT", [128, 8, NTP], BF16)
        self.gbt = sb("gbt", [128, 2, D], F32)
        self.cstf = sb("cstf", [128, len(CSTF_NAMES), 128], F32)
        self.cstb = sb("cstb", [128, len(CSTB_NAMES), 128], BF16)
        self.i16b = sb("i16b", [128, 256], BF16)
        self.wcq = sb("wcq_s", [128, 24, 4], F32)
        self.wcs = sb("wcs_s", [128, 8, 3], F32)
        self.wonb = sb("wonb", [128, 128], F32)
        self.smallb = sb("smallb", [128, 16], F32)
        self.negA = sb("negA", [128, 8], F32)
        self.histq = sb("histq", [128, 24, 3], F32)
        self.hists = sb("hists", [128, 8, 2], F32)
        self.S = sb("S", [128, H, 128], F32)
        self.Sbf = sb("Sbf", [128, H, 128], BF16)
        self.A1 = sb("A1", [128, 24, NTP], BF16)
        self.ztok = sb("ztok", [128, 4, D], BF16)
        self.ktok = sb("ktok", [128, H, 128], BF16)
        self.vtok = sb("vtok", [128, H, 128], BF16)
        self.batok = sb("batok", [128, 4, 16], F32)
        self.beta = sb("beta", [128, 4, 8], F32)
        self.gtok = sb("gtok", [128, 4, 8], F32)
        self.tf = [sb("tf%d" % i, [128, NTP + 4], F32) for i in range(7)]
        self.tfi = 0
        self.tb16 = [sb("tb%d" % i, [128, NTP], BF16) for i in range(4)]
        self.tbi = 0
        self.t1 = sb("t1", [128, D], F32)
        self.t2 = sb("t2", [128, D], F32)
        self.otok = sb("otok", [128, D], F32)
        self.xb16 = sb("xb16", [128, D], BF16)
        self.stat = sb("stat", [128, 4, 16], F32)
        self.stat3 = sb("stat3", [128, 16], F32)
        self.xb16b = sb("xb16b", [128, D], BF16)
        self.pT = sb("pT", [128, 2, NTP], BF16)
        self.pb = sb("pb", [128, 256], BF16)
        GQ = [("decTm", F32), ("Lg", F32), ("qkTm", BF16), ("MT", BF16), ("M", BF16),
              ("Na", BF16), ("Nb", BF16), ("Nc", BF16), ("Nd", BF16), ("Pa", BF16), ("Pb", BF16),
              ("Pc", BF16), ("Pd", BF16)]
        self.gqs = []
        gA = {}
        for nm, dt in GQ:
            gA[nm] = sb("g_" + nm, [128, 4, 128], dt)[:]
        self.gqs.append((gA, "gA_"))
        self.arenaB = sb("arenaB", [128, 15 * 512], BF16)
        gB = {}
        o = 0
        for nm, dt in GQ:
            n = 1024 if dt == F32 else 512
            v = self.arenaB[:, o:o + n]
            if dt == F32:
                v = v.bitcast(F32)
            gB[nm] = v.rearrange("p (h d) -> p h d", h=4)
            o += n
        self.gqs.append((gB, "gB_"))
        self.kdec = sb("kdec", [128, H, 128], BF16)
        self.gsm = sb("gsm", [128, 64], F32)
        self.hsq = sb("hsq", [128, 24, 3, NSAMP], F32)
        self.hss = sb("hss", [128, 8, 2, NSAMP], F32)
        self.Sin = [sb("Sin%d" % i, [128, H, 128], F32) for i in range(2)]
        self.Sinb = [sb("Sinb%d" % i, [128, H, 128], BF16) for i in range(2)]
        self.kTm = self.arenaB[:, 0:2048].rearrange("p (h a b) -> p h a b", h=H, a=NSAMP)
        self.qTm = self.arenaB[:, 2048:4096].rearrange("p (h a b) -> p h a b", h=H, a=NSAMP)
        self.kmask = [sb("kmask%d" % i, [NSAMP, D], BF16) for i in range(2)]
        self.qtok = sb("qtok", [NSAMP, D], BF16)
        self.abc = sb("abc", [128, 128], F32)
        self.ps = [self.es.enter_context(nc.psum_tensor("ps%d" % i, [128, 512], F32)) for i in range(8)]
        self.psb = [p.bitcast(BF16) for p in self.ps]

        self.recording = True
        self.slab_i = 0
        self.setup()
        npass = self.debug.get("npass", 4)
        for pi in range(npass):
            self.layer_pass(pi, NTP, sample=False, last=(pi == npass - 1))
            if pi == 0:
                self.recording = False
                self.nslab_pass = len(self.slab_off)
        if not self.debug.get("nosample", False):
            gbk = ["gB_" + nm for nm in ("decTm", "Lg", "qkTm", "MT", "M", "Na", "Nb", "Nc", "Nd", "Pa", "Pb", "Pc", "Pd")]
            P.add("dve", lambda e: e.memset(self.gsm[:, 60:64], 0.0), gbk, gbk + ["kTm", "qTm"])
            self.layer_pass(0, NSAMP, sample=True, last=True)
        P.add("sp", None, r=self.outkeys)
        P.finalize()
        P.emit(self.es)

    def cf(self, name):
        return self.cstf[:, CSTF_NAMES.index(name), :]

    def cb(self, name):
        return self.cstb[:, CSTB_NAMES.index(name), :]

    def cb4(self, name):
        i = CSTB_NAMES.index(name)
        return self.cstb[:, i:i + 1, :].to_broadcast([128, 4, 128])

    def cf4(self, name):
        i = CSTF_NAMES.index(name)
        return self.cstf[:, i:i + 1, :].to_broadcast([128, 4, 128])

    def tmpf(self):
        i = self.tfi % len(self.tf)
        self.tfi += 1
        return self.tf[i], "tf%d" % i

    def tmpb(self):
        i = self.tbi % len(self.tb16)
        self.tbi += 1
        return self.tb16[i], "tb%d" % i

    def setup(self):
        d = self.dma
        d("sp", self.cstf[:], self.cst_d, "c0", [], ["cstf"])
        nb = len(CSTB_NAMES) * 128
        for k in range(0, nb, 1024):
            n = min(1024, nb - k)
            d("sp", self.t1[:, 0:n], self.cst2_d[:, k:k + n], "c1", [], ["t1"])
            self.cp("dve", self.cstb[:].rearrange("p c d -> p (c d)")[:, k:k + n], self.t1[:, 0:n], ["t1"], ["cstb"])
        d("sp", self.t2[:, 0:256], self.i16_d, "c1", [], ["t2"])
        self.cp("dve", self.i16b[:], self.t2[:, 0:256], ["t2"], ["i16b"])
        d("sp", self.wcq[:].rearrange("p c j -> p (c j)"), self.wcq_d, "c2", [], ["wcq"])
        d("sp", self.wcs[:].rearrange("p c j -> p (c j)"), self.wcs_d, "c3", [], ["wcs"])
        d("sp", self.wonb[:], self.won_d.partition_broadcast(128), "c4", [], ["wonb"])
        d("sp", self.smallb[:], self.smallp.rearrange("a b -> (a b)").partition_broadcast(128), "c5", [], ["smallb"])
        self.act(self.negA[:], self.smallb[:, 0:8], AF.Exp, ["smallb"], ["negA"])
        self.ts("dve", self.negA[:], self.negA[:], -1.0, ALU.mult, ["negA"], ["negA"])
        self.P.add("dve", lambda e: e.memset(self.S[:], 0.0), [], ["S0", "S1"])
        self.P.add("dve", lambda e: e.memset(self.Sbf[:], 0.0), [], ["Sbf0", "Sbf1"])
        self.P.add("dve", lambda e: e.memset(self.histq[:], 0.0), [], ["histq"])
        self.P.add("dve", lambda e: e.memset(self.hists[:], 0.0), [], ["hists"])

    def make_xT(self, tb, TB, bank):
        ps, psk = self.psb[bank], "ps%d" % bank
        xb, xbk = ((self.xb16, "xb16"), (self.xb16b, "xb16b"))[tb % 2]
        self.cp("act", xb[:TB, :], self.xres[:TB, tb, :], ["xres%d" % tb], [xbk])
        for c in range(8):
            self.tr(ps[:, c * TB:(c + 1) * TB], xb[:TB, c * 128:(c + 1) * 128], self.cb("ident")[:TB, :TB],
                    [xbk, "cstb"], [psk])
        self.cp("dve", self.xT[:, :, tb * TB:(tb + 1) * TB],
                ps[:, 0:8 * TB].rearrange("p (c t) -> p c t", c=8), [psk], ["xT%d" % tb])

    def layer_norm(self, idx, NB, TB, final_out=None):
        self.dma("aux", self.gbt[:, 0, :], self.lnp[2 * idx, :].partition_broadcast(128), "gb0", [], ["gbt0"])
        self.dma("aux", self.gbt[:, 1, :], self.lnp[2 * idx + 1, :].partition_broadcast(128), "gb1", [], ["gbt1"])
        eps = LN_EPS / (ALPHA * ALPHA)
        st = self.stat
        for tb in range(NB):
            xr = self.xres[:TB, tb, :]
            xk = "xres%d" % tb
            self.P.add("dve", lambda e, xr=xr, tb=tb: e.bn_stats(st[:TB, tb, 0:6], xr[:, 0:512]), [xk], ["stat"])
            self.P.add("dve", lambda e, xr=xr, tb=tb: e.bn_stats(st[:TB, tb, 6:12], xr[:, 512:1024]), [xk], ["stat"])
            self.P.add("dve", lambda e, tb=tb: e.bn_aggr(st[:TB, tb, 12:14], st[:TB, tb, 0:12]), ["stat"], ["stat"])
        self.act(st[:TB, 0:NB, 14], st[:TB, 0:NB, 13], AF.Sqrt, ["stat"], ["stat2"], bias=eps)
        self.P.add("dve", lambda e: e.reciprocal(st[:TB, 0:NB, 15], st[:TB, 0:NB, 14]), ["stat2"], ["stat2"])
        for tb in range(NB):
            xr = self.xres[:TB, tb, :]
            xk = "xres%d" % tb
            self.stt(self.t1[:TB, :], xr, st[:TB, tb, 12:13], self.gbt[:TB, 0, :], ALU.subtract, ALU.mult,
                     [xk, "stat", "gbt0"], ["t1"])
            self.stt(xr, self.t1[:TB, :], st[:TB, tb, 15:16], self.gbt[:TB, 1, :], ALU.mult, ALU.add,
                     ["t1", "stat2", "gbt1"], [xk])
            if final_out is not None:
                ok = final_out[1] + str(tb)
                self.dma("aux", final_out[0][tb * TB:(tb + 1) * TB, :], xr, "yo%d" % tb, [xk], [ok])
                self.outkeys.append(ok)
            else:
                self.make_xT(tb, TB, (2 * tb) % 8)

    def ffn(self, pfx, NB, TB):
        NT = NB * TB
        for j0 in range(0, NJ, 2):
            (wg, wu), wk = self.slab([(pfx + "_w_gate", 0, D, j0 * 128, j0 * 128 + 256),
                                      (pfx + "_w_up", 0, D, j0 * 128, j0 * 128 + 256)])
            for jj in range(2):
                j = j0 + jj
                bg, bu = 2 * (j % 2), 2 * (j % 2) + 1
                for kc in range(8):
                    self.mm(self.ps[bg][:, :NT], wg[:, kc, jj * 128:(jj + 1) * 128], self.xT[:, kc, :NT],
                            [wk, "xT0", "xT1", "xT2", "xT3"], ["ps%d" % bg], start=(kc == 0), stop=(kc == 7))
                for kc in range(8):
                    self.mm(self.ps[bu][:, :NT], wu[:, kc, jj * 128:(jj + 1) * 128], self.xT[:, kc, :NT],
                            [wk, "xT0", "xT1", "xT2", "xT3"], ["ps%d" % bu], start=(kc == 0), stop=(kc == 7))
                t, tk = self.tmpf()
                self.act(t[:, :NT], self.ps[bg][:, :NT], AF.Silu, ["ps%d" % bg], [tk])
                self.tt("dve", self.A1[:, j, :NT], t[:, :NT], self.ps[bu][:, :NT], ALU.mult,
                        [tk, "ps%d" % bu], ["A1.%d" % j])
        for j0 in range(0, NJ, 4):
            j1 = min(NJ, j0 + 4)
            (wd,), wk = self.slab([(pfx + "_w_down", j0 * 128, j1 * 128, 0, D)])
            for jj in range(j1 - j0):
                j = j0 + jj
                for tb in range(NB):
                    for nh in range(2):
                        b = tb * 2 + nh
                        self.mm(self.ps[b][:TB, :], self.A1[:, j, tb * TB:(tb + 1) * TB], wd[:, jj, nh * 512:(nh + 1) * 512],
                                [wk, "A1.%d" % j], ["ps%d" % b], start=(j == 0), stop=(j == NJ - 1))
        c = 0.5 / ALPHA
        for tb in range(NB):
            for nh in range(2):
                b = tb * 2 + nh
                xr = self.xres[:TB, tb, nh * 512:(nh + 1) * 512]
                self.stt(xr, self.ps[b][:TB, :], c, xr, ALU.mult, ALU.add, ["ps%d" % b, "xres%d" % tb], ["xres%d" % tb])

    def layer_pass(self, pi, NT, sample, last):
        TB = min(128, NT)
        NB = NT // TB
        self.slab_i = 0 if self.recording else self.slab_i
        if not sample:
            src, psrc = self.x[pi * NT:(pi + 1) * NT, :], self.pp[pi * NT:(pi + 1) * NT, :]
            yout = (self.y[pi * NT:(pi + 1) * NT, :], "y%d_" % pi)
        else:
            src, psrc = self.xs, self.psm
            yout = (self.ys, "ys_")
        for tb in range(NB):
            self.dma("aux", self.xres[:TB, tb, :], src[tb * TB:(tb + 1) * TB, :], "x%d" % tb, [], ["xres%d" % tb])
            self.make_xT(tb, TB, tb % 8)
        stop = self.debug.get("stop")
        self.ffn("ffn1", NB, TB)
        if stop == "ffn1":
            return self.dump(yout, NB, TB)
        self.layer_norm(0, NB, TB)
        if stop == "ln1":
            return self.dump(yout, NB, TB)
        self.mixers(pi, NB, TB, sample, last)
        if stop == "mix":
            return self.dump(yout, NB, TB)
        self.layer_norm(1, NB, TB)
        self.ffn("ffn2", NB, TB)
        self.layer_norm(2, NB, TB)
        if stop == "ln3":
            return self.dump(yout, NB, TB)
        self.ple(psrc, NB, TB)
        self.layer_norm(3, NB, TB, final_out=yout)

    def dump(self, yout, NB, TB):
        for tb in range(NB):
            ok = yout[1] + str(tb)
            self.dma("aux", yout[0][tb * TB:(tb + 1) * TB, :], self.xres[:TB, tb, :], "yo%d" % tb, ["xres%d" % tb], [ok])
            self.outkeys.append(ok)

    def ple(self, psrc, NB, TB):
        NT = NB * TB
        for tb in range(NB):
            pf, pfk = self.tmpf()
            self.dma("aux", pf[:TB, 0:256], psrc[tb * TB:(tb + 1) * TB, :], "pf", [], [pfk])
            self.cp("act", self.pb[:TB, :], pf[:TB, 0:256], [pfk], ["pb"])
            for c in range(2):
                self.tr(self.psb[7][:, c * TB:(c + 1) * TB], self.pb[:TB, c * 128:(c + 1) * 128],
                        self.cb("ident")[:TB, :TB], ["pb", "cstb"], ["ps7"])
            self.cp("dve", self.pT[:, :, tb * TB:(tb + 1) * TB],
                    self.psb[7][:, 0:2 * TB].rearrange("p (c t) -> p c t", c=2), ["ps7"], ["pT"])
        for nh in range(2):
            (wg,), wgk = self.slab([("w_ple_gate", 0, D, nh * 512, (nh + 1) * 512)])
            (wp,), wpk = self.slab([("w_ple_proj", 0, 256, nh * 512, (nh + 1) * 512)])
            for tb in range(NB):
                bg, bp = 2 * (tb % 2), 2 * (tb % 2) + 1
                for kc in range(8):
                    self.mm(self.ps[bg][:TB, :], self.xT[:, kc, tb * TB:(tb + 1) * TB], wg[:, kc, :],
                            [wgk, "xT0", "xT1", "xT2", "xT3"], ["ps%d" % bg], start=(kc == 0), stop=(kc == 7))
                for kc in range(2):
                    self.mm(self.ps[bp][:TB, :], self.pT[:, kc, tb * TB:(tb + 1) * TB], wp[:, kc, :],
                            [wpk, "pT"], ["ps%d" % bp], start=(kc == 0), stop=(kc == 1))
                t, tk = self.tmpf()
                self.act(t[:TB, :512], self.ps[bg][:TB, :], AF.Sigmoid, ["ps%d" % bg], [tk])
                self.tt("dve", t[:TB, :512], t[:TB, :512], self.ps[bp][:TB, :], ALU.mult, [tk, "ps%d" % bp], [tk])
                xr = self.xres[:TB, tb, nh * 512:(nh + 1) * 512]
                self.stt(xr, t[:TB, :512], 1.0 / ALPHA, xr, ALU.mult, ALU.add, [tk, "xres%d" % tb], ["xres%d" % tb])

    def conv_chunk(self, psbank, NT, taps_hist, wts, ntap, hist_tile, hist_key, sample, src_is_psum=True, src=None):
        H_ = ntap - 1
        cbt, cbk = self.tmpf()
        if src_is_psum:
            self.cp("act", cbt[:, H_:H_ + NT], self.ps[psbank][:, :NT], ["ps%d" % psbank], [cbk])
        else:
            src(cbt[:, H_:H_ + NT], cbk)
        if not sample:
            self.cp("dve", cbt[:, 0:H_], hist_tile, [hist_key], [cbk])
            self.cp("dve", hist_tile, cbt[:, NT:NT + H_], [cbk], [hist_key])
            taps = [cbt[:, j:j + NT] for j in range(ntap)]
            tr_ = [cbk]
        else:
            taps = [taps_hist[j] for j in range(H_)] + [cbt[:, H_:H_ + NT]]
            tr_ = [cbk, "hsamp"]
        acc, ak = self.tmpf()
        self.ts("dve", acc[:, :NT], taps[0], wts[0], ALU.mult, tr_ + ["wc"], [ak])
        for j in range(1, ntap):
            self.stt(acc[:, :NT], taps[j], wts[j], acc[:, :NT], ALU.mult, ALU.add, tr_ + ["wc", ak], [ak])
        return acc, ak, cbt, cbk

    def mixers(self, pi, NB, TB, sample, last):
        NT = NB * TB
        A1 = self.A1
        if sample:
            self.load_sample_hist()
        pend = None

        def finish(p):
            c, so, sk, sq, sqk, cbt, cbk = p
            if sample:
                self.tr(self.ps[6][:NT, (c % 4) * 128:(c % 4 + 1) * 128], cbt[:, 3:3 + NT], self.cf("ident"),
                        [cbk, "cstf"], ["ps6"])
                if c % 4 == 3:
                    stg, stk = self.stage(c // 8)
                    self.cp("act", stg[:NT, (c % 8 - 3) * 128:(c % 8 + 1) * 128], self.ps[6][:NT, :], ["ps6"], [stk])
            if c < 16:
                b2 = 2 + c % 2
                self.mm(self.ps[b2][:, :NT], self.cb("ones"), sq[:, :NT], [sqk, "cstb"], ["ps%d" % b2])
                sd, sdk = self.tmpf()
                self.act(sd[:, :NT], self.ps[b2][:, :NT], AF.Ln, ["ps%d" % b2], [sdk], bias=L2_EPS)
                self.act(sd[:, :NT], sd[:, :NT], AF.Exp, [sdk], [sdk], scale=-0.5)
                const = 128.0 ** -0.5 if c < 8 else 1.0
                self.stt(A1[:, c, :NT], so[:, :NT], const, sd[:, :NT], ALU.mult, ALU.mult, [sk, sdk], ["A1.%d" % c])

        for g in range(6):
            (wq,), wk = self.slab([("w_in", 0, D, g * 512, (g + 1) * 512)])
            for jj in range(4):
                c = g * 4 + jj
                bank = c % 2
                for kc in range(8):
                    self.mm(self.ps[bank][:, :NT], wq[:, kc, jj * 128:(jj + 1) * 128], self.xT[:, kc, :NT],
                            [wk, "xT0", "xT1", "xT2", "xT3"], ["ps%d" % bank], start=(kc == 0), stop=(kc == 7))
                th = [self.hsq[:, c, j, :] for j in range(3)] if sample else None
                wts = [self.wcq[:, c, j:j + 1] for j in range(4)]
                acc, ak, cbt, cbk = self.conv_chunk(bank, NT, th, wts, 4, self.histq[:, c, :], "histq%d" % c, sample)
                if c >= 16:
                    self.act(A1[:, c, :NT], acc[:, :NT], AF.Silu, [ak], ["A1.%d" % c])
                    cur = (c, None, None, None, None, cbt, cbk)
                else:
                    self.act(acc[:, :NT], acc[:, :NT], AF.Silu, [ak], [ak])
                    sq, sqk = self.tmpb()
                    self.act(sq[:, :NT], acc[:, :NT], AF.Square, [ak], [sqk])
                    cur = (c, acc, ak, sq, sqk, cbt, cbk)
                if pend is not None:
                    finish(pend)
                pend = cur
        finish(pend)
        if sample:
            for k in range(3):
                stg, stk = self.stage(k)
                self.dma("aux", self.sqs[:, 2, k * 1024:(k + 1) * 1024], stg[:NSAMP, :], "so0", [stk], ["sqs2_%d" % k])
                self.outkeys.append("sqs2_%d" % k)
            self.dma("aux", self.sqs[:, 0:2, :], self.sq[:, 1:3, :], "so1", [], ["sqs01"])
            self.outkeys += ["sqs01"]
        elif last:
            for j in range(3):
                self.dma("aux", self.sqp[j, :].rearrange("(c p) -> p c", p=128), self.histq[:, :, j], "so0",
                         ["histq%d" % c for c in range(24)], ["sqp%d" % j], slow=True)
                self.outkeys.append("sqp%d" % j)
        if self.debug.get("mstop") == "A":
            return
        for nh in range(2):
            (wz,), wk = self.slab([("w_in", 0, D, Z0 + nh * 512, Z0 + (nh + 1) * 512)])
            for tb in range(NB):
                b = 4 + tb % 2
                for kc in range(8):
                    self.mm(self.ps[b][:TB, :], self.xT[:, kc, tb * TB:(tb + 1) * TB], wz[:, kc, :],
                            [wk, "xT0", "xT1", "xT2", "xT3"], ["ps%d" % b], start=(kc == 0), stop=(kc == 7))
                self.act(self.ztok[:TB, tb, nh * 512:(nh + 1) * 512], self.ps[b][:TB, :], AF.Silu, ["ps%d" % b], ["ztok%d" % tb])
        (wba,), wk = self.slab([("w_in", 0, D, BETA0, BETA0 + 16)])
        for tb in range(NB):
            for kc in range(8):
                self.mm(self.ps[6][:TB, 0:16], self.xT[:, kc, tb * TB:(tb + 1) * TB], wba[:, kc, :],
                        [wk, "xT0", "xT1", "xT2", "xT3"], ["ps6"], start=(kc == 0), stop=(kc == 7))
            self.act(self.beta[:TB, tb, :], self.ps[6][:TB, 0:8], AF.Sigmoid, ["ps6"], ["beta"])
            self.tt("dve", self.batok[:TB, tb, 8:16], self.ps[6][:TB, 8:16], self.smallb[:TB, 8:16], ALU.add,
                    ["ps6", "smallb"], ["batok"])
        for tb in range(NB):
            self.act(self.batok[:TB, tb, 0:8], self.batok[:TB, tb, 8:16], AF.Exp, ["batok"], ["batok"])
        for tb in range(NB):
            self.act(self.batok[:TB, tb, 0:8], self.batok[:TB, tb, 0:8], AF.Ln, ["batok"], ["batok"], bias=1.0)
            self.tt("dve", self.gtok[:TB, tb, :], self.batok[:TB, tb, 0:8], self.negA[:TB, :], ALU.mult,
                    ["batok", "negA"], ["gtok"])
        if self.debug.get("mstop") == "B":
            return
        if sample:
            self.gdn_sample()
        else:
            for tb in range(NB):
                self.gdn_block(tb)
            if last:
                self.dma("aux", self.sgp.rearrange("h k v -> k h v"), self.S[:], "so1", ["S0", "S1"], ["sgp"])
                self.outkeys.append("sgp")
        if self.debug.get("mstop") == "C":
            return
        for c in range(8):
            (wB, wC, wH), wk = self.slab([("w_in", 0, D, B0 + c * 128, B0 + (c + 1) * 128),
                                          ("w_in", 0, D, C0 + c * 128, C0 + (c + 1) * 128),
                                          ("w_in", 0, D, H0 + c * 128, H0 + (c + 1) * 128)])
            bB, bC, bH = 0 + 3 * (c % 2), 1 + 3 * (c % 2), 2 + 3 * (c % 2)
            for (w_, b_) in ((wC, bC), (wH, bH), (wB, bB)):
                for kc in range(8):
                    self.mm(self.ps[b_][:, :NT], w_[:, kc, :], self.xT[:, kc, :NT], [wk, "xT0", "xT1", "xT2", "xT3"], ["ps%d" % b_],
                            start=(kc == 0), stop=(kc == 7))
            ct, ck = self.tmpf()
            self.cp("act", ct[:, :NT], self.ps[bC][:, :NT], ["ps%d" % bC], [ck])

            def src(dst, dk, ct=ct, ck=ck, bH=bH):
                self.tt("dve", dst, ct[:, :NT], self.ps[bH][:, :NT], ALU.mult, [ck, "ps%d" % bH], [dk])
            th = [self.hss[:, c, j, :] for j in range(2)] if sample else None
            wts = [self.wcs[:, c, j:j + 1] for j in range(3)]
            acc, ak, cbt, cbk = self.conv_chunk(None, NT, th, wts, 3, self.hists[:, c, :], "hists%d" % c, sample,
                                                src_is_psum=False, src=src)
            if sample:
                self.tr(self.ps[6][:NT, (c % 4) * 128:(c % 4 + 1) * 128], cbt[:, 2:2 + NT], self.cf("ident"),
                        [cbk, "cstf"], ["ps6"])
                if c % 4 == 3:
                    stg, stk = self.stage(0)
                    self.cp("act", stg[:NT, (c - 3) * 128:(c + 1) * 128], self.ps[6][:NT, :], ["ps6"], [stk])
            self.tt("dve", A1[:, c, :NT], acc[:, :NT], self.ps[bB][:, :NT], ALU.mult, [ak, "ps%d" % bB], ["A1.%d" % c])
        if sample:
            stg, stk = self.stage(0)
            self.dma("aux", self.sss[:, 1, :], stg[:NSAMP, 0:D], "so2", [stk], ["sss1"])
            self.dma("aux", self.sss[:, 0:1, :], self.ssc[:, 1:2, :], "so3", [], ["sss0"])
            self.outkeys += ["sss1", "sss0"]
        elif last:
            for j in range(2):
                self.dma("aux", self.ssp[j, :].rearrange("(c p) -> p c", p=128), self.hists[:, :, j], "so2",
                         ["hists%d" % c for c in range(8)], ["ssp%d" % j], slow=True)
                self.outkeys.append("ssp%d" % j)
        if self.debug.get("mstop") == "D":
            return
        for c in range(8):
            (wpg, wgg, wps, wgs), wk = self.slab([("w_p_gdn", 0, D, c * 128, (c + 1) * 128),
                                                  ("w_in", 0, D, GG0 + c * 128, GG0 + (c + 1) * 128),
                                                  ("w_p_sc", 0, D, c * 128, (c + 1) * 128),
                                                  ("w_in", 0, D, GS0 + c * 128, GS0 + (c + 1) * 128)])
            o = 4 * (c % 2)
            for (w_, b_, rhs_, rk) in ((wpg, o, A1[:, 16:24, :], ["A1.%d" % k for k in range(16, 24)]),
                                       (wgg, o + 1, self.xT, ["xT0", "xT1", "xT2", "xT3"]),
                                       (wps, o + 2, A1[:, 0:8, :], ["A1.%d" % k for k in range(8)]),
                                       (wgs, o + 3, self.xT, ["xT0", "xT1", "xT2", "xT3"])):
                for kc in range(8):
                    self.mm(self.ps[b_][:, :NT], w_[:, kc, :], rhs_[:, kc, :NT], [wk] + rk, ["ps%d" % b_],
                            start=(kc == 0), stop=(kc == 7))
            s1, s1k = self.tmpf()
            self.act(s1[:, :NT], self.ps[o + 1][:, :NT], AF.Sigmoid, ["ps%d" % (o + 1)], [s1k])
            self.tt("dve", s1[:, :NT], s1[:, :NT], self.ps[o][:, :NT], ALU.mult, [s1k, "ps%d" % o], [s1k])
            s2, s2k = self.tmpf()
            self.act(s2[:, :NT], self.ps[o + 3][:, :NT], AF.Sigmoid, ["ps%d" % (o + 3)], [s2k])
            self.tt("dve", s2[:, :NT], s2[:, :NT], self.ps[o + 2][:, :NT], ALU.mult, [s2k, "ps%d" % (o + 2)], [s2k])
            self.tt("dve", A1[:, 8 + c, :NT], s1[:, :NT], s2[:, :NT], ALU.add, [s1k, s2k], ["A1.%d" % (8 + c)])
        for nh in range(2):
            (wo,), wk = self.slab([("w_o", 0, D, nh * 512, (nh + 1) * 512)])
            for tb in range(NB):
                b = tb % 2
                for kc in range(8):
                    self.mm(self.ps[b][:TB, :], A1[:, 8 + kc, tb * TB:(tb + 1) * TB], wo[:, kc, :],
                            [wk, "A1.%d" % (8 + kc)], ["ps%d" % b], start=(kc == 0), stop=(kc == 7))
                xr = self.xres[:TB, tb, nh * 512:(nh + 1) * 512]
                self.stt(xr, self.ps[b][:TB, :], 1.0 / ALPHA, xr, ALU.mult, ALU.add, ["ps%d" % b, "xres%d" % tb], ["xres%d" % tb])

    def onorm_and_T(self, tb, TB):
        o3 = self.otok[:TB, :].rearrange("p (h d) -> p h d", h=H)
        t13 = self.t1[:TB, :].rearrange("p (h d) -> p h d", h=H)
        t23 = self.t2[:TB, :].rearrange("p (h d) -> p h d", h=H)
        st = self.stat3
        self.act(self.t1[:TB, :], self.otok[:TB, :], AF.Square, ["otok"], ["t1"])
        self.P.add("dve", lambda e: e.tensor_reduce(st[:TB, 0:8], t13, AX.X, ALU.add), ["t1"], ["stat3"])
        self.act(st[:TB, 0:8], st[:TB, 0:8], AF.Sqrt, ["stat3"], ["stat3"], bias=RMS_EPS, scale=1.0 / 128.0)
        self.P.add("dve", lambda e: e.reciprocal(st[:TB, 8:16], st[:TB, 0:8]), ["stat3"], ["stat3"])
        self.tt("dve", t13, o3, st[:TB, 8:16].unsqueeze(2).to_broadcast([TB, H, 128]), ALU.mult, ["otok", "stat3"], ["t1"])
        z3 = self.ztok[:TB, tb, :].rearrange("p (h d) -> p h d", h=H)
        self.tt("dve", t23, z3, self.wonb[:TB, :].unsqueeze(1).to_broadcast([TB, H, 128]), ALU.mult,
                ["ztok%d" % tb, "wonb"], ["t2"])
        self.tt("dve", self.xb16[:TB, :], self.t1[:TB, :], self.t2[:TB, :], ALU.mult, ["t1", "t2"], ["xb16"])
        for c in range(8):
            self.tr(self.psb[7][:, c * TB:(c + 1) * TB], self.xb16[:TB, c * 128:(c + 1) * 128], self.cb("ident")[:TB, :TB],
                    ["xb16", "cstb"], ["ps7"])
        self.cp("act", self.A1[:, 16:24, tb * TB:(tb + 1) * TB],
                self.psb[7][:, 0:8 * TB].rearrange("p (c t) -> p c t", c=8), ["ps7"],
                ["A1.%d" % k for k in range(16, 24)])

    def inv_chain(self, tb, hg, G, gp, pb):
        A1 = self.A1
        blk = slice(tb * 128, (tb + 1) * 128)
        g8 = self.gtok[:, tb, :]
        hs = [hg * 4 + hh for hh in range(4)]
        K = lambda nm: gp + nm
        pk = ["ps%d" % x for x in pb]
        ps = [self.ps[x] for x in pb]
        psb = [self.psb[x] for x in pb]
        f4 = lambda t: t.rearrange("p h d -> p (h d)")
        kq_r = ["A1.%d" % (8 + h) for h in hs] + ["A1.%d" % h for h in hs]
        for hh, h in enumerate(hs):
            self.ts("dve", G["Lg"][:, hh, :], self.cf("ltri"), g8[:, h:h + 1], ALU.mult, ["cstf", "gtok"], [K("Lg")])
        yield
        for hh, h in enumerate(hs):
            cs = slice(hh * 128, (hh + 1) * 128)
            self.mm(ps[0][:, cs], self.cf("su"), G["Lg"][:, hh, :], ["cstf", K("Lg")], [pk[0]])
            self.mm(ps[1][:, cs], A1[:, 8 + h, blk], A1[:, 8 + h, blk], kq_r, [pk[1]])
            self.mm(ps[2][:, cs], A1[:, 8 + h, blk], A1[:, h, blk], kq_r, [pk[2]])
        yield
        self.act(f4(G["decTm"]), ps[0][:, :], AF.Exp, [pk[0]], [K("decTm")])
        yield
        self.tt("dve", G["decTm"], G["decTm"], self.cf4("muincl"), ALU.mult, [K("decTm"), "cstf"], [K("decTm")])
        yield
        self.tt("dve", f4(G["qkTm"]), ps[2][:, :], f4(G["decTm"]), ALU.mult, [pk[2], K("decTm")], [K("qkTm")])
        self.tt("dve", f4(G["Lg"]), ps[1][:, :], f4(G["decTm"]), ALU.mult, [pk[1], K("decTm")], [K("Lg")])
        yield
        self.tt("dve", G["MT"], G["Lg"],
                self.beta[:, tb, hg * 4:hg * 4 + 4].unsqueeze(2).to_broadcast([128, 4, 128]), ALU.mult,
                [K("Lg"), "beta"], [K("MT")])
        yield
        for hh in range(4):
            self.tr(psb[3][:, hh * 128:(hh + 1) * 128], G["MT"][:, hh, :], self.cb("ident"), [K("MT"), "cstb"], [pk[3]])
        yield
        self.cp("act", f4(G["M"]), psb[3][:, 0:512], [pk[3]], [K("M")])
        yield
        Nn, Nt, N2, N2t = "Na", "Nb", "Nc", "Nd"
        Pn, Pt, Pn2, Pt2 = "Pa", "Pb", "Pc", "Pd"
        self.tt("dve", G[Nn], G["M"], self.cb4("mndn"), ALU.mult, [K("M"), "cstb"], [K(Nn)])
        self.tt("dve", G[Nt], G["MT"], self.cb4("mndtn"), ALU.mult, [K("MT"), "cstb"], [K(Nt)])
        yield
        self.tt("dve", G[Pn], G[Nn], self.cb4("ident"), ALU.add, [K(Nn), "cstb"], [K(Pn)])
        self.tt("dve", G[Pt], G[Nt], self.cb4("ident"), ALU.add, [K(Nt), "cstb"], [K(Pt)])
        nstep = int(np.log2(NBK)) - 1
        for s_ in range(nstep):
            for hh in range(4):
                cs = slice(hh * 128, (hh + 1) * 128)
                self.mm(ps[0][:, cs], G[Nt][:, hh, :], G[Nn][:, hh, :], [K(Nt), K(Nn)], [pk[0]])
                self.mm(ps[1][:, cs], G[Nn][:, hh, :], G[Nt][:, hh, :], [K(Nt), K(Nn)], [pk[1]])
            yield
            self.cp("act", f4(G[N2]), ps[0][:, :], [pk[0]], [K(N2)])
            self.cp("act", f4(G[N2t]), ps[1][:, :], [pk[1]], [K(N2t)])
            yield
            for hh in range(4):
                cs = slice(hh * 128, (hh + 1) * 128)
                self.mm(ps[2][:, cs], G[N2t][:, hh, :], G[Pn][:, hh, :], [K(N2t), K(Pn)], [pk[2]])
                self.mm(ps[3][:, cs], G[N2][:, hh, :], G[Pt][:, hh, :], [K(N2), K(Pt)], [pk[3]])
            yield
            self.tt("dve", f4(G[Pn2]), f4(G[Pn]), ps[2][:, :], ALU.add, [K(Pn), pk[2]], [K(Pn2)])
            self.tt("dve", f4(G[Pt2]), f4(G[Pt]), ps[3][:, :], ALU.add, [K(Pt), pk[3]], [K(Pt2)])
            yield
            Nn, Nt, N2, N2t = N2, N2t, Nn, Nt
            Pn, Pt, Pn2, Pt2 = Pn2, Pt2, Pn, Pt
        T, U, T2, U2 = Pn, Pt, Pn2, Pt2
        E_, F_, X_, Y_ = Nn, Nt, N2, N2t
        b = NBK
        while b < 128:
            lastlvl = (b == 64)
            self.tt("dve", G[E_], G["M"], self.cb4("me%d" % b), ALU.mult, [K("M"), "cstb"], [K(E_)])
            if not lastlvl:
                self.tt("dve", G[F_], G["MT"], self.cb4("me%dt" % b), ALU.mult, [K("MT"), "cstb"], [K(F_)])
            yield
            for hh in range(4):
                cs = slice(hh * 128, (hh + 1) * 128)
                self.mm(ps[0][:, cs], G[E_][:, hh, :], G[U][:, hh, :], [K(E_), K(U)], [pk[0]])
                if not lastlvl:
                    self.mm(ps[1][:, cs], G[F_][:, hh, :], G[T][:, hh, :], [K(F_), K(T)], [pk[1]])
            yield
            self.cp("act", f4(G[Y_]), ps[0][:, :], [pk[0]], [K(Y_)])
            if not lastlvl:
                self.cp("act", f4(G[X_]), ps[1][:, :], [pk[1]], [K(X_)])
            yield
            for hh in range(4):
                cs = slice(hh * 128, (hh + 1) * 128)
                self.mm(ps[2][:, cs], G[T][:, hh, :], G[Y_][:, hh, :], [K(T), K(Y_)], [pk[2]])
                if not lastlvl:
                    self.mm(ps[3][:, cs], G[U][:, hh, :], G[X_][:, hh, :], [K(U), K(X_)], [pk[3]])
            yield
            self.tt("dve", f4(G[U2]), f4(G[U]), ps[2][:, :], ALU.subtract, [K(U), pk[2]], [K(U2)])
            if not lastlvl:
                self.tt("dve", f4(G[T2]), f4(G[T]), ps[3][:, :], ALU.subtract, [K(T), pk[3]], [K(T2)])
            yield
            T, U, T2, U2 = T2, U2, T, U
            b *= 2
        self.inv_result[hg] = U

    def scan_chain(self, tb, hg, G, gp, U, bx, by):
        A1 = self.A1
        blk = slice(tb * 128, (tb + 1) * 128)
        sm = self.gsm
        hs = [hg * 4 + hh for hh in range(4)]
        hsl = slice(hg * 4, hg * 4 + 4)
        X, Y = self.ps[bx], self.ps[by]
        xk, yk = "ps%d" % bx, "ps%d" % by
        X3 = X[:, :].rearrange("p (h d) -> p h d", h=4)
        Y3 = Y[:, :].rearrange("p (h d) -> p h d", h=4)
        bc = lambda ap: ap.unsqueeze(2).to_broadcast([128, 4, 128])
        Sk, Sbk = "S%d" % hg, "Sbf%d" % hg
        for hh, h in enumerate(hs):
            cs = slice(hh * 128, (hh + 1) * 128)
            self.mm(X[:, cs], A1[:, 8 + h, blk], self.Sbf[:, h, :], ["A1.%d" % (8 + h), Sbk], [xk])
            self.mm(Y[:, cs], A1[:, h, blk], self.Sbf[:, h, :], ["A1.%d" % h, Sbk], [yk])
        yield
        tS, tSk = self.tmpf()
        tS3 = tS[:, 0:512].rearrange("p (h d) -> p h d", h=4)
        self.tt("dve", tS3, X3, bc(sm[:, 24 + hg * 4:28 + hg * 4]), ALU.mult, [xk, "gsm"], [tSk])
        o1, o1k = self.tmpf()
        o13 = o1[:, 0:512].rearrange("p (h d) -> p h d", h=4)
        self.tt("dve", o13, Y3, bc(sm[:, 16 + hg * 4:20 + hg * 4]), ALU.mult, [yk, "gsm"], [o1k])
        yield
        r, rk = self.tmpb()
        r3 = r[:, :].rearrange("p (h d) -> p h d", h=4)
        self.tt("dve", r3, tS3, self.vtok[:, hsl, :], ALU.add, [tSk, "vtok"], [rk])
        yield
        for hh in range(4):
            cs = slice(hh * 128, (hh + 1) * 128)
            self.mm(X[:, cs], G[U][:, hh, :], r3[:, hh, :], [gp + U, rk], [xk])
        yield
        vn, vk = self.tmpb()
        vn3 = vn[:, :].rearrange("p (h d) -> p h d", h=4)
        self.tt("dve", vn3, X3, bc(self.beta[:, tb, hsl]), ALU.mult, [xk, "beta"], [vk])
        yield
        for hh, h in enumerate(hs):
            cs = slice(hh * 128, (hh + 1) * 128)
            self.mm(Y[:, cs], G["qkTm"][:, hh, :], vn3[:, hh, :], [gp + "qkTm", vk], [yk])
            self.mm(X[:, cs], self.kdec[:, h, :], vn3[:, hh, :], ["kdec", vk], [xk])
        yield
        self.tt("dve", self.otok[:, hg * 512:(hg + 1) * 512], o1[:, 0:512], Y[:, :], ALU.add, [o1k, yk], ["otok"])
        self.tt("dve", self.S[:, hsl, :], self.S[:, hsl, :], bc(sm[:, 40 + hg * 4:44 + hg * 4]), ALU.mult, [Sk, "gsm"], [Sk])
        yield
        self.tt("dve", self.S[:, hsl, :], self.S[:, hsl, :], X3, ALU.add, [Sk, xk], [Sk])
        yield
        self.cp("act", self.Sbf[:, hsl, :], self.S[:, hsl, :], [Sk], [Sbk])

    def lockstep(self, gens):
        gens = list(gens)
        while gens:
            nxt = []
            for g in gens:
                try:
                    next(g)
                    nxt.append(g)
                except StopIteration:
                    pass
            gens = nxt

    def gdn_block(self, tb):
        A1 = self.A1
        blk = slice(tb * 128, (tb + 1) * 128)
        sm = self.gsm
        g8 = self.gtok[:, tb, :]
        for (dst, dk, u0, bank) in ((self.ktok, "ktok", 8, 5), (self.vtok, "vtok", 16, 6)):
            for h in range(H):
                self.tr(self.psb[bank][:, h * 128:(h + 1) * 128], A1[:, u0 + h, blk], self.cb("ident"),
                        ["A1.%d" % (u0 + h), "cstb"], ["ps%d" % bank])
            self.cp("act", dst[:].rearrange("p h d -> p (h d)"), self.psb[bank][:, 0:1024], ["ps%d" % bank], [dk])
        self.mm(self.ps[7][:, 0:8], self.cf("ltri"), g8, ["cstf", "gtok"], ["ps7"])
        self.mm(self.ps[7][:, 8:16], self.cf("ones"), g8, ["cstf", "gtok"], ["ps7"])
        self.cp("dve", sm[:, 0:16], self.ps[7][:, 0:16], ["ps7"], ["gsm"])
        self.act(sm[:, 16:24], sm[:, 0:8], AF.Exp, ["gsm"], ["gsm"])
        self.ts("dve", sm[:, 24:32], sm[:, 16:24], -1.0, ALU.mult, ["gsm"], ["gsm"])
        self.tt("dve", sm[:, 32:40], sm[:, 8:16], sm[:, 0:8], ALU.subtract, ["gsm"], ["gsm"])
        self.act(sm[:, 32:40], sm[:, 32:40], AF.Exp, ["gsm"], ["gsm"])
        self.act(sm[:, 40:48], sm[:, 8:16], AF.Exp, ["gsm"], ["gsm"])
        self.tt("dve", self.kdec[:], self.ktok[:], sm[:, 32:40].unsqueeze(2).to_broadcast([128, H, 128]), ALU.mult,
                ["ktok", "gsm"], ["kdec"])
        self.inv_result = {}
        self.lockstep([self.inv_chain(tb, hg, self.gqs[hg][0], self.gqs[hg][1], [4 * hg + i for i in range(4)])
                       for hg in range(2)])
        self.lockstep([self.scan_chain(tb, hg, self.gqs[hg][0], self.gqs[hg][1], self.inv_result[hg], 2 * hg, 2 * hg + 1)
                       for hg in range(2)])
        self.onorm_and_T(tb, 128)

    def stage(self, k):
        return [(self.t1, "t1"), (self.t2, "t2"), (self.otok, "otok")][k]

    def load_sample_hist(self):
        for (srcd, dst, nch, nj) in ((self.sq, self.hsq, 24, 3), (self.ssc, self.hss, 8, 2)):
            for j in range(nj):
                for k in range(nch // 8):
                    t, tk = self.stage(k)
                    self.dma("aux", t[:NSAMP, :], srcd[:, j, k * 1024:(k + 1) * 1024], "hl", [], [tk])
                    for cc in range(8):
                        self.tr(self.ps[6][:, cc * NSAMP:(cc + 1) * NSAMP], t[:NSAMP, cc * 128:(cc + 1) * 128],
                                self.cf("ident")[:NSAMP, :NSAMP], [tk, "cstf"], ["ps6"])
                    self.cp("dve", dst[:, k * 8:(k + 1) * 8, j, :],
                            self.ps[6][:, 0:8 * NSAMP].rearrange("p (c b) -> p c b", c=8), ["ps6"], ["hsamp"])

    def gdn_sample(self):
        A1 = self.A1
        NS = NSAMP
        sm = self.gsm
        st = self.stat
        for (dst, dk, u0, bank) in ((self.qtok[:NS, :], "qtok", 0, 4), (self.ktok[:NS].rearrange("p h d -> p (h d)"), "ktok", 8, 5),
                                    (self.vtok[:NS].rearrange("p h d -> p (h d)"), "vtok", 16, 6)):
            for h in range(H):
                self.tr(self.psb[bank][:NS, h * 128:(h + 1) * 128], A1[:, u0 + h, 0:NS], self.cb("ident"),
                        ["A1.%d" % (u0 + h), "cstb"], ["ps%d" % bank])
            self.cp("act", dst, self.psb[bank][:NS, 0:1024], ["ps%d" % bank], [dk])
        a = sm[:NS, 0:8]
        self.act(a, self.gtok[:NS, 0, :], AF.Exp, ["gtok"], ["gsm"])
        q3 = self.qtok[:NS, :].rearrange("p (h d) -> p h d", h=H)
        t13 = self.t1[:NS, :].rearrange("p (h d) -> p h d", h=H)
        t23 = self.t2[:NS, :].rearrange("p (h d) -> p h d", h=H)
        o3 = self.otok[:NS, :].rearrange("p (h d) -> p h d", h=H)
        self.tt("dve", t13, q3, self.ktok[:NS], ALU.mult, ["qtok", "ktok"], ["t1"])
        self.P.add("dve", lambda e: e.tensor_reduce(sm[:NS, 8:16], t13, AX.X, ALU.add), ["t1"], ["gsm"])
        i16 = self.i16b[:].rearrange("p (a b) -> p a b", a=NS)
        for h in range(H):
            self.tt("dve", self.kTm[:, h, :, :], A1[:, 8 + h:9 + h, 0:NS].to_broadcast([128, NS, NS]), i16, ALU.mult,
                    ["A1.%d" % (8 + h), "i16b"], ["kTm"])
            self.tt("dve", self.qTm[:, h, :, :], A1[:, h:h + 1, 0:NS].to_broadcast([128, NS, NS]), i16, ALU.mult,
                    ["A1.%d" % h, "i16b"], ["qTm"])
        for b in range(NS):
            i2 = b % 2
            self.dma("aux", self.Sin[i2][:], self.sg[b].rearrange("h k v -> k h v"), "sin%d" % i2, [], ["Sin%d" % i2])
            self.cp("act" if b % 2 == 0 else "dve", self.Sinb[i2][:], self.Sin[i2][:], ["Sin%d" % i2], ["Sinb%d" % i2])
            for h in range(H):
                bk, bq = h // 4, 2 + h // 4
                cs = slice((h % 4) * 128, (h % 4 + 1) * 128)
                first = (b == 0 and h % 4 == 0)
                self.mm(self.ps[bk][:NS, cs], self.kTm[:, h, b, :], self.Sinb[i2][:, h, :], ["kTm", "Sinb%d" % i2],
                        ["ps%d" % bk], start=first, stop=(b == NS - 1))
                self.mm(self.ps[bq][:NS, cs], self.qTm[:, h, b, :], self.Sinb[i2][:, h, :], ["qTm", "Sinb%d" % i2],
                        ["ps%d" % bq], start=first, stop=(b == NS - 1))
        a_b = a.unsqueeze(2).to_broadcast([NS, H, 128])
        for half in range(2):
            hsl = slice(half * 4, half * 4 + 4)
            k3 = self.ps[half][:NS, :].rearrange("p (h d) -> p h d", h=4)
            qs3 = self.ps[2 + half][:NS, :].rearrange("p (h d) -> p h d", h=4)
            ab = a[:, hsl].unsqueeze(2).to_broadcast([NS, 4, 128])
            self.tt("dve", t13[:, hsl, :], k3, ab, ALU.mult, ["ps%d" % half, "gsm"], ["t1"])
            self.tt("dve", t13[:, hsl, :], self.vtok[:NS, hsl, :], t13[:, hsl, :], ALU.subtract, ["vtok", "t1"], ["t1"])
            self.tt("dve", t13[:, hsl, :], t13[:, hsl, :],
                    self.beta[:NS, 0, hsl].unsqueeze(2).to_broadcast([NS, 4, 128]), ALU.mult, ["t1", "beta"], ["t1"])
            self.tt("dve", t23[:, hsl, :], qs3, ab, ALU.mult, ["ps%d" % (2 + half), "gsm"], ["t2"])
            self.tt("dve", o3[:, hsl, :], t13[:, hsl, :], sm[:NS, 8 + half * 4:12 + half * 4].unsqueeze(2).to_broadcast([NS, 4, 128]),
                    ALU.mult, ["t1", "gsm"], ["otok"])
            self.tt("dve", o3[:, hsl, :], o3[:, hsl, :], t23[:, hsl, :], ALU.add, ["otok", "t2"], ["otok"])
        dbf = self.xb16
        self.cp("act", dbf[:NS, :], self.t1[:NS, :], ["t1"], ["xb16"])
        ad = self.t2[:NS, 0:128].rearrange("p (b h) -> p b h", b=NS)
        idr = self.cf("ident")[:NS, 0:NS].unsqueeze(2).to_broadcast([NS, NS, H])
        self.tt("dve", ad, a.unsqueeze(1).to_broadcast([NS, NS, H]), idr, ALU.mult, ["gsm", "cstf"], ["t2"])
        self.mm(self.ps[4][:, 0:128], self.cf("ones")[:NS, :], self.t2[:NS, 0:128], ["cstf", "t2"], ["ps4"])
        self.cp("dve", self.abc[:], self.ps[4][:, 0:128], ["ps4"], ["abc"])
        kflat = self.ktok[:NS].rearrange("p h d -> p (h d)")
        for b in range(NS):
            i2 = b % 2
            self.dma("aux", self.Sin[i2][:], self.sg[b].rearrange("h k v -> k h v"), "sin%d" % i2, [], ["Sin%d" % i2])
            self.ts("dve", self.kmask[i2][:NS, :], kflat, self.cf("ident")[:NS, b:b + 1], ALU.mult,
                    ["ktok", "cstf"], ["kmask%d" % i2])
            for h in range(H):
                pb_ = 5 + h // 4
                cs = slice((h % 4) * 128, (h % 4 + 1) * 128)
                self.mm(self.ps[pb_][:, cs], self.kmask[i2][:NS, h * 128:(h + 1) * 128], dbf[:NS, h * 128:(h + 1) * 128],
                        ["kmask%d" % i2, "xb16"], ["ps%d" % pb_])
                self.stt(self.Sin[i2][:, h, :], self.Sin[i2][:, h, :], self.abc[:, b * 8 + h:b * 8 + h + 1],
                         self.ps[pb_][:, cs], ALU.mult, ALU.add, ["Sin%d" % i2, "abc", "ps%d" % pb_], ["Sin%d" % i2])
            self.dma("aux", self.sgs[b].rearrange("h k v -> k h v"), self.Sin[i2][:], "sout%d" % i2, ["Sin%d" % i2], ["sgs%d" % b])
            self.outkeys.append("sgs%d" % b)
        self.onorm_and_T(0, NS)


_CACHE = {}


WBIG_LEN = 2 * (3 * D * HID) + D * IN_W + 4 * D * D + 256 * D


def pack_wbig(weights, specs, offs, tot):
    out = np.empty((tot,), np.float32)
    for spec, (off, n) in zip(specs, offs):
        parts = []
        for (name, r0, r1, c0, c1) in spec:
            w = weights[name][r0:r1, c0:c1]
            kc = (r1 - r0) // 128
            parts.append(w.reshape(kc, 128, c1 - c0).transpose(1, 0, 2).reshape(128, kc * (c1 - c0)))
        out[off:off + 128 * n] = np.concatenate(parts, axis=1).reshape(-1)
    return out


def kernel(x_prompt, x_sample, p_prompt, p_sample, state_gdn, state_qkv_conv, state_sc_conv,
           ffn1_w_gate, ffn1_w_up, ffn1_w_down, ln1_g, ln1_b,
           w_in, w_conv_qkv, A_log, dt_bias, w_onorm, w_p_gdn, w_conv_sc, w_p_sc, w_o, ln2_g, ln2_b,
           ffn2_w_gate, ffn2_w_up, ffn2_w_down, ln3_g, ln3_b,
           w_ple_gate, w_ple_proj, ln4_g, ln4_b, _debug=None):
    f = lambda a: np.ascontiguousarray(np.asarray(a, dtype=np.float32))
    weights = {"ffn1_w_gate": f(ffn1_w_gate)[0], "ffn1_w_up": f(ffn1_w_up)[0], "ffn1_w_down": f(ffn1_w_down)[0],
               "w_in": f(w_in)[0], "w_p_gdn": f(w_p_gdn)[0], "w_p_sc": f(w_p_sc)[0], "w_o": f(w_o)[0],
               "ffn2_w_gate": f(ffn2_w_gate)[0], "ffn2_w_up": f(ffn2_w_up)[0], "ffn2_w_down": f(ffn2_w_down)[0],
               "w_ple_gate": f(w_ple_gate)[0], "w_ple_proj": f(w_ple_proj)[0]}
    bld = Builder(debug=_debug)
    bld.wbig_len = WBIG_LEN
    nc = bld.build()
    assert bld.slab_tot == WBIG_LEN or _debug, (bld.slab_tot, WBIG_LEN)
    assert bld.slab_tot <= WBIG_LEN
    wbig = np.zeros((WBIG_LEN,), np.float32)
    wbig[:bld.slab_tot] = pack_wbig(weights, bld.slab_specs, bld.slab_off, bld.slab_tot)
    lnp = np.stack([f(ln1_g)[0], f(ln1_b)[0], f(ln2_g)[0], f(ln2_b)[0], f(ln3_g)[0], f(ln3_b)[0], f(ln4_g)[0], f(ln4_b)[0]])
    wcq = np.ascontiguousarray(f(w_conv_qkv)[0].reshape(4, 24, 128).transpose(2, 1, 0).reshape(128, 96))
    wcs = np.ascontiguousarray(f(w_conv_sc)[0].reshape(3, 8, 128).transpose(2, 1, 0).reshape(128, 24))
    smallp = np.stack([f(A_log)[0], f(dt_bias)[0]])
    cst, cst2 = make_consts()
    cst2 = np.ascontiguousarray(cst2.reshape(128, -1))
    i16 = np.ascontiguousarray(np.broadcast_to(np.eye(16, dtype=np.float32).reshape(1, 256), (128, 256)))
    xp = f(x_prompt)
    xsm = f(x_sample)[:, 0, :]
    ppr = f(p_prompt)[0]
    psm = f(p_sample)[0, :, 0, :]
    sg = f(state_gdn)[0]
    sq = f(state_qkv_conv)[0]
    ssc = f(state_sc_conv)[0]
    in_maps = []
    for c in range(8):
        sl = slice(c * NSAMP, (c + 1) * NSAMP)
        in_maps.append({"x": xp[c], "pp": ppr[c], "xs": xsm[sl], "psm": psm[sl], "sg": sg[sl], "sq": sq[sl], "ssc": ssc[sl],
                        "wbig": wbig, "lnp": lnp, "wcq": wcq, "wcs": wcs, "smallp": smallp, "won": f(w_onorm)[0],
                        "cst": cst, "cst2": cst2, "i16": i16})
    ncores = (_debug or {}).get("ncores", 8)
    res = run_bass_kernel_spmd(nc, in_maps[:ncores], core_ids=list(range(ncores)))
    R = list(res.results)
    while len(R) < 8:
        R.append({k: np.zeros_like(v) for k, v in R[0].items()})
    y = np.stack([R[c]["y"] for c in range(8)])
    ys = np.concatenate([R[c]["ys"] for c in range(8)])[:, None, :]
    sgp = np.stack([R[c]["sgp"] for c in range(8)])[None]
    sqp = np.stack([R[c]["sqp"] for c in range(8)])[None]
    ssp = np.stack([R[c]["ssp"] for c in range(8)])[None]
    sgs = np.concatenate([R[c]["sgs"] for c in range(8)])[None]
    sqs = np.concatenate([R[c]["sqs"] for c in range(8)])[None]
    sss = np.concatenate([R[c]["sss"] for c in range(8)])[None]
    return (y.astype(np.float32), ys.astype(np.float32), sgp.astype(np.float32), sqp.astype(np.float32),
            ssp.astype(np.float32), sgs.astype(np.float32), sqs.astype(np.float32), sss.astype(np.float32))
```

```python
import contextlib
import numpy as np
import concourse.bass as bass
import concourse.mybir as mybir
from concourse.bass_utils import run_bass_kernel_spmd

F32 = mybir.dt.float32
BF16 = mybir.dt.bfloat16
AF = mybir.ActivationFunctionType
ALU = mybir.AluOpType
AX = mybir.AxisListType

D = 1024
SEQ = 2048
NSAMP = 16
HID = 2816
NJ = HID // 128
H = 8
QKV_W = 3072
IN_W = 9232
Z0, BETA0, A0, B0, C0, H0, GG0, GS0 = 3072, 4096, 4104, 4112, 5136, 6160, 7184, 8208
ALPHA = 2.0 ** 0.25
LN_EPS = 1e-5
RMS_EPS = 1e-6
L2_EPS = 1e-6
NTP = 512
SLOT = 4096
NSLOT = 4
NBK = 16

COMPUTE = ("pe", "act", "dve", "pool")


class _Op:
    __slots__ = ("eng", "fn", "r", "w", "key", "eidx", "kn", "waits", "done", "inc")

    def __init__(self, eng, fn, r, w, key):
        self.eng = eng
        self.fn = fn
        self.r = r
        self.w = w
        self.key = key
        self.eidx = -1
        self.kn = 0
        self.waits = []
        self.done = None
        self.inc = False


class Prog:
    def __init__(self, nc):
        self.nc = nc
        self.ops = []

    def add(self, eng, fn, r=(), w=(), key=None):
        self.ops.append(_Op(eng, fn, tuple(r), tuple(w), key))

    def finalize(self):
        last_w = {}
        readers = {}
        issue = {e: {} for e in ("pe", "act", "dve", "pool", "sp")}
        ecount = {e: 0 for e in issue}
        kcount = {}
        kops = {}
        eops = {e: [] for e in issue}
        for op in self.ops:
            e = op.eng
            deps = set()
            for res in op.r:
                lw = last_w.get(res)
                if lw is not None:
                    deps.add(lw)
            for res in op.w:
                lw = last_w.get(res)
                if lw is not None:
                    deps.add(lw)
                for rd in readers.get(res, ()):
                    deps.add(rd)
            deps.discard(op)
            clock = issue[e]
            if op.key is None:
                op.eidx = ecount[e]
                ecount[e] += 1
                eops[e].append(op)
            else:
                n = kcount.get(op.key, 0) + 1
                kcount[op.key] = n
                op.kn = n
                kops.setdefault(op.key, []).append(op)
                if n > 1:
                    deps.add(kops[op.key][n - 2])
            best = {}
            dma_deps = []
            for d in deps:
                if d.key is None:
                    b = best.get(d.eng)
                    if b is None or d.eidx > b.eidx:
                        best[d.eng] = d
                else:
                    dma_deps.append(d)
            newclock = None
            for f, d in best.items():
                if clock.get(f, -1) >= d.eidx:
                    continue
                if f == e and op.key is None:
                    if e == "pe":
                        continue
                    if e != "pool" and (op.eidx - d.eidx) > 2:
                        continue
                op.waits.append(("c", f, d.eidx))
                d.inc = True
                if newclock is None:
                    newclock = dict(clock)
                for k, v in d.done.items():
                    if newclock.get(k, -1) < v:
                        newclock[k] = v
            for d in dma_deps:
                kk = ("dma", d.key)
                cur = clock if newclock is None else newclock
                if cur.get(kk, 0) >= d.kn:
                    continue
                op.waits.append(("d", d.key, d.kn))
                if newclock is None:
                    newclock = dict(clock)
                for k, v in d.done.items():
                    if newclock.get(k, -1) < v:
                        newclock[k] = v
            if newclock is not None:
                issue[e] = newclock
                clock = newclock
            done = dict(clock)
            if op.key is None:
                done[e] = op.eidx
            else:
                done[("dma", op.key)] = op.kn
            op.done = done
            for res in op.r:
                readers.setdefault(res, []).append(op)
            for res in op.w:
                last_w[res] = op
                readers[res] = []
        self.rank = {}
        for e, lst in eops.items():
            k = 0
            for op in lst:
                if op.inc:
                    k += 1
                    self.rank[(e, op.eidx)] = k
        self.keys = list(kcount.keys())
        for op in self.ops:
            op.done = None
            if len(op.waits) > 1:
                m = {}
                for t, a, b in op.waits:
                    if (t, a) not in m or m[(t, a)] < b:
                        m[(t, a)] = b
                op.waits = [(t, a, b) for (t, a), b in m.items()]

    def emit(self, es):
        nc = self.nc
        sems = {}
        for e in COMPUTE:
            sems[e] = es.enter_context(nc.semaphore("s_" + e))
        ksem = {}
        for k in self.keys:
            ksem[k] = es.enter_context(nc.semaphore("k_" + str(k)))
        block = es.enter_context(nc.Block())
        rank = self.rank

        def run(ename, eng):
            for op in self.ops:
                if op.eng != ename:
                    continue
                for t, a, b in op.waits:
                    if t == "c":
                        eng.wait_ge(sems[a], rank[(a, b)])
                    else:
                        eng.wait_ge(ksem[a], 16 * b)
                if op.fn is None:
                    continue
                ins = op.fn(eng)
                if op.key is not None:
                    ins.then_inc(ksem[op.key], 16)
                elif op.inc:
                    ins.then_inc(sems[ename], 1)

        @block.tensor
        def _(eng):
            run("pe", eng)

        @block.scalar
        def _(eng):
            run("act", eng)

        @block.vector
        def _(eng):
            run("dve", eng)

        @block.gpsimd
        def _(eng):
            run("pool", eng)

        @block.sync
        def _(eng):
            run("sp", eng)


CSTF_NAMES = ["ident", "ltri", "su", "muincl", "ones"]
CSTB_NAMES = ["ident", "ones", "mndn", "mndtn", "me16", "me16t", "me32", "me32t", "me64", "me64t"]


def make_consts():
    i = np.arange(128)[:, None]
    j = np.arange(128)[None, :]
    c = {}
    c["ident"] = (i == j)
    c["ltri"] = (i <= j)
    c["su"] = (i > j)
    c["muincl"] = (j >= i)
    c["ones"] = np.ones((128, 128), bool)
    nd = (i // NBK == j // NBK) & (i > j)
    c["mndn"] = -1.0 * nd
    c["mndtn"] = -1.0 * nd.T
    for b in (16, 32, 64):
        e = (i // (2 * b) == j // (2 * b)) & ((i % (2 * b)) >= b) & ((j % (2 * b)) < b)
        c["me%d" % b] = e
        c["me%dt" % b] = e.T
    arrf = np.stack([np.asarray(c[n], np.float32) for n in CSTF_NAMES], axis=1)
    arrb = np.stack([np.asarray(c[n], np.float32) for n in CSTB_NAMES], axis=1)
    return np.ascontiguousarray(arrf), np.ascontiguousarray(arrb)


class Builder:
    def __init__(self, debug=None):
        self.debug = debug or {}
        self.slab_specs = []
        self.slab_off = []
        self.slab_tot = 0
        self.nslab_pass = None

    def mm(self, out, lhsT, rhs, r, w, start=True, stop=True):
        self.P.add("pe", lambda e: e.matmul(out, lhsT, rhs, start=start, stop=stop), r, w)

    def tr(self, out, in_, ident, r, w):
        self.P.add("pe", lambda e: e.transpose(out, in_, ident), r, w)

    def act(self, out, in_, func, r, w, bias=None, scale=None):
        kw = {}
        if bias is not None:
            kw["bias"] = bias
        if scale is not None:
            kw["scale"] = scale
        self.P.add("act", lambda e: e.activation(out, in_, func, **kw), r, w)

    def tt(self, eng, out, in0, in1, op, r, w):
        self.P.add(eng, lambda e: e.tensor_tensor(out, in0, in1, op), r, w)

    def ts(self, eng, out, in0, s1, op0, r, w, s2=None, op1=None):
        if op1 is None:
            self.P.add(eng, lambda e: e.tensor_scalar(out, in0, s1, None, op0), r, w)
        else:
            self.P.add(eng, lambda e: e.tensor_scalar(out, in0, s1, s2, op0, op1), r, w)

    def stt(self, out, in0, scalar, in1, op0, op1, r, w):
        self.P.add("dve", lambda e: e.scalar_tensor_tensor(out, in0, scalar, in1, op0, op1), r, w)

    def cp(self, eng, out, in_, r, w):
        if eng == "act":
            self.P.add("act", lambda e: e.activation(out, in_, AF.Copy), r, w)
        else:
            self.P.add(eng, lambda e: e.tensor_copy(out, in_), r, w)

    def dq(self):
        return "sp" if self.recording else "pool"

    def dma(self, eng, out, in_, key, r, w, slow=False):
        if eng == "aux":
            eng = self.dq()
        if slow:
            self.P.add(eng, lambda e: e.dma_start(out=out, in_=in_, allow_slow_non_contiguous=True), r, w, key=key)
        else:
            self.P.add(eng, lambda e: e.dma_start(out=out, in_=in_), r, w, key=key)

    def slab(self, spec):
        if self.recording:
            self.slab_specs.append(spec)
            n = sum(((r1 - r0) // 128) * (c1 - c0) for (_, r0, r1, c0, c1) in spec)
            assert n <= SLOT, n
            self.slab_off.append((self.slab_tot, n))
            self.slab_tot += 128 * n
        si = self.slab_i % self.nslab_pass if self.nslab_pass else self.slab_i
        off, n = self.slab_off[si]
        slot = self.slab_i % NSLOT
        self.slab_i += 1
        t = self.wring[slot]
        key = "w%d" % slot
        scr = self.wscr[off:off + 128 * n].rearrange("(p n) -> p n", p=128)
        if self.recording:
            src = self.wbig[off:off + 128 * n].rearrange("(p n) -> p n", p=128)
            self.dma("pool", t[:, 0:n], src, key, r=[], w=[key])
            if self.debug.get("npass", 4) > 1 or not self.debug.get("nosample", False):
                self.dma("sp", scr, t[:, 0:n], "wb%d" % slot, r=[key], w=["wscr%d" % si])
        else:
            self.dma("sp", t[:, 0:n], scr, key, r=["wscr%d" % si], w=[key])
        views = []
        o = 0
        for (_, r0, r1, c0, c1) in spec:
            kc = (r1 - r0) // 128
            nc_ = c1 - c0
            views.append(t[:, o:o + kc * nc_].rearrange("p (k n) -> p k n", k=kc))
            o += kc * nc_
        return views, key

    def build(self):
        nc = bass.Bass("TRN2", target_bir_lowering=False)
        self.nc = nc
        self.es = contextlib.ExitStack()
        with self.es:
            self._build_inner()
        return nc

    def dram_in(self, name, shape, dt=F32):
        return self.nc.dram_tensor(name, list(shape), dt, kind="ExternalInput").ap()

    def dram_out(self, name, shape, dt=F32):
        return self.nc.dram_tensor(name, list(shape), dt, kind="ExternalOutput").ap()

    def sb(self, name, shape, dt):
        return self.es.enter_context(self.nc.sbuf_tensor(name, list(shape), dt))

    def _build_inner(self):
        nc = self.nc
        self.P = Prog(nc)
        P = self.P
        self.x = self.dram_in("x", [SEQ, D])
        self.pp = self.dram_in("pp", [SEQ, 256])
        self.xs = self.dram_in("xs", [NSAMP, D])
        self.psm = self.dram_in("psm", [NSAMP, 256])
        self.sg = self.dram_in("sg", [NSAMP, H, 128, 128])
        self.sq = self.dram_in("sq", [NSAMP, 3, QKV_W])
        self.ssc = self.dram_in("ssc", [NSAMP, 2, D])
        self.wbig = self.dram_in("wbig", [self.wbig_len])
        self.wscr = self.nc.dram_tensor("wscr", [self.wbig_len], BF16, kind="Internal").ap()
        self.lnp = self.dram_in("lnp", [8, D])
        self.wcq_d = self.dram_in("wcq", [128, 24 * 4])
        self.wcs_d = self.dram_in("wcs", [128, 8 * 3])
        self.smallp = self.dram_in("smallp", [2, 8])
        self.won_d = self.dram_in("won", [128])
        self.cst_d = self.dram_in("cst", [128, len(CSTF_NAMES), 128])
        self.cst2_d = self.dram_in("cst2", [128, len(CSTB_NAMES) * 128])
        self.i16_d = self.dram_in("i16", [128, 256])
        self.y = self.dram_out("y", [SEQ, D])
        self.ys = self.dram_out("ys", [NSAMP, D])
        self.sgp = self.dram_out("sgp", [H, 128, 128])
        self.sqp = self.dram_out("sqp", [3, QKV_W])
        self.ssp = self.dram_out("ssp", [2, D])
        self.sgs = self.dram_out("sgs", [NSAMP, H, 128, 128])
        self.sqs = self.dram_out("sqs", [NSAMP, 3, QKV_W])
        self.sss = self.dram_out("sss", [NSAMP, 2, D])
        self.outkeys = []

        sb = self.sb
        self.wring = [sb("wr%d" % i, [128, SLOT], BF16) for i in range(NSLOT)]
        self.xres = sb("xres", [128, 4, D], F32)
        self.xT = sb("xT", [128, 8, NTP], BF16)
        self.gbt = sb("gbt", [128, 2, D], F32)
        self.cstf = sb("cstf", [128, len(CSTF_NAMES), 128], F32)
        self.cstb = sb("cstb", [128, len(CSTB_NAMES), 128], BF16)
        self.i16b = sb("i16b", [128, 256], BF16)
        self.wcq = sb("wcq_s", [128, 24, 4], F32)
        self.wcs = sb("wcs_s", [128, 8, 3], F32)
        self.wonb = sb("wonb", [128, 128], F32)
        self.smallb = sb("smallb", [128, 16], F32)
        self.negA = sb("negA", [128, 8], F32)
        self.histq = sb("histq", [128, 24, 3], F32)
        self.hists = sb("hists", [128, 8, 2], F32)
        self.S = sb("S", [128, H, 128], F32)
        self.Sbf = sb("Sbf", [128, H, 128], BF16)
        self.A1 = sb("A1", [128, 24, NTP], BF16)
        self.ztok = sb("ztok", [128, 4, D], BF16)
        self.ktok = sb("ktok", [128, H, 128], BF16)
        self.vtok = sb("vtok", [128, H, 128], BF16)
        self.batok = sb("batok", [128, 4, 16], F32)
        self.beta = sb("beta", [128, 4, 8], F32)
        self.gtok = sb("gtok", [128, 4, 8], F32)
        self.tf = [sb("tf%d" % i, [128, NTP + 4], F32) for i in range(12)]
        self.tfi = 0
        self.tb16 = [sb("tb%d" % i, [128, NTP], BF16) for i in range(4)]
        self.tbi = 0
        self.t1 = sb("t1", [128, D], F32)
        self.t2 = sb("t2", [128, D], F32)
        self.otok = sb("otok", [128, D], F32)
        self.xb16 = sb("xb16", [128, D], BF16)
        self.stat = sb("stat", [128, 4, 16], F32)
        self.stat3 = sb("stat3", [128, 16], F32)
        self.xb16b = sb("xb16b", [128, D], BF16)
        self.pT = sb("pT", [128, 2, NTP], BF16)
        self.pb = sb("pb", [128, 256], BF16)
        GQ = [("decTm", F32), ("Lg", F32), ("qkTm", BF16), ("MT", BF16), ("M", BF16),
              ("Na", BF16), ("Nb", BF16), ("Nc", BF16), ("Nd", BF16), ("Pa", BF16), ("Pb", BF16),
              ("Pc", BF16), ("Pd", BF16)]
        self.gqs = []
        self.arenaA = sb("arenaA", [128, 15 * 512], BF16)
        self.arenaB = sb("arenaB", [128, 15 * 512], BF16)
        self.gq_names = [nm for nm, _ in GQ]
        for ar, pfx in ((self.arenaA, "gA_"), (self.arenaB, "gB_")):
            gX = {}
            o = 0
            for nm, dt in GQ:
                n = 1024 if dt == F32 else 512
                v = ar[:, o:o + n]
                if dt == F32:
                    v = v.bitcast(F32)
                gX[nm] = v.rearrange("p (h d) -> p h d", h=4)
                o += n
            self.gqs.append((gX, pfx))
        self.kdec = sb("kdec", [128, H, 128], BF16)
        self.gsm = sb("gsm", [128, 64], F32)
        fA = lambda o, n: self.arenaA[:, o:o + n]
        fB = lambda o, n: self.arenaB[:, o:o + n]
        self.Sin = [fA(k * 2048, 2048).bitcast(F32).rearrange("p (h d) -> p h d", h=H) for k in range(3)]
        self.hss = fA(6144, 512).bitcast(F32).rearrange("p (c j b) -> p c j b", c=8, j=2)
        self.Sinb = [fA(6656, 1024).rearrange("p (h d) -> p h d", h=H), fB(6400, 1024).rearrange("p (h d) -> p h d", h=H)]
        self.kTm = fB(0, 2048).rearrange("p (h a b) -> p h a b", h=H, a=NSAMP)
        self.qTm = fB(2048, 2048).rearrange("p (h a b) -> p h a b", h=H, a=NSAMP)
        self.hsq = fB(4096, 2304).bitcast(F32).rearrange("p (c j b) -> p c j b", c=24, j=3)
        self.kmask = [sb("kmask%d" % i, [NSAMP, D], BF16) for i in range(2)]
        self.qtok = sb("qtok", [NSAMP, D], BF16)
        self.abc = sb("abc", [128, 128], F32)
        self.ps = [self.es.enter_context(nc.psum_tensor("ps%d" % i, [128, 512], F32)) for i in range(8)]
        self.psb = [p.bitcast(BF16) for p in self.ps]

        self.recording = True
        self.slab_i = 0
        self.setup()
        npass = self.debug.get("npass", 4)
        for pi in range(npass):
            self.layer_pass(pi, NTP, sample=False, last=(pi == npass - 1))
            if pi == 0:
                self.recording = False
                self.nslab_pass = len(self.slab_off)
        if not self.debug.get("nosample", False):
            gbk = [p + nm for p in ("gA_", "gB_") for nm in self.gq_names]
            P.add("dve", lambda e: e.memset(self.gsm[:, 60:64], 0.0), gbk,
                  gbk + ["kTm", "qTm", "hsamp", "Sin0", "Sin1", "Sin2", "Sinb0", "Sinb1"])
            self.layer_pass(0, NSAMP, sample=True, last=True)
        P.add("sp", None, r=self.outkeys)
        P.finalize()
        P.emit(self.es)

    def cf(self, name):
        return self.cstf[:, CSTF_NAMES.index(name), :]

    def cb(self, name):
        return self.cstb[:, CSTB_NAMES.index(name), :]

    def cb4(self, name):
        i = CSTB_NAMES.index(name)
        return self.cstb[:, i:i + 1, :].to_broadcast([128, 4, 128])

    def cf4(self, name):
        i = CSTF_NAMES.index(name)
        return self.cstf[:, i:i + 1, :].to_broadcast([128, 4, 128])

    def tmpf(self):
        i = self.tfi % len(self.tf)
        self.tfi += 1
        return self.tf[i], "tf%d" % i

    def tmpb(self):
        i = self.tbi % len(self.tb16)
        self.tbi += 1
        return self.tb16[i], "tb%d" % i

    def setup(self):
        d = self.dma
        d("sp", self.cstf[:], self.cst_d, "c0", [], ["cstf"])
        nb = len(CSTB_NAMES) * 128
        for k in range(0, nb, 1024):
            n = min(1024, nb - k)
            d("sp", self.t1[:, 0:n], self.cst2_d[:, k:k + n], "c1", [], ["t1"])
            self.cp("dve", self.cstb[:].rearrange("p c d -> p (c d)")[:, k:k + n], self.t1[:, 0:n], ["t1"], ["cstb"])
        d("sp", self.t2[:, 0:256], self.i16_d, "c1", [], ["t2"])
        self.cp("dve", self.i16b[:], self.t2[:, 0:256], ["t2"], ["i16b"])
        d("sp", self.wcq[:].rearrange("p c j -> p (c j)"), self.wcq_d, "c2", [], ["wcq"])
        d("sp", self.wcs[:].rearrange("p c j -> p (c j)"), self.wcs_d, "c3", [], ["wcs"])
        d("sp", self.wonb[:], self.won_d.partition_broadcast(128), "c4", [], ["wonb"])
        d("sp", self.smallb[:], self.smallp.rearrange("a b -> (a b)").partition_broadcast(128), "c5", [], ["smallb"])
        self.act(self.negA[:], self.smallb[:, 0:8], AF.Exp, ["smallb"], ["negA"])
        self.ts("dve", self.negA[:], self.negA[:], -1.0, ALU.mult, ["negA"], ["negA"])
        self.P.add("dve", lambda e: e.memset(self.S[:], 0.0), [], ["S0", "S1"])
        self.P.add("dve", lambda e: e.memset(self.Sbf[:], 0.0), [], ["Sbf0", "Sbf1"])
        self.P.add("dve", lambda e: e.memset(self.histq[:], 0.0), [], ["histq"])
        self.P.add("dve", lambda e: e.memset(self.hists[:], 0.0), [], ["hists"])

    def make_xT(self, tb, TB, bank):
        ps, psk = self.psb[bank], "ps%d" % bank
        xb, xbk = ((self.xb16, "xb16"), (self.xb16b, "xb16b"))[tb % 2]
        self.cp("act", xb[:TB, :], self.xres[:TB, tb, :], ["xres%d" % tb], [xbk])
        for c in range(8):
            self.tr(ps[:, c * TB:(c + 1) * TB], xb[:TB, c * 128:(c + 1) * 128], self.cb("ident")[:TB, :TB],
                    [xbk, "cstb"], [psk])
        self.cp("dve", self.xT[:, :, tb * TB:(tb + 1) * TB],
                ps[:, 0:8 * TB].rearrange("p (c t) -> p c t", c=8), [psk], ["xT%d" % tb])

    def layer_norm(self, idx, NB, TB, final_out=None):
        self.dma("aux", self.gbt[:, 0, :], self.lnp[2 * idx, :].partition_broadcast(128), "gb0", [], ["gbt0"])
        self.dma("aux", self.gbt[:, 1, :], self.lnp[2 * idx + 1, :].partition_broadcast(128), "gb1", [], ["gbt1"])
        eps = LN_EPS / (ALPHA * ALPHA)
        st = self.stat
        for tb in range(NB):
            xr = self.xres[:TB, tb, :]
            xk = "xres%d" % tb
            self.P.add("dve", lambda e, xr=xr, tb=tb: e.bn_stats(st[:TB, tb, 0:6], xr[:, 0:512]), [xk], ["stat"])
            self.P.add("dve", lambda e, xr=xr, tb=tb: e.bn_stats(st[:TB, tb, 6:12], xr[:, 512:1024]), [xk], ["stat"])
            self.P.add("dve", lambda e, tb=tb: e.bn_aggr(st[:TB, tb, 12:14], st[:TB, tb, 0:12]), ["stat"], ["stat"])
        self.act(st[:TB, 0:NB, 14], st[:TB, 0:NB, 13], AF.Sqrt, ["stat"], ["stat2"], bias=eps)
        self.P.add("dve", lambda e: e.reciprocal(st[:TB, 0:NB, 15], st[:TB, 0:NB, 14]), ["stat2"], ["stat2"])
        for tb in range(NB):
            xr = self.xres[:TB, tb, :]
            xk = "xres%d" % tb
            self.stt(self.t1[:TB, :], xr, st[:TB, tb, 12:13], self.gbt[:TB, 0, :], ALU.subtract, ALU.mult,
                     [xk, "stat", "gbt0"], ["t1"])
            self.stt(xr, self.t1[:TB, :], st[:TB, tb, 15:16], self.gbt[:TB, 1, :], ALU.mult, ALU.add,
                     ["t1", "stat2", "gbt1"], [xk])
            if final_out is not None:
                ok = final_out[1] + str(tb)
                self.dma("aux", final_out[0][tb * TB:(tb + 1) * TB, :], xr, "yo%d" % tb, [xk], [ok])
                self.outkeys.append(ok)
            else:
                self.make_xT(tb, TB, (2 * tb) % 8)

    def ffn(self, pfx, NB, TB):
        NT = NB * TB
        for j0 in range(0, NJ, 2):
            (wg, wu), wk = self.slab([(pfx + "_w_gate", 0, D, j0 * 128, j0 * 128 + 256),
                                      (pfx + "_w_up", 0, D, j0 * 128, j0 * 128 + 256)])
            for jj in range(2):
                j = j0 + jj
                bg, bu = 2 * (j % 2), 2 * (j % 2) + 1
                for kc in range(8):
                    self.mm(self.ps[bg][:, :NT], wg[:, kc, jj * 128:(jj + 1) * 128], self.xT[:, kc, :NT],
                            [wk, "xT0", "xT1", "xT2", "xT3"], ["ps%d" % bg], start=(kc == 0), stop=(kc == 7))
                for kc in range(8):
                    self.mm(self.ps[bu][:, :NT], wu[:, kc, jj * 128:(jj + 1) * 128], self.xT[:, kc, :NT],
                            [wk, "xT0", "xT1", "xT2", "xT3"], ["ps%d" % bu], start=(kc == 0), stop=(kc == 7))
                t, tk = self.tmpf()
                self.act(t[:, :NT], self.ps[bg][:, :NT], AF.Silu, ["ps%d" % bg], [tk])
                self.tt("dve", self.A1[:, j, :NT], t[:, :NT], self.ps[bu][:, :NT], ALU.mult,
                        [tk, "ps%d" % bu], ["A1.%d" % j])
        for j0 in range(0, NJ, 4):
            j1 = min(NJ, j0 + 4)
            (wd,), wk = self.slab([(pfx + "_w_down", j0 * 128, j1 * 128, 0, D)])
            for jj in range(j1 - j0):
                j = j0 + jj
                for tb in range(NB):
                    for nh in range(2):
                        b = tb * 2 + nh
                        self.mm(self.ps[b][:TB, :], self.A1[:, j, tb * TB:(tb + 1) * TB], wd[:, jj, nh * 512:(nh + 1) * 512],
                                [wk, "A1.%d" % j], ["ps%d" % b], start=(j == 0), stop=(j == NJ - 1))
        c = 0.5 / ALPHA
        for tb in range(NB):
            for nh in range(2):
                b = tb * 2 + nh
                xr = self.xres[:TB, tb, nh * 512:(nh + 1) * 512]
                self.stt(xr, self.ps[b][:TB, :], c, xr, ALU.mult, ALU.add, ["ps%d" % b, "xres%d" % tb], ["xres%d" % tb])

    def layer_pass(self, pi, NT, sample, last):
        TB = min(128, NT)
        NB = NT // TB
        self.slab_i = 0 if self.recording else self.slab_i
        if not sample:
            src, psrc = self.x[pi * NT:(pi + 1) * NT, :], self.pp[pi * NT:(pi + 1) * NT, :]
            yout = (self.y[pi * NT:(pi + 1) * NT, :], "y%d_" % pi)
        else:
            src, psrc = self.xs, self.psm
            yout = (self.ys, "ys_")
        for tb in range(NB):
            self.dma("aux", self.xres[:TB, tb, :], src[tb * TB:(tb + 1) * TB, :], "x%d" % tb, [], ["xres%d" % tb])
            self.make_xT(tb, TB, tb % 8)
        stop = self.debug.get("stop")
        self.ffn("ffn1", NB, TB)
        if stop == "ffn1":
            return self.dump(yout, NB, TB)
        self.layer_norm(0, NB, TB)
        if stop == "ln1":
            return self.dump(yout, NB, TB)
        self.mixers(pi, NB, TB, sample, last)
        if stop == "mix":
            return self.dump(yout, NB, TB)
        self.layer_norm(1, NB, TB)
        self.ffn("ffn2", NB, TB)
        self.layer_norm(2, NB, TB)
        if stop == "ln3":
            return self.dump(yout, NB, TB)
        self.ple(psrc, NB, TB)
        self.layer_norm(3, NB, TB, final_out=yout)

    def dump(self, yout, NB, TB):
        for tb in range(NB):
            ok = yout[1] + str(tb)
            self.dma("aux", yout[0][tb * TB:(tb + 1) * TB, :], self.xres[:TB, tb, :], "yo%d" % tb, ["xres%d" % tb], [ok])
            self.outkeys.append(ok)

    def ple(self, psrc, NB, TB):
        NT = NB * TB
        for tb in range(NB):
            pf, pfk = self.tmpf()
            self.dma("aux", pf[:TB, 0:256], psrc[tb * TB:(tb + 1) * TB, :], "pf", [], [pfk])
            self.cp("act", self.pb[:TB, :], pf[:TB, 0:256], [pfk], ["pb"])
            for c in range(2):
                self.tr(self.psb[7][:, c * TB:(c + 1) * TB], self.pb[:TB, c * 128:(c + 1) * 128],
                        self.cb("ident")[:TB, :TB], ["pb", "cstb"], ["ps7"])
            self.cp("dve", self.pT[:, :, tb * TB:(tb + 1) * TB],
                    self.psb[7][:, 0:2 * TB].rearrange("p (c t) -> p c t", c=2), ["ps7"], ["pT"])
        for nh in range(2):
            (wg,), wgk = self.slab([("w_ple_gate", 0, D, nh * 512, (nh + 1) * 512)])
            (wp,), wpk = self.slab([("w_ple_proj", 0, 256, nh * 512, (nh + 1) * 512)])
            for tb in range(NB):
                bg, bp = 2 * (tb % 2), 2 * (tb % 2) + 1
                for kc in range(8):
                    self.mm(self.ps[bg][:TB, :], self.xT[:, kc, tb * TB:(tb + 1) * TB], wg[:, kc, :],
                            [wgk, "xT0", "xT1", "xT2", "xT3"], ["ps%d" % bg], start=(kc == 0), stop=(kc == 7))
                for kc in range(2):
                    self.mm(self.ps[bp][:TB, :], self.pT[:, kc, tb * TB:(tb + 1) * TB], wp[:, kc, :],
                            [wpk, "pT"], ["ps%d" % bp], start=(kc == 0), stop=(kc == 1))
                t, tk = self.tmpf()
                self.act(t[:TB, :512], self.ps[bg][:TB, :], AF.Sigmoid, ["ps%d" % bg], [tk])
                self.tt("dve", t[:TB, :512], t[:TB, :512], self.ps[bp][:TB, :], ALU.mult, [tk, "ps%d" % bp], [tk])
                xr = self.xres[:TB, tb, nh * 512:(nh + 1) * 512]
                self.stt(xr, t[:TB, :512], 1.0 / ALPHA, xr, ALU.mult, ALU.add, [tk, "xres%d" % tb], ["xres%d" % tb])

    def conv_chunk(self, psbank, NT, taps_hist, wts, ntap, hist_tile, hist_key, sample, src_is_psum=True, src=None):
        H_ = ntap - 1
        cbt, cbk = self.tmpf()
        if src_is_psum:
            self.cp("act", cbt[:, H_:H_ + NT], self.ps[psbank][:, :NT], ["ps%d" % psbank], [cbk])
        else:
            src(cbt[:, H_:H_ + NT], cbk)
        if not sample:
            self.cp("dve", cbt[:, 0:H_], hist_tile, [hist_key], [cbk])
            self.cp("dve", hist_tile, cbt[:, NT:NT + H_], [cbk], [hist_key])
            taps = [cbt[:, j:j + NT] for j in range(ntap)]
            tr_ = [cbk]
        else:
            taps = [taps_hist[j] for j in range(H_)] + [cbt[:, H_:H_ + NT]]
            tr_ = [cbk, "hsamp"]
        acc, ak = self.tmpf()
        self.ts("dve", acc[:, :NT], taps[0], wts[0], ALU.mult, tr_ + ["wc"], [ak])
        for j in range(1, ntap):
            self.stt(acc[:, :NT], taps[j], wts[j], acc[:, :NT], ALU.mult, ALU.add, tr_ + ["wc", ak], [ak])
        return acc, ak, cbt, cbk

    def mixers(self, pi, NB, TB, sample, last):
        NT = NB * TB
        A1 = self.A1
        if sample:
            self.load_sample_hist()
        def finish(grp):
            for (c, so, sk, sq, sqk, cbt, cbk) in grp:
                if sample:
                    self.tr(self.ps[6][:NT, (c % 4) * 128:(c % 4 + 1) * 128], cbt[:, 3:3 + NT], self.cf("ident"),
                            [cbk, "cstf"], ["ps6"])
                    if c % 4 == 3:
                        stg, stk = self.stage(c // 8)
                        self.cp("act", stg[:NT, (c % 8 - 3) * 128:(c % 8 + 1) * 128], self.ps[6][:NT, :], ["ps6"], [stk])
            qk = [g_ for g_ in grp if g_[0] < 16]
            sds = []
            for (c, so, sk, sq, sqk, cbt, cbk) in qk:
                b2 = 4 + c % 2 if sample else 4 + c % 4
                self.mm(self.ps[b2][:, :NT], self.cb("ones"), sq[:, :NT], [sqk, "cstb"], ["ps%d" % b2])
            for (c, so, sk, sq, sqk, cbt, cbk) in qk:
                b2 = 4 + c % 2 if sample else 4 + c % 4
                sd, sdk = self.tmpf()
                sds.append((sd, sdk))
                self.act(sd[:, :NT], self.ps[b2][:, :NT], AF.Ln, ["ps%d" % b2], [sdk], bias=L2_EPS)
            for (sd, sdk) in sds:
                self.act(sd[:, :NT], sd[:, :NT], AF.Exp, [sdk], [sdk], scale=-0.5)
            for (c, so, sk, sq, sqk, cbt, cbk), (sd, sdk) in zip(qk, sds):
                const = 128.0 ** -0.5 if c < 8 else 1.0
                self.stt(A1[:, c, :NT], so[:, :NT], const, sd[:, :NT], ALU.mult, ALU.mult, [sk, sdk], ["A1.%d" % c])

        pend = None
        for g in range(6):
            (wq,), wk = self.slab([("w_in", 0, D, g * 512, (g + 1) * 512)])
            for pr in range(2):
                cs_ = [g * 4 + pr * 2, g * 4 + pr * 2 + 1]
                for c in cs_:
                    jj = c % 4
                    bank = c % 4
                    for kc in range(8):
                        self.mm(self.ps[bank][:, :NT], wq[:, kc, jj * 128:(jj + 1) * 128], self.xT[:, kc, :NT],
                                [wk, "xT0", "xT1", "xT2", "xT3"], ["ps%d" % bank], start=(kc == 0), stop=(kc == 7))
                convs = []
                for c in cs_:
                    th = [self.hsq[:, c, j, :] for j in range(3)] if sample else None
                    wts = [self.wcq[:, c, j:j + 1] for j in range(4)]
                    convs.append(self.conv_chunk(c % 4, NT, th, wts, 4, self.histq[:, c, :], "histq%d" % c, sample))
                cur = []
                for c, (acc, ak, cbt, cbk) in zip(cs_, convs):
                    if c >= 16:
                        self.act(A1[:, c, :NT], acc[:, :NT], AF.Silu, [ak], ["A1.%d" % c])
                        cur.append((c, None, None, None, None, cbt, cbk))
                    else:
                        self.act(acc[:, :NT], acc[:, :NT], AF.Silu, [ak], [ak])
                        sq, sqk = self.tmpb()
                        self.act(sq[:, :NT], acc[:, :NT], AF.Square, [ak], [sqk])
                        cur.append((c, acc, ak, sq, sqk, cbt, cbk))
                if pend is not None:
                    finish(pend)
                pend = cur
        finish(pend)
        if sample:
            for k in range(3):
                stg, stk = self.stage(k)
                self.dma("aux", self.sqs[:, 2, k * 1024:(k + 1) * 1024], stg[:NSAMP, :], "so0", [stk], ["sqs2_%d" % k])
                self.outkeys.append("sqs2_%d" % k)
            self.dma("aux", self.sqs[:, 0:2, :], self.sq[:, 1:3, :], "so1", [], ["sqs01"])
            self.outkeys += ["sqs01"]
        elif last:
            for j in range(3):
                self.dma("aux", self.sqp[j, :].rearrange("(c p) -> p c", p=128), self.histq[:, :, j], "so0",
                         ["histq%d" % c for c in range(24)], ["sqp%d" % j], slow=True)
                self.outkeys.append("sqp%d" % j)
        if self.debug.get("mstop") == "A":
            return
        for nh in range(2):
            (wz,), wk = self.slab([("w_in", 0, D, Z0 + nh * 512, Z0 + (nh + 1) * 512)])
            for tb in range(NB):
                b = 4 + tb % 2
                for kc in range(8):
                    self.mm(self.ps[b][:TB, :], self.xT[:, kc, tb * TB:(tb + 1) * TB], wz[:, kc, :],
                            [wk, "xT0", "xT1", "xT2", "xT3"], ["ps%d" % b], start=(kc == 0), stop=(kc == 7))
                self.act(self.ztok[:TB, tb, nh * 512:(nh + 1) * 512], self.ps[b][:TB, :], AF.Silu, ["ps%d" % b], ["ztok%d" % tb])
        (wba,), wk = self.slab([("w_in", 0, D, BETA0, BETA0 + 16)])
        for tb in range(NB):
            for kc in range(8):
                self.mm(self.ps[6][:TB, 0:16], self.xT[:, kc, tb * TB:(tb + 1) * TB], wba[:, kc, :],
                        [wk, "xT0", "xT1", "xT2", "xT3"], ["ps6"], start=(kc == 0), stop=(kc == 7))
            self.act(self.beta[:TB, tb, :], self.ps[6][:TB, 0:8], AF.Sigmoid, ["ps6"], ["beta"])
            self.tt("dve", self.batok[:TB, tb, 8:16], self.ps[6][:TB, 8:16], self.smallb[:TB, 8:16], ALU.add,
                    ["ps6", "smallb"], ["batok"])
        for tb in range(NB):
            self.act(self.batok[:TB, tb, 0:8], self.batok[:TB, tb, 8:16], AF.Exp, ["batok"], ["batok"])
        for tb in range(NB):
            self.act(self.batok[:TB, tb, 0:8], self.batok[:TB, tb, 0:8], AF.Ln, ["batok"], ["batok"], bias=1.0)
            self.tt("dve", self.gtok[:TB, tb, :], self.batok[:TB, tb, 0:8], self.negA[:TB, :], ALU.mult,
                    ["batok", "negA"], ["gtok"])
        if self.debug.get("mstop") == "B":
            return
        if sample:
            self.gdn_sample()
        else:
            for tb in range(NB):
                self.gdn_block(tb)
            if last:
                self.dma("aux", self.sgp.rearrange("h k v -> k h v"), self.S[:], "so1", ["S0", "S1"], ["sgp"])
                self.outkeys.append("sgp")
        if self.debug.get("mstop") == "C":
            return
        for c in range(8):
            (wB, wC, wH), wk = self.slab([("w_in", 0, D, B0 + c * 128, B0 + (c + 1) * 128),
                                          ("w_in", 0, D, C0 + c * 128, C0 + (c + 1) * 128),
                                          ("w_in", 0, D, H0 + c * 128, H0 + (c + 1) * 128)])
            bB, bC, bH = 0 + 3 * (c % 2), 1 + 3 * (c % 2), 2 + 3 * (c % 2)
            for (w_, b_) in ((wC, bC), (wH, bH), (wB, bB)):
                for kc in range(8):
                    self.mm(self.ps[b_][:, :NT], w_[:, kc, :], self.xT[:, kc, :NT], [wk, "xT0", "xT1", "xT2", "xT3"], ["ps%d" % b_],
                            start=(kc == 0), stop=(kc == 7))
            ct, ck = self.tmpf()
            self.cp("act", ct[:, :NT], self.ps[bC][:, :NT], ["ps%d" % bC], [ck])

            def src(dst, dk, ct=ct, ck=ck, bH=bH):
                self.tt("dve", dst, ct[:, :NT], self.ps[bH][:, :NT], ALU.mult, [ck, "ps%d" % bH], [dk])
            th = [self.hss[:, c, j, :] for j in range(2)] if sample else None
            wts = [self.wcs[:, c, j:j + 1] for j in range(3)]
            acc, ak, cbt, cbk = self.conv_chunk(None, NT, th, wts, 3, self.hists[:, c, :], "hists%d" % c, sample,
                                                src_is_psum=False, src=src)
            if sample:
                self.tr(self.ps[6][:NT, (c % 4) * 128:(c % 4 + 1) * 128], cbt[:, 2:2 + NT], self.cf("ident"),
                        [cbk, "cstf"], ["ps6"])
                if c % 4 == 3:
                    stg, stk = self.stage(0)
                    self.cp("act", stg[:NT, (c - 3) * 128:(c + 1) * 128], self.ps[6][:NT, :], ["ps6"], [stk])
            self.tt("dve", A1[:, c, :NT], acc[:, :NT], self.ps[bB][:, :NT], ALU.mult, [ak, "ps%d" % bB], ["A1.%d" % c])
        if sample:
            stg, stk = self.stage(0)
            self.dma("aux", self.sss[:, 1, :], stg[:NSAMP, 0:D], "so2", [stk], ["sss1"])
            self.dma("aux", self.sss[:, 0:1, :], self.ssc[:, 1:2, :], "so3", [], ["sss0"])
            self.outkeys += ["sss1", "sss0"]
        elif last:
            for j in range(2):
                self.dma("aux", self.ssp[j, :].rearrange("(c p) -> p c", p=128), self.hists[:, :, j], "so2",
                         ["hists%d" % c for c in range(8)], ["ssp%d" % j], slow=True)
                self.outkeys.append("ssp%d" % j)
        if self.debug.get("mstop") == "D":
            return
        for c in range(8):
            (wpg, wgg, wps, wgs), wk = self.slab([("w_p_gdn", 0, D, c * 128, (c + 1) * 128),
                                                  ("w_in", 0, D, GG0 + c * 128, GG0 + (c + 1) * 128),
                                                  ("w_p_sc", 0, D, c * 128, (c + 1) * 128),
                                                  ("w_in", 0, D, GS0 + c * 128, GS0 + (c + 1) * 128)])
            o = 4 * (c % 2)
            for (w_, b_, rhs_, rk) in ((wpg, o, A1[:, 16:24, :], ["A1.%d" % k for k in range(16, 24)]),
                                       (wgg, o + 1, self.xT, ["xT0", "xT1", "xT2", "xT3"]),
                                       (wps, o + 2, A1[:, 0:8, :], ["A1.%d" % k for k in range(8)]),
                                       (wgs, o + 3, self.xT, ["xT0", "xT1", "xT2", "xT3"])):
                for kc in range(8):
                    self.mm(self.ps[b_][:, :NT], w_[:, kc, :], rhs_[:, kc, :NT], [wk] + rk, ["ps%d" % b_],
                            start=(kc == 0), stop=(kc == 7))
            s1, s1k = self.tmpf()
            self.act(s1[:, :NT], self.ps[o + 1][:, :NT], AF.Sigmoid, ["ps%d" % (o + 1)], [s1k])
            self.tt("dve", s1[:, :NT], s1[:, :NT], self.ps[o][:, :NT], ALU.mult, [s1k, "ps%d" % o], [s1k])
            s2, s2k = self.tmpf()
            self.act(s2[:, :NT], self.ps[o + 3][:, :NT], AF.Sigmoid, ["ps%d" % (o + 3)], [s2k])
            self.tt("dve", s2[:, :NT], s2[:, :NT], self.ps[o + 2][:, :NT], ALU.mult, [s2k, "ps%d" % (o + 2)], [s2k])
            self.tt("dve", A1[:, 8 + c, :NT], s1[:, :NT], s2[:, :NT], ALU.add, [s1k, s2k], ["A1.%d" % (8 + c)])
        for nh in range(2):
            (wo,), wk = self.slab([("w_o", 0, D, nh * 512, (nh + 1) * 512)])
            for tb in range(NB):
                b = tb % 2
                for kc in range(8):
                    self.mm(self.ps[b][:TB, :], A1[:, 8 + kc, tb * TB:(tb + 1) * TB], wo[:, kc, :],
                            [wk, "A1.%d" % (8 + kc)], ["ps%d" % b], start=(kc == 0), stop=(kc == 7))
                xr = self.xres[:TB, tb, nh * 512:(nh + 1) * 512]
                self.stt(xr, self.ps[b][:TB, :], 1.0 / ALPHA, xr, ALU.mult, ALU.add, ["ps%d" % b, "xres%d" % tb], ["xres%d" % tb])

    def onorm_and_T(self, tb, TB):
        o3 = self.otok[:TB, :].rearrange("p (h d) -> p h d", h=H)
        t13 = self.t1[:TB, :].rearrange("p (h d) -> p h d", h=H)
        t23 = self.t2[:TB, :].rearrange("p (h d) -> p h d", h=H)
        st = self.stat3
        self.act(self.t1[:TB, :], self.otok[:TB, :], AF.Square, ["otok"], ["t1"])
        self.P.add("dve", lambda e: e.tensor_reduce(st[:TB, 0:8], t13, AX.X, ALU.add), ["t1"], ["stat3"])
        self.act(st[:TB, 0:8], st[:TB, 0:8], AF.Sqrt, ["stat3"], ["stat3"], bias=RMS_EPS, scale=1.0 / 128.0)
        self.P.add("dve", lambda e: e.reciprocal(st[:TB, 8:16], st[:TB, 0:8]), ["stat3"], ["stat3"])
        self.tt("dve", t13, o3, st[:TB, 8:16].unsqueeze(2).to_broadcast([TB, H, 128]), ALU.mult, ["otok", "stat3"], ["t1"])
        z3 = self.ztok[:TB, tb, :].rearrange("p (h d) -> p h d", h=H)
        self.tt("dve", t23, z3, self.wonb[:TB, :].unsqueeze(1).to_broadcast([TB, H, 128]), ALU.mult,
                ["ztok%d" % tb, "wonb"], ["t2"])
        self.tt("dve", self.xb16[:TB, :], self.t1[:TB, :], self.t2[:TB, :], ALU.mult, ["t1", "t2"], ["xb16"])
        for c in range(8):
            self.tr(self.psb[7][:, c * TB:(c + 1) * TB], self.xb16[:TB, c * 128:(c + 1) * 128], self.cb("ident")[:TB, :TB],
                    ["xb16", "cstb"], ["ps7"])
        self.cp("act", self.A1[:, 16:24, tb * TB:(tb + 1) * TB],
                self.psb[7][:, 0:8 * TB].rearrange("p (c t) -> p c t", c=8), ["ps7"],
                ["A1.%d" % k for k in range(16, 24)])

    def inv_chain(self, tb, hg, G, gp, pb):
        A1 = self.A1
        blk = slice(tb * 128, (tb + 1) * 128)
        g8 = self.gtok[:, tb, :]
        hs = [hg * 4 + hh for hh in range(4)]
        K = lambda nm: gp + nm
        pk = ["ps%d" % x for x in pb]
        ps = [self.ps[x] for x in pb]
        psb = [self.psb[x] for x in pb]
        f4 = lambda t: t.rearrange("p h d -> p (h d)")
        kq_r = ["A1.%d" % (8 + h) for h in hs] + ["A1.%d" % h for h in hs]
        for hh, h in enumerate(hs):
            self.ts("dve", G["Lg"][:, hh, :], self.cf("ltri"), g8[:, h:h + 1], ALU.mult, ["cstf", "gtok"], [K("Lg")])
        yield
        for hh, h in enumerate(hs):
            cs = slice(hh * 128, (hh + 1) * 128)
            self.mm(ps[0][:, cs], self.cf("su"), G["Lg"][:, hh, :], ["cstf", K("Lg")], [pk[0]])
            self.mm(ps[1][:, cs], A1[:, 8 + h, blk], A1[:, 8 + h, blk], kq_r, [pk[1]])
            self.mm(ps[2][:, cs], A1[:, 8 + h, blk], A1[:, h, blk], kq_r, [pk[2]])
        yield
        self.act(f4(G["decTm"]), ps[0][:, :], AF.Exp, [pk[0]], [K("decTm")])
        yield
        self.tt("dve", G["decTm"], G["decTm"], self.cf4("muincl"), ALU.mult, [K("decTm"), "cstf"], [K("decTm")])
        yield
        self.tt("dve", f4(G["qkTm"]), ps[2][:, :], f4(G["decTm"]), ALU.mult, [pk[2], K("decTm")], [K("qkTm")])
        self.tt("dve", f4(G["Lg"]), ps[1][:, :], f4(G["decTm"]), ALU.mult, [pk[1], K("decTm")], [K("Lg")])
        yield
        self.tt("dve", G["MT"], G["Lg"],
                self.beta[:, tb, hg * 4:hg * 4 + 4].unsqueeze(2).to_broadcast([128, 4, 128]), ALU.mult,
                [K("Lg"), "beta"], [K("MT")])
        yield
        for hh in range(4):
            self.tr(psb[3][:, hh * 128:(hh + 1) * 128], G["MT"][:, hh, :], self.cb("ident"), [K("MT"), "cstb"], [pk[3]])
        yield
        self.cp("act", f4(G["M"]), psb[3][:, 0:512], [pk[3]], [K("M")])
        yield
        Nn, Nt, N2, N2t = "Na", "Nb", "Nc", "Nd"
        Pn, Pt, Pn2, Pt2 = "Pa", "Pb", "Pc", "Pd"
        self.tt("dve", G[Nn], G["M"], self.cb4("mndn"), ALU.mult, [K("M"), "cstb"], [K(Nn)])
        self.tt("dve", G[Nt], G["MT"], self.cb4("mndtn"), ALU.mult, [K("MT"), "cstb"], [K(Nt)])
        yield
        self.tt("dve", G[Pn], G[Nn], self.cb4("ident"), ALU.add, [K(Nn), "cstb"], [K(Pn)])
        self.tt("dve", G[Pt], G[Nt], self.cb4("ident"), ALU.add, [K(Nt), "cstb"], [K(Pt)])
        nstep = int(np.log2(NBK)) - 1
        for s_ in range(nstep):
            for hh in range(4):
                cs = slice(hh * 128, (hh + 1) * 128)
                self.mm(ps[0][:, cs], G[Nt][:, hh, :], G[Nn][:, hh, :], [K(Nt), K(Nn)], [pk[0]])
                self.mm(ps[1][:, cs], G[Nn][:, hh, :], G[Nt][:, hh, :], [K(Nt), K(Nn)], [pk[1]])
            yield
            self.cp("act", f4(G[N2]), ps[0][:, :], [pk[0]], [K(N2)])
            self.cp("act", f4(G[N2t]), ps[1][:, :], [pk[1]], [K(N2t)])
            yield
            for hh in range(4):
                cs = slice(hh * 128, (hh + 1) * 128)
                self.mm(ps[2][:, cs], G[N2t][:, hh, :], G[Pn][:, hh, :], [K(N2t), K(Pn)], [pk[2]])
                self.mm(ps[3][:, cs], G[N2][:, hh, :], G[Pt][:, hh, :], [K(N2), K(Pt)], [pk[3]])
            yield
            self.tt("dve", f4(G[Pn2]), f4(G[Pn]), ps[2][:, :], ALU.add, [K(Pn), pk[2]], [K(Pn2)])
            self.tt("dve", f4(G[Pt2]), f4(G[Pt]), ps[3][:, :], ALU.add, [K(Pt), pk[3]], [K(Pt2)])
            yield
            Nn, Nt, N2, N2t = N2, N2t, Nn, Nt
            Pn, Pt, Pn2, Pt2 = Pn2, Pt2, Pn, Pt
        T, U, T2, U2 = Pn, Pt, Pn2, Pt2
        E_, F_, X_, Y_ = Nn, Nt, N2, N2t
        b = NBK
        while b < 128:
            lastlvl = (b == 64)
            self.tt("dve", G[E_], G["M"], self.cb4("me%d" % b), ALU.mult, [K("M"), "cstb"], [K(E_)])
            if not lastlvl:
                self.tt("dve", G[F_], G["MT"], self.cb4("me%dt" % b), ALU.mult, [K("MT"), "cstb"], [K(F_)])
            yield
            for hh in range(4):
                cs = slice(hh * 128, (hh + 1) * 128)
                self.mm(ps[0][:, cs], G[E_][:, hh, :], G[U][:, hh, :], [K(E_), K(U)], [pk[0]])
                if not lastlvl:
                    self.mm(ps[1][:, cs], G[F_][:, hh, :], G[T][:, hh, :], [K(F_), K(T)], [pk[1]])
            yield
            self.cp("act", f4(G[Y_]), ps[0][:, :], [pk[0]], [K(Y_)])
            if not lastlvl:
                self.cp("act", f4(G[X_]), ps[1][:, :], [pk[1]], [K(X_)])
            yield
            for hh in range(4):
                cs = slice(hh * 128, (hh + 1) * 128)
                self.mm(ps[2][:, cs], G[T][:, hh, :], G[Y_][:, hh, :], [K(T), K(Y_)], [pk[2]])
                if not lastlvl:
                    self.mm(ps[3][:, cs], G[U][:, hh, :], G[X_][:, hh, :], [K(U), K(X_)], [pk[3]])
            yield
            self.tt("dve", f4(G[U2]), f4(G[U]), ps[2][:, :], ALU.subtract, [K(U), pk[2]], [K(U2)])
            if not lastlvl:
                self.tt("dve", f4(G[T2]), f4(G[T]), ps[3][:, :], ALU.subtract, [K(T), pk[3]], [K(T2)])
            yield
            T, U, T2, U2 = T2, U2, T, U
            b *= 2
        self.inv_result[hg] = U

    def scan_chain(self, tb, hg, G, gp, U, bx, by):
        A1 = self.A1
        blk = slice(tb * 128, (tb + 1) * 128)
        sm = self.gsm
        hs = [hg * 4 + hh for hh in range(4)]
        hsl = slice(hg * 4, hg * 4 + 4)
        X, Y = self.ps[bx], self.ps[by]
        xk, yk = "ps%d" % bx, "ps%d" % by
        X3 = X[:, :].rearrange("p (h d) -> p h d", h=4)
        Y3 = Y[:, :].rearrange("p (h d) -> p h d", h=4)
        bc = lambda ap: ap.unsqueeze(2).to_broadcast([128, 4, 128])
        Sk, Sbk = "S%d" % hg, "Sbf%d" % hg
        for hh, h in enumerate(hs):
            cs = slice(hh * 128, (hh + 1) * 128)
            self.mm(X[:, cs], A1[:, 8 + h, blk], self.Sbf[:, h, :], ["A1.%d" % (8 + h), Sbk], [xk])
            self.mm(Y[:, cs], A1[:, h, blk], self.Sbf[:, h, :], ["A1.%d" % h, Sbk], [yk])
        yield
        tS, tSk = self.tmpf()
        tS3 = tS[:, 0:512].rearrange("p (h d) -> p h d", h=4)
        self.tt("dve", tS3, X3, bc(sm[:, 24 + hg * 4:28 + hg * 4]), ALU.mult, [xk, "gsm"], [tSk])
        o1, o1k = self.tmpf()
        o13 = o1[:, 0:512].rearrange("p (h d) -> p h d", h=4)
        self.tt("dve", o13, Y3, bc(sm[:, 16 + hg * 4:20 + hg * 4]), ALU.mult, [yk, "gsm"], [o1k])
        yield
        r, rk = self.tmpb()
        r3 = r[:, :].rearrange("p (h d) -> p h d", h=4)
        self.tt("dve", r3, tS3, self.vtok[:, hsl, :], ALU.add, [tSk, "vtok"], [rk])
        yield
        for hh in range(4):
            cs = slice(hh * 128, (hh + 1) * 128)
            self.mm(X[:, cs], G[U][:, hh, :], r3[:, hh, :], [gp + U, rk], [xk])
        yield
        vn, vk = self.tmpb()
        vn3 = vn[:, :].rearrange("p (h d) -> p h d", h=4)
        self.tt("dve", vn3, X3, bc(self.beta[:, tb, hsl]), ALU.mult, [xk, "beta"], [vk])
        yield
        for hh, h in enumerate(hs):
            cs = slice(hh * 128, (hh + 1) * 128)
            self.mm(Y[:, cs], G["qkTm"][:, hh, :], vn3[:, hh, :], [gp + "qkTm", vk], [yk])
            self.mm(X[:, cs], self.kdec[:, h, :], vn3[:, hh, :], ["kdec", vk], [xk])
        yield
        self.tt("dve", self.otok[:, hg * 512:(hg + 1) * 512], o1[:, 0:512], Y[:, :], ALU.add, [o1k, yk], ["otok"])
        self.tt("dve", self.S[:, hsl, :], self.S[:, hsl, :], bc(sm[:, 40 + hg * 4:44 + hg * 4]), ALU.mult, [Sk, "gsm"], [Sk])
        yield
        self.tt("dve", self.S[:, hsl, :], self.S[:, hsl, :], X3, ALU.add, [Sk, xk], [Sk])
        yield
        self.cp("act", self.Sbf[:, hsl, :], self.S[:, hsl, :], [Sk], [Sbk])

    def lockstep(self, gens):
        gens = list(gens)
        while gens:
            nxt = []
            for g in gens:
                try:
                    next(g)
                    nxt.append(g)
                except StopIteration:
                    pass
            gens = nxt

    def gdn_block(self, tb):
        A1 = self.A1
        blk = slice(tb * 128, (tb + 1) * 128)
        sm = self.gsm
        g8 = self.gtok[:, tb, :]
        for (dst, dk, u0, bank) in ((self.ktok, "ktok", 8, 5), (self.vtok, "vtok", 16, 6)):
            for h in range(H):
                self.tr(self.psb[bank][:, h * 128:(h + 1) * 128], A1[:, u0 + h, blk], self.cb("ident"),
                        ["A1.%d" % (u0 + h), "cstb"], ["ps%d" % bank])
            self.cp("act", dst[:].rearrange("p h d -> p (h d)"), self.psb[bank][:, 0:1024], ["ps%d" % bank], [dk])
        self.mm(self.ps[7][:, 0:8], self.cf("ltri"), g8, ["cstf", "gtok"], ["ps7"])
        self.mm(self.ps[7][:, 8:16], self.cf("ones"), g8, ["cstf", "gtok"], ["ps7"])
        self.cp("dve", sm[:, 0:16], self.ps[7][:, 0:16], ["ps7"], ["gsm"])
        self.act(sm[:, 16:24], sm[:, 0:8], AF.Exp, ["gsm"], ["gsm"])
        self.ts("dve", sm[:, 24:32], sm[:, 16:24], -1.0, ALU.mult, ["gsm"], ["gsm"])
        self.tt("dve", sm[:, 32:40], sm[:, 8:16], sm[:, 0:8], ALU.subtract, ["gsm"], ["gsm"])
        self.act(sm[:, 32:40], sm[:, 32:40], AF.Exp, ["gsm"], ["gsm"])
        self.act(sm[:, 40:48], sm[:, 8:16], AF.Exp, ["gsm"], ["gsm"])
        self.tt("dve", self.kdec[:], self.ktok[:], sm[:, 32:40].unsqueeze(2).to_broadcast([128, H, 128]), ALU.mult,
                ["ktok", "gsm"], ["kdec"])
        self.inv_result = {}
        self.lockstep([self.inv_chain(tb, hg, self.gqs[hg][0], self.gqs[hg][1], [4 * hg + i for i in range(4)])
                       for hg in range(2)])
        self.lockstep([self.scan_chain(tb, hg, self.gqs[hg][0], self.gqs[hg][1], self.inv_result[hg], 2 * hg, 2 * hg + 1)
                       for hg in range(2)])
        self.onorm_and_T(tb, 128)

    def stage(self, k):
        return [(self.t1, "t1"), (self.t2, "t2"), (self.otok, "otok")][k]

    def load_sample_hist(self):
        for (srcd, dst, nch, nj) in ((self.sq, self.hsq, 24, 3), (self.ssc, self.hss, 8, 2)):
            for j in range(nj):
                for k in range(nch // 8):
                    t, tk = self.stage(k)
                    self.dma("aux", t[:NSAMP, :], srcd[:, j, k * 1024:(k + 1) * 1024], "hl", [], [tk])
                    for cc in range(8):
                        self.tr(self.ps[6][:, cc * NSAMP:(cc + 1) * NSAMP], t[:NSAMP, cc * 128:(cc + 1) * 128],
                                self.cf("ident")[:NSAMP, :NSAMP], [tk, "cstf"], ["ps6"])
                    self.cp("dve", dst[:, k * 8:(k + 1) * 8, j, :],
                            self.ps[6][:, 0:8 * NSAMP].rearrange("p (c b) -> p c b", c=8), ["ps6"], ["hsamp"])

    def gdn_sample(self):
        A1 = self.A1
        NS = NSAMP
        sm = self.gsm
        st = self.stat
        for (dst, dk, u0, bank) in ((self.qtok[:NS, :], "qtok", 0, 4), (self.ktok[:NS].rearrange("p h d -> p (h d)"), "ktok", 8, 5),
                                    (self.vtok[:NS].rearrange("p h d -> p (h d)"), "vtok", 16, 6)):
            for h in range(H):
                self.tr(self.psb[bank][:NS, h * 128:(h + 1) * 128], A1[:, u0 + h, 0:NS], self.cb("ident"),
                        ["A1.%d" % (u0 + h), "cstb"], ["ps%d" % bank])
            self.cp("act", dst, self.psb[bank][:NS, 0:1024], ["ps%d" % bank], [dk])
        a = sm[:NS, 0:8]
        self.act(a, self.gtok[:NS, 0, :], AF.Exp, ["gtok"], ["gsm"])
        q3 = self.qtok[:NS, :].rearrange("p (h d) -> p h d", h=H)
        t13 = self.t1[:NS, :].rearrange("p (h d) -> p h d", h=H)
        t23 = self.t2[:NS, :].rearrange("p (h d) -> p h d", h=H)
        o3 = self.otok[:NS, :].rearrange("p (h d) -> p h d", h=H)
        self.tt("dve", t13, q3, self.ktok[:NS], ALU.mult, ["qtok", "ktok"], ["t1"])
        self.P.add("dve", lambda e: e.tensor_reduce(sm[:NS, 8:16], t13, AX.X, ALU.add), ["t1"], ["gsm"])
        i16 = self.i16b[:].rearrange("p (a b) -> p a b", a=NS)
        for h in range(H):
            self.tt("dve", self.kTm[:, h, :, :], A1[:, 8 + h:9 + h, 0:NS].to_broadcast([128, NS, NS]), i16, ALU.mult,
                    ["A1.%d" % (8 + h), "i16b"], ["kTm"])
            self.tt("dve", self.qTm[:, h, :, :], A1[:, h:h + 1, 0:NS].to_broadcast([128, NS, NS]), i16, ALU.mult,
                    ["A1.%d" % h, "i16b"], ["qTm"])
        for b in range(NS):
            i3, i2 = b % 3, b % 2
            self.dma("aux", self.Sin[i3], self.sg[b].rearrange("h k v -> k h v"), "sin%d" % i3, [], ["Sin%d" % i3])
            self.cp("act" if b % 2 == 0 else "dve", self.Sinb[i2], self.Sin[i3], ["Sin%d" % i3], ["Sinb%d" % i2])
            for h in range(H):
                bk, bq = h // 4, 2 + h // 4
                cs = slice((h % 4) * 128, (h % 4 + 1) * 128)
                first = (b == 0 and h % 4 == 0)
                self.mm(self.ps[bk][:NS, cs], self.kTm[:, h, b, :], self.Sinb[i2][:, h, :], ["kTm", "Sinb%d" % i2],
                        ["ps%d" % bk], start=first, stop=(b == NS - 1))
                self.mm(self.ps[bq][:NS, cs], self.qTm[:, h, b, :], self.Sinb[i2][:, h, :], ["qTm", "Sinb%d" % i2],
                        ["ps%d" % bq], start=first, stop=(b == NS - 1))
        a_b = a.unsqueeze(2).to_broadcast([NS, H, 128])
        for half in range(2):
            hsl = slice(half * 4, half * 4 + 4)
            k3 = self.ps[half][:NS, :].rearrange("p (h d) -> p h d", h=4)
            qs3 = self.ps[2 + half][:NS, :].rearrange("p (h d) -> p h d", h=4)
            ab = a[:, hsl].unsqueeze(2).to_broadcast([NS, 4, 128])
            self.tt("dve", t13[:, hsl, :], k3, ab, ALU.mult, ["ps%d" % half, "gsm"], ["t1"])
            self.tt("dve", t13[:, hsl, :], self.vtok[:NS, hsl, :], t13[:, hsl, :], ALU.subtract, ["vtok", "t1"], ["t1"])
            self.tt("dve", t13[:, hsl, :], t13[:, hsl, :],
                    self.beta[:NS, 0, hsl].unsqueeze(2).to_broadcast([NS, 4, 128]), ALU.mult, ["t1", "beta"], ["t1"])
            self.tt("dve", t23[:, hsl, :], qs3, ab, ALU.mult, ["ps%d" % (2 + half), "gsm"], ["t2"])
            self.tt("dve", o3[:, hsl, :], t13[:, hsl, :], sm[:NS, 8 + half * 4:12 + half * 4].unsqueeze(2).to_broadcast([NS, 4, 128]),
                    ALU.mult, ["t1", "gsm"], ["otok"])
            self.tt("dve", o3[:, hsl, :], o3[:, hsl, :], t23[:, hsl, :], ALU.add, ["otok", "t2"], ["otok"])
        dbf = self.xb16
        self.cp("act", dbf[:NS, :], self.t1[:NS, :], ["t1"], ["xb16"])
        ad = self.t2[:NS, 0:128].rearrange("p (b h) -> p b h", b=NS)
        idr = self.cf("ident")[:NS, 0:NS].unsqueeze(2).to_broadcast([NS, NS, H])
        self.tt("dve", ad, a.unsqueeze(1).to_broadcast([NS, NS, H]), idr, ALU.mult, ["gsm", "cstf"], ["t2"])
        self.mm(self.ps[4][:, 0:128], self.cf("ones")[:NS, :], self.t2[:NS, 0:128], ["cstf", "t2"], ["ps4"])
        self.cp("dve", self.abc[:], self.ps[4][:, 0:128], ["ps4"], ["abc"])
        kflat = self.ktok[:NS].rearrange("p h d -> p (h d)")
        for b in range(NS):
            i2 = b % 2
            i3 = (b + 1) % 3
            self.dma("aux", self.Sin[i3], self.sg[b].rearrange("h k v -> k h v"), "sin%d" % i3, [], ["Sin%d" % i3])
            self.ts("dve", self.kmask[i2][:NS, :], kflat, self.cf("ident")[:NS, b:b + 1], ALU.mult,
                    ["ktok", "cstf"], ["kmask%d" % i2])
            for h in range(H):
                pb_ = 5 + h // 4
                cs = slice((h % 4) * 128, (h % 4 + 1) * 128)
                self.mm(self.ps[pb_][:, cs], self.kmask[i2][:NS, h * 128:(h + 1) * 128], dbf[:NS, h * 128:(h + 1) * 128],
                        ["kmask%d" % i2, "xb16"], ["ps%d" % pb_])
                self.stt(self.Sin[i3][:, h, :], self.Sin[i3][:, h, :], self.abc[:, b * 8 + h:b * 8 + h + 1],
                         self.ps[pb_][:, cs], ALU.mult, ALU.add, ["Sin%d" % i3, "abc", "ps%d" % pb_], ["Sin%d" % i3])
            self.dma("aux", self.sgs[b].rearrange("h k v -> k h v"), self.Sin[i3], "sout%d" % i3, ["Sin%d" % i3], ["sgs%d" % b])
            self.outkeys.append("sgs%d" % b)
        self.onorm_and_T(0, NS)


_CACHE = {}


WBIG_LEN = 2 * (3 * D * HID) + D * IN_W + 4 * D * D + 256 * D


def pack_wbig(weights, specs, offs, tot):
    out = np.empty((tot,), np.float32)
    for spec, (off, n) in zip(specs, offs):
        parts = []
        for (name, r0, r1, c0, c1) in spec:
            w = weights[name][r0:r1, c0:c1]
            kc = (r1 - r0) // 128
            parts.append(w.reshape(kc, 128, c1 - c0).transpose(1, 0, 2).reshape(128, kc * (c1 - c0)))
        out[off:off + 128 * n] = np.concatenate(parts, axis=1).reshape(-1)
    return out


def kernel(x_prompt, x_sample, p_prompt, p_sample, state_gdn, state_qkv_conv, state_sc_conv,
           ffn1_w_gate, ffn1_w_up, ffn1_w_down, ln1_g, ln1_b,
           w_in, w_conv_qkv, A_log, dt_bias, w_onorm, w_p_gdn, w_conv_sc, w_p_sc, w_o, ln2_g, ln2_b,
           ffn2_w_gate, ffn2_w_up, ffn2_w_down, ln3_g, ln3_b,
           w_ple_gate, w_ple_proj, ln4_g, ln4_b, _debug=None):
    f = lambda a: np.ascontiguousarray(np.asarray(a, dtype=np.float32))
    weights = {"ffn1_w_gate": f(ffn1_w_gate)[0], "ffn1_w_up": f(ffn1_w_up)[0], "ffn1_w_down": f(ffn1_w_down)[0],
               "w_in": f(w_in)[0], "w_p_gdn": f(w_p_gdn)[0], "w_p_sc": f(w_p_sc)[0], "w_o": f(w_o)[0],
               "ffn2_w_gate": f(ffn2_w_gate)[0], "ffn2_w_up": f(ffn2_w_up)[0], "ffn2_w_down": f(ffn2_w_down)[0],
               "w_ple_gate": f(w_ple_gate)[0], "w_ple_proj": f(w_ple_proj)[0]}
    bld = Builder(debug=_debug)
    bld.wbig_len = WBIG_LEN
    nc = bld.build()
    assert bld.slab_tot == WBIG_LEN or _debug, (bld.slab_tot, WBIG_LEN)
    assert bld.slab_tot <= WBIG_LEN
    wbig = np.zeros((WBIG_LEN,), np.float32)
    wbig[:bld.slab_tot] = pack_wbig(weights, bld.slab_specs, bld.slab_off, bld.slab_tot)
    lnp = np.stack([f(ln1_g)[0], f(ln1_b)[0], f(ln2_g)[0], f(ln2_b)[0], f(ln3_g)[0], f(ln3_b)[0], f(ln4_g)[0], f(ln4_b)[0]])
    wcq = np.ascontiguousarray(f(w_conv_qkv)[0].reshape(4, 24, 128).transpose(2, 1, 0).reshape(128, 96))
    wcs = np.ascontiguousarray(f(w_conv_sc)[0].reshape(3, 8, 128).transpose(2, 1, 0).reshape(128, 24))
    smallp = np.stack([f(A_log)[0], f(dt_bias)[0]])
    cst, cst2 = make_consts()
    cst2 = np.ascontiguousarray(cst2.reshape(128, -1))
    i16 = np.ascontiguousarray(np.broadcast_to(np.eye(16, dtype=np.float32).reshape(1, 256), (128, 256)))
    xp = f(x_prompt)
    xsm = f(x_sample)[:, 0, :]
    ppr = f(p_prompt)[0]
    psm = f(p_sample)[0, :, 0, :]
    sg = f(state_gdn)[0]
    sq = f(state_qkv_conv)[0]
    ssc = f(state_sc_conv)[0]
    in_maps = []
    for c in range(8):
        sl = slice(c * NSAMP, (c + 1) * NSAMP)
        in_maps.append({"x": xp[c], "pp": ppr[c], "xs": xsm[sl], "psm": psm[sl], "sg": sg[sl], "sq": sq[sl], "ssc": ssc[sl],
                        "wbig": wbig, "lnp": lnp, "wcq": wcq, "wcs": wcs, "smallp": smallp, "won": f(w_onorm)[0],
                        "cst": cst, "cst2": cst2, "i16": i16})
    ncores = (_debug or {}).get("ncores", 8)
    res = run_bass_kernel_spmd(nc, in_maps[:ncores], core_ids=list(range(ncores)))
    R = list(res.results)
    while len(R) < 8:
        R.append({k: np.zeros_like(v) for k, v in R[0].items()})
    y = np.stack([R[c]["y"] for c in range(8)])
    ys = np.concatenate([R[c]["ys"] for c in range(8)])[:, None, :]
    sgp = np.stack([R[c]["sgp"] for c in range(8)])[None]
    sqp = np.stack([R[c]["sqp"] for c in range(8)])[None]
    ssp = np.stack([R[c]["ssp"] for c in range(8)])[None]
    sgs = np.concatenate([R[c]["sgs"] for c in range(8)])[None]
    sqs = np.concatenate([R[c]["sqs"] for c in range(8)])[None]
    sss = np.concatenate([R[c]["sss"] for c in range(8)])[None]
    return (y.astype(np.float32), ys.astype(np.float32), sgp.astype(np.float32), sqp.astype(np.float32),
            ssp.astype(np.float32), sgs.astype(np.float32), sqs.astype(np.float32), sss.astype(np.float32))
```

```python
import contextlib
import numpy as np
import concourse.bass as bass
import concourse.mybir as mybir
from concourse.bass_utils import run_bass_kernel_spmd

F32 = mybir.dt.float32
BF16 = mybir.dt.bfloat16
AF = mybir.ActivationFunctionType
ALU = mybir.AluOpType
AX = mybir.AxisListType

D = 1024
SEQ = 2048
NSAMP = 16
HID = 2816
NJ = HID // 128
H = 8
QKV_W = 3072
IN_W = 9232
Z0, BETA0, A0, B0, C0, H0, GG0, GS0 = 3072, 4096, 4104, 4112, 5136, 6160, 7184, 8208
ALPHA = 2.0 ** 0.25
LN_EPS = 1e-5
RMS_EPS = 1e-6
L2_EPS = 1e-6
NTP = 512
SLOT = 4096
NSLOT = 4
NBK = 16

COMPUTE = ("pe", "act", "dve", "pool")


class _Op:
    __slots__ = ("eng", "fn", "r", "w", "key", "eidx", "kn", "waits", "done", "inc")

    def __init__(self, eng, fn, r, w, key):
        self.eng = eng
        self.fn = fn
        self.r = r
        self.w = w
        self.key = key
        self.eidx = -1
        self.kn = 0
        self.waits = []
        self.done = None
        self.inc = False


class Prog:
    def __init__(self, nc):
        self.nc = nc
        self.ops = []

    def add(self, eng, fn, r=(), w=(), key=None):
        self.ops.append(_Op(eng, fn, tuple(r), tuple(w), key))

    def finalize(self):
        last_w = {}
        readers = {}
        issue = {e: {} for e in ("pe", "act", "dve", "pool", "sp")}
        ecount = {e: 0 for e in issue}
        kcount = {}
        kops = {}
        eops = {e: [] for e in issue}
        for op in self.ops:
            e = op.eng
            deps = set()
            for res in op.r:
                lw = last_w.get(res)
                if lw is not None:
                    deps.add(lw)
            for res in op.w:
                lw = last_w.get(res)
                if lw is not None:
                    deps.add(lw)
                for rd in readers.get(res, ()):
                    deps.add(rd)
            deps.discard(op)
            clock = issue[e]
            if op.key is None:
                op.eidx = ecount[e]
                ecount[e] += 1
                eops[e].append(op)
            else:
                n = kcount.get(op.key, 0) + 1
                kcount[op.key] = n
                op.kn = n
                kops.setdefault(op.key, []).append(op)
                if n > 1:
                    deps.add(kops[op.key][n - 2])
            best = {}
            dma_deps = []
            for d in deps:
                if d.key is None:
                    b = best.get(d.eng)
                    if b is None or d.eidx > b.eidx:
                        best[d.eng] = d
                else:
                    dma_deps.append(d)
            newclock = None
            for f, d in best.items():
                if clock.get(f, -1) >= d.eidx:
                    continue
                if f == e and op.key is None:
                    if e == "pe":
                        continue
                    if e != "pool" and (op.eidx - d.eidx) > 12:
                        continue
                op.waits.append(("c", f, d.eidx))
                d.inc = True
                if newclock is None:
                    newclock = dict(clock)
                for k, v in d.done.items():
                    if newclock.get(k, -1) < v:
                        newclock[k] = v
            for d in dma_deps:
                kk = ("dma", d.key)
                cur = clock if newclock is None else newclock
                if cur.get(kk, 0) >= d.kn:
                    continue
                op.waits.append(("d", d.key, d.kn))
                if newclock is None:
                    newclock = dict(clock)
                for k, v in d.done.items():
                    if newclock.get(k, -1) < v:
                        newclock[k] = v
            if newclock is not None:
                issue[e] = newclock
                clock = newclock
            done = dict(clock)
            if op.key is None:
                done[e] = op.eidx
            else:
                done[("dma", op.key)] = op.kn
            op.done = done
            for res in op.r:
                readers.setdefault(res, []).append(op)
            for res in op.w:
                last_w[res] = op
                readers[res] = []
        self.rank = {}
        for e, lst in eops.items():
            k = 0
            for op in lst:
                if op.inc:
                    k += 1
                    self.rank[(e, op.eidx)] = k
        self.keys = list(kcount.keys())
        for op in self.ops:
            op.done = None
            if len(op.waits) > 1:
                m = {}
                for t, a, b in op.waits:
                    if (t, a) not in m or m[(t, a)] < b:
                        m[(t, a)] = b
                op.waits = [(t, a, b) for (t, a), b in m.items()]

    def emit(self, es):
        nc = self.nc
        sems = {}
        for e in COMPUTE:
            sems[e] = es.enter_context(nc.semaphore("s_" + e))
        ksem = {}
        for k in self.keys:
            ksem[k] = es.enter_context(nc.semaphore("k_" + str(k)))
        block = es.enter_context(nc.Block())
        rank = self.rank

        def run(ename, eng):
            for op in self.ops:
                if op.eng != ename:
                    continue
                for t, a, b in op.waits:
                    if t == "c":
                        eng.wait_ge(sems[a], rank[(a, b)])
                    else:
                        eng.wait_ge(ksem[a], 16 * b)
                if op.fn is None:
                    continue
                ins = op.fn(eng)
                if op.key is not None:
                    ins.then_inc(ksem[op.key], 16)
                elif op.inc:
                    ins.then_inc(sems[ename], 1)

        @block.tensor
        def _(eng):
            run("pe", eng)

        @block.scalar
        def _(eng):
            run("act", eng)

        @block.vector
        def _(eng):
            run("dve", eng)

        @block.gpsimd
        def _(eng):
            run("pool", eng)

        @block.sync
        def _(eng):
            run("sp", eng)


CSTF_NAMES = ["ident", "ltri", "su", "muincl", "ones"]
CSTB_NAMES = ["ident", "ones", "mndn", "mndtn", "me16", "me16t", "me32", "me32t", "me64", "me64t"]


def make_consts():
    i = np.arange(128)[:, None]
    j = np.arange(128)[None, :]
    c = {}
    c["ident"] = (i == j)
    c["ltri"] = (i <= j)
    c["su"] = (i > j)
    c["muincl"] = (j >= i)
    c["ones"] = np.ones((128, 128), bool)
    nd = (i // NBK == j // NBK) & (i > j)
    c["mndn"] = -1.0 * nd
    c["mndtn"] = -1.0 * nd.T
    for b in (16, 32, 64):
        e = (i // (2 * b) == j // (2 * b)) & ((i % (2 * b)) >= b) & ((j % (2 * b)) < b)
        c["me%d" % b] = e
        c["me%dt" % b] = e.T
    arrf = np.stack([np.asarray(c[n], np.float32) for n in CSTF_NAMES], axis=1)
    arrb = np.stack([np.asarray(c[n], np.float32) for n in CSTB_NAMES], axis=1)
    return np.ascontiguousarray(arrf), np.ascontiguousarray(arrb)


class Builder:
    def __init__(self, debug=None):
        self.debug = debug or {}
        self.slab_specs = []
        self.slab_off = []
        self.slab_tot = 0
        self.nslab_pass = None

    def mm(self, out, lhsT, rhs, r, w, start=True, stop=True):
        self.P.add("pe", lambda e: e.matmul(out, lhsT, rhs, start=start, stop=stop), r, w)

    def tr(self, out, in_, ident, r, w):
        self.P.add("pe", lambda e: e.transpose(out, in_, ident), r, w)

    def act(self, out, in_, func, r, w, bias=None, scale=None):
        kw = {}
        if bias is not None:
            kw["bias"] = bias
        if scale is not None:
            kw["scale"] = scale
        self.P.add("act", lambda e: e.activation(out, in_, func, **kw), r, w)

    def tt(self, eng, out, in0, in1, op, r, w):
        self.P.add(eng, lambda e: e.tensor_tensor(out, in0, in1, op), r, w)

    def ts(self, eng, out, in0, s1, op0, r, w, s2=None, op1=None):
        if op1 is None:
            self.P.add(eng, lambda e: e.tensor_scalar(out, in0, s1, None, op0), r, w)
        else:
            self.P.add(eng, lambda e: e.tensor_scalar(out, in0, s1, s2, op0, op1), r, w)

    def stt(self, out, in0, scalar, in1, op0, op1, r, w):
        self.P.add("dve", lambda e: e.scalar_tensor_tensor(out, in0, scalar, in1, op0, op1), r, w)

    def cp(self, eng, out, in_, r, w):
        if eng == "act":
            self.P.add("act", lambda e: e.activation(out, in_, AF.Copy), r, w)
        else:
            self.P.add(eng, lambda e: e.tensor_copy(out, in_), r, w)

    def dq(self):
        return "sp" if self.recording else "pool"

    def dma(self, eng, out, in_, key, r, w, slow=False):
        if eng == "aux":
            eng = self.dq()
        if slow:
            self.P.add(eng, lambda e: e.dma_start(out=out, in_=in_, allow_slow_non_contiguous=True), r, w, key=key)
        else:
            self.P.add(eng, lambda e: e.dma_start(out=out, in_=in_), r, w, key=key)

    def slab(self, spec):
        if self.recording:
            self.slab_specs.append(spec)
            n = sum(((r1 - r0) // 128) * (c1 - c0) for (_, r0, r1, c0, c1) in spec)
            assert n <= SLOT, n
            self.slab_off.append((self.slab_tot, n))
            self.slab_tot += 128 * n
        si = self.slab_i % self.nslab_pass if self.nslab_pass else self.slab_i
        off, n = self.slab_off[si]
        slot = self.slab_i % NSLOT
        self.slab_i += 1
        t = self.wring[slot]
        key = "w%d" % slot
        scr = self.wscr[off:off + 128 * n].rearrange("(p n) -> p n", p=128)
        if self.recording:
            src = self.wbig[off:off + 128 * n].rearrange("(p n) -> p n", p=128)
            self.dma("pool", t[:, 0:n], src, key, r=[], w=[key])
            if self.debug.get("npass", 4) > 1 or not self.debug.get("nosample", False):
                self.dma("sp", scr, t[:, 0:n], "wb%d" % slot, r=[key], w=["wscr%d" % si])
        else:
            self.dma("sp", t[:, 0:n], scr, key, r=["wscr%d" % si], w=[key])
        views = []
        o = 0
        for (_, r0, r1, c0, c1) in spec:
            kc = (r1 - r0) // 128
            nc_ = c1 - c0
            views.append(t[:, o:o + kc * nc_].rearrange("p (k n) -> p k n", k=kc))
            o += kc * nc_
        return views, key

    def build(self):
        nc = bass.Bass("TRN2", target_bir_lowering=False)
        self.nc = nc
        self.es = contextlib.ExitStack()
        with self.es:
            self._build_inner()
        return nc

    def dram_in(self, name, shape, dt=F32):
        return self.nc.dram_tensor(name, list(shape), dt, kind="ExternalInput").ap()

    def dram_out(self, name, shape, dt=F32):
        return self.nc.dram_tensor(name, list(shape), dt, kind="ExternalOutput").ap()

    def sb(self, name, shape, dt):
        return self.es.enter_context(self.nc.sbuf_tensor(name, list(shape), dt))

    def _build_inner(self):
        nc = self.nc
        self.P = Prog(nc)
        P = self.P
        self.x = self.dram_in("x", [SEQ, D])
        self.pp = self.dram_in("pp", [SEQ, 256])
        self.xs = self.dram_in("xs", [NSAMP, D])
        self.psm = self.dram_in("psm", [NSAMP, 256])
        self.sg = self.dram_in("sg", [NSAMP, H, 128, 128])
        self.sq = self.dram_in("sq", [NSAMP, 3, QKV_W])
        self.ssc = self.dram_in("ssc", [NSAMP, 2, D])
        self.wbig = self.dram_in("wbig", [self.wbig_len])
        self.wscr = self.nc.dram_tensor("wscr", [self.wbig_len], BF16, kind="Internal").ap()
        self.lnp = self.dram_in("lnp", [8, D])
        self.wcq_d = self.dram_in("wcq", [128, 24 * 4])
        self.wcs_d = self.dram_in("wcs", [128, 8 * 3])
        self.smallp = self.dram_in("smallp", [2, 8])
        self.won_d = self.dram_in("won", [128])
        self.cst_d = self.dram_in("cst", [128, len(CSTF_NAMES), 128])
        self.cst2_d = self.dram_in("cst2", [128, len(CSTB_NAMES) * 128])
        self.i16_d = self.dram_in("i16", [128, 256])
        self.y = self.dram_out("y", [SEQ, D])
        self.ys = self.dram_out("ys", [NSAMP, D])
        self.sgp = self.dram_out("sgp", [H, 128, 128])
        self.sqp = self.dram_out("sqp", [3, QKV_W])
        self.ssp = self.dram_out("ssp", [2, D])
        self.sgs = self.dram_out("sgs", [NSAMP, H, 128, 128])
        self.sqs = self.dram_out("sqs", [NSAMP, 3, QKV_W])
        self.sss = self.dram_out("sss", [NSAMP, 2, D])
        self.outkeys = []

        sb = self.sb
        self.wring = [sb("wr%d" % i, [128, SLOT], BF16) for i in range(NSLOT)]
        self.xres = sb("xres", [128, 4, D], F32)
        self.xT = sb("xT", [128, 8, NTP], BF16)
        self.gbt = sb("gbt", [128, 2, D], F32)
        self.cstf = sb("cstf", [128, len(CSTF_NAMES), 128], F32)
        self.cstb = sb("cstb", [128, len(CSTB_NAMES), 128], BF16)
        self.i16b = sb("i16b", [128, 256], BF16)
        self.wcq = sb("wcq_s", [128, 24, 4], F32)
        self.wcs = sb("wcs_s", [128, 8, 3], F32)
        self.wonb = sb("wonb", [128, 128], F32)
        self.smallb = sb("smallb", [128, 16], F32)
        self.negA = sb("negA", [128, 8], F32)
        self.histq = sb("histq", [128, 24, 3], F32)
        self.hists = sb("hists", [128, 8, 2], F32)
        self.S = sb("S", [128, H, 128], F32)
        self.Sbf = sb("Sbf", [128, H, 128], BF16)
        self.A1 = sb("A1", [128, 24, NTP], BF16)
        self.ztok = sb("ztok", [128, 4, D], BF16)
        self.ktok = sb("ktok", [128, H, 128], BF16)
        self.vtok = sb("vtok", [128, H, 128], BF16)
        self.batok = sb("batok", [128, 4, 16], F32)
        self.beta = sb("beta", [128, 4, 8], F32)
        self.gtok = sb("gtok", [128, 4, 8], F32)
        self.tf = [sb("tf%d" % i, [128, NTP + 4], F32) for i in range(11)]
        self.tfi = 0
        self.tb16 = [sb("tb%d" % i, [128, NTP], BF16) for i in range(4)]
        self.tbi = 0
        self.t1 = sb("t1", [128, D], F32)
        self.t2 = sb("t2", [128, D], F32)
        self.otok = sb("otok", [128, D], F32)
        self.xb16 = sb("xb16", [128, D], BF16)
        self.stat = sb("stat", [128, 4, 16], F32)
        self.stat3 = sb("stat3", [128, 16], F32)
        self.xb16b = sb("xb16b", [128, D], BF16)
        self.pT = sb("pT", [128, 2, NTP], BF16)
        self.pb = sb("pb", [128, 256], BF16)
        GQ = [("decTm", F32), ("Lg", F32), ("qkTm", BF16), ("MT", BF16), ("M", BF16),
              ("Na", BF16), ("Nb", BF16), ("Nc", BF16), ("Nd", BF16), ("Pa", BF16), ("Pb", BF16),
              ("Pc", BF16), ("Pd", BF16)]
        self.gqs = []
        self.arenaA = sb("arenaA", [128, 15 * 512], BF16)
        self.arenaB = sb("arenaB", [128, 15 * 512], BF16)
        self.gq_names = [nm for nm, _ in GQ]
        for ar, pfx in ((self.arenaA, "gA_"), (self.arenaB, "gB_")):
            gX = {}
            o = 0
            for nm, dt in GQ:
                n = 1024 if dt == F32 else 512
                v = ar[:, o:o + n]
                if dt == F32:
                    v = v.bitcast(F32)
                gX[nm] = v.rearrange("p (h d) -> p h d", h=4)
                o += n
            self.gqs.append((gX, pfx))
        self.kdec2 = [sb("kdec%d" % i, [128, H, 128], BF16) for i in range(2)]
        self.gsm2 = [sb("gsm%d" % i, [128, 64], F32) for i in range(2)]
        self.vtok2 = sb("vtok2", [128, H, 128], BF16)
        self.gsm = self.gsm2[0]
        self.kdec = self.kdec2[0]
        fA = lambda o, n: self.arenaA[:, o:o + n]
        fB = lambda o, n: self.arenaB[:, o:o + n]
        self.Sin = [fA(k * 2048, 2048).bitcast(F32).rearrange("p (h d) -> p h d", h=H) for k in range(3)]
        self.hss = fA(6144, 512).bitcast(F32).rearrange("p (c j b) -> p c j b", c=8, j=2)
        self.Sinb = [fA(6656, 1024).rearrange("p (h d) -> p h d", h=H), fB(6400, 1024).rearrange("p (h d) -> p h d", h=H)]
        self.kTm = fB(0, 2048).rearrange("p (h a b) -> p h a b", h=H, a=NSAMP)
        self.qTm = fB(2048, 2048).rearrange("p (h a b) -> p h a b", h=H, a=NSAMP)
        self.hsq = fB(4096, 2304).bitcast(F32).rearrange("p (c j b) -> p c j b", c=24, j=3)
        self.kmask = [sb("kmask%d" % i, [NSAMP, D], BF16) for i in range(2)]
        self.qtok = sb("qtok", [NSAMP, D], BF16)
        self.abc = sb("abc", [128, 128], F32)
        self.ps = [self.es.enter_context(nc.psum_tensor("ps%d" % i, [128, 512], F32)) for i in range(8)]
        self.psb = [p.bitcast(BF16) for p in self.ps]

        self.recording = True
        self.slab_i = 0
        self.setup()
        npass = self.debug.get("npass", 4)
        for pi in range(npass):
            self.layer_pass(pi, NTP, sample=False, last=(pi == npass - 1))
            if pi == 0:
                self.recording = False
                self.nslab_pass = len(self.slab_off)
        if not self.debug.get("nosample", False):
            gbk = [p + nm for p in ("gA_", "gB_") for nm in self.gq_names]
            gbk += ["gsm0", "gsm1", "vtok0", "vtok1", "kdec0", "kdec1"]
            P.add("dve", lambda e: e.memset(self.gsm[:, 60:64], 0.0), gbk,
                  gbk + ["kTm", "qTm", "hsamp", "Sin0", "Sin1", "Sin2", "Sinb0", "Sinb1", "gsm", "vtok"])
            self.layer_pass(0, NSAMP, sample=True, last=True)
        P.add("sp", None, r=self.outkeys)
        P.finalize()
        P.emit(self.es)

    def cf(self, name):
        return self.cstf[:, CSTF_NAMES.index(name), :]

    def cb(self, name):
        return self.cstb[:, CSTB_NAMES.index(name), :]

    def cb4(self, name):
        i = CSTB_NAMES.index(name)
        return self.cstb[:, i:i + 1, :].to_broadcast([128, 4, 128])

    def cf4(self, name):
        i = CSTF_NAMES.index(name)
        return self.cstf[:, i:i + 1, :].to_broadcast([128, 4, 128])

    def tmpf(self):
        i = self.tfi % len(self.tf)
        self.tfi += 1
        return self.tf[i], "tf%d" % i

    def tmpb(self):
        i = self.tbi % len(self.tb16)
        self.tbi += 1
        return self.tb16[i], "tb%d" % i

    def setup(self):
        d = self.dma
        d("sp", self.cstf[:], self.cst_d, "c0", [], ["cstf"])
        nb = len(CSTB_NAMES) * 128
        for k in range(0, nb, 1024):
            n = min(1024, nb - k)
            d("sp", self.t1[:, 0:n], self.cst2_d[:, k:k + n], "c1", [], ["t1"])
            self.cp("dve", self.cstb[:].rearrange("p c d -> p (c d)")[:, k:k + n], self.t1[:, 0:n], ["t1"], ["cstb"])
        d("sp", self.t2[:, 0:256], self.i16_d, "c1", [], ["t2"])
        self.cp("dve", self.i16b[:], self.t2[:, 0:256], ["t2"], ["i16b"])
        d("sp", self.wcq[:].rearrange("p c j -> p (c j)"), self.wcq_d, "c2", [], ["wcq"])
        d("sp", self.wcs[:].rearrange("p c j -> p (c j)"), self.wcs_d, "c3", [], ["wcs"])
        d("sp", self.wonb[:], self.won_d.partition_broadcast(128), "c4", [], ["wonb"])
        d("sp", self.smallb[:], self.smallp.rearrange("a b -> (a b)").partition_broadcast(128), "c5", [], ["smallb"])
        self.act(self.negA[:], self.smallb[:, 0:8], AF.Exp, ["smallb"], ["negA"])
        self.ts("dve", self.negA[:], self.negA[:], -1.0, ALU.mult, ["negA"], ["negA"])
        self.P.add("dve", lambda e: e.memset(self.S[:], 0.0), [], ["S0", "S1"])
        self.P.add("dve", lambda e: e.memset(self.Sbf[:], 0.0), [], ["Sbf0", "Sbf1"])
        self.P.add("dve", lambda e: e.memset(self.histq[:], 0.0), [], ["histq"])
        self.P.add("dve", lambda e: e.memset(self.hists[:], 0.0), [], ["hists"])

    def make_xT(self, tb, TB, bank):
        ps, psk = self.psb[bank], "ps%d" % bank
        xb, xbk = ((self.xb16, "xb16"), (self.xb16b, "xb16b"))[tb % 2]
        self.cp("act", xb[:TB, :], self.xres[:TB, tb, :], ["xres%d" % tb], [xbk])
        for c in range(8):
            self.tr(ps[:, c * TB:(c + 1) * TB], xb[:TB, c * 128:(c + 1) * 128], self.cb("ident")[:TB, :TB],
                    [xbk, "cstb"], [psk])
        self.cp("dve", self.xT[:, :, tb * TB:(tb + 1) * TB],
                ps[:, 0:8 * TB].rearrange("p (c t) -> p c t", c=8), [psk], ["xT%d" % tb])

    def layer_norm(self, idx, NB, TB, final_out=None):
        self.dma("aux", self.gbt[:, 0, :], self.lnp[2 * idx, :].partition_broadcast(128), "gb0", [], ["gbt0"])
        self.dma("aux", self.gbt[:, 1, :], self.lnp[2 * idx + 1, :].partition_broadcast(128), "gb1", [], ["gbt1"])
        eps = LN_EPS / (ALPHA * ALPHA)
        st = self.stat
        for tb in range(NB):
            xr = self.xres[:TB, tb, :]
            xk = "xres%d" % tb
            self.P.add("dve", lambda e, xr=xr, tb=tb: e.bn_stats(st[:TB, tb, 0:6], xr[:, 0:512]), [xk], ["stat"])
            self.P.add("dve", lambda e, xr=xr, tb=tb: e.bn_stats(st[:TB, tb, 6:12], xr[:, 512:1024]), [xk], ["stat"])
            self.P.add("dve", lambda e, tb=tb: e.bn_aggr(st[:TB, tb, 12:14], st[:TB, tb, 0:12]), ["stat"], ["stat"])
        self.act(st[:TB, 0:NB, 14], st[:TB, 0:NB, 13], AF.Sqrt, ["stat"], ["stat2"], bias=eps)
        self.P.add("dve", lambda e: e.reciprocal(st[:TB, 0:NB, 15], st[:TB, 0:NB, 14]), ["stat2"], ["stat2"])
        for tb in range(NB):
            xr = self.xres[:TB, tb, :]
            xk = "xres%d" % tb
            self.stt(self.t1[:TB, :], xr, st[:TB, tb, 12:13], self.gbt[:TB, 0, :], ALU.subtract, ALU.mult,
                     [xk, "stat", "gbt0"], ["t1"])
            self.stt(xr, self.t1[:TB, :], st[:TB, tb, 15:16], self.gbt[:TB, 1, :], ALU.mult, ALU.add,
                     ["t1", "stat2", "gbt1"], [xk])
            if final_out is not None:
                ok = final_out[1] + str(tb)
                self.dma("aux", final_out[0][tb * TB:(tb + 1) * TB, :], xr, "yo%d" % tb, [xk], [ok])
                self.outkeys.append(ok)
            else:
                self.make_xT(tb, TB, (2 * tb) % 8)

    def ffn(self, pfx, NB, TB):
        NT = NB * TB
        for j0 in range(0, NJ, 2):
            (wg, wu), wk = self.slab([(pfx + "_w_gate", 0, D, j0 * 128, j0 * 128 + 256),
                                      (pfx + "_w_up", 0, D, j0 * 128, j0 * 128 + 256)])
            for jj in range(2):
                j = j0 + jj
                bg, bu = 2 * (j % 2), 2 * (j % 2) + 1
                for kc in range(8):
                    self.mm(self.ps[bg][:, :NT], wg[:, kc, jj * 128:(jj + 1) * 128], self.xT[:, kc, :NT],
                            [wk, "xT0", "xT1", "xT2", "xT3"], ["ps%d" % bg], start=(kc == 0), stop=(kc == 7))
                for kc in range(8):
                    self.mm(self.ps[bu][:, :NT], wu[:, kc, jj * 128:(jj + 1) * 128], self.xT[:, kc, :NT],
                            [wk, "xT0", "xT1", "xT2", "xT3"], ["ps%d" % bu], start=(kc == 0), stop=(kc == 7))
                t, tk = self.tmpf()
                self.act(t[:, :NT], self.ps[bg][:, :NT], AF.Silu, ["ps%d" % bg], [tk])
                self.tt("dve", self.A1[:, j, :NT], t[:, :NT], self.ps[bu][:, :NT], ALU.mult,
                        [tk, "ps%d" % bu], ["A1.%d" % j])
        for j0 in range(0, NJ, 4):
            j1 = min(NJ, j0 + 4)
            (wd,), wk = self.slab([(pfx + "_w_down", j0 * 128, j1 * 128, 0, D)])
            for jj in range(j1 - j0):
                j = j0 + jj
                for tb in range(NB):
                    for nh in range(2):
                        b = tb * 2 + nh
                        self.mm(self.ps[b][:TB, :], self.A1[:, j, tb * TB:(tb + 1) * TB], wd[:, jj, nh * 512:(nh + 1) * 512],
                                [wk, "A1.%d" % j], ["ps%d" % b], start=(j == 0), stop=(j == NJ - 1))
        c = 0.5 / ALPHA
        for tb in range(NB):
            for nh in range(2):
                b = tb * 2 + nh
                xr = self.xres[:TB, tb, nh * 512:(nh + 1) * 512]
                self.stt(xr, self.ps[b][:TB, :], c, xr, ALU.mult, ALU.add, ["ps%d" % b, "xres%d" % tb], ["xres%d" % tb])

    def layer_pass(self, pi, NT, sample, last):
        TB = min(128, NT)
        NB = NT // TB
        self.slab_i = 0 if self.recording else self.slab_i
        if not sample:
            src, psrc = self.x[pi * NT:(pi + 1) * NT, :], self.pp[pi * NT:(pi + 1) * NT, :]
            yout = (self.y[pi * NT:(pi + 1) * NT, :], "y%d_" % pi)
        else:
            src, psrc = self.xs, self.psm
            yout = (self.ys, "ys_")
        for tb in range(NB):
            self.dma("aux", self.xres[:TB, tb, :], src[tb * TB:(tb + 1) * TB, :], "x%d" % tb, [], ["xres%d" % tb])
            self.make_xT(tb, TB, tb % 8)
        stop = self.debug.get("stop")
        self.ffn("ffn1", NB, TB)
        if stop == "ffn1":
            return self.dump(yout, NB, TB)
        self.layer_norm(0, NB, TB)
        if stop == "ln1":
            return self.dump(yout, NB, TB)
        self.mixers(pi, NB, TB, sample, last)
        if stop == "mix":
            return self.dump(yout, NB, TB)
        self.layer_norm(1, NB, TB)
        self.ffn("ffn2", NB, TB)
        self.layer_norm(2, NB, TB)
        if stop == "ln3":
            return self.dump(yout, NB, TB)
        self.ple(psrc, NB, TB)
        self.layer_norm(3, NB, TB, final_out=yout)

    def dump(self, yout, NB, TB):
        for tb in range(NB):
            ok = yout[1] + str(tb)
            self.dma("aux", yout[0][tb * TB:(tb + 1) * TB, :], self.xres[:TB, tb, :], "yo%d" % tb, ["xres%d" % tb], [ok])
            self.outkeys.append(ok)

    def ple(self, psrc, NB, TB):
        NT = NB * TB
        for tb in range(NB):
            pf, pfk = self.tmpf()
            self.dma("aux", pf[:TB, 0:256], psrc[tb * TB:(tb + 1) * TB, :], "pf", [], [pfk])
            self.cp("act", self.pb[:TB, :], pf[:TB, 0:256], [pfk], ["pb"])
            for c in range(2):
                self.tr(self.psb[7][:, c * TB:(c + 1) * TB], self.pb[:TB, c * 128:(c + 1) * 128],
                        self.cb("ident")[:TB, :TB], ["pb", "cstb"], ["ps7"])
            self.cp("dve", self.pT[:, :, tb * TB:(tb + 1) * TB],
                    self.psb[7][:, 0:2 * TB].rearrange("p (c t) -> p c t", c=2), ["ps7"], ["pT"])
        for nh in range(2):
            (wg,), wgk = self.slab([("w_ple_gate", 0, D, nh * 512, (nh + 1) * 512)])
            (wp,), wpk = self.slab([("w_ple_proj", 0, 256, nh * 512, (nh + 1) * 512)])
            for tb in range(NB):
                bg, bp = 2 * (tb % 2), 2 * (tb % 2) + 1
                for kc in range(8):
                    self.mm(self.ps[bg][:TB, :], self.xT[:, kc, tb * TB:(tb + 1) * TB], wg[:, kc, :],
                            [wgk, "xT0", "xT1", "xT2", "xT3"], ["ps%d" % bg], start=(kc == 0), stop=(kc == 7))
                for kc in range(2):
                    self.mm(self.ps[bp][:TB, :], self.pT[:, kc, tb * TB:(tb + 1) * TB], wp[:, kc, :],
                            [wpk, "pT"], ["ps%d" % bp], start=(kc == 0), stop=(kc == 1))
                t, tk = self.tmpf()
                self.act(t[:TB, :512], self.ps[bg][:TB, :], AF.Sigmoid, ["ps%d" % bg], [tk])
                self.tt("dve", t[:TB, :512], t[:TB, :512], self.ps[bp][:TB, :], ALU.mult, [tk, "ps%d" % bp], [tk])
                xr = self.xres[:TB, tb, nh * 512:(nh + 1) * 512]
                self.stt(xr, t[:TB, :512], 1.0 / ALPHA, xr, ALU.mult, ALU.add, [tk, "xres%d" % tb], ["xres%d" % tb])

    def conv_chunk(self, psbank, NT, taps_hist, wts, ntap, hist_tile, hist_key, sample, src_is_psum=True, src=None):
        H_ = ntap - 1
        cbt, cbk = self.tmpf()
        if src_is_psum:
            self.cp("act", cbt[:, H_:H_ + NT], self.ps[psbank][:, :NT], ["ps%d" % psbank], [cbk])
        else:
            src(cbt[:, H_:H_ + NT], cbk)
        if not sample:
            self.cp("dve", cbt[:, 0:H_], hist_tile, [hist_key], [cbk])
            self.cp("dve", hist_tile, cbt[:, NT:NT + H_], [cbk], [hist_key])
            taps = [cbt[:, j:j + NT] for j in range(ntap)]
            tr_ = [cbk]
        else:
            taps = [taps_hist[j] for j in range(H_)] + [cbt[:, H_:H_ + NT]]
            tr_ = [cbk, "hsamp"]
        acc, ak = self.tmpf()
        self.ts("dve", acc[:, :NT], taps[0], wts[0], ALU.mult, tr_ + ["wc"], [ak])
        for j in range(1, ntap):
            self.stt(acc[:, :NT], taps[j], wts[j], acc[:, :NT], ALU.mult, ALU.add, tr_ + ["wc", ak], [ak])
        return acc, ak, cbt, cbk

    def mixers(self, pi, NB, TB, sample, last):
        NT = NB * TB
        A1 = self.A1
        if sample:
            self.load_sample_hist()
        def finish(grp):
            for (c, so, sk, sq, sqk, cbt, cbk) in grp:
                if sample:
                    self.tr(self.ps[6][:NT, (c % 4) * 128:(c % 4 + 1) * 128], cbt[:, 3:3 + NT], self.cf("ident"),
                            [cbk, "cstf"], ["ps6"])
                    if c % 4 == 3:
                        stg, stk = self.stage(c // 8)
                        self.cp("act", stg[:NT, (c % 8 - 3) * 128:(c % 8 + 1) * 128], self.ps[6][:NT, :], ["ps6"], [stk])
            qk = [g_ for g_ in grp if g_[0] < 16]
            sds = []
            for (c, so, sk, sq, sqk, cbt, cbk) in qk:
                b2 = 4 + c % 2 if sample else 4 + c % 4
                self.mm(self.ps[b2][:, :NT], self.cb("ones"), sq[:, :NT], [sqk, "cstb"], ["ps%d" % b2])
            for (c, so, sk, sq, sqk, cbt, cbk) in qk:
                b2 = 4 + c % 2 if sample else 4 + c % 4
                sd, sdk = self.tmpf()
                sds.append((sd, sdk))
                self.act(sd[:, :NT], self.ps[b2][:, :NT], AF.Ln, ["ps%d" % b2], [sdk], bias=L2_EPS)
            for (sd, sdk) in sds:
                self.act(sd[:, :NT], sd[:, :NT], AF.Exp, [sdk], [sdk], scale=-0.5)
            for (c, so, sk, sq, sqk, cbt, cbk), (sd, sdk) in zip(qk, sds):
                const = 128.0 ** -0.5 if c < 8 else 1.0
                self.stt(A1[:, c, :NT], so[:, :NT], const, sd[:, :NT], ALU.mult, ALU.mult, [sk, sdk], ["A1.%d" % c])

        pend = None
        for g in range(6):
            (wq,), wk = self.slab([("w_in", 0, D, g * 512, (g + 1) * 512)])
            for pr in range(2):
                cs_ = [g * 4 + pr * 2, g * 4 + pr * 2 + 1]
                for c in cs_:
                    jj = c % 4
                    bank = c % 4
                    for kc in range(8):
                        self.mm(self.ps[bank][:, :NT], wq[:, kc, jj * 128:(jj + 1) * 128], self.xT[:, kc, :NT],
                                [wk, "xT0", "xT1", "xT2", "xT3"], ["ps%d" % bank], start=(kc == 0), stop=(kc == 7))
                convs = []
                for c in cs_:
                    th = [self.hsq[:, c, j, :] for j in range(3)] if sample else None
                    wts = [self.wcq[:, c, j:j + 1] for j in range(4)]
                    convs.append(self.conv_chunk(c % 4, NT, th, wts, 4, self.histq[:, c, :], "histq%d" % c, sample))
                cur = []
                for c, (acc, ak, cbt, cbk) in zip(cs_, convs):
                    if c >= 16:
                        self.act(A1[:, c, :NT], acc[:, :NT], AF.Silu, [ak], ["A1.%d" % c])
                        cur.append((c, None, None, None, None, cbt, cbk))
                    else:
                        self.act(acc[:, :NT], acc[:, :NT], AF.Silu, [ak], [ak])
                        sq, sqk = self.tmpb()
                        self.act(sq[:, :NT], acc[:, :NT], AF.Square, [ak], [sqk])
                        cur.append((c, acc, ak, sq, sqk, cbt, cbk))
                if pend is not None:
                    finish(pend)
                pend = cur
        finish(pend)
        if sample:
            for k in range(3):
                stg, stk = self.stage(k)
                self.dma("aux", self.sqs[:, 2, k * 1024:(k + 1) * 1024], stg[:NSAMP, :], "so0", [stk], ["sqs2_%d" % k])
                self.outkeys.append("sqs2_%d" % k)
            self.dma("aux", self.sqs[:, 0:2, :], self.sq[:, 1:3, :], "so1", [], ["sqs01"])
            self.outkeys += ["sqs01"]
        elif last:
            for j in range(3):
                self.dma("aux", self.sqp[j, :].rearrange("(c p) -> p c", p=128), self.histq[:, :, j], "so0",
                         ["histq%d" % c for c in range(24)], ["sqp%d" % j], slow=True)
                self.outkeys.append("sqp%d" % j)
        if self.debug.get("mstop") == "A":
            return
        for nh in range(2):
            (wz,), wk = self.slab([("w_in", 0, D, Z0 + nh * 512, Z0 + (nh + 1) * 512)])
            for tb in range(NB):
                b = 4 + tb % 2
                for kc in range(8):
                    self.mm(self.ps[b][:TB, :], self.xT[:, kc, tb * TB:(tb + 1) * TB], wz[:, kc, :],
                            [wk, "xT0", "xT1", "xT2", "xT3"], ["ps%d" % b], start=(kc == 0), stop=(kc == 7))
                self.act(self.ztok[:TB, tb, nh * 512:(nh + 1) * 512], self.ps[b][:TB, :], AF.Silu, ["ps%d" % b], ["ztok%d" % tb])
        (wba,), wk = self.slab([("w_in", 0, D, BETA0, BETA0 + 16)])
        for tb in range(NB):
            for kc in range(8):
                self.mm(self.ps[6][:TB, 0:16], self.xT[:, kc, tb * TB:(tb + 1) * TB], wba[:, kc, :],
                        [wk, "xT0", "xT1", "xT2", "xT3"], ["ps6"], start=(kc == 0), stop=(kc == 7))
            self.act(self.beta[:TB, tb, :], self.ps[6][:TB, 0:8], AF.Sigmoid, ["ps6"], ["beta"])
            self.tt("dve", self.batok[:TB, tb, 8:16], self.ps[6][:TB, 8:16], self.smallb[:TB, 8:16], ALU.add,
                    ["ps6", "smallb"], ["batok"])
        for tb in range(NB):
            self.act(self.batok[:TB, tb, 0:8], self.batok[:TB, tb, 8:16], AF.Exp, ["batok"], ["batok"])
        for tb in range(NB):
            self.act(self.batok[:TB, tb, 0:8], self.batok[:TB, tb, 0:8], AF.Ln, ["batok"], ["batok"], bias=1.0)
            self.tt("dve", self.gtok[:TB, tb, :], self.batok[:TB, tb, 0:8], self.negA[:TB, :], ALU.mult,
                    ["batok", "negA"], ["gtok"])
        if self.debug.get("mstop") == "B":
            return
        if sample:
            self.gdn_sample()
        else:
            self.gdn_all(NB)
            if last:
                self.dma("aux", self.sgp.rearrange("h k v -> k h v"), self.S[:], "so1", ["S0", "S1"], ["sgp"])
                self.outkeys.append("sgp")
        if self.debug.get("mstop") == "C":
            return
        for c in range(8):
            (wB, wC, wH), wk = self.slab([("w_in", 0, D, B0 + c * 128, B0 + (c + 1) * 128),
                                          ("w_in", 0, D, C0 + c * 128, C0 + (c + 1) * 128),
                                          ("w_in", 0, D, H0 + c * 128, H0 + (c + 1) * 128)])
            bB, bC, bH = 0 + 3 * (c % 2), 1 + 3 * (c % 2), 2 + 3 * (c % 2)
            for (w_, b_) in ((wC, bC), (wH, bH), (wB, bB)):
                for kc in range(8):
                    self.mm(self.ps[b_][:, :NT], w_[:, kc, :], self.xT[:, kc, :NT], [wk, "xT0", "xT1", "xT2", "xT3"], ["ps%d" % b_],
                            start=(kc == 0), stop=(kc == 7))
            ct, ck = self.tmpf()
            self.cp("act", ct[:, :NT], self.ps[bC][:, :NT], ["ps%d" % bC], [ck])

            def src(dst, dk, ct=ct, ck=ck, bH=bH):
                self.tt("dve", dst, ct[:, :NT], self.ps[bH][:, :NT], ALU.mult, [ck, "ps%d" % bH], [dk])
            th = [self.hss[:, c, j, :] for j in range(2)] if sample else None
            wts = [self.wcs[:, c, j:j + 1] for j in range(3)]
            acc, ak, cbt, cbk = self.conv_chunk(None, NT, th, wts, 3, self.hists[:, c, :], "hists%d" % c, sample,
                                                src_is_psum=False, src=src)
            if sample:
                self.tr(self.ps[6][:NT, (c % 4) * 128:(c % 4 + 1) * 128], cbt[:, 2:2 + NT], self.cf("ident"),
                        [cbk, "cstf"], ["ps6"])
                if c % 4 == 3:
                    stg, stk = self.stage(0)
                    self.cp("act", stg[:NT, (c - 3) * 128:(c + 1) * 128], self.ps[6][:NT, :], ["ps6"], [stk])
            self.tt("dve", A1[:, c, :NT], acc[:, :NT], self.ps[bB][:, :NT], ALU.mult, [ak, "ps%d" % bB], ["A1.%d" % c])
        if sample:
            stg, stk = self.stage(0)
            self.dma("aux", self.sss[:, 1, :], stg[:NSAMP, 0:D], "so2", [stk], ["sss1"])
            self.dma("aux", self.sss[:, 0:1, :], self.ssc[:, 1:2, :], "so3", [], ["sss0"])
            self.outkeys += ["sss1", "sss0"]
        elif last:
            for j in range(2):
                self.dma("aux", self.ssp[j, :].rearrange("(c p) -> p c", p=128), self.hists[:, :, j], "so2",
                         ["hists%d" % c for c in range(8)], ["ssp%d" % j], slow=True)
                self.outkeys.append("ssp%d" % j)
        if self.debug.get("mstop") == "D":
            return
        for c in range(8):
            (wpg, wgg, wps, wgs), wk = self.slab([("w_p_gdn", 0, D, c * 128, (c + 1) * 128),
                                                  ("w_in", 0, D, GG0 + c * 128, GG0 + (c + 1) * 128),
                                                  ("w_p_sc", 0, D, c * 128, (c + 1) * 128),
                                                  ("w_in", 0, D, GS0 + c * 128, GS0 + (c + 1) * 128)])
            o = 4 * (c % 2)
            for (w_, b_, rhs_, rk) in ((wpg, o, A1[:, 16:24, :], ["A1.%d" % k for k in range(16, 24)]),
                                       (wgg, o + 1, self.xT, ["xT0", "xT1", "xT2", "xT3"]),
                                       (wps, o + 2, A1[:, 0:8, :], ["A1.%d" % k for k in range(8)]),
                                       (wgs, o + 3, self.xT, ["xT0", "xT1", "xT2", "xT3"])):
                for kc in range(8):
                    self.mm(self.ps[b_][:, :NT], w_[:, kc, :], rhs_[:, kc, :NT], [wk] + rk, ["ps%d" % b_],
                            start=(kc == 0), stop=(kc == 7))
            s1, s1k = self.tmpf()
            self.act(s1[:, :NT], self.ps[o + 1][:, :NT], AF.Sigmoid, ["ps%d" % (o + 1)], [s1k])
            self.tt("dve", s1[:, :NT], s1[:, :NT], self.ps[o][:, :NT], ALU.mult, [s1k, "ps%d" % o], [s1k])
            s2, s2k = self.tmpf()
            self.act(s2[:, :NT], self.ps[o + 3][:, :NT], AF.Sigmoid, ["ps%d" % (o + 3)], [s2k])
            self.tt("dve", s2[:, :NT], s2[:, :NT], self.ps[o + 2][:, :NT], ALU.mult, [s2k, "ps%d" % (o + 2)], [s2k])
            self.tt("dve", A1[:, 8 + c, :NT], s1[:, :NT], s2[:, :NT], ALU.add, [s1k, s2k], ["A1.%d" % (8 + c)])
        for nh in range(2):
            (wo,), wk = self.slab([("w_o", 0, D, nh * 512, (nh + 1) * 512)])
            for tb in range(NB):
                b = tb % 2
                for kc in range(8):
                    self.mm(self.ps[b][:TB, :], A1[:, 8 + kc, tb * TB:(tb + 1) * TB], wo[:, kc, :],
                            [wk, "A1.%d" % (8 + kc)], ["ps%d" % b], start=(kc == 0), stop=(kc == 7))
                xr = self.xres[:TB, tb, nh * 512:(nh + 1) * 512]
                self.stt(xr, self.ps[b][:TB, :], 1.0 / ALPHA, xr, ALU.mult, ALU.add, ["ps%d" % b, "xres%d" % tb], ["xres%d" % tb])

    def onorm_and_T(self, tb, TB):
        self.lockstep([self.onorm_gen(tb, TB)])

    def onorm_gen(self, tb, TB):
        o3 = self.otok[:TB, :].rearrange("p (h d) -> p h d", h=H)
        t13 = self.t1[:TB, :].rearrange("p (h d) -> p h d", h=H)
        t23 = self.t2[:TB, :].rearrange("p (h d) -> p h d", h=H)
        st = self.stat3
        self.act(self.t1[:TB, :], self.otok[:TB, :], AF.Square, ["otok"], ["t1"])
        yield
        self.P.add("dve", lambda e: e.tensor_reduce(st[:TB, 0:8], t13, AX.X, ALU.add), ["t1"], ["stat3"])
        self.act(st[:TB, 0:8], st[:TB, 0:8], AF.Sqrt, ["stat3"], ["stat3"], bias=RMS_EPS, scale=1.0 / 128.0)
        yield
        self.P.add("dve", lambda e: e.reciprocal(st[:TB, 8:16], st[:TB, 0:8]), ["stat3"], ["stat3"])
        yield
        self.tt("dve", t13, o3, st[:TB, 8:16].unsqueeze(2).to_broadcast([TB, H, 128]), ALU.mult, ["otok", "stat3"], ["t1"])
        z3 = self.ztok[:TB, tb, :].rearrange("p (h d) -> p h d", h=H)
        self.tt("dve", t23, z3, self.wonb[:TB, :].unsqueeze(1).to_broadcast([TB, H, 128]), ALU.mult,
                ["ztok%d" % tb, "wonb"], ["t2"])
        yield
        self.tt("dve", self.xb16[:TB, :], self.t1[:TB, :], self.t2[:TB, :], ALU.mult, ["t1", "t2"], ["xb16"])
        yield
        for c in range(8):
            self.tr(self.psb[7][:, c * TB:(c + 1) * TB], self.xb16[:TB, c * 128:(c + 1) * 128], self.cb("ident")[:TB, :TB],
                    ["xb16", "cstb"], ["ps7"])
        yield
        self.cp("act", self.A1[:, 16:24, tb * TB:(tb + 1) * TB],
                self.psb[7][:, 0:8 * TB].rearrange("p (c t) -> p c t", c=8), ["ps7"],
                ["A1.%d" % k for k in range(16, 24)])

    def inv_chain(self, tb, hg, G, gp, pb):
        A1 = self.A1
        blk = slice(tb * 128, (tb + 1) * 128)
        g8 = self.gtok[:, tb, :]
        hs = [hg * 4 + hh for hh in range(4)]
        K = lambda nm: gp + nm
        pk = ["ps%d" % x for x in pb]
        ps = [self.ps[x] for x in pb]
        psb = [self.psb[x] for x in pb]
        f4 = lambda t: t.rearrange("p h d -> p (h d)")
        kq_r = ["A1.%d" % (8 + h) for h in hs] + ["A1.%d" % h for h in hs]
        for hh, h in enumerate(hs):
            self.ts("dve", G["Lg"][:, hh, :], self.cf("ltri"), g8[:, h:h + 1], ALU.mult, ["cstf", "gtok"], [K("Lg")])
        yield
        for hh, h in enumerate(hs):
            cs = slice(hh * 128, (hh + 1) * 128)
            self.mm(ps[0][:, cs], self.cf("su"), G["Lg"][:, hh, :], ["cstf", K("Lg")], [pk[0]])
            self.mm(ps[1][:, cs], A1[:, 8 + h, blk], A1[:, 8 + h, blk], kq_r, [pk[1]])
            self.mm(ps[2][:, cs], A1[:, 8 + h, blk], A1[:, h, blk], kq_r, [pk[2]])
        yield
        self.act(f4(G["decTm"]), ps[0][:, :], AF.Exp, [pk[0]], [K("decTm")])
        yield
        self.tt("dve", G["decTm"], G["decTm"], self.cf4("muincl"), ALU.mult, [K("decTm"), "cstf"], [K("decTm")])
        yield
        self.tt("dve", f4(G["qkTm"]), ps[2][:, :], f4(G["decTm"]), ALU.mult, [pk[2], K("decTm")], [K("qkTm")])
        self.tt("dve", f4(G["Lg"]), ps[1][:, :], f4(G["decTm"]), ALU.mult, [pk[1], K("decTm")], [K("Lg")])
        yield
        self.tt("dve", G["MT"], G["Lg"],
                self.beta[:, tb, hg * 4:hg * 4 + 4].unsqueeze(2).to_broadcast([128, 4, 128]), ALU.mult,
                [K("Lg"), "beta"], [K("MT")])
        yield
        for hh in range(4):
            self.tr(psb[3][:, hh * 128:(hh + 1) * 128], G["MT"][:, hh, :], self.cb("ident"), [K("MT"), "cstb"], [pk[3]])
        yield
        self.cp("act", f4(G["M"]), psb[3][:, 0:512], [pk[3]], [K("M")])
        yield
        Nn, Nt, N2, N2t = "Na", "Nb", "Nc", "Nd"
        Pn, Pt, Pn2, Pt2 = "Pa", "Pb", "Pc", "Pd"
        self.tt("dve", G[Nn], G["M"], self.cb4("mndn"), ALU.mult, [K("M"), "cstb"], [K(Nn)])
        self.tt("dve", G[Nt], G["MT"], self.cb4("mndtn"), ALU.mult, [K("MT"), "cstb"], [K(Nt)])
        yield
        self.tt("dve", G[Pn], G[Nn], self.cb4("ident"), ALU.add, [K(Nn), "cstb"], [K(Pn)])
        self.tt("dve", G[Pt], G[Nt], self.cb4("ident"), ALU.add, [K(Nt), "cstb"], [K(Pt)])
        nstep = int(np.log2(NBK)) - 1
        for s_ in range(nstep):
            for hh in range(4):
                cs = slice(hh * 128, (hh + 1) * 128)
                self.mm(ps[0][:, cs], G[Nt][:, hh, :], G[Nn][:, hh, :], [K(Nt), K(Nn)], [pk[0]])
                self.mm(ps[1][:, cs], G[Nn][:, hh, :], G[Nt][:, hh, :], [K(Nt), K(Nn)], [pk[1]])
            yield
            self.cp("act", f4(G[N2]), ps[0][:, :], [pk[0]], [K(N2)])
            self.cp("act", f4(G[N2t]), ps[1][:, :], [pk[1]], [K(N2t)])
            yield
            for hh in range(4):
                cs = slice(hh * 128, (hh + 1) * 128)
                self.mm(ps[2][:, cs], G[N2t][:, hh, :], G[Pn][:, hh, :], [K(N2t), K(Pn)], [pk[2]])
                self.mm(ps[3][:, cs], G[N2][:, hh, :], G[Pt][:, hh, :], [K(N2), K(Pt)], [pk[3]])
            yield
            self.tt("dve", f4(G[Pn2]), f4(G[Pn]), ps[2][:, :], ALU.add, [K(Pn), pk[2]], [K(Pn2)])
            self.tt("dve", f4(G[Pt2]), f4(G[Pt]), ps[3][:, :], ALU.add, [K(Pt), pk[3]], [K(Pt2)])
            yield
            Nn, Nt, N2, N2t = N2, N2t, Nn, Nt
            Pn, Pt, Pn2, Pt2 = Pn2, Pt2, Pn, Pt
        T, U, T2, U2 = Pn, Pt, Pn2, Pt2
        E_, F_, X_, Y_ = Nn, Nt, N2, N2t
        b = NBK
        while b < 128:
            lastlvl = (b == 64)
            self.tt("dve", G[E_], G["M"], self.cb4("me%d" % b), ALU.mult, [K("M"), "cstb"], [K(E_)])
            if not lastlvl:
                self.tt("dve", G[F_], G["MT"], self.cb4("me%dt" % b), ALU.mult, [K("MT"), "cstb"], [K(F_)])
            yield
            for hh in range(4):
                cs = slice(hh * 128, (hh + 1) * 128)
                self.mm(ps[0][:, cs], G[E_][:, hh, :], G[U][:, hh, :], [K(E_), K(U)], [pk[0]])
                if not lastlvl:
                    self.mm(ps[1][:, cs], G[F_][:, hh, :], G[T][:, hh, :], [K(F_), K(T)], [pk[1]])
            yield
            self.cp("act", f4(G[Y_]), ps[0][:, :], [pk[0]], [K(Y_)])
            if not lastlvl:
                self.cp("act", f4(G[X_]), ps[1][:, :], [pk[1]], [K(X_)])
            yield
            for hh in range(4):
                cs = slice(hh * 128, (hh + 1) * 128)
                self.mm(ps[2][:, cs], G[T][:, hh, :], G[Y_][:, hh, :], [K(T), K(Y_)], [pk[2]])
                if not lastlvl:
                    self.mm(ps[3][:, cs], G[U][:, hh, :], G[X_][:, hh, :], [K(U), K(X_)], [pk[3]])
            yield
            self.tt("dve", f4(G[U2]), f4(G[U]), ps[2][:, :], ALU.subtract, [K(U), pk[2]], [K(U2)])
            if not lastlvl:
                self.tt("dve", f4(G[T2]), f4(G[T]), ps[3][:, :], ALU.subtract, [K(T), pk[3]], [K(T2)])
            yield
            T, U, T2, U2 = T2, U2, T, U
            b *= 2
        self.inv_result[hg] = U

    def scan_chain(self, tb, hg, G, gp, U, bx, by):
        A1 = self.A1
        blk = slice(tb * 128, (tb + 1) * 128)
        pb_ = tb % 2
        sm, smk = self.gsm2[pb_], "gsm%d" % pb_
        vtok, vtk = (self.vtok, self.vtok2)[pb_], "vtok%d" % pb_
        kdec, kdk = self.kdec2[pb_], "kdec%d" % pb_
        hs = [hg * 4 + hh for hh in range(4)]
        hsl = slice(hg * 4, hg * 4 + 4)
        X, Y = self.ps[bx], self.ps[by]
        xk, yk = "ps%d" % bx, "ps%d" % by
        X3 = X[:, :].rearrange("p (h d) -> p h d", h=4)
        Y3 = Y[:, :].rearrange("p (h d) -> p h d", h=4)
        bc = lambda ap: ap.unsqueeze(2).to_broadcast([128, 4, 128])
        Sk, Sbk = "S%d" % hg, "Sbf%d" % hg
        for hh, h in enumerate(hs):
            cs = slice(hh * 128, (hh + 1) * 128)
            self.mm(X[:, cs], A1[:, 8 + h, blk], self.Sbf[:, h, :], ["A1.%d" % (8 + h), Sbk], [xk])
            self.mm(Y[:, cs], A1[:, h, blk], self.Sbf[:, h, :], ["A1.%d" % h, Sbk], [yk])
        yield
        tS, tSk = self.tmpf()
        tS3 = tS[:, 0:512].rearrange("p (h d) -> p h d", h=4)
        self.tt("dve", tS3, X3, bc(sm[:, 24 + hg * 4:28 + hg * 4]), ALU.mult, [xk, smk], [tSk])
        o1, o1k = self.tmpf()
        o13 = o1[:, 0:512].rearrange("p (h d) -> p h d", h=4)
        self.tt("dve", o13, Y3, bc(sm[:, 16 + hg * 4:20 + hg * 4]), ALU.mult, [yk, smk], [o1k])
        yield
        r, rk = self.tmpb()
        r3 = r[:, :].rearrange("p (h d) -> p h d", h=4)
        self.tt("dve", r3, tS3, vtok[:, hsl, :], ALU.add, [tSk, vtk], [rk])
        yield
        for hh in range(4):
            cs = slice(hh * 128, (hh + 1) * 128)
            self.mm(X[:, cs], G[U][:, hh, :], r3[:, hh, :], [gp + U, rk], [xk])
        yield
        vn, vk = self.tmpb()
        vn3 = vn[:, :].rearrange("p (h d) -> p h d", h=4)
        self.tt("dve", vn3, X3, bc(self.beta[:, tb, hsl]), ALU.mult, [xk, "beta"], [vk])
        yield
        for hh, h in enumerate(hs):
            cs = slice(hh * 128, (hh + 1) * 128)
            self.mm(Y[:, cs], G["qkTm"][:, hh, :], vn3[:, hh, :], [gp + "qkTm", vk], [yk])
            self.mm(X[:, cs], kdec[:, h, :], vn3[:, hh, :], [kdk, vk], [xk])
        yield
        self.tt("dve", self.otok[:, hg * 512:(hg + 1) * 512], o1[:, 0:512], Y[:, :], ALU.add, [o1k, yk], ["otok"])
        self.tt("dve", self.S[:, hsl, :], self.S[:, hsl, :], bc(sm[:, 40 + hg * 4:44 + hg * 4]), ALU.mult, [Sk, smk], [Sk])
        yield
        self.tt("dve", self.S[:, hsl, :], self.S[:, hsl, :], X3, ALU.add, [Sk, xk], [Sk])
        yield
        self.cp("act", self.Sbf[:, hsl, :], self.S[:, hsl, :], [Sk], [Sbk])

    def lockstep(self, gens):
        gens = list(gens)
        while gens:
            nxt = []
            for g in gens:
                try:
                    next(g)
                    nxt.append(g)
                except StopIteration:
                    pass
            gens = nxt

    def gdn_prep(self, tb):
        A1 = self.A1
        blk = slice(tb * 128, (tb + 1) * 128)
        pb_ = tb % 2
        sm, smk = self.gsm2[pb_], "gsm%d" % pb_
        vtok, vtk = (self.vtok, self.vtok2)[pb_], "vtok%d" % pb_
        kdec, kdk = self.kdec2[pb_], "kdec%d" % pb_
        g8 = self.gtok[:, tb, :]
        for (dst, dk, u0, bank) in ((self.ktok, "ktok", 8, 5), (vtok, vtk, 16, 6)):
            for h in range(H):
                self.tr(self.psb[bank][:, h * 128:(h + 1) * 128], A1[:, u0 + h, blk], self.cb("ident"),
                        ["A1.%d" % (u0 + h), "cstb"], ["ps%d" % bank])
            self.cp("act", dst[:].rearrange("p h d -> p (h d)"), self.psb[bank][:, 0:1024], ["ps%d" % bank], [dk])
        self.mm(self.ps[7][:, 0:8], self.cf("ltri"), g8, ["cstf", "gtok"], ["ps7"])
        self.mm(self.ps[7][:, 8:16], self.cf("ones"), g8, ["cstf", "gtok"], ["ps7"])
        self.cp("dve", sm[:, 0:16], self.ps[7][:, 0:16], ["ps7"], [smk])
        self.act(sm[:, 16:24], sm[:, 0:8], AF.Exp, [smk], [smk])
        self.ts("dve", sm[:, 24:32], sm[:, 16:24], -1.0, ALU.mult, [smk], [smk])
        self.tt("dve", sm[:, 32:40], sm[:, 8:16], sm[:, 0:8], ALU.subtract, [smk], [smk])
        self.act(sm[:, 32:40], sm[:, 32:40], AF.Exp, [smk], [smk])
        self.act(sm[:, 40:48], sm[:, 8:16], AF.Exp, [smk], [smk])
        self.tt("dve", kdec[:], self.ktok[:], sm[:, 32:40].unsqueeze(2).to_broadcast([128, H, 128]), ALU.mult,
                ["ktok", smk], [kdk])

    def gdn_all(self, NB):
        self.gdn_prep(0)
        prev_on = None
        for tb in range(NB):
            self.inv_result = {}
            gens = [self.inv_chain(tb, hg, self.gqs[hg][0], self.gqs[hg][1], [4 * hg + i for i in range(4)])
                    for hg in range(2)]
            if prev_on is not None:
                self.lockstep([prev_on])
            self.lockstep(gens)
            if tb + 1 < NB:
                self.gdn_prep(tb + 1)
            self.lockstep([self.scan_chain(tb, hg, self.gqs[hg][0], self.gqs[hg][1], self.inv_result[hg], 2 * hg, 2 * hg + 1)
                           for hg in range(2)])
            prev_on = self.onorm_gen(tb, 128)
        self.lockstep([prev_on])

    def stage(self, k):
        return [(self.t1, "t1"), (self.t2, "t2"), (self.otok, "otok")][k]

    def load_sample_hist(self):
        for (srcd, dst, nch, nj) in ((self.sq, self.hsq, 24, 3), (self.ssc, self.hss, 8, 2)):
            for j in range(nj):
                for k in range(nch // 8):
                    t, tk = self.stage(k)
                    self.dma("aux", t[:NSAMP, :], srcd[:, j, k * 1024:(k + 1) * 1024], "hl", [], [tk])
                    for cc in range(8):
                        self.tr(self.ps[6][:, cc * NSAMP:(cc + 1) * NSAMP], t[:NSAMP, cc * 128:(cc + 1) * 128],
                                self.cf("ident")[:NSAMP, :NSAMP], [tk, "cstf"], ["ps6"])
                    self.cp("dve", dst[:, k * 8:(k + 1) * 8, j, :],
                            self.ps[6][:, 0:8 * NSAMP].rearrange("p (c b) -> p c b", c=8), ["ps6"], ["hsamp"])

    def gdn_sample(self):
        A1 = self.A1
        NS = NSAMP
        sm = self.gsm
        st = self.stat
        for (dst, dk, u0, bank) in ((self.qtok[:NS, :], "qtok", 0, 4), (self.ktok[:NS].rearrange("p h d -> p (h d)"), "ktok", 8, 5),
                                    (self.vtok[:NS].rearrange("p h d -> p (h d)"), "vtok", 16, 6)):
            for h in range(H):
                self.tr(self.psb[bank][:NS, h * 128:(h + 1) * 128], A1[:, u0 + h, 0:NS], self.cb("ident"),
                        ["A1.%d" % (u0 + h), "cstb"], ["ps%d" % bank])
            self.cp("act", dst, self.psb[bank][:NS, 0:1024], ["ps%d" % bank], [dk])
        a = sm[:NS, 0:8]
        self.act(a, self.gtok[:NS, 0, :], AF.Exp, ["gtok"], ["gsm"])
        q3 = self.qtok[:NS, :].rearrange("p (h d) -> p h d", h=H)
        t13 = self.t1[:NS, :].rearrange("p (h d) -> p h d", h=H)
        t23 = self.t2[:NS, :].rearrange("p (h d) -> p h d", h=H)
        o3 = self.otok[:NS, :].rearrange("p (h d) -> p h d", h=H)
        self.tt("dve", t13, q3, self.ktok[:NS], ALU.mult, ["qtok", "ktok"], ["t1"])
        self.P.add("dve", lambda e: e.tensor_reduce(sm[:NS, 8:16], t13, AX.X, ALU.add), ["t1"], ["gsm"])
        i16 = self.i16b[:].rearrange("p (a b) -> p a b", a=NS)
        for h in range(H):
            self.tt("dve", self.kTm[:, h, :, :], A1[:, 8 + h:9 + h, 0:NS].to_broadcast([128, NS, NS]), i16, ALU.mult,
                    ["A1.%d" % (8 + h), "i16b"], ["kTm"])
            self.tt("dve", self.qTm[:, h, :, :], A1[:, h:h + 1, 0:NS].to_broadcast([128, NS, NS]), i16, ALU.mult,
                    ["A1.%d" % h, "i16b"], ["qTm"])
        for b in range(NS):
            i3, i2 = b % 3, b % 2
            self.dma("aux", self.Sin[i3], self.sg[b].rearrange("h k v -> k h v"), "sin%d" % i3, [], ["Sin%d" % i3])
            self.cp("act" if b % 2 == 0 else "dve", self.Sinb[i2], self.Sin[i3], ["Sin%d" % i3], ["Sinb%d" % i2])
            for h in range(H):
                bk, bq = h // 4, 2 + h // 4
                cs = slice((h % 4) * 128, (h % 4 + 1) * 128)
                first = (b == 0 and h % 4 == 0)
                self.mm(self.ps[bk][:NS, cs], self.kTm[:, h, b, :], self.Sinb[i2][:, h, :], ["kTm", "Sinb%d" % i2],
                        ["ps%d" % bk], start=first, stop=(b == NS - 1))
                self.mm(self.ps[bq][:NS, cs], self.qTm[:, h, b, :], self.Sinb[i2][:, h, :], ["qTm", "Sinb%d" % i2],
                        ["ps%d" % bq], start=first, stop=(b == NS - 1))
        a_b = a.unsqueeze(2).to_broadcast([NS, H, 128])
        for half in range(2):
            hsl = slice(half * 4, half * 4 + 4)
            k3 = self.ps[half][:NS, :].rearrange("p (h d) -> p h d", h=4)
            qs3 = self.ps[2 + half][:NS, :].rearrange("p (h d) -> p h d", h=4)
            ab = a[:, hsl].unsqueeze(2).to_broadcast([NS, 4, 128])
            self.tt("dve", t13[:, hsl, :], k3, ab, ALU.mult, ["ps%d" % half, "gsm"], ["t1"])
            self.tt("dve", t13[:, hsl, :], self.vtok[:NS, hsl, :], t13[:, hsl, :], ALU.subtract, ["vtok", "t1"], ["t1"])
            self.tt("dve", t13[:, hsl, :], t13[:, hsl, :],
                    self.beta[:NS, 0, hsl].unsqueeze(2).to_broadcast([NS, 4, 128]), ALU.mult, ["t1", "beta"], ["t1"])
            self.tt("dve", t23[:, hsl, :], qs3, ab, ALU.mult, ["ps%d" % (2 + half), "gsm"], ["t2"])
            self.tt("dve", o3[:, hsl, :], t13[:, hsl, :], sm[:NS, 8 + half * 4:12 + half * 4].unsqueeze(2).to_broadcast([NS, 4, 128]),
                    ALU.mult, ["t1", "gsm"], ["otok"])
            self.tt("dve", o3[:, hsl, :], o3[:, hsl, :], t23[:, hsl, :], ALU.add, ["otok", "t2"], ["otok"])
        dbf = self.xb16
        self.cp("act", dbf[:NS, :], self.t1[:NS, :], ["t1"], ["xb16"])
        ad = self.t2[:NS, 0:128].rearrange("p (b h) -> p b h", b=NS)
        idr = self.cf("ident")[:NS, 0:NS].unsqueeze(2).to_broadcast([NS, NS, H])
        self.tt("dve", ad, a.unsqueeze(1).to_broadcast([NS, NS, H]), idr, ALU.mult, ["gsm", "cstf"], ["t2"])
        self.mm(self.ps[4][:, 0:128], self.cf("ones")[:NS, :], self.t2[:NS, 0:128], ["cstf", "t2"], ["ps4"])
        self.cp("dve", self.abc[:], self.ps[4][:, 0:128], ["ps4"], ["abc"])
        kflat = self.ktok[:NS].rearrange("p h d -> p (h d)")
        for b in range(NS):
            i2 = b % 2
            i3 = (b + 1) % 3
            self.dma("aux", self.Sin[i3], self.sg[b].rearrange("h k v -> k h v"), "sin%d" % i3, [], ["Sin%d" % i3])
            self.ts("dve", self.kmask[i2][:NS, :], kflat, self.cf("ident")[:NS, b:b + 1], ALU.mult,
                    ["ktok", "cstf"], ["kmask%d" % i2])
            for h in range(H):
                pb_ = 5 + h // 4
                cs = slice((h % 4) * 128, (h % 4 + 1) * 128)
                self.mm(self.ps[pb_][:, cs], self.kmask[i2][:NS, h * 128:(h + 1) * 128], dbf[:NS, h * 128:(h + 1) * 128],
                        ["kmask%d" % i2, "xb16"], ["ps%d" % pb_])
                self.stt(self.Sin[i3][:, h, :], self.Sin[i3][:, h, :], self.abc[:, b * 8 + h:b * 8 + h + 1],
                         self.ps[pb_][:, cs], ALU.mult, ALU.add, ["Sin%d" % i3, "abc", "ps%d" % pb_], ["Sin%d" % i3])
            self.dma("aux", self.sgs[b].rearrange("h k v -> k h v"), self.Sin[i3], "sout%d" % i3, ["Sin%d" % i3], ["sgs%d" % b])
            self.outkeys.append("sgs%d" % b)
        self.onorm_and_T(0, NS)


_CACHE = {}


WBIG_LEN = 2 * (3 * D * HID) + D * IN_W + 4 * D * D + 256 * D


def pack_wbig(weights, specs, offs, tot):
    out = np.empty((tot,), np.float32)
    for spec, (off, n) in zip(specs, offs):
        parts = []
        for (name, r0, r1, c0, c1) in spec:
            w = weights[name][r0:r1, c0:c1]
            kc = (r1 - r0) // 128
            parts.append(w.reshape(kc, 128, c1 - c0).transpose(1, 0, 2).reshape(128, kc * (c1 - c0)))
        out[off:off + 128 * n] = np.concatenate(parts, axis=1).reshape(-1)
    return out


def kernel(x_prompt, x_sample, p_prompt, p_sample, state_gdn, state_qkv_conv, state_sc_conv,
           ffn1_w_gate, ffn1_w_up, ffn1_w_down, ln1_g, ln1_b,
           w_in, w_conv_qkv, A_log, dt_bias, w_onorm, w_p_gdn, w_conv_sc, w_p_sc, w_o, ln2_g, ln2_b,
           ffn2_w_gate, ffn2_w_up, ffn2_w_down, ln3_g, ln3_b,
           w_ple_gate, w_ple_proj, ln4_g, ln4_b, _debug=None):
    f = lambda a: np.ascontiguousarray(np.asarray(a, dtype=np.float32))
    weights = {"ffn1_w_gate": f(ffn1_w_gate)[0], "ffn1_w_up": f(ffn1_w_up)[0], "ffn1_w_down": f(ffn1_w_down)[0],
               "w_in": f(w_in)[0], "w_p_gdn": f(w_p_gdn)[0], "w_p_sc": f(w_p_sc)[0], "w_o": f(w_o)[0],
               "ffn2_w_gate": f(ffn2_w_gate)[0], "ffn2_w_up": f(ffn2_w_up)[0], "ffn2_w_down": f(ffn2_w_down)[0],
               "w_ple_gate": f(w_ple_gate)[0], "w_ple_proj": f(w_ple_proj)[0]}
    bld = Builder(debug=_debug)
    bld.wbig_len = WBIG_LEN
    nc = bld.build()
    assert bld.slab_tot == WBIG_LEN or _debug, (bld.slab_tot, WBIG_LEN)
    assert bld.slab_tot <= WBIG_LEN
    wbig = np.zeros((WBIG_LEN,), np.float32)
    wbig[:bld.slab_tot] = pack_wbig(weights, bld.slab_specs, bld.slab_off, bld.slab_tot)
    lnp = np.stack([f(ln1_g)[0], f(ln1_b)[0], f(ln2_g)[0], f(ln2_b)[0], f(ln3_g)[0], f(ln3_b)[0], f(ln4_g)[0], f(ln4_b)[0]])
    wcq = np.ascontiguousarray(f(w_conv_qkv)[0].reshape(4, 24, 128).transpose(2, 1, 0).reshape(128, 96))
    wcs = np.ascontiguousarray(f(w_conv_sc)[0].reshape(3, 8, 128).transpose(2, 1, 0).reshape(128, 24))
    smallp = np.stack([f(A_log)[0], f(dt_bias)[0]])
    cst, cst2 = make_consts()
    cst2 = np.ascontiguousarray(cst2.reshape(128, -1))
    i16 = np.ascontiguousarray(np.broadcast_to(np.eye(16, dtype=np.float32).reshape(1, 256), (128, 256)))
    xp = f(x_prompt)
    xsm = f(x_sample)[:, 0, :]
    ppr = f(p_prompt)[0]
    psm = f(p_sample)[0, :, 0, :]
    sg = f(state_gdn)[0]
    sq = f(state_qkv_conv)[0]
    ssc = f(state_sc_conv)[0]
    in_maps = []
    for c in range(8):
        sl = slice(c * NSAMP, (c + 1) * NSAMP)
        in_maps.append({"x": xp[c], "pp": ppr[c], "xs": xsm[sl], "psm": psm[sl], "sg": sg[sl], "sq": sq[sl], "ssc": ssc[sl],
                        "wbig": wbig, "lnp": lnp, "wcq": wcq, "wcs": wcs, "smallp": smallp, "won": f(w_onorm)[0],
                        "cst": cst, "cst2": cst2, "i16": i16})
    ncores = (_debug or {}).get("ncores", 8)
    res = run_bass_kernel_spmd(nc, in_maps[:ncores], core_ids=list(range(ncores)))
    R = list(res.results)
    while len(R) < 8:
        R.append({k: np.zeros_like(v) for k, v in R[0].items()})
    y = np.stack([R[c]["y"] for c in range(8)])
    ys = np.concatenate([R[c]["ys"] for c in range(8)])[:, None, :]
    sgp = np.stack([R[c]["sgp"] for c in range(8)])[None]
    sqp = np.stack([R[c]["sqp"] for c in range(8)])[None]
    ssp = np.stack([R[c]["ssp"] for c in range(8)])[None]
    sgs = np.concatenate([R[c]["sgs"] for c in range(8)])[None]
    sqs = np.concatenate([R[c]["sqs"] for c in range(8)])[None]
    sss = np.concatenate([R[c]["sss"] for c in range(8)])[None]
    return (y.astype(np.float32), ys.astype(np.float32), sgp.astype(np.float32), sqp.astype(np.float32),
            ssp.astype(np.float32), sgs.astype(np.float32), sqs.astype(np.float32), sss.astype(np.float32))
```

```python
import contextlib
import numpy as np
import concourse.bass as bass
import concourse.mybir as mybir
from concourse.bass_utils import run_bass_kernel_spmd

F32 = mybir.dt.float32
BF16 = mybir.dt.bfloat16
AF = mybir.ActivationFunctionType
ALU = mybir.AluOpType
AX = mybir.AxisListType

D = 1024
SEQ = 2048
NSAMP = 16
HID = 2816
NJ = HID // 128
H = 8
QKV_W = 3072
IN_W = 9232
Z0, BETA0, A0, B0, C0, H0, GG0, GS0 = 3072, 4096, 4104, 4112, 5136, 6160, 7184, 8208
ALPHA = 2.0 ** 0.25
LN_EPS = 1e-5
RMS_EPS = 1e-6
L2_EPS = 1e-6
NTP = 512
SLOT = 4096
NSLOT = 4
NBK = 16

COMPUTE = ("pe", "act", "dve", "pool")


class _Op:
    __slots__ = ("eng", "fn", "r", "w", "key", "eidx", "kn", "waits", "done", "inc")

    def __init__(self, eng, fn, r, w, key):
        self.eng = eng
        self.fn = fn
        self.r = r
        self.w = w
        self.key = key
        self.eidx = -1
        self.kn = 0
        self.waits = []
        self.done = None
        self.inc = False


class Prog:
    def __init__(self, nc):
        self.nc = nc
        self.ops = []

    def add(self, eng, fn, r=(), w=(), key=None):
        self.ops.append(_Op(eng, fn, tuple(r), tuple(w), key))

    def finalize(self):
        last_w = {}
        readers = {}
        issue = {e: {} for e in ("pe", "act", "dve", "pool", "sp")}
        ecount = {e: 0 for e in issue}
        kcount = {}
        kops = {}
        eops = {e: [] for e in issue}
        for op in self.ops:
            e = op.eng
            deps = set()
            for res in op.r:
                lw = last_w.get(res)
                if lw is not None:
                    deps.add(lw)
            for res in op.w:
                lw = last_w.get(res)
                if lw is not None:
                    deps.add(lw)
                for rd in readers.get(res, ()):
                    deps.add(rd)
            deps.discard(op)
            clock = issue[e]
            if op.key is None:
                op.eidx = ecount[e]
                ecount[e] += 1
                eops[e].append(op)
            else:
                n = kcount.get(op.key, 0) + 1
                kcount[op.key] = n
                op.kn = n
                kops.setdefault(op.key, []).append(op)
                if n > 1:
                    deps.add(kops[op.key][n - 2])
            best = {}
            dma_deps = []
            for d in deps:
                if d.key is None:
                    b = best.get(d.eng)
                    if b is None or d.eidx > b.eidx:
                        best[d.eng] = d
                else:
                    dma_deps.append(d)
            newclock = None
            for f, d in best.items():
                if clock.get(f, -1) >= d.eidx:
                    continue
                if f == e and op.key is None:
                    if e == "pe":
                        continue
                    if e != "pool" and (op.eidx - d.eidx) > 12:
                        continue
                op.waits.append(("c", f, d.eidx))
                d.inc = True
                if newclock is None:
                    newclock = dict(clock)
                for k, v in d.done.items():
                    if newclock.get(k, -1) < v:
                        newclock[k] = v
            for d in dma_deps:
                kk = ("dma", d.key)
                cur = clock if newclock is None else newclock
                if cur.get(kk, 0) >= d.kn:
                    continue
                op.waits.append(("d", d.key, d.kn))
                if newclock is None:
                    newclock = dict(clock)
                for k, v in d.done.items():
                    if newclock.get(k, -1) < v:
                        newclock[k] = v
            if newclock is not None:
                issue[e] = newclock
                clock = newclock
            done = dict(clock)
            if op.key is None:
                done[e] = op.eidx
            else:
                done[("dma", op.key)] = op.kn
            op.done = done
            for res in op.r:
                readers.setdefault(res, []).append(op)
            for res in op.w:
                last_w[res] = op
                readers[res] = []
        self.rank = {}
        for e, lst in eops.items():
            k = 0
            for op in lst:
                if op.inc:
                    k += 1
                    self.rank[(e, op.eidx)] = k
        self.keys = list(kcount.keys())
        for op in self.ops:
            op.done = None
            if len(op.waits) > 1:
                m = {}
                for t, a, b in op.waits:
                    if (t, a) not in m or m[(t, a)] < b:
                        m[(t, a)] = b
                op.waits = [(t, a, b) for (t, a), b in m.items()]

    def emit(self, es):
        nc = self.nc
        sems = {}
        for e in COMPUTE:
            sems[e] = es.enter_context(nc.semaphore("s_" + e))
        ksem = {}
        for k in self.keys:
            ksem[k] = es.enter_context(nc.semaphore("k_" + str(k)))
        block = es.enter_context(nc.Block())
        rank = self.rank

        def run(ename, eng):
            for op in self.ops:
                if op.eng != ename:
                    continue
                for t, a, b in op.waits:
                    if t == "c":
                        eng.wait_ge(sems[a], rank[(a, b)])
                    else:
                        eng.wait_ge(ksem[a], 16 * b)
                if op.fn is None:
                    continue
                ins = op.fn(eng)
                if op.key is not None:
                    ins.then_inc(ksem[op.key], 16)
                elif op.inc:
                    ins.then_inc(sems[ename], 1)

        @block.tensor
        def _(eng):
            run("pe", eng)

        @block.scalar
        def _(eng):
            run("act", eng)

        @block.vector
        def _(eng):
            run("dve", eng)

        @block.gpsimd
        def _(eng):
            run("pool", eng)

        @block.sync
        def _(eng):
            run("sp", eng)


CSTF_NAMES = ["ident", "ltri", "su", "muincl", "ones"]
CSTB_NAMES = ["ident", "ones", "mndn", "mndtn", "me16", "me16t", "me32", "me32t", "me64", "me64t"]


def make_consts():
    i = np.arange(128)[:, None]
    j = np.arange(128)[None, :]
    c = {}
    c["ident"] = (i == j)
    c["ltri"] = (i <= j)
    c["su"] = (i > j)
    c["muincl"] = (j >= i)
    c["ones"] = np.ones((128, 128), bool)
    nd = (i // NBK == j // NBK) & (i > j)
    c["mndn"] = -1.0 * nd
    c["mndtn"] = -1.0 * nd.T
    for b in (16, 32, 64):
        e = (i // (2 * b) == j // (2 * b)) & ((i % (2 * b)) >= b) & ((j % (2 * b)) < b)
        c["me%d" % b] = e
        c["me%dt" % b] = e.T
    arrf = np.stack([np.asarray(c[n], np.float32) for n in CSTF_NAMES], axis=1)
    arrb = np.stack([np.asarray(c[n], np.float32) for n in CSTB_NAMES], axis=1)
    return np.ascontiguousarray(arrf), np.ascontiguousarray(arrb)


class Builder:
    def __init__(self, debug=None):
        self.debug = debug or {}
        self.slab_specs = []
        self.slab_off = []
        self.slab_tot = 0
        self.nslab_pass = None

    def mm(self, out, lhsT, rhs, r, w, start=True, stop=True):
        self.P.add("pe", lambda e: e.matmul(out, lhsT, rhs, start=start, stop=stop), r, w)

    def tr(self, out, in_, ident, r, w):
        self.P.add("pe", lambda e: e.transpose(out, in_, ident), r, w)

    def act(self, out, in_, func, r, w, bias=None, scale=None):
        kw = {}
        if bias is not None:
            kw["bias"] = bias
        if scale is not None:
            kw["scale"] = scale
        self.P.add("act", lambda e: e.activation(out, in_, func, **kw), r, w)

    def tt(self, eng, out, in0, in1, op, r, w):
        self.P.add(eng, lambda e: e.tensor_tensor(out, in0, in1, op), r, w)

    def ts(self, eng, out, in0, s1, op0, r, w, s2=None, op1=None):
        if op1 is None:
            self.P.add(eng, lambda e: e.tensor_scalar(out, in0, s1, None, op0), r, w)
        else:
            self.P.add(eng, lambda e: e.tensor_scalar(out, in0, s1, s2, op0, op1), r, w)

    def stt(self, out, in0, scalar, in1, op0, op1, r, w):
        self.P.add("dve", lambda e: e.scalar_tensor_tensor(out, in0, scalar, in1, op0, op1), r, w)

    def cp(self, eng, out, in_, r, w):
        if eng == "act":
            self.P.add("act", lambda e: e.activation(out, in_, AF.Copy), r, w)
        else:
            self.P.add(eng, lambda e: e.tensor_copy(out, in_), r, w)

    def dq(self):
        return "sp" if self.recording else "pool"

    def dma(self, eng, out, in_, key, r, w, slow=False):
        if eng == "aux":
            eng = self.dq()
        if slow:
            self.P.add(eng, lambda e: e.dma_start(out=out, in_=in_, allow_slow_non_contiguous=True), r, w, key=key)
        else:
            self.P.add(eng, lambda e: e.dma_start(out=out, in_=in_), r, w, key=key)

    def slab(self, spec):
        if self.recording:
            self.slab_specs.append(spec)
            n = sum(((r1 - r0) // 128) * (c1 - c0) for (_, r0, r1, c0, c1) in spec)
            assert n <= SLOT, n
            self.slab_off.append((self.slab_tot, n))
            self.slab_tot += 128 * n
        si = self.slab_i % self.nslab_pass if self.nslab_pass else self.slab_i
        off, n = self.slab_off[si]
        slot = self.slab_i % NSLOT
        self.slab_i += 1
        t = self.wring[slot]
        key = "w%d" % slot
        scr = self.wscr[off:off + 128 * n].rearrange("(p n) -> p n", p=128)
        if self.recording:
            src = self.wbig[off:off + 128 * n].rearrange("(p n) -> p n", p=128)
            self.dma("pool", t[:, 0:n], src, key, r=[], w=[key])
            if self.debug.get("npass", 4) > 1 or not self.debug.get("nosample", False):
                self.dma("sp", scr, t[:, 0:n], "wb%d" % slot, r=[key], w=["wscr%d" % si])
        else:
            self.dma("sp", t[:, 0:n], scr, key, r=["wscr%d" % si], w=[key])
        views = []
        o = 0
        for (_, r0, r1, c0, c1) in spec:
            kc = (r1 - r0) // 128
            nc_ = c1 - c0
            views.append(t[:, o:o + kc * nc_].rearrange("p (k n) -> p k n", k=kc))
            o += kc * nc_
        return views, key

    def build(self):
        nc = bass.Bass("TRN2", target_bir_lowering=False)
        self.nc = nc
        self.es = contextlib.ExitStack()
        with self.es:
            self._build_inner()
        return nc

    def dram_in(self, name, shape, dt=F32):
        return self.nc.dram_tensor(name, list(shape), dt, kind="ExternalInput").ap()

    def dram_out(self, name, shape, dt=F32):
        return self.nc.dram_tensor(name, list(shape), dt, kind="ExternalOutput").ap()

    def sb(self, name, shape, dt):
        return self.es.enter_context(self.nc.sbuf_tensor(name, list(shape), dt))

    def _build_inner(self):
        nc = self.nc
        self.P = Prog(nc)
        P = self.P
        self.x = self.dram_in("x", [SEQ, D])
        self.pp = self.dram_in("pp", [SEQ, 256])
        self.xs = self.dram_in("xs", [NSAMP, D])
        self.psm = self.dram_in("psm", [NSAMP, 256])
        self.sg = self.dram_in("sg", [NSAMP, H, 128, 128])
        self.sq = self.dram_in("sq", [NSAMP, 3, QKV_W])
        self.ssc = self.dram_in("ssc", [NSAMP, 2, D])
        self.wbig = self.dram_in("wbig", [self.wbig_len])
        self.wscr = self.nc.dram_tensor("wscr", [self.wbig_len], BF16, kind="Internal").ap()
        self.lnp = self.dram_in("lnp", [8, D])
        self.wcq_d = self.dram_in("wcq", [128, 24 * 4])
        self.wcs_d = self.dram_in("wcs", [128, 8 * 3])
        self.smallp = self.dram_in("smallp", [2, 8])
        self.won_d = self.dram_in("won", [128])
        self.cst_d = self.dram_in("cst", [128, len(CSTF_NAMES), 128])
        self.cst2_d = self.dram_in("cst2", [128, len(CSTB_NAMES) * 128])
        self.i16_d = self.dram_in("i16", [128, 256])
        self.y = self.dram_out("y", [SEQ, D])
        self.ys = self.dram_out("ys", [NSAMP, D])
        self.sgp = self.dram_out("sgp", [H, 128, 128])
        self.sqp = self.dram_out("sqp", [3, QKV_W])
        self.ssp = self.dram_out("ssp", [2, D])
        self.sgs = self.dram_out("sgs", [NSAMP, H, 128, 128])
        self.sqs = self.dram_out("sqs", [NSAMP, 3, QKV_W])
        self.sss = self.dram_out("sss", [NSAMP, 2, D])
        self.outkeys = []

        sb = self.sb
        self.wring = [sb("wr%d" % i, [128, SLOT], BF16) for i in range(NSLOT)]
        self.xres = sb("xres", [128, 4, D], F32)
        self.xT = sb("xT", [128, 8, NTP], BF16)
        self.gbt = sb("gbt", [128, 2, D], F32)
        self.cstf = sb("cstf", [128, len(CSTF_NAMES), 128], F32)
        self.cstb = sb("cstb", [128, len(CSTB_NAMES), 128], BF16)
        self.i16b = sb("i16b", [128, 256], BF16)
        self.wcq = sb("wcq_s", [128, 24, 4], F32)
        self.wcs = sb("wcs_s", [128, 8, 3], F32)
        self.wonb = sb("wonb", [128, 128], F32)
        self.smallb = sb("smallb", [128, 16], F32)
        self.negA = sb("negA", [128, 8], F32)
        self.histq = sb("histq", [128, 24, 3], F32)
        self.hists = sb("hists", [128, 8, 2], F32)
        self.S = sb("S", [128, H, 128], F32)
        self.Sbf = sb("Sbf", [128, H, 128], BF16)
        self.A1 = sb("A1", [128, 24, NTP], BF16)
        self.ztok = sb("ztok", [128, 4, D], BF16)
        self.ktok = sb("ktok", [128, H, 128], BF16)
        self.vtok = sb("vtok", [128, H, 128], BF16)
        self.batok = sb("batok", [128, 4, 16], F32)
        self.beta = sb("beta", [128, 4, 8], F32)
        self.gtok = sb("gtok", [128, 4, 8], F32)
        self.tf = [sb("tf%d" % i, [128, NTP + 4], F32) for i in range(11)]
        self.tfi = 0
        self.tb16 = [sb("tb%d" % i, [128, NTP], BF16) for i in range(4)]
        self.tbi = 0
        self.t1 = sb("t1", [128, D], F32)
        self.t2 = sb("t2", [128, D], F32)
        self.otok = sb("otok", [128, D], F32)
        self.xb16 = sb("xb16", [128, D], BF16)
        self.stat = sb("stat", [128, 4, 16], F32)
        self.stat3 = sb("stat3", [128, 16], F32)
        self.xb16b = sb("xb16b", [128, D], BF16)
        self.pT = sb("pT", [128, 2, NTP], BF16)
        self.pb = sb("pb", [128, 256], BF16)
        GQ = [("decTm", F32), ("Lg", F32), ("qkTm", BF16), ("MT", BF16), ("M", BF16),
              ("Na", BF16), ("Nb", BF16), ("Nc", BF16), ("Nd", BF16), ("Pa", BF16), ("Pb", BF16),
              ("Pc", BF16), ("Pd", BF16)]
        self.gqs = []
        self.arenaA = sb("arenaA", [128, 15 * 512], BF16)
        self.arenaB = sb("arenaB", [128, 15 * 512], BF16)
        self.gq_names = [nm for nm, _ in GQ]
        for ar, pfx in ((self.arenaA, "gA_"), (self.arenaB, "gB_")):
            gX = {}
            o = 0
            for nm, dt in GQ:
                n = 1024 if dt == F32 else 512
                v = ar[:, o:o + n]
                if dt == F32:
                    v = v.bitcast(F32)
                gX[nm] = v.rearrange("p (h d) -> p h d", h=4)
                o += n
            self.gqs.append((gX, pfx))
        self.kdec2 = [sb("kdec%d" % i, [128, H, 128], BF16) for i in range(2)]
        self.gsm2 = [sb("gsm%d" % i, [128, 64], F32) for i in range(2)]
        self.vtok2 = sb("vtok2", [128, H, 128], BF16)
        self.Uk = [sb("Uk%d" % i, [128, 4, 128], BF16) for i in range(2)]
        self.Qk = [sb("Qk%d" % i, [128, 4, 128], BF16) for i in range(2)]
        self.gsm = self.gsm2[0]
        self.kdec = self.kdec2[0]
        fA = lambda o, n: self.arenaA[:, o:o + n]
        fB = lambda o, n: self.arenaB[:, o:o + n]
        self.Sin = [fA(k * 2048, 2048).bitcast(F32).rearrange("p (h d) -> p h d", h=H) for k in range(3)]
        self.hss = fA(6144, 512).bitcast(F32).rearrange("p (c j b) -> p c j b", c=8, j=2)
        self.Sinb = [fA(6656, 1024).rearrange("p (h d) -> p h d", h=H), fB(6400, 1024).rearrange("p (h d) -> p h d", h=H)]
        self.kTm = fB(0, 2048).rearrange("p (h a b) -> p h a b", h=H, a=NSAMP)
        self.qTm = fB(2048, 2048).rearrange("p (h a b) -> p h a b", h=H, a=NSAMP)
        self.hsq = fB(4096, 2304).bitcast(F32).rearrange("p (c j b) -> p c j b", c=24, j=3)
        self.kmask = [sb("kmask%d" % i, [NSAMP, D], BF16) for i in range(2)]
        self.qtok = sb("qtok", [NSAMP, D], BF16)
        self.abc = sb("abc", [128, 128], F32)
        self.ps = [self.es.enter_context(nc.psum_tensor("ps%d" % i, [128, 512], F32)) for i in range(8)]
        self.psb = [p.bitcast(BF16) for p in self.ps]

        self.recording = True
        self.slab_i = 0
        self.setup()
        npass = self.debug.get("npass", 4)
        for pi in range(npass):
            self.layer_pass(pi, NTP, sample=False, last=(pi == npass - 1))
            if pi == 0:
                self.recording = False
                self.nslab_pass = len(self.slab_off)
        if not self.debug.get("nosample", False):
            gbk = [p + nm for p in ("gA_", "gB_") for nm in self.gq_names]
            gbk += ["gsm0", "gsm1", "vtok0", "vtok1", "kdec0", "kdec1"]
            P.add("dve", lambda e: e.memset(self.gsm[:, 60:64], 0.0), gbk,
                  gbk + ["kTm", "qTm", "hsamp", "Sin0", "Sin1", "Sin2", "Sinb0", "Sinb1", "gsm", "vtok"])
            self.layer_pass(0, NSAMP, sample=True, last=True)
        P.add("sp", None, r=self.outkeys)
        P.finalize()
        P.emit(self.es)

    def cf(self, name):
        return self.cstf[:, CSTF_NAMES.index(name), :]

    def cb(self, name):
        return self.cstb[:, CSTB_NAMES.index(name), :]

    def cb4(self, name):
        i = CSTB_NAMES.index(name)
        return self.cstb[:, i:i + 1, :].to_broadcast([128, 4, 128])

    def cf4(self, name):
        i = CSTF_NAMES.index(name)
        return self.cstf[:, i:i + 1, :].to_broadcast([128, 4, 128])

    def tmpf(self):
        i = self.tfi % len(self.tf)
        self.tfi += 1
        return self.tf[i], "tf%d" % i

    def tmpb(self):
        i = self.tbi % len(self.tb16)
        self.tbi += 1
        return self.tb16[i], "tb%d" % i

    def setup(self):
        d = self.dma
        d("sp", self.cstf[:], self.cst_d, "c0", [], ["cstf"])
        nb = len(CSTB_NAMES) * 128
        for k in range(0, nb, 1024):
            n = min(1024, nb - k)
            d("sp", self.t1[:, 0:n], self.cst2_d[:, k:k + n], "c1", [], ["t1"])
            self.cp("dve", self.cstb[:].rearrange("p c d -> p (c d)")[:, k:k + n], self.t1[:, 0:n], ["t1"], ["cstb"])
        d("sp", self.t2[:, 0:256], self.i16_d, "c1", [], ["t2"])
        self.cp("dve", self.i16b[:], self.t2[:, 0:256], ["t2"], ["i16b"])
        d("sp", self.wcq[:].rearrange("p c j -> p (c j)"), self.wcq_d, "c2", [], ["wcq"])
        d("sp", self.wcs[:].rearrange("p c j -> p (c j)"), self.wcs_d, "c3", [], ["wcs"])
        d("sp", self.wonb[:], self.won_d.partition_broadcast(128), "c4", [], ["wonb"])
        d("sp", self.smallb[:], self.smallp.rearrange("a b -> (a b)").partition_broadcast(128), "c5", [], ["smallb"])
        self.act(self.negA[:], self.smallb[:, 0:8], AF.Exp, ["smallb"], ["negA"])
        self.ts("dve", self.negA[:], self.negA[:], -1.0, ALU.mult, ["negA"], ["negA"])
        self.P.add("dve", lambda e: e.memset(self.S[:], 0.0), [], ["S0", "S1"])
        self.P.add("dve", lambda e: e.memset(self.Sbf[:], 0.0), [], ["Sbf0", "Sbf1"])
        self.P.add("dve", lambda e: e.memset(self.histq[:], 0.0), [], ["histq"])
        self.P.add("dve", lambda e: e.memset(self.hists[:], 0.0), [], ["hists"])

    def make_xT(self, tb, TB, bank):
        ps, psk = self.psb[bank], "ps%d" % bank
        xb, xbk = ((self.xb16, "xb16"), (self.xb16b, "xb16b"))[tb % 2]
        self.cp("act", xb[:TB, :], self.xres[:TB, tb, :], ["xres%d" % tb], [xbk])
        for c in range(8):
            self.tr(ps[:, c * TB:(c + 1) * TB], xb[:TB, c * 128:(c + 1) * 128], self.cb("ident")[:TB, :TB],
                    [xbk, "cstb"], [psk])
        self.cp("dve", self.xT[:, :, tb * TB:(tb + 1) * TB],
                ps[:, 0:8 * TB].rearrange("p (c t) -> p c t", c=8), [psk], ["xT%d" % tb])

    def layer_norm(self, idx, NB, TB, final_out=None):
        self.dma("aux", self.gbt[:, 0, :], self.lnp[2 * idx, :].partition_broadcast(128), "gb0", [], ["gbt0"])
        self.dma("aux", self.gbt[:, 1, :], self.lnp[2 * idx + 1, :].partition_broadcast(128), "gb1", [], ["gbt1"])
        eps = LN_EPS / (ALPHA * ALPHA)
        st = self.stat
        for tb in range(NB):
            xr = self.xres[:TB, tb, :]
            xk = "xres%d" % tb
            self.P.add("dve", lambda e, xr=xr, tb=tb: e.bn_stats(st[:TB, tb, 0:6], xr[:, 0:512]), [xk], ["stat"])
            self.P.add("dve", lambda e, xr=xr, tb=tb: e.bn_stats(st[:TB, tb, 6:12], xr[:, 512:1024]), [xk], ["stat"])
            self.P.add("dve", lambda e, tb=tb: e.bn_aggr(st[:TB, tb, 12:14], st[:TB, tb, 0:12]), ["stat"], ["stat"])
        self.act(st[:TB, 0:NB, 14], st[:TB, 0:NB, 13], AF.Sqrt, ["stat"], ["stat2"], bias=eps)
        self.P.add("dve", lambda e: e.reciprocal(st[:TB, 0:NB, 15], st[:TB, 0:NB, 14]), ["stat2"], ["stat2"])
        for tb in range(NB):
            xr = self.xres[:TB, tb, :]
            xk = "xres%d" % tb
            self.stt(self.t1[:TB, :], xr, st[:TB, tb, 12:13], self.gbt[:TB, 0, :], ALU.subtract, ALU.mult,
                     [xk, "stat", "gbt0"], ["t1"])
            self.stt(xr, self.t1[:TB, :], st[:TB, tb, 15:16], self.gbt[:TB, 1, :], ALU.mult, ALU.add,
                     ["t1", "stat2", "gbt1"], [xk])
            if final_out is not None:
                ok = final_out[1] + str(tb)
                self.dma("aux", final_out[0][tb * TB:(tb + 1) * TB, :], xr, "yo%d" % tb, [xk], [ok])
                self.outkeys.append(ok)
            else:
                self.make_xT(tb, TB, (2 * tb) % 8)

    def ffn(self, pfx, NB, TB):
        NT = NB * TB
        for j0 in range(0, NJ, 2):
            (wg, wu), wk = self.slab([(pfx + "_w_gate", 0, D, j0 * 128, j0 * 128 + 256),
                                      (pfx + "_w_up", 0, D, j0 * 128, j0 * 128 + 256)])
            for jj in range(2):
                j = j0 + jj
                bg, bu = 2 * (j % 2), 2 * (j % 2) + 1
                for kc in range(8):
                    self.mm(self.ps[bg][:, :NT], wg[:, kc, jj * 128:(jj + 1) * 128], self.xT[:, kc, :NT],
                            [wk, "xT0", "xT1", "xT2", "xT3"], ["ps%d" % bg], start=(kc == 0), stop=(kc == 7))
                for kc in range(8):
                    self.mm(self.ps[bu][:, :NT], wu[:, kc, jj * 128:(jj + 1) * 128], self.xT[:, kc, :NT],
                            [wk, "xT0", "xT1", "xT2", "xT3"], ["ps%d" % bu], start=(kc == 0), stop=(kc == 7))
                t, tk = self.tmpf()
                self.act(t[:, :NT], self.ps[bg][:, :NT], AF.Silu, ["ps%d" % bg], [tk])
                self.tt("dve", self.A1[:, j, :NT], t[:, :NT], self.ps[bu][:, :NT], ALU.mult,
                        [tk, "ps%d" % bu], ["A1.%d" % j])
        for j0 in range(0, NJ, 4):
            j1 = min(NJ, j0 + 4)
            (wd,), wk = self.slab([(pfx + "_w_down", j0 * 128, j1 * 128, 0, D)])
            for jj in range(j1 - j0):
                j = j0 + jj
                for tb in range(NB):
                    for nh in range(2):
                        b = tb * 2 + nh
                        self.mm(self.ps[b][:TB, :], self.A1[:, j, tb * TB:(tb + 1) * TB], wd[:, jj, nh * 512:(nh + 1) * 512],
                                [wk, "A1.%d" % j], ["ps%d" % b], start=(j == 0), stop=(j == NJ - 1))
        c = 0.5 / ALPHA
        for tb in range(NB):
            for nh in range(2):
                b = tb * 2 + nh
                xr = self.xres[:TB, tb, nh * 512:(nh + 1) * 512]
                self.stt(xr, self.ps[b][:TB, :], c, xr, ALU.mult, ALU.add, ["ps%d" % b, "xres%d" % tb], ["xres%d" % tb])

    def layer_pass(self, pi, NT, sample, last):
        TB = min(128, NT)
        NB = NT // TB
        self.slab_i = 0 if self.recording else self.slab_i
        if not sample:
            src, psrc = self.x[pi * NT:(pi + 1) * NT, :], self.pp[pi * NT:(pi + 1) * NT, :]
            yout = (self.y[pi * NT:(pi + 1) * NT, :], "y%d_" % pi)
        else:
            src, psrc = self.xs, self.psm
            yout = (self.ys, "ys_")
        for tb in range(NB):
            self.dma("aux", self.xres[:TB, tb, :], src[tb * TB:(tb + 1) * TB, :], "x%d" % tb, [], ["xres%d" % tb])
            self.make_xT(tb, TB, tb % 8)
        stop = self.debug.get("stop")
        self.ffn("ffn1", NB, TB)
        if stop == "ffn1":
            return self.dump(yout, NB, TB)
        self.layer_norm(0, NB, TB)
        if stop == "ln1":
            return self.dump(yout, NB, TB)
        self.mixers(pi, NB, TB, sample, last)
        if stop == "mix":
            return self.dump(yout, NB, TB)
        self.layer_norm(1, NB, TB)
        self.ffn("ffn2", NB, TB)
        self.layer_norm(2, NB, TB)
        if stop == "ln3":
            return self.dump(yout, NB, TB)
        self.ple(psrc, NB, TB)
        self.layer_norm(3, NB, TB, final_out=yout)

    def dump(self, yout, NB, TB):
        for tb in range(NB):
            ok = yout[1] + str(tb)
            self.dma("aux", yout[0][tb * TB:(tb + 1) * TB, :], self.xres[:TB, tb, :], "yo%d" % tb, ["xres%d" % tb], [ok])
            self.outkeys.append(ok)

    def ple(self, psrc, NB, TB):
        NT = NB * TB
        for tb in range(NB):
            pf, pfk = self.tmpf()
            self.dma("aux", pf[:TB, 0:256], psrc[tb * TB:(tb + 1) * TB, :], "pf", [], [pfk])
            self.cp("act", self.pb[:TB, :], pf[:TB, 0:256], [pfk], ["pb"])
            for c in range(2):
                self.tr(self.psb[7][:, c * TB:(c + 1) * TB], self.pb[:TB, c * 128:(c + 1) * 128],
                        self.cb("ident")[:TB, :TB], ["pb", "cstb"], ["ps7"])
            self.cp("dve", self.pT[:, :, tb * TB:(tb + 1) * TB],
                    self.psb[7][:, 0:2 * TB].rearrange("p (c t) -> p c t", c=2), ["ps7"], ["pT"])
        for nh in range(2):
            (wg,), wgk = self.slab([("w_ple_gate", 0, D, nh * 512, (nh + 1) * 512)])
            (wp,), wpk = self.slab([("w_ple_proj", 0, 256, nh * 512, (nh + 1) * 512)])
            for tb in range(NB):
                bg, bp = 2 * (tb % 2), 2 * (tb % 2) + 1
                for kc in range(8):
                    self.mm(self.ps[bg][:TB, :], self.xT[:, kc, tb * TB:(tb + 1) * TB], wg[:, kc, :],
                            [wgk, "xT0", "xT1", "xT2", "xT3"], ["ps%d" % bg], start=(kc == 0), stop=(kc == 7))
                for kc in range(2):
                    self.mm(self.ps[bp][:TB, :], self.pT[:, kc, tb * TB:(tb + 1) * TB], wp[:, kc, :],
                            [wpk, "pT"], ["ps%d" % bp], start=(kc == 0), stop=(kc == 1))
                t, tk = self.tmpf()
                self.act(t[:TB, :512], self.ps[bg][:TB, :], AF.Sigmoid, ["ps%d" % bg], [tk])
                self.tt("dve", t[:TB, :512], t[:TB, :512], self.ps[bp][:TB, :], ALU.mult, [tk, "ps%d" % bp], [tk])
                xr = self.xres[:TB, tb, nh * 512:(nh + 1) * 512]
                self.stt(xr, t[:TB, :512], 1.0 / ALPHA, xr, ALU.mult, ALU.add, [tk, "xres%d" % tb], ["xres%d" % tb])

    def conv_chunk(self, psbank, NT, taps_hist, wts, ntap, hist_tile, hist_key, sample, src_is_psum=True, src=None):
        H_ = ntap - 1
        cbt, cbk = self.tmpf()
        if src_is_psum:
            self.cp("act", cbt[:, H_:H_ + NT], self.ps[psbank][:, :NT], ["ps%d" % psbank], [cbk])
        else:
            src(cbt[:, H_:H_ + NT], cbk)
        if not sample:
            self.cp("dve", cbt[:, 0:H_], hist_tile, [hist_key], [cbk])
            self.cp("dve", hist_tile, cbt[:, NT:NT + H_], [cbk], [hist_key])
            taps = [cbt[:, j:j + NT] for j in range(ntap)]
            tr_ = [cbk]
        else:
            taps = [taps_hist[j] for j in range(H_)] + [cbt[:, H_:H_ + NT]]
            tr_ = [cbk, "hsamp"]
        acc, ak = self.tmpf()
        self.ts("dve", acc[:, :NT], taps[0], wts[0], ALU.mult, tr_ + ["wc"], [ak])
        for j in range(1, ntap):
            self.stt(acc[:, :NT], taps[j], wts[j], acc[:, :NT], ALU.mult, ALU.add, tr_ + ["wc", ak], [ak])
        return acc, ak, cbt, cbk

    def mixers(self, pi, NB, TB, sample, last):
        NT = NB * TB
        A1 = self.A1
        if sample:
            self.load_sample_hist()
        def finish(grp):
            for (c, so, sk, sq, sqk, cbt, cbk) in grp:
                if sample:
                    self.tr(self.ps[6][:NT, (c % 4) * 128:(c % 4 + 1) * 128], cbt[:, 3:3 + NT], self.cf("ident"),
                            [cbk, "cstf"], ["ps6"])
                    if c % 4 == 3:
                        stg, stk = self.stage(c // 8)
                        self.cp("act", stg[:NT, (c % 8 - 3) * 128:(c % 8 + 1) * 128], self.ps[6][:NT, :], ["ps6"], [stk])
            qk = [g_ for g_ in grp if g_[0] < 16]
            sds = []
            for (c, so, sk, sq, sqk, cbt, cbk) in qk:
                b2 = 4 + c % 2 if sample else 4 + c % 4
                self.mm(self.ps[b2][:, :NT], self.cb("ones"), sq[:, :NT], [sqk, "cstb"], ["ps%d" % b2])
            for (c, so, sk, sq, sqk, cbt, cbk) in qk:
                b2 = 4 + c % 2 if sample else 4 + c % 4
                sd, sdk = self.tmpf()
                sds.append((sd, sdk))
                self.act(sd[:, :NT], self.ps[b2][:, :NT], AF.Ln, ["ps%d" % b2], [sdk], bias=L2_EPS)
            for (sd, sdk) in sds:
                self.act(sd[:, :NT], sd[:, :NT], AF.Exp, [sdk], [sdk], scale=-0.5)
            for (c, so, sk, sq, sqk, cbt, cbk), (sd, sdk) in zip(qk, sds):
                const = 128.0 ** -0.5 if c < 8 else 1.0
                self.stt(A1[:, c, :NT], so[:, :NT], const, sd[:, :NT], ALU.mult, ALU.mult, [sk, sdk], ["A1.%d" % c])

        pend = None
        for g in range(6):
            (wq,), wk = self.slab([("w_in", 0, D, g * 512, (g + 1) * 512)])
            for pr in range(2):
                cs_ = [g * 4 + pr * 2, g * 4 + pr * 2 + 1]
                for c in cs_:
                    jj = c % 4
                    bank = c % 4
                    for kc in range(8):
                        self.mm(self.ps[bank][:, :NT], wq[:, kc, jj * 128:(jj + 1) * 128], self.xT[:, kc, :NT],
                                [wk, "xT0", "xT1", "xT2", "xT3"], ["ps%d" % bank], start=(kc == 0), stop=(kc == 7))
                convs = []
                for c in cs_:
                    th = [self.hsq[:, c, j, :] for j in range(3)] if sample else None
                    wts = [self.wcq[:, c, j:j + 1] for j in range(4)]
                    convs.append(self.conv_chunk(c % 4, NT, th, wts, 4, self.histq[:, c, :], "histq%d" % c, sample))
                cur = []
                for c, (acc, ak, cbt, cbk) in zip(cs_, convs):
                    if c >= 16:
                        self.act(A1[:, c, :NT], acc[:, :NT], AF.Silu, [ak], ["A1.%d" % c])
                        cur.append((c, None, None, None, None, cbt, cbk))
                    else:
                        self.act(acc[:, :NT], acc[:, :NT], AF.Silu, [ak], [ak])
                        sq, sqk = self.tmpb()
                        self.act(sq[:, :NT], acc[:, :NT], AF.Square, [ak], [sqk])
                        cur.append((c, acc, ak, sq, sqk, cbt, cbk))
                if pend is not None:
                    finish(pend)
                pend = cur
        finish(pend)
        if sample:
            for k in range(3):
                stg, stk = self.stage(k)
                self.dma("aux", self.sqs[:, 2, k * 1024:(k + 1) * 1024], stg[:NSAMP, :], "so0", [stk], ["sqs2_%d" % k])
                self.outkeys.append("sqs2_%d" % k)
            self.dma("aux", self.sqs[:, 0:2, :], self.sq[:, 1:3, :], "so1", [], ["sqs01"])
            self.outkeys += ["sqs01"]
        elif last:
            for j in range(3):
                self.dma("aux", self.sqp[j, :].rearrange("(c p) -> p c", p=128), self.histq[:, :, j], "so0",
                         ["histq%d" % c for c in range(24)], ["sqp%d" % j], slow=True)
                self.outkeys.append("sqp%d" % j)
        if self.debug.get("mstop") == "A":
            return
        for nh in range(2):
            (wz,), wk = self.slab([("w_in", 0, D, Z0 + nh * 512, Z0 + (nh + 1) * 512)])
            for tb in range(NB):
                b = 4 + tb % 2
                for kc in range(8):
                    self.mm(self.ps[b][:TB, :], self.xT[:, kc, tb * TB:(tb + 1) * TB], wz[:, kc, :],
                            [wk, "xT0", "xT1", "xT2", "xT3"], ["ps%d" % b], start=(kc == 0), stop=(kc == 7))
                self.act(self.ztok[:TB, tb, nh * 512:(nh + 1) * 512], self.ps[b][:TB, :], AF.Silu, ["ps%d" % b], ["ztok%d" % tb])
        (wba,), wk = self.slab([("w_in", 0, D, BETA0, BETA0 + 16)])
        for tb in range(NB):
            for kc in range(8):
                self.mm(self.ps[6][:TB, 0:16], self.xT[:, kc, tb * TB:(tb + 1) * TB], wba[:, kc, :],
                        [wk, "xT0", "xT1", "xT2", "xT3"], ["ps6"], start=(kc == 0), stop=(kc == 7))
            self.act(self.beta[:TB, tb, :], self.ps[6][:TB, 0:8], AF.Sigmoid, ["ps6"], ["beta"])
            self.tt("dve", self.batok[:TB, tb, 8:16], self.ps[6][:TB, 8:16], self.smallb[:TB, 8:16], ALU.add,
                    ["ps6", "smallb"], ["batok"])
        for tb in range(NB):
            self.act(self.batok[:TB, tb, 0:8], self.batok[:TB, tb, 8:16], AF.Exp, ["batok"], ["batok"])
        for tb in range(NB):
            self.act(self.batok[:TB, tb, 0:8], self.batok[:TB, tb, 0:8], AF.Ln, ["batok"], ["batok"], bias=1.0)
            self.tt("dve", self.gtok[:TB, tb, :], self.batok[:TB, tb, 0:8], self.negA[:TB, :], ALU.mult,
                    ["batok", "negA"], ["gtok"])
        if self.debug.get("mstop") == "B":
            return
        if sample:
            self.gdn_sample()
        else:
            self.gdn_all(NB)
            if last:
                self.dma("aux", self.sgp.rearrange("h k v -> k h v"), self.S[:], "so1", ["S0", "S1"], ["sgp"])
                self.outkeys.append("sgp")
        if self.debug.get("mstop") == "C":
            return
        for c in range(8):
            (wB, wC, wH), wk = self.slab([("w_in", 0, D, B0 + c * 128, B0 + (c + 1) * 128),
                                          ("w_in", 0, D, C0 + c * 128, C0 + (c + 1) * 128),
                                          ("w_in", 0, D, H0 + c * 128, H0 + (c + 1) * 128)])
            bB, bC, bH = 0 + 3 * (c % 2), 1 + 3 * (c % 2), 2 + 3 * (c % 2)
            for (w_, b_) in ((wC, bC), (wH, bH), (wB, bB)):
                for kc in range(8):
                    self.mm(self.ps[b_][:, :NT], w_[:, kc, :], self.xT[:, kc, :NT], [wk, "xT0", "xT1", "xT2", "xT3"], ["ps%d" % b_],
                            start=(kc == 0), stop=(kc == 7))
            ct, ck = self.tmpf()
            self.cp("act", ct[:, :NT], self.ps[bC][:, :NT], ["ps%d" % bC], [ck])

            def src(dst, dk, ct=ct, ck=ck, bH=bH):
                self.tt("dve", dst, ct[:, :NT], self.ps[bH][:, :NT], ALU.mult, [ck, "ps%d" % bH], [dk])
            th = [self.hss[:, c, j, :] for j in range(2)] if sample else None
            wts = [self.wcs[:, c, j:j + 1] for j in range(3)]
            acc, ak, cbt, cbk = self.conv_chunk(None, NT, th, wts, 3, self.hists[:, c, :], "hists%d" % c, sample,
                                                src_is_psum=False, src=src)
            if sample:
                self.tr(self.ps[6][:NT, (c % 4) * 128:(c % 4 + 1) * 128], cbt[:, 2:2 + NT], self.cf("ident"),
                        [cbk, "cstf"], ["ps6"])
                if c % 4 == 3:
                    stg, stk = self.stage(0)
                    self.cp("act", stg[:NT, (c - 3) * 128:(c + 1) * 128], self.ps[6][:NT, :], ["ps6"], [stk])
            self.tt("dve", A1[:, c, :NT], acc[:, :NT], self.ps[bB][:, :NT], ALU.mult, [ak, "ps%d" % bB], ["A1.%d" % c])
        if sample:
            stg, stk = self.stage(0)
            self.dma("aux", self.sss[:, 1, :], stg[:NSAMP, 0:D], "so2", [stk], ["sss1"])
            self.dma("aux", self.sss[:, 0:1, :], self.ssc[:, 1:2, :], "so3", [], ["sss0"])
            self.outkeys += ["sss1", "sss0"]
        elif last:
            for j in range(2):
                self.dma("aux", self.ssp[j, :].rearrange("(c p) -> p c", p=128), self.hists[:, :, j], "so2",
                         ["hists%d" % c for c in range(8)], ["ssp%d" % j], slow=True)
                self.outkeys.append("ssp%d" % j)
        if self.debug.get("mstop") == "D":
            return
        for c in range(8):
            (wpg, wgg, wps, wgs), wk = self.slab([("w_p_gdn", 0, D, c * 128, (c + 1) * 128),
                                                  ("w_in", 0, D, GG0 + c * 128, GG0 + (c + 1) * 128),
                                                  ("w_p_sc", 0, D, c * 128, (c + 1) * 128),
                                                  ("w_in", 0, D, GS0 + c * 128, GS0 + (c + 1) * 128)])
            o = 4 * (c % 2)
            for (w_, b_, rhs_, rk) in ((wpg, o, A1[:, 16:24, :], ["A1.%d" % k for k in range(16, 24)]),
                                       (wgg, o + 1, self.xT, ["xT0", "xT1", "xT2", "xT3"]),
                                       (wps, o + 2, A1[:, 0:8, :], ["A1.%d" % k for k in range(8)]),
                                       (wgs, o + 3, self.xT, ["xT0", "xT1", "xT2", "xT3"])):
                for kc in range(8):
                    self.mm(self.ps[b_][:, :NT], w_[:, kc, :], rhs_[:, kc, :NT], [wk] + rk, ["ps%d" % b_],
                            start=(kc == 0), stop=(kc == 7))
            s1, s1k = self.tmpf()
            self.act(s1[:, :NT], self.ps[o + 1][:, :NT], AF.Sigmoid, ["ps%d" % (o + 1)], [s1k])
            self.tt("dve", s1[:, :NT], s1[:, :NT], self.ps[o][:, :NT], ALU.mult, [s1k, "ps%d" % o], [s1k])
            s2, s2k = self.tmpf()
            self.act(s2[:, :NT], self.ps[o + 3][:, :NT], AF.Sigmoid, ["ps%d" % (o + 3)], [s2k])
            self.tt("dve", s2[:, :NT], s2[:, :NT], self.ps[o + 2][:, :NT], ALU.mult, [s2k, "ps%d" % (o + 2)], [s2k])
            self.tt("dve", A1[:, 8 + c, :NT], s1[:, :NT], s2[:, :NT], ALU.add, [s1k, s2k], ["A1.%d" % (8 + c)])
        for nh in range(2):
            (wo,), wk = self.slab([("w_o", 0, D, nh * 512, (nh + 1) * 512)])
            for tb in range(NB):
                b = tb % 2
                for kc in range(8):
                    self.mm(self.ps[b][:TB, :], A1[:, 8 + kc, tb * TB:(tb + 1) * TB], wo[:, kc, :],
                            [wk, "A1.%d" % (8 + kc)], ["ps%d" % b], start=(kc == 0), stop=(kc == 7))
                xr = self.xres[:TB, tb, nh * 512:(nh + 1) * 512]
                self.stt(xr, self.ps[b][:TB, :], 1.0 / ALPHA, xr, ALU.mult, ALU.add, ["ps%d" % b, "xres%d" % tb], ["xres%d" % tb])

    def onorm_and_T(self, tb, TB):
        self.lockstep([self.onorm_gen(tb, TB)])

    def onorm_gen(self, tb, TB, bank=7):
        o3 = self.otok[:TB, :].rearrange("p (h d) -> p h d", h=H)
        t13 = self.t1[:TB, :].rearrange("p (h d) -> p h d", h=H)
        t23 = self.t2[:TB, :].rearrange("p (h d) -> p h d", h=H)
        st = self.stat3
        self.act(self.t1[:TB, :], self.otok[:TB, :], AF.Square, ["otok"], ["t1"])
        yield
        self.P.add("dve", lambda e: e.tensor_reduce(st[:TB, 0:8], t13, AX.X, ALU.add), ["t1"], ["stat3"])
        self.act(st[:TB, 0:8], st[:TB, 0:8], AF.Sqrt, ["stat3"], ["stat3"], bias=RMS_EPS, scale=1.0 / 128.0)
        yield
        self.P.add("dve", lambda e: e.reciprocal(st[:TB, 8:16], st[:TB, 0:8]), ["stat3"], ["stat3"])
        yield
        self.tt("dve", t13, o3, st[:TB, 8:16].unsqueeze(2).to_broadcast([TB, H, 128]), ALU.mult, ["otok", "stat3"], ["t1"])
        z3 = self.ztok[:TB, tb, :].rearrange("p (h d) -> p h d", h=H)
        self.tt("dve", t23, z3, self.wonb[:TB, :].unsqueeze(1).to_broadcast([TB, H, 128]), ALU.mult,
                ["ztok%d" % tb, "wonb"], ["t2"])
        yield
        self.tt("dve", self.xb16[:TB, :], self.t1[:TB, :], self.t2[:TB, :], ALU.mult, ["t1", "t2"], ["xb16"])
        yield
        for c in range(8):
            self.tr(self.psb[bank][:, c * TB:(c + 1) * TB], self.xb16[:TB, c * 128:(c + 1) * 128], self.cb("ident")[:TB, :TB],
                    ["xb16", "cstb"], ["ps%d" % bank])
        yield
        self.cp("act", self.A1[:, 16:24, tb * TB:(tb + 1) * TB],
                self.psb[bank][:, 0:8 * TB].rearrange("p (c t) -> p c t", c=8), ["ps%d" % bank],
                ["A1.%d" % k for k in range(16, 24)])

    def inv_chain(self, tb, hg, G, gp, pb):
        A1 = self.A1
        blk = slice(tb * 128, (tb + 1) * 128)
        g8 = self.gtok[:, tb, :]
        hs = [hg * 4 + hh for hh in range(4)]
        K = lambda nm: gp + nm
        rot = [0]

        def nb():
            x = pb[rot[0] % len(pb)]
            rot[0] += 1
            return x
        b0, b1, b2 = nb(), nb(), nb()
        f4 = lambda t: t.rearrange("p h d -> p (h d)")
        kq_r = ["A1.%d" % (8 + h) for h in hs] + ["A1.%d" % h for h in hs]
        for hh, h in enumerate(hs):
            self.ts("dve", G["Lg"][:, hh, :], self.cf("ltri"), g8[:, h:h + 1], ALU.mult, ["cstf", "gtok"], [K("Lg")])
        yield
        for hh, h in enumerate(hs):
            cs = slice(hh * 128, (hh + 1) * 128)
            self.mm(self.ps[b0][:, cs], self.cf("su"), G["Lg"][:, hh, :], ["cstf", K("Lg")], ["ps%d" % b0])
            self.mm(self.ps[b1][:, cs], A1[:, 8 + h, blk], A1[:, 8 + h, blk], kq_r, ["ps%d" % b1])
            self.mm(self.ps[b2][:, cs], A1[:, 8 + h, blk], A1[:, h, blk], kq_r, ["ps%d" % b2])
        yield
        self.act(f4(G["decTm"]), self.ps[b0][:, :], AF.Exp, ["ps%d" % b0], [K("decTm")])
        yield
        self.tt("dve", G["decTm"], G["decTm"], self.cf4("muincl"), ALU.mult, [K("decTm"), "cstf"], [K("decTm")])
        yield
        self.tt("dve", f4(G["qkTm"]), self.ps[b2][:, :], f4(G["decTm"]), ALU.mult, ["ps%d" % b2, K("decTm")], [K("qkTm")])
        self.tt("dve", f4(G["Lg"]), self.ps[b1][:, :], f4(G["decTm"]), ALU.mult, ["ps%d" % b1, K("decTm")], [K("Lg")])
        yield
        self.tt("dve", G["MT"], G["Lg"],
                self.beta[:, tb, hg * 4:hg * 4 + 4].unsqueeze(2).to_broadcast([128, 4, 128]), ALU.mult,
                [K("Lg"), "beta"], [K("MT")])
        yield
        bt = nb()
        for hh in range(4):
            self.tr(self.psb[bt][:, hh * 128:(hh + 1) * 128], G["MT"][:, hh, :], self.cb("ident"), [K("MT"), "cstb"], ["ps%d" % bt])
        yield
        self.cp("act", f4(G["M"]), self.psb[bt][:, 0:512], ["ps%d" % bt], [K("M")])
        yield
        Nn, Nt, N2, N2t = "Na", "Nb", "Nc", "Nd"
        Pn, Pt, Pn2, Pt2 = "Pa", "Pb", "Pc", "Pd"
        self.tt("dve", G[Nn], G["M"], self.cb4("mndn"), ALU.mult, [K("M"), "cstb"], [K(Nn)])
        self.tt("dve", G[Nt], G["MT"], self.cb4("mndtn"), ALU.mult, [K("MT"), "cstb"], [K(Nt)])
        yield
        self.tt("dve", G[Pn], G[Nn], self.cb4("ident"), ALU.add, [K(Nn), "cstb"], [K(Pn)])
        self.tt("dve", G[Pt], G[Nt], self.cb4("ident"), ALU.add, [K(Nt), "cstb"], [K(Pt)])
        nstep = int(np.log2(NBK)) - 1
        for s_ in range(nstep):
            ba, bb = nb(), nb()
            for hh in range(4):
                cs = slice(hh * 128, (hh + 1) * 128)
                self.mm(self.ps[ba][:, cs], G[Nt][:, hh, :], G[Nn][:, hh, :], [K(Nt), K(Nn)], ["ps%d" % ba])
                self.mm(self.ps[bb][:, cs], G[Nn][:, hh, :], G[Nt][:, hh, :], [K(Nt), K(Nn)], ["ps%d" % bb])
            yield
            self.cp("act", f4(G[N2]), self.ps[ba][:, :], ["ps%d" % ba], [K(N2)])
            self.cp("act", f4(G[N2t]), self.ps[bb][:, :], ["ps%d" % bb], [K(N2t)])
            yield
            bc_, bd = nb(), nb()
            for hh in range(4):
                cs = slice(hh * 128, (hh + 1) * 128)
                self.mm(self.ps[bc_][:, cs], G[N2t][:, hh, :], G[Pn][:, hh, :], [K(N2t), K(Pn)], ["ps%d" % bc_])
                self.mm(self.ps[bd][:, cs], G[N2][:, hh, :], G[Pt][:, hh, :], [K(N2), K(Pt)], ["ps%d" % bd])
            yield
            self.tt("dve", f4(G[Pn2]), f4(G[Pn]), self.ps[bc_][:, :], ALU.add, [K(Pn), "ps%d" % bc_], [K(Pn2)])
            self.tt("dve", f4(G[Pt2]), f4(G[Pt]), self.ps[bd][:, :], ALU.add, [K(Pt), "ps%d" % bd], [K(Pt2)])
            yield
            Nn, Nt, N2, N2t = N2, N2t, Nn, Nt
            Pn, Pt, Pn2, Pt2 = Pn2, Pt2, Pn, Pt
        T, U, T2, U2 = Pn, Pt, Pn2, Pt2
        E_, F_, X_, Y_ = Nn, Nt, N2, N2t
        b = NBK
        while b < 128:
            lastlvl = (b == 64)
            self.tt("dve", G[E_], G["M"], self.cb4("me%d" % b), ALU.mult, [K("M"), "cstb"], [K(E_)])
            if not lastlvl:
                self.tt("dve", G[F_], G["MT"], self.cb4("me%dt" % b), ALU.mult, [K("MT"), "cstb"], [K(F_)])
            yield
            ba, bb = nb(), nb()
            for hh in range(4):
                cs = slice(hh * 128, (hh + 1) * 128)
                self.mm(self.ps[ba][:, cs], G[E_][:, hh, :], G[U][:, hh, :], [K(E_), K(U)], ["ps%d" % ba])
                if not lastlvl:
                    self.mm(self.ps[bb][:, cs], G[F_][:, hh, :], G[T][:, hh, :], [K(F_), K(T)], ["ps%d" % bb])
            yield
            self.cp("act", f4(G[Y_]), self.ps[ba][:, :], ["ps%d" % ba], [K(Y_)])
            if not lastlvl:
                self.cp("act", f4(G[X_]), self.ps[bb][:, :], ["ps%d" % bb], [K(X_)])
            yield
            bc_, bd = nb(), nb()
            for hh in range(4):
                cs = slice(hh * 128, (hh + 1) * 128)
                self.mm(self.ps[bc_][:, cs], G[T][:, hh, :], G[Y_][:, hh, :], [K(T), K(Y_)], ["ps%d" % bc_])
                if not lastlvl:
                    self.mm(self.ps[bd][:, cs], G[U][:, hh, :], G[X_][:, hh, :], [K(U), K(X_)], ["ps%d" % bd])
            yield
            self.tt("dve", f4(G[U2]), f4(G[U]), self.ps[bc_][:, :], ALU.subtract, [K(U), "ps%d" % bc_], [K(U2)])
            if not lastlvl:
                self.tt("dve", f4(G[T2]), f4(G[T]), self.ps[bd][:, :], ALU.subtract, [K(T), "ps%d" % bd], [K(T2)])
            yield
            T, U, T2, U2 = T2, U2, T, U
            b *= 2
        self.cp("dve", self.Uk[hg][:], G[U], [K(U)], ["Uk%d" % hg])
        self.cp("dve", self.Qk[hg][:], G["qkTm"], [K("qkTm")], ["Qk%d" % hg])

    def scan_chain(self, tb, hg, bx, by):
        A1 = self.A1
        blk = slice(tb * 128, (tb + 1) * 128)
        pb_ = tb % 2
        sm, smk = self.gsm2[pb_], "gsm%d" % pb_
        vtok, vtk = (self.vtok, self.vtok2)[pb_], "vtok%d" % pb_
        kdec, kdk = self.kdec2[pb_], "kdec%d" % pb_
        hs = [hg * 4 + hh for hh in range(4)]
        hsl = slice(hg * 4, hg * 4 + 4)
        X, Y = self.ps[bx], self.ps[by]
        xk, yk = "ps%d" % bx, "ps%d" % by
        X3 = X[:, :].rearrange("p (h d) -> p h d", h=4)
        Y3 = Y[:, :].rearrange("p (h d) -> p h d", h=4)
        bc = lambda ap: ap.unsqueeze(2).to_broadcast([128, 4, 128])
        Sk, Sbk = "S%d" % hg, "Sbf%d" % hg
        for hh, h in enumerate(hs):
            cs = slice(hh * 128, (hh + 1) * 128)
            self.mm(X[:, cs], A1[:, 8 + h, blk], self.Sbf[:, h, :], ["A1.%d" % (8 + h), Sbk], [xk])
            self.mm(Y[:, cs], A1[:, h, blk], self.Sbf[:, h, :], ["A1.%d" % h, Sbk], [yk])
        yield
        tS, tSk = self.tmpf()
        tS3 = tS[:, 0:512].rearrange("p (h d) -> p h d", h=4)
        self.tt("dve", tS3, X3, bc(sm[:, 24 + hg * 4:28 + hg * 4]), ALU.mult, [xk, smk], [tSk])
        o1, o1k = self.tmpf()
        o13 = o1[:, 0:512].rearrange("p (h d) -> p h d", h=4)
        self.tt("dve", o13, Y3, bc(sm[:, 16 + hg * 4:20 + hg * 4]), ALU.mult, [yk, smk], [o1k])
        yield
        r, rk = self.tmpb()
        r3 = r[:, :].rearrange("p (h d) -> p h d", h=4)
        self.tt("dve", r3, tS3, vtok[:, hsl, :], ALU.add, [tSk, vtk], [rk])
        yield
        for hh in range(4):
            cs = slice(hh * 128, (hh + 1) * 128)
            self.mm(X[:, cs], self.Uk[hg][:, hh, :], r3[:, hh, :], ["Uk%d" % hg, rk], [xk])
        yield
        vn, vk = self.tmpb()
        vn3 = vn[:, :].rearrange("p (h d) -> p h d", h=4)
        self.tt("dve", vn3, X3, bc(self.beta[:, tb, hsl]), ALU.mult, [xk, "beta"], [vk])
        yield
        for hh, h in enumerate(hs):
            cs = slice(hh * 128, (hh + 1) * 128)
            self.mm(Y[:, cs], self.Qk[hg][:, hh, :], vn3[:, hh, :], ["Qk%d" % hg, vk], [yk])
            self.mm(X[:, cs], kdec[:, h, :], vn3[:, hh, :], [kdk, vk], [xk])
        yield
        self.tt("dve", self.otok[:, hg * 512:(hg + 1) * 512], o1[:, 0:512], Y[:, :], ALU.add, [o1k, yk], ["otok"])
        self.tt("dve", self.S[:, hsl, :], self.S[:, hsl, :], bc(sm[:, 40 + hg * 4:44 + hg * 4]), ALU.mult, [Sk, smk], [Sk])
        yield
        self.tt("dve", self.S[:, hsl, :], self.S[:, hsl, :], X3, ALU.add, [Sk, xk], [Sk])
        yield
        self.cp("act", self.Sbf[:, hsl, :], self.S[:, hsl, :], [Sk], [Sbk])

    def lockstep(self, gens):
        gens = list(gens)
        while gens:
            nxt = []
            for g in gens:
                try:
                    next(g)
                    nxt.append(g)
                except StopIteration:
                    pass
            gens = nxt

    def gdn_prep(self, tb):
        A1 = self.A1
        blk = slice(tb * 128, (tb + 1) * 128)
        pb_ = tb % 2
        sm, smk = self.gsm2[pb_], "gsm%d" % pb_
        vtok, vtk = (self.vtok, self.vtok2)[pb_], "vtok%d" % pb_
        kdec, kdk = self.kdec2[pb_], "kdec%d" % pb_
        g8 = self.gtok[:, tb, :]
        self.mm(self.ps[7][:, 0:8], self.cf("ltri"), g8, ["cstf", "gtok"], ["ps7"])
        self.mm(self.ps[7][:, 8:16], self.cf("ones"), g8, ["cstf", "gtok"], ["ps7"])
        self.cp("dve", sm[:, 0:16], self.ps[7][:, 0:16], ["ps7"], [smk])
        yield
        self.act(sm[:, 16:24], sm[:, 0:8], AF.Exp, [smk], [smk])
        self.tt("dve", sm[:, 32:40], sm[:, 8:16], sm[:, 0:8], ALU.subtract, [smk], [smk])
        yield
        self.ts("dve", sm[:, 24:32], sm[:, 16:24], -1.0, ALU.mult, [smk], [smk])
        self.act(sm[:, 32:40], sm[:, 32:40], AF.Exp, [smk], [smk])
        self.act(sm[:, 40:48], sm[:, 8:16], AF.Exp, [smk], [smk])
        for (dst, dk, u0, bank) in ((kdec, kdk, 8, 6), (vtok, vtk, 16, 7)):
            for h in range(H):
                self.tr(self.psb[bank][:, h * 128:(h + 1) * 128], A1[:, u0 + h, blk], self.cb("ident"),
                        ["A1.%d" % (u0 + h), "cstb"], ["ps%d" % bank])
        yield
        self.cp("act", vtok[:].rearrange("p h d -> p (h d)"), self.psb[7][:, 0:1024], ["ps7"], [vtk])
        self.tt("dve", kdec[:], self.psb[6][:, 0:1024].rearrange("p (h d) -> p h d", h=H),
                sm[:, 32:40].unsqueeze(2).to_broadcast([128, H, 128]), ALU.mult, ["ps6", smk], [kdk])

    def gdn_prep_old(self, tb):
        A1 = self.A1
        blk = slice(tb * 128, (tb + 1) * 128)
        pb_ = tb % 2
        sm, smk = self.gsm2[pb_], "gsm%d" % pb_
        vtok, vtk = (self.vtok, self.vtok2)[pb_], "vtok%d" % pb_
        kdec, kdk = self.kdec2[pb_], "kdec%d" % pb_
        g8 = self.gtok[:, tb, :]
        for (dst, dk, u0, bank) in ((self.ktok, "ktok", 8, 5), (vtok, vtk, 16, 6)):
            for h in range(H):
                self.tr(self.psb[bank][:, h * 128:(h + 1) * 128], A1[:, u0 + h, blk], self.cb("ident"),
                        ["A1.%d" % (u0 + h), "cstb"], ["ps%d" % bank])
            self.cp("act", dst[:].rearrange("p h d -> p (h d)"), self.psb[bank][:, 0:1024], ["ps%d" % bank], [dk])
        self.mm(self.ps[7][:, 0:8], self.cf("ltri"), g8, ["cstf", "gtok"], ["ps7"])
        self.mm(self.ps[7][:, 8:16], self.cf("ones"), g8, ["cstf", "gtok"], ["ps7"])
        self.cp("dve", sm[:, 0:16], self.ps[7][:, 0:16], ["ps7"], [smk])
        self.act(sm[:, 16:24], sm[:, 0:8], AF.Exp, [smk], [smk])
        self.ts("dve", sm[:, 24:32], sm[:, 16:24], -1.0, ALU.mult, [smk], [smk])
        self.tt("dve", sm[:, 32:40], sm[:, 8:16], sm[:, 0:8], ALU.subtract, [smk], [smk])
        self.act(sm[:, 32:40], sm[:, 32:40], AF.Exp, [smk], [smk])
        self.act(sm[:, 40:48], sm[:, 8:16], AF.Exp, [smk], [smk])
        self.tt("dve", kdec[:], self.ktok[:], sm[:, 32:40].unsqueeze(2).to_broadcast([128, H, 128]), ALU.mult,
                ["ktok", smk], [kdk])

    def seq_chain(self, tb, NB):
        for hg in range(2):
            for _ in self.scan_chain(tb, hg, 6, 7):
                yield
            yield
        for _ in self.onorm_gen(tb, 128, bank=6):
            yield
        if tb + 1 < NB:
            yield
            for _ in self.gdn_prep(tb + 1):
                yield

    def gdn_all(self, NB):
        if self.debug.get("mstop") == "Y":
            nbk_ = 4 if self.debug.get("gstop") == "b4" else 3
            inv = lambda tb: [self.inv_chain(tb, hg, self.gqs[hg][0], self.gqs[hg][1], [nbk_ * hg + i for i in range(nbk_)])
                              for hg in range(2)]
            for tb in range(NB):
                if self.debug.get("gstop") == "oldprep":
                    self.gdn_prep_old(tb)
                else:
                    self.lockstep([self.gdn_prep(tb)])
                self.lockstep(inv(tb))
                self.lockstep([self.scan_chain(tb, 0, 6, 7)])
                self.lockstep([self.scan_chain(tb, 1, 6, 7)])
                self.lockstep([self.onorm_gen(tb, 128, bank=6)])
            return
        self.lockstep([self.gdn_prep(0)])
        inv = lambda tb: [self.inv_chain(tb, hg, self.gqs[hg][0], self.gqs[hg][1], [3 * hg + i for i in range(3)])
                          for hg in range(2)]
        self.lockstep(inv(0))
        for tb in range(NB):
            gens = [self.seq_chain(tb, NB)]
            if tb + 1 < NB:
                gens = inv(tb + 1) + gens
            self.lockstep(gens)

    def stage(self, k):
        return [(self.t1, "t1"), (self.t2, "t2"), (self.otok, "otok")][k]

    def load_sample_hist(self):
        for (srcd, dst, nch, nj) in ((self.sq, self.hsq, 24, 3), (self.ssc, self.hss, 8, 2)):
            for j in range(nj):
                for k in range(nch // 8):
                    t, tk = self.stage(k)
                    self.dma("aux", t[:NSAMP, :], srcd[:, j, k * 1024:(k + 1) * 1024], "hl", [], [tk])
                    for cc in range(8):
                        self.tr(self.ps[6][:, cc * NSAMP:(cc + 1) * NSAMP], t[:NSAMP, cc * 128:(cc + 1) * 128],
                                self.cf("ident")[:NSAMP, :NSAMP], [tk, "cstf"], ["ps6"])
                    self.cp("dve", dst[:, k * 8:(k + 1) * 8, j, :],
                            self.ps[6][:, 0:8 * NSAMP].rearrange("p (c b) -> p c b", c=8), ["ps6"], ["hsamp"])

    def gdn_sample(self):
        A1 = self.A1
        NS = NSAMP
        sm = self.gsm
        st = self.stat
        for (dst, dk, u0, bank) in ((self.qtok[:NS, :], "qtok", 0, 4), (self.ktok[:NS].rearrange("p h d -> p (h d)"), "ktok", 8, 5),
                                    (self.vtok[:NS].rearrange("p h d -> p (h d)"), "vtok", 16, 6)):
            for h in range(H):
                self.tr(self.psb[bank][:NS, h * 128:(h + 1) * 128], A1[:, u0 + h, 0:NS], self.cb("ident"),
                        ["A1.%d" % (u0 + h), "cstb"], ["ps%d" % bank])
            self.cp("act", dst, self.psb[bank][:NS, 0:1024], ["ps%d" % bank], [dk])
        a = sm[:NS, 0:8]
        self.act(a, self.gtok[:NS, 0, :], AF.Exp, ["gtok"], ["gsm"])
        q3 = self.qtok[:NS, :].rearrange("p (h d) -> p h d", h=H)
        t13 = self.t1[:NS, :].rearrange("p (h d) -> p h d", h=H)
        t23 = self.t2[:NS, :].rearrange("p (h d) -> p h d", h=H)
        o3 = self.otok[:NS, :].rearrange("p (h d) -> p h d", h=H)
        self.tt("dve", t13, q3, self.ktok[:NS], ALU.mult, ["qtok", "ktok"], ["t1"])
        self.P.add("dve", lambda e: e.tensor_reduce(sm[:NS, 8:16], t13, AX.X, ALU.add), ["t1"], ["gsm"])
        i16 = self.i16b[:].rearrange("p (a b) -> p a b", a=NS)
        for h in range(H):
            self.tt("dve", self.kTm[:, h, :, :], A1[:, 8 + h:9 + h, 0:NS].to_broadcast([128, NS, NS]), i16, ALU.mult,
                    ["A1.%d" % (8 + h), "i16b"], ["kTm"])
            self.tt("dve", self.qTm[:, h, :, :], A1[:, h:h + 1, 0:NS].to_broadcast([128, NS, NS]), i16, ALU.mult,
                    ["A1.%d" % h, "i16b"], ["qTm"])
        for b in range(NS):
            i3, i2 = b % 3, b % 2
            self.dma("aux", self.Sin[i3], self.sg[b].rearrange("h k v -> k h v"), "sin%d" % i3, [], ["Sin%d" % i3])
            self.cp("act" if b % 2 == 0 else "dve", self.Sinb[i2], self.Sin[i3], ["Sin%d" % i3], ["Sinb%d" % i2])
            for h in range(H):
                bk, bq = h // 4, 2 + h // 4
                cs = slice((h % 4) * 128, (h % 4 + 1) * 128)
                first = (b == 0 and h % 4 == 0)
                self.mm(self.ps[bk][:NS, cs], self.kTm[:, h, b, :], self.Sinb[i2][:, h, :], ["kTm", "Sinb%d" % i2],
                        ["ps%d" % bk], start=first, stop=(b == NS - 1))
                self.mm(self.ps[bq][:NS, cs], self.qTm[:, h, b, :], self.Sinb[i2][:, h, :], ["qTm", "Sinb%d" % i2],
                        ["ps%d" % bq], start=first, stop=(b == NS - 1))
        a_b = a.unsqueeze(2).to_broadcast([NS, H, 128])
        for half in range(2):
            hsl = slice(half * 4, half * 4 + 4)
            k3 = self.ps[half][:NS, :].rearrange("p (h d) -> p h d", h=4)
            qs3 = self.ps[2 + half][:NS, :].rearrange("p (h d) -> p h d", h=4)
            ab = a[:, hsl].unsqueeze(2).to_broadcast([NS, 4, 128])
            self.tt("dve", t13[:, hsl, :], k3, ab, ALU.mult, ["ps%d" % half, "gsm"], ["t1"])
            self.tt("dve", t13[:, hsl, :], self.vtok[:NS, hsl, :], t13[:, hsl, :], ALU.subtract, ["vtok", "t1"], ["t1"])
            self.tt("dve", t13[:, hsl, :], t13[:, hsl, :],
                    self.beta[:NS, 0, hsl].unsqueeze(2).to_broadcast([NS, 4, 128]), ALU.mult, ["t1", "beta"], ["t1"])
            self.tt("dve", t23[:, hsl, :], qs3, ab, ALU.mult, ["ps%d" % (2 + half), "gsm"], ["t2"])
            self.tt("dve", o3[:, hsl, :], t13[:, hsl, :], sm[:NS, 8 + half * 4:12 + half * 4].unsqueeze(2).to_broadcast([NS, 4, 128]),
                    ALU.mult, ["t1", "gsm"], ["otok"])
            self.tt("dve", o3[:, hsl, :], o3[:, hsl, :], t23[:, hsl, :], ALU.add, ["otok", "t2"], ["otok"])
        dbf = self.xb16
        self.cp("act", dbf[:NS, :], self.t1[:NS, :], ["t1"], ["xb16"])
        ad = self.t2[:NS, 0:128].rearrange("p (b h) -> p b h", b=NS)
        idr = self.cf("ident")[:NS, 0:NS].unsqueeze(2).to_broadcast([NS, NS, H])
        self.tt("dve", ad, a.unsqueeze(1).to_broadcast([NS, NS, H]), idr, ALU.mult, ["gsm", "cstf"], ["t2"])
        self.mm(self.ps[4][:, 0:128], self.cf("ones")[:NS, :], self.t2[:NS, 0:128], ["cstf", "t2"], ["ps4"])
        self.cp("dve", self.abc[:], self.ps[4][:, 0:128], ["ps4"], ["abc"])
        kflat = self.ktok[:NS].rearrange("p h d -> p (h d)")
        for b in range(NS):
            i2 = b % 2
            i3 = (b + 1) % 3
            self.dma("aux", self.Sin[i3], self.sg[b].rearrange("h k v -> k h v"), "sin%d" % i3, [], ["Sin%d" % i3])
            self.ts("dve", self.kmask[i2][:NS, :], kflat, self.cf("ident")[:NS, b:b + 1], ALU.mult,
                    ["ktok", "cstf"], ["kmask%d" % i2])
            for h in range(H):
                pb_ = 5 + h // 4
                cs = slice((h % 4) * 128, (h % 4 + 1) * 128)
                self.mm(self.ps[pb_][:, cs], self.kmask[i2][:NS, h * 128:(h + 1) * 128], dbf[:NS, h * 128:(h + 1) * 128],
                        ["kmask%d" % i2, "xb16"], ["ps%d" % pb_])
                self.stt(self.Sin[i3][:, h, :], self.Sin[i3][:, h, :], self.abc[:, b * 8 + h:b * 8 + h + 1],
                         self.ps[pb_][:, cs], ALU.mult, ALU.add, ["Sin%d" % i3, "abc", "ps%d" % pb_], ["Sin%d" % i3])
            self.dma("aux", self.sgs[b].rearrange("h k v -> k h v"), self.Sin[i3], "sout%d" % i3, ["Sin%d" % i3], ["sgs%d" % b])
            self.outkeys.append("sgs%d" % b)
        self.onorm_and_T(0, NS)


_CACHE = {}


WBIG_LEN = 2 * (3 * D * HID) + D * IN_W + 4 * D * D + 256 * D


def pack_wbig(weights, specs, offs, tot):
    out = np.empty((tot,), np.float32)
    for spec, (off, n) in zip(specs, offs):
        parts = []
        for (name, r0, r1, c0, c1) in spec:
            w = weights[name][r0:r1, c0:c1]
            kc = (r1 - r0) // 128
            parts.append(w.reshape(kc, 128, c1 - c0).transpose(1, 0, 2).reshape(128, kc * (c1 - c0)))
        out[off:off + 128 * n] = np.concatenate(parts, axis=1).reshape(-1)
    return out


def kernel(x_prompt, x_sample, p_prompt, p_sample, state_gdn, state_qkv_conv, state_sc_conv,
           ffn1_w_gate, ffn1_w_up, ffn1_w_down, ln1_g, ln1_b,
           w_in, w_conv_qkv, A_log, dt_bias, w_onorm, w_p_gdn, w_conv_sc, w_p_sc, w_o, ln2_g, ln2_b,
           ffn2_w_gate, ffn2_w_up, ffn2_w_down, ln3_g, ln3_b,
           w_ple_gate, w_ple_proj, ln4_g, ln4_b, _debug=None):
    f = lambda a: np.ascontiguousarray(np.asarray(a, dtype=np.float32))
    weights = {"ffn1_w_gate": f(ffn1_w_gate)[0], "ffn1_w_up": f(ffn1_w_up)[0], "ffn1_w_down": f(ffn1_w_down)[0],
               "w_in": f(w_in)[0], "w_p_gdn": f(w_p_gdn)[0], "w_p_sc": f(w_p_sc)[0], "w_o": f(w_o)[0],
               "ffn2_w_gate": f(ffn2_w_gate)[0], "ffn2_w_up": f(ffn2_w_up)[0], "ffn2_w_down": f(ffn2_w_down)[0],
               "w_ple_gate": f(w_ple_gate)[0], "w_ple_proj": f(w_ple_proj)[0]}
    bld = Builder(debug=_debug)
    bld.wbig_len = WBIG_LEN
    nc = bld.build()
    assert bld.slab_tot == WBIG_LEN or _debug, (bld.slab_tot, WBIG_LEN)
    assert bld.slab_tot <= WBIG_LEN
    wbig = np.zeros((WBIG_LEN,), np.float32)
    wbig[:bld.slab_tot] = pack_wbig(weights, bld.slab_specs, bld.slab_off, bld.slab_tot)
    lnp = np.stack([f(ln1_g)[0], f(ln1_b)[0], f(ln2_g)[0], f(ln2_b)[0], f(ln3_g)[0], f(ln3_b)[0], f(ln4_g)[0], f(ln4_b)[0]])
    wcq = np.ascontiguousarray(f(w_conv_qkv)[0].reshape(4, 24, 128).transpose(2, 1, 0).reshape(128, 96))
    wcs = np.ascontiguousarray(f(w_conv_sc)[0].reshape(3, 8, 128).transpose(2, 1, 0).reshape(128, 24))
    smallp = np.stack([f(A_log)[0], f(dt_bias)[0]])
    cst, cst2 = make_consts()
    cst2 = np.ascontiguousarray(cst2.reshape(128, -1))
    i16 = np.ascontiguousarray(np.broadcast_to(np.eye(16, dtype=np.float32).reshape(1, 256), (128, 256)))
    xp = f(x_prompt)
    xsm = f(x_sample)[:, 0, :]
    ppr = f(p_prompt)[0]
    psm = f(p_sample)[0, :, 0, :]
    sg = f(state_gdn)[0]
    sq = f(state_qkv_conv)[0]
    ssc = f(state_sc_conv)[0]
    in_maps = []
    for c in range(8):
        sl = slice(c * NSAMP, (c + 1) * NSAMP)
        in_maps.append({"x": xp[c], "pp": ppr[c], "xs": xsm[sl], "psm": psm[sl], "sg": sg[sl], "sq": sq[sl], "ssc": ssc[sl],
                        "wbig": wbig, "lnp": lnp, "wcq": wcq, "wcs": wcs, "smallp": smallp, "won": f(w_onorm)[0],
                        "cst": cst, "cst2": cst2, "i16": i16})
    ncores = (_debug or {}).get("ncores", 8)
    res = run_bass_kernel_spmd(nc, in_maps[:ncores], core_ids=list(range(ncores)))
    R = list(res.results)
    while len(R) < 8:
        R.append({k: np.zeros_like(v) for k, v in R[0].items()})
    y = np.stack([R[c]["y"] for c in range(8)])
    ys = np.concatenate([R[c]["ys"] for c in range(8)])[:, None, :]
    sgp = np.stack([R[c]["sgp"] for c in range(8)])[None]
    sqp = np.stack([R[c]["sqp"] for c in range(8)])[None]
    ssp = np.stack([R[c]["ssp"] for c in range(8)])[None]
    sgs = np.concatenate([R[c]["sgs"] for c in range(8)])[None]
    sqs = np.concatenate([R[c]["sqs"] for c in range(8)])[None]
    sss = np.concatenate([R[c]["sss"] for c in range(8)])[None]
    return (y.astype(np.float32), ys.astype(np.float32), sgp.astype(np.float32), sqp.astype(np.float32),
            ssp.astype(np.float32), sgs.astype(np.float32), sqs.astype(np.float32), sss.astype(np.float32))
```

```python
import contextlib
import numpy as np
import concourse.bass as bass
import concourse.mybir as mybir
from concourse.bass_utils import run_bass_kernel_spmd

F32 = mybir.dt.float32
BF16 = mybir.dt.bfloat16
AF = mybir.ActivationFunctionType
ALU = mybir.AluOpType
AX = mybir.AxisListType

D = 1024
SEQ = 2048
NSAMP = 16
HID = 2816
NJ = HID // 128
H = 8
QKV_W = 3072
IN_W = 9232
Z0, BETA0, A0, B0, C0, H0, GG0, GS0 = 3072, 4096, 4104, 4112, 5136, 6160, 7184, 8208
ALPHA = 2.0 ** 0.25
LN_EPS = 1e-5
RMS_EPS = 1e-6
L2_EPS = 1e-6
NTP = 512
SLOT = 4096
NSLOT = 4
NBK = 16

COMPUTE = ("pe", "act", "dve", "pool")


class _Op:
    __slots__ = ("eng", "fn", "r", "w", "key", "eidx", "kn", "waits", "done", "inc")

    def __init__(self, eng, fn, r, w, key):
        self.eng = eng
        self.fn = fn
        self.r = r
        self.w = w
        self.key = key
        self.eidx = -1
        self.kn = 0
        self.waits = []
        self.done = None
        self.inc = False


class Prog:
    def __init__(self, nc):
        self.nc = nc
        self.ops = []

    def add(self, eng, fn, r=(), w=(), key=None):
        self.ops.append(_Op(eng, fn, tuple(r), tuple(w), key))

    def finalize(self):
        last_w = {}
        readers = {}
        issue = {e: {} for e in ("pe", "act", "dve", "pool", "sp")}
        ecount = {e: 0 for e in issue}
        kcount = {}
        kops = {}
        eops = {e: [] for e in issue}
        for op in self.ops:
            e = op.eng
            deps = set()
            for res in op.r:
                lw = last_w.get(res)
                if lw is not None:
                    deps.add(lw)
            for res in op.w:
                lw = last_w.get(res)
                if lw is not None:
                    deps.add(lw)
                for rd in readers.get(res, ()):
                    deps.add(rd)
            deps.discard(op)
            clock = issue[e]
            if op.key is None:
                op.eidx = ecount[e]
                ecount[e] += 1
                eops[e].append(op)
            else:
                n = kcount.get(op.key, 0) + 1
                kcount[op.key] = n
                op.kn = n
                kops.setdefault(op.key, []).append(op)
                if n > 1:
                    deps.add(kops[op.key][n - 2])
            best = {}
            dma_deps = []
            for d in deps:
                if d.key is None:
                    b = best.get(d.eng)
                    if b is None or d.eidx > b.eidx:
                        best[d.eng] = d
                else:
                    dma_deps.append(d)
            newclock = None
            for f, d in best.items():
                if clock.get(f, -1) >= d.eidx:
                    continue
                if f == e and op.key is None:
                    if e == "pe":
                        continue
                    if e != "pool" and (op.eidx - d.eidx) > 12:
                        continue
                op.waits.append(("c", f, d.eidx))
                d.inc = True
                if newclock is None:
                    newclock = dict(clock)
                for k, v in d.done.items():
                    if newclock.get(k, -1) < v:
                        newclock[k] = v
            for d in dma_deps:
                kk = ("dma", d.key)
                cur = clock if newclock is None else newclock
                if cur.get(kk, 0) >= d.kn:
                    continue
                op.waits.append(("d", d.key, d.kn))
                if newclock is None:
                    newclock = dict(clock)
                for k, v in d.done.items():
                    if newclock.get(k, -1) < v:
                        newclock[k] = v
            if newclock is not None:
                issue[e] = newclock
                clock = newclock
            done = dict(clock)
            if op.key is None:
                done[e] = op.eidx
            else:
                done[("dma", op.key)] = op.kn
            op.done = done
            for res in op.r:
                readers.setdefault(res, []).append(op)
            for res in op.w:
                last_w[res] = op
                readers[res] = []
        self.rank = {}
        for e, lst in eops.items():
            k = 0
            for op in lst:
                if op.inc:
                    k += 1
                    self.rank[(e, op.eidx)] = k
        self.keys = list(kcount.keys())
        for op in self.ops:
            op.done = None
            if len(op.waits) > 1:
                m = {}
                for t, a, b in op.waits:
                    if (t, a) not in m or m[(t, a)] < b:
                        m[(t, a)] = b
                op.waits = [(t, a, b) for (t, a), b in m.items()]

    def emit(self, es):
        nc = self.nc
        sems = {}
        for e in COMPUTE:
            sems[e] = es.enter_context(nc.semaphore("s_" + e))
        ksem = {}
        for k in self.keys:
            ksem[k] = es.enter_context(nc.semaphore("k_" + str(k)))
        block = es.enter_context(nc.Block())
        rank = self.rank

        def run(ename, eng):
            for op in self.ops:
                if op.eng != ename:
                    continue
                for t, a, b in op.waits:
                    if t == "c":
                        eng.wait_ge(sems[a], rank[(a, b)])
                    else:
                        eng.wait_ge(ksem[a], 16 * b)
                if op.fn is None:
                    continue
                ins = op.fn(eng)
                if op.key is not None:
                    ins.then_inc(ksem[op.key], 16)
                elif op.inc:
                    ins.then_inc(sems[ename], 1)

        @block.tensor
        def _(eng):
            run("pe", eng)

        @block.scalar
        def _(eng):
            run("act", eng)

        @block.vector
        def _(eng):
            run("dve", eng)

        @block.gpsimd
        def _(eng):
            run("pool", eng)

        @block.sync
        def _(eng):
            run("sp", eng)


CSTF_NAMES = ["ident", "ltri", "su", "muincl", "ones"]
CSTB_NAMES = ["ident", "ones", "mndn", "mndtn", "me16", "me16t", "me32", "me32t", "me64", "me64t"]


def make_consts():
    i = np.arange(128)[:, None]
    j = np.arange(128)[None, :]
    c = {}
    c["ident"] = (i == j)
    c["ltri"] = (i <= j)
    c["su"] = (i > j)
    c["muincl"] = (j >= i)
    c["ones"] = np.ones((128, 128), bool)
    nd = (i // NBK == j // NBK) & (i > j)
    c["mndn"] = -1.0 * nd
    c["mndtn"] = -1.0 * nd.T
    for b in (16, 32, 64):
        e = (i // (2 * b) == j // (2 * b)) & ((i % (2 * b)) >= b) & ((j % (2 * b)) < b)
        c["me%d" % b] = e
        c["me%dt" % b] = e.T
    arrf = np.stack([np.asarray(c[n], np.float32) for n in CSTF_NAMES], axis=1)
    arrb = np.stack([np.asarray(c[n], np.float32) for n in CSTB_NAMES], axis=1)
    return np.ascontiguousarray(arrf), np.ascontiguousarray(arrb)


class Builder:
    def __init__(self, debug=None):
        self.debug = debug or {}
        self.slab_specs = []
        self.slab_off = []
        self.slab_tot = 0
        self.nslab_pass = None

    def mm(self, out, lhsT, rhs, r, w, start=True, stop=True):
        self.P.add("pe", lambda e: e.matmul(out, lhsT, rhs, start=start, stop=stop), r, w)

    def tr(self, out, in_, ident, r, w):
        self.P.add("pe", lambda e: e.transpose(out, in_, ident), r, w)

    def act(self, out, in_, func, r, w, bias=None, scale=None):
        kw = {}
        if bias is not None:
            kw["bias"] = bias
        if scale is not None:
            kw["scale"] = scale
        self.P.add("act", lambda e: e.activation(out, in_, func, **kw), r, w)

    def tt(self, eng, out, in0, in1, op, r, w):
        self.P.add(eng, lambda e: e.tensor_tensor(out, in0, in1, op), r, w)

    def ts(self, eng, out, in0, s1, op0, r, w, s2=None, op1=None):
        if op1 is None:
            self.P.add(eng, lambda e: e.tensor_scalar(out, in0, s1, None, op0), r, w)
        else:
            self.P.add(eng, lambda e: e.tensor_scalar(out, in0, s1, s2, op0, op1), r, w)

    def stt(self, out, in0, scalar, in1, op0, op1, r, w):
        self.P.add("dve", lambda e: e.scalar_tensor_tensor(out, in0, scalar, in1, op0, op1), r, w)

    def cp(self, eng, out, in_, r, w):
        if eng == "act":
            self.P.add("act", lambda e: e.activation(out, in_, AF.Copy), r, w)
        else:
            self.P.add(eng, lambda e: e.tensor_copy(out, in_), r, w)

    def dq(self):
        return "sp" if self.recording else "pool"

    def dma(self, eng, out, in_, key, r, w, slow=False):
        if eng == "aux":
            eng = self.dq()
        if slow:
            self.P.add(eng, lambda e: e.dma_start(out=out, in_=in_, allow_slow_non_contiguous=True), r, w, key=key)
        else:
            self.P.add(eng, lambda e: e.dma_start(out=out, in_=in_), r, w, key=key)

    def slab(self, spec):
        if self.recording:
            self.slab_specs.append(spec)
            n = sum(((r1 - r0) // 128) * (c1 - c0) for (_, r0, r1, c0, c1) in spec)
            assert n <= SLOT, n
            self.slab_off.append((self.slab_tot, n))
            self.slab_tot += 128 * n
        si = self.slab_i % self.nslab_pass if self.nslab_pass else self.slab_i
        off, n = self.slab_off[si]
        slot = self.slab_i % NSLOT
        self.slab_i += 1
        t = self.wring[slot]
        key = "w%d" % slot
        scr = self.wscr[off:off + 128 * n].rearrange("(p n) -> p n", p=128)
        if self.recording:
            src = self.wbig[off:off + 128 * n].rearrange("(p n) -> p n", p=128)
            self.dma("pool", t[:, 0:n], src, key, r=[], w=[key])
            if self.debug.get("npass", 4) > 1 or not self.debug.get("nosample", False):
                self.dma("sp", scr, t[:, 0:n], "wb%d" % slot, r=[key], w=["wscr%d" % si])
        else:
            self.dma("sp", t[:, 0:n], scr, key, r=["wscr%d" % si], w=[key])
        views = []
        o = 0
        for (_, r0, r1, c0, c1) in spec:
            kc = (r1 - r0) // 128
            nc_ = c1 - c0
            views.append(t[:, o:o + kc * nc_].rearrange("p (k n) -> p k n", k=kc))
            o += kc * nc_
        return views, key

    def build(self):
        nc = bass.Bass("TRN2", target_bir_lowering=False)
        self.nc = nc
        self.es = contextlib.ExitStack()
        with self.es:
            self._build_inner()
        return nc

    def dram_in(self, name, shape, dt=F32):
        return self.nc.dram_tensor(name, list(shape), dt, kind="ExternalInput").ap()

    def dram_out(self, name, shape, dt=F32):
        return self.nc.dram_tensor(name, list(shape), dt, kind="ExternalOutput").ap()

    def sb(self, name, shape, dt):
        return self.es.enter_context(self.nc.sbuf_tensor(name, list(shape), dt))

    def _build_inner(self):
        nc = self.nc
        self.P = Prog(nc)
        P = self.P
        self.x = self.dram_in("x", [SEQ, D])
        self.pp = self.dram_in("pp", [SEQ, 256])
        self.xs = self.dram_in("xs", [NSAMP, D])
        self.psm = self.dram_in("psm", [NSAMP, 256])
        self.sg = self.dram_in("sg", [NSAMP, H, 128, 128])
        self.sq = self.dram_in("sq", [NSAMP, 3, QKV_W])
        self.ssc = self.dram_in("ssc", [NSAMP, 2, D])
        self.wbig = self.dram_in("wbig", [self.wbig_len])
        self.wscr = self.nc.dram_tensor("wscr", [self.wbig_len], BF16, kind="Internal").ap()
        self.lnp = self.dram_in("lnp", [8, D])
        self.wcq_d = self.dram_in("wcq", [128, 24 * 4])
        self.wcs_d = self.dram_in("wcs", [128, 8 * 3])
        self.smallp = self.dram_in("smallp", [2, 8])
        self.won_d = self.dram_in("won", [128])
        self.cst_d = self.dram_in("cst", [128, len(CSTF_NAMES), 128])
        self.cst2_d = self.dram_in("cst2", [128, len(CSTB_NAMES) * 128])
        self.i16_d = self.dram_in("i16", [128, 256])
        self.y = self.dram_out("y", [SEQ, D])
        self.ys = self.dram_out("ys", [NSAMP, D])
        self.sgp = self.dram_out("sgp", [H, 128, 128])
        self.sqp = self.dram_out("sqp", [3, QKV_W])
        self.ssp = self.dram_out("ssp", [2, D])
        self.sgs = self.dram_out("sgs", [NSAMP, H, 128, 128])
        self.sqs = self.dram_out("sqs", [NSAMP, 3, QKV_W])
        self.sss = self.dram_out("sss", [NSAMP, 2, D])
        self.outkeys = []

        sb = self.sb
        self.wring = [sb("wr%d" % i, [128, SLOT], BF16) for i in range(NSLOT)]
        self.xres = sb("xres", [128, 4, D], F32)
        self.xT = sb("xT", [128, 8, NTP], BF16)
        self.gbt = sb("gbt", [128, 2, D], F32)
        self.cstf = sb("cstf", [128, len(CSTF_NAMES), 128], F32)
        self.cstb = sb("cstb", [128, len(CSTB_NAMES), 128], BF16)
        self.i16b = sb("i16b", [128, 256], BF16)
        self.wcq = sb("wcq_s", [128, 24, 4], F32)
        self.wcs = sb("wcs_s", [128, 8, 3], F32)
        self.wonb = sb("wonb", [128, 128], F32)
        self.smallb = sb("smallb", [128, 16], F32)
        self.negA = sb("negA", [128, 8], F32)
        self.histq = sb("histq", [128, 24, 3], F32)
        self.hists = sb("hists", [128, 8, 2], F32)
        self.S = sb("S", [128, H, 128], F32)
        self.Sbf = sb("Sbf", [128, H, 128], BF16)
        self.A1 = sb("A1", [128, 24, NTP], BF16)
        self.ztok = sb("ztok", [128, 4, D], BF16)
        self.ktok = sb("ktok", [128, H, 128], BF16)
        self.vtok = sb("vtok", [128, H, 128], BF16)
        self.batok = sb("batok", [128, 4, 16], F32)
        self.beta = sb("beta", [128, 4, 8], F32)
        self.gtok = sb("gtok", [128, 4, 8], F32)
        self.tf = [sb("tf%d" % i, [128, NTP + 4], F32) for i in range(11)]
        self.tfi = 0
        self.tb16 = [sb("tb%d" % i, [128, NTP], BF16) for i in range(4)]
        self.tbi = 0
        self.t1 = sb("t1", [128, D], F32)
        self.t2 = sb("t2", [128, D], F32)
        self.otok = sb("otok", [128, D], F32)
        self.xb16 = sb("xb16", [128, D], BF16)
        self.stat = sb("stat", [128, 4, 16], F32)
        self.stat3 = sb("stat3", [128, 16], F32)
        self.xb16b = sb("xb16b", [128, D], BF16)
        self.pT = sb("pT", [128, 2, NTP], BF16)
        self.pb = sb("pb", [128, 256], BF16)
        GQ = [("decTm", F32), ("Lg", F32), ("qkTm", BF16), ("MT", BF16), ("M", BF16),
              ("Na", BF16), ("Nb", BF16), ("Nc", BF16), ("Nd", BF16), ("Pa", BF16), ("Pb", BF16),
              ("Pc", BF16), ("Pd", BF16)]
        self.gqs = []
        self.arenaA = sb("arenaA", [128, 15 * 512], BF16)
        self.arenaB = sb("arenaB", [128, 15 * 512], BF16)
        self.gq_names = [nm for nm, _ in GQ]
        for ar, pfx in ((self.arenaA, "gA_"), (self.arenaB, "gB_")):
            gX = {}
            o = 0
            for nm, dt in GQ:
                n = 1024 if dt == F32 else 512
                v = ar[:, o:o + n]
                if dt == F32:
                    v = v.bitcast(F32)
                gX[nm] = v.rearrange("p (h d) -> p h d", h=4)
                o += n
            self.gqs.append((gX, pfx))
        self.kdec2 = [sb("kdec%d" % i, [128, H, 128], BF16) for i in range(2)]
        self.gsm2 = [sb("gsm%d" % i, [128, 64], F32) for i in range(2)]
        self.vtok2 = sb("vtok2", [128, H, 128], BF16)
        self.Uk = [sb("Uk%d" % i, [128, 4, 128], BF16) for i in range(2)]
        self.Qk = [sb("Qk%d" % i, [128, 4, 128], BF16) for i in range(2)]
        self.gsm = self.gsm2[0]
        self.kdec = self.kdec2[0]
        fA = lambda o, n: self.arenaA[:, o:o + n]
        fB = lambda o, n: self.arenaB[:, o:o + n]
        self.Sin = [fA(k * 2048, 2048).bitcast(F32).rearrange("p (h d) -> p h d", h=H) for k in range(3)]
        self.hss = fA(6144, 512).bitcast(F32).rearrange("p (c j b) -> p c j b", c=8, j=2)
        self.Sinb = [fA(6656, 1024).rearrange("p (h d) -> p h d", h=H), fB(6400, 1024).rearrange("p (h d) -> p h d", h=H)]
        self.kTm = fB(0, 2048).rearrange("p (h a b) -> p h a b", h=H, a=NSAMP)
        self.qTm = fB(2048, 2048).rearrange("p (h a b) -> p h a b", h=H, a=NSAMP)
        self.hsq = fB(4096, 2304).bitcast(F32).rearrange("p (c j b) -> p c j b", c=24, j=3)
        self.kmask = [sb("kmask%d" % i, [NSAMP, D], BF16) for i in range(2)]
        self.qtok = sb("qtok", [NSAMP, D], BF16)
        self.abc = sb("abc", [128, 128], F32)
        self.ps = [self.es.enter_context(nc.psum_tensor("ps%d" % i, [128, 512], F32)) for i in range(8)]
        self.psb = [p.bitcast(BF16) for p in self.ps]

        self.recording = True
        self.slab_i = 0
        self.setup()
        npass = self.debug.get("npass", 4)
        for pi in range(npass):
            self.layer_pass(pi, NTP, sample=False, last=(pi == npass - 1))
            if pi == 0:
                self.recording = False
                self.nslab_pass = len(self.slab_off)
        if not self.debug.get("nosample", False):
            gbk = [p + nm for p in ("gA_", "gB_") for nm in self.gq_names]
            gbk += ["gsm0", "gsm1", "vtok0", "vtok1", "kdec0", "kdec1"]
            P.add("dve", lambda e: e.memset(self.gsm[:, 60:64], 0.0), gbk,
                  gbk + ["kTm", "qTm", "hsamp", "Sin0", "Sin1", "Sin2", "Sinb0", "Sinb1", "gsm", "vtok"])
            self.layer_pass(0, NSAMP, sample=True, last=True)
        P.add("sp", None, r=self.outkeys)
        P.finalize()
        P.emit(self.es)

    def cf(self, name):
        return self.cstf[:, CSTF_NAMES.index(name), :]

    def cb(self, name):
        return self.cstb[:, CSTB_NAMES.index(name), :]

    def cb4(self, name):
        i = CSTB_NAMES.index(name)
        return self.cstb[:, i:i + 1, :].to_broadcast([128, 4, 128])

    def cf4(self, name):
        i = CSTF_NAMES.index(name)
        return self.cstf[:, i:i + 1, :].to_broadcast([128, 4, 128])

    def tmpf(self):
        i = self.tfi % len(self.tf)
        self.tfi += 1
        return self.tf[i], "tf%d" % i

    def tmpb(self):
        i = self.tbi % len(self.tb16)
        self.tbi += 1
        return self.tb16[i], "tb%d" % i

    def setup(self):
        d = self.dma
        d("sp", self.cstf[:], self.cst_d, "c0", [], ["cstf"])
        nb = len(CSTB_NAMES) * 128
        for k in range(0, nb, 1024):
            n = min(1024, nb - k)
            d("sp", self.t1[:, 0:n], self.cst2_d[:, k:k + n], "c1", [], ["t1"])
            self.cp("dve", self.cstb[:].rearrange("p c d -> p (c d)")[:, k:k + n], self.t1[:, 0:n], ["t1"], ["cstb"])
        d("sp", self.t2[:, 0:256], self.i16_d, "c1", [], ["t2"])
        self.cp("dve", self.i16b[:], self.t2[:, 0:256], ["t2"], ["i16b"])
        d("sp", self.wcq[:].rearrange("p c j -> p (c j)"), self.wcq_d, "c2", [], ["wcq"])
        d("sp", self.wcs[:].rearrange("p c j -> p (c j)"), self.wcs_d, "c3", [], ["wcs"])
        d("sp", self.wonb[:], self.won_d.partition_broadcast(128), "c4", [], ["wonb"])
        d("sp", self.smallb[:], self.smallp.rearrange("a b -> (a b)").partition_broadcast(128), "c5", [], ["smallb"])
        self.act(self.negA[:], self.smallb[:, 0:8], AF.Exp, ["smallb"], ["negA"])
        self.ts("dve", self.negA[:], self.negA[:], -1.0, ALU.mult, ["negA"], ["negA"])
        self.P.add("dve", lambda e: e.memset(self.S[:], 0.0), [], ["S0", "S1"])
        self.P.add("dve", lambda e: e.memset(self.Sbf[:], 0.0), [], ["Sbf0", "Sbf1"])
        self.P.add("dve", lambda e: e.memset(self.histq[:], 0.0), [], ["histq"])
        self.P.add("dve", lambda e: e.memset(self.hists[:], 0.0), [], ["hists"])

    def make_xT(self, tb, TB, bank, staged=False):
        ps, psk = self.psb[bank], "ps%d" % bank
        xb, xbk = ((self.xb16, "xb16"), (self.xb16b, "xb16b"))[tb % 2]
        if not staged:
            self.cp("act", xb[:TB, :], self.xres[:TB, tb, :], ["xres%d" % tb], [xbk])
        for c in range(8):
            self.tr(ps[:, c * TB:(c + 1) * TB], xb[:TB, c * 128:(c + 1) * 128], self.cb("ident")[:TB, :TB],
                    [xbk, "cstb"], [psk])
        self.cp("dve", self.xT[:, :, tb * TB:(tb + 1) * TB],
                ps[:, 0:8 * TB].rearrange("p (c t) -> p c t", c=8), [psk], ["xT%d" % tb])

    def layer_norm(self, idx, NB, TB, final_out=None):
        self.dma("aux", self.gbt[:, 0, :], self.lnp[2 * idx, :].partition_broadcast(128), "gb0", [], ["gbt0"])
        self.dma("aux", self.gbt[:, 1, :], self.lnp[2 * idx + 1, :].partition_broadcast(128), "gb1", [], ["gbt1"])
        eps = LN_EPS / (ALPHA * ALPHA)
        st = self.stat
        for tb in range(NB):
            xr = self.xres[:TB, tb, :]
            xk = "xres%d" % tb
            self.P.add("dve", lambda e, xr=xr, tb=tb: e.bn_stats(st[:TB, tb, 0:6], xr[:, 0:512]), [xk], ["stat"])
            self.P.add("dve", lambda e, xr=xr, tb=tb: e.bn_stats(st[:TB, tb, 6:12], xr[:, 512:1024]), [xk], ["stat"])
            self.P.add("dve", lambda e, tb=tb: e.bn_aggr(st[:TB, tb, 12:14], st[:TB, tb, 0:12]), ["stat"], ["stat"])
        self.act(st[:TB, 0:NB, 14], st[:TB, 0:NB, 13], AF.Sqrt, ["stat"], ["stat2"], bias=eps)
        self.P.add("dve", lambda e: e.reciprocal(st[:TB, 0:NB, 15], st[:TB, 0:NB, 14]), ["stat2"], ["stat2"])
        for tb in range(NB):
            xr = self.xres[:TB, tb, :]
            xk = "xres%d" % tb
            self.stt(self.t1[:TB, :], xr, st[:TB, tb, 12:13], self.gbt[:TB, 0, :], ALU.subtract, ALU.mult,
                     [xk, "stat", "gbt0"], ["t1"])
            self.stt(xr, self.t1[:TB, :], st[:TB, tb, 15:16], self.gbt[:TB, 1, :], ALU.mult, ALU.add,
                     ["t1", "stat2", "gbt1"], [xk])
            if final_out is not None:
                ok = final_out[1] + str(tb)
                self.dma("aux", final_out[0][tb * TB:(tb + 1) * TB, :], xr, "yo%d" % tb, [xk], [ok])
                self.outkeys.append(ok)
            else:
                self.make_xT(tb, TB, (2 * tb) % 8)

    def ffn(self, pfx, NB, TB):
        NT = NB * TB
        for j0 in range(0, NJ, 2):
            (wg, wu), wk = self.slab([(pfx + "_w_gate", 0, D, j0 * 128, j0 * 128 + 256),
                                      (pfx + "_w_up", 0, D, j0 * 128, j0 * 128 + 256)])
            for jj in range(2):
                j = j0 + jj
                bg, bu = 2 * (j % 2), 2 * (j % 2) + 1
                for kc in range(8):
                    self.mm(self.ps[bg][:, :NT], wg[:, kc, jj * 128:(jj + 1) * 128], self.xT[:, kc, :NT],
                            [wk, "xT0", "xT1", "xT2", "xT3"], ["ps%d" % bg], start=(kc == 0), stop=(kc == 7))
                for kc in range(8):
                    self.mm(self.ps[bu][:, :NT], wu[:, kc, jj * 128:(jj + 1) * 128], self.xT[:, kc, :NT],
                            [wk, "xT0", "xT1", "xT2", "xT3"], ["ps%d" % bu], start=(kc == 0), stop=(kc == 7))
                t, tk = self.tmpf()
                self.act(t[:, :NT], self.ps[bg][:, :NT], AF.Silu, ["ps%d" % bg], [tk])
                self.tt("dve", self.A1[:, j, :NT], t[:, :NT], self.ps[bu][:, :NT], ALU.mult,
                        [tk, "ps%d" % bu], ["A1.%d" % j])
        for j0 in range(0, NJ, 4):
            j1 = min(NJ, j0 + 4)
            (wd,), wk = self.slab([(pfx + "_w_down", j0 * 128, j1 * 128, 0, D)])
            for jj in range(j1 - j0):
                j = j0 + jj
                for tb in range(NB):
                    for nh in range(2):
                        b = tb * 2 + nh
                        self.mm(self.ps[b][:TB, :], self.A1[:, j, tb * TB:(tb + 1) * TB], wd[:, jj, nh * 512:(nh + 1) * 512],
                                [wk, "A1.%d" % j], ["ps%d" % b], start=(j == 0), stop=(j == NJ - 1))
        c = 0.5 / ALPHA
        for tb in range(NB):
            for nh in range(2):
                b = tb * 2 + nh
                xr = self.xres[:TB, tb, nh * 512:(nh + 1) * 512]
                self.stt(xr, self.ps[b][:TB, :], c, xr, ALU.mult, ALU.add, ["ps%d" % b, "xres%d" % tb], ["xres%d" % tb])

    def layer_pass(self, pi, NT, sample, last):
        TB = min(128, NT)
        NB = NT // TB
        self.slab_i = 0 if self.recording else self.slab_i
        if not sample:
            src, psrc = self.x[pi * NT:(pi + 1) * NT, :], self.pp[pi * NT:(pi + 1) * NT, :]
            yout = (self.y[pi * NT:(pi + 1) * NT, :], "y%d_" % pi)
        else:
            src, psrc = self.xs, self.psm
            yout = (self.ys, "ys_")
        for tb in range(NB):
            xb, xbk = ((self.xb16, "xb16"), (self.xb16b, "xb16b"))[tb % 2]
            self.dma("pool", xb[:TB, :], src[tb * TB:(tb + 1) * TB, :], "xc%d" % (tb % 2), [], [xbk])
            self.make_xT(tb, TB, tb % 8, staged=True)
        for tb in range(NB):
            self.dma("aux", self.xres[:TB, tb, :], src[tb * TB:(tb + 1) * TB, :], "x%d" % tb, [], ["xres%d" % tb])
        stop = self.debug.get("stop")
        self.ffn("ffn1", NB, TB)
        if stop == "ffn1":
            return self.dump(yout, NB, TB)
        self.layer_norm(0, NB, TB)
        if stop == "ln1":
            return self.dump(yout, NB, TB)
        self.mixers(pi, NB, TB, sample, last)
        if stop == "mix":
            return self.dump(yout, NB, TB)
        self.layer_norm(1, NB, TB)
        self.ffn("ffn2", NB, TB)
        self.layer_norm(2, NB, TB)
        if stop == "ln3":
            return self.dump(yout, NB, TB)
        self.ple(psrc, NB, TB)
        self.layer_norm(3, NB, TB, final_out=yout)

    def dump(self, yout, NB, TB):
        for tb in range(NB):
            ok = yout[1] + str(tb)
            self.dma("aux", yout[0][tb * TB:(tb + 1) * TB, :], self.xres[:TB, tb, :], "yo%d" % tb, ["xres%d" % tb], [ok])
            self.outkeys.append(ok)

    def ple(self, psrc, NB, TB):
        NT = NB * TB
        for tb in range(NB):
            pf, pfk = self.tmpf()
            self.dma("aux", pf[:TB, 0:256], psrc[tb * TB:(tb + 1) * TB, :], "pf", [], [pfk])
            self.cp("act", self.pb[:TB, :], pf[:TB, 0:256], [pfk], ["pb"])
            for c in range(2):
                self.tr(self.psb[7][:, c * TB:(c + 1) * TB], self.pb[:TB, c * 128:(c + 1) * 128],
                        self.cb("ident")[:TB, :TB], ["pb", "cstb"], ["ps7"])
            self.cp("dve", self.pT[:, :, tb * TB:(tb + 1) * TB],
                    self.psb[7][:, 0:2 * TB].rearrange("p (c t) -> p c t", c=2), ["ps7"], ["pT"])
        for nh in range(2):
            (wg,), wgk = self.slab([("w_ple_gate", 0, D, nh * 512, (nh + 1) * 512)])
            (wp,), wpk = self.slab([("w_ple_proj", 0, 256, nh * 512, (nh + 1) * 512)])
            for tb in range(NB):
                bg, bp = 2 * (tb % 2), 2 * (tb % 2) + 1
                for kc in range(8):
                    self.mm(self.ps[bg][:TB, :], self.xT[:, kc, tb * TB:(tb + 1) * TB], wg[:, kc, :],
                            [wgk, "xT0", "xT1", "xT2", "xT3"], ["ps%d" % bg], start=(kc == 0), stop=(kc == 7))
                for kc in range(2):
                    self.mm(self.ps[bp][:TB, :], self.pT[:, kc, tb * TB:(tb + 1) * TB], wp[:, kc, :],
                            [wpk, "pT"], ["ps%d" % bp], start=(kc == 0), stop=(kc == 1))
                t, tk = self.tmpf()
                self.act(t[:TB, :512], self.ps[bg][:TB, :], AF.Sigmoid, ["ps%d" % bg], [tk])
                self.tt("dve", t[:TB, :512], t[:TB, :512], self.ps[bp][:TB, :], ALU.mult, [tk, "ps%d" % bp], [tk])
                xr = self.xres[:TB, tb, nh * 512:(nh + 1) * 512]
                self.stt(xr, t[:TB, :512], 1.0 / ALPHA, xr, ALU.mult, ALU.add, [tk, "xres%d" % tb], ["xres%d" % tb])

    def conv_chunk(self, psbank, NT, taps_hist, wts, ntap, hist_tile, hist_key, sample, src_is_psum=True, src=None):
        H_ = ntap - 1
        cbt, cbk = self.tmpf()
        if src_is_psum:
            self.cp("act", cbt[:, H_:H_ + NT], self.ps[psbank][:, :NT], ["ps%d" % psbank], [cbk])
        else:
            src(cbt[:, H_:H_ + NT], cbk)
        if not sample:
            self.cp("dve", cbt[:, 0:H_], hist_tile, [hist_key], [cbk])
            self.cp("dve", hist_tile, cbt[:, NT:NT + H_], [cbk], [hist_key])
            taps = [cbt[:, j:j + NT] for j in range(ntap)]
            tr_ = [cbk]
        else:
            taps = [taps_hist[j] for j in range(H_)] + [cbt[:, H_:H_ + NT]]
            tr_ = [cbk, "hsamp"]
        acc, ak = self.tmpf()
        self.ts("dve", acc[:, :NT], taps[0], wts[0], ALU.mult, tr_ + ["wc"], [ak])
        for j in range(1, ntap):
            self.stt(acc[:, :NT], taps[j], wts[j], acc[:, :NT], ALU.mult, ALU.add, tr_ + ["wc", ak], [ak])
        return acc, ak, cbt, cbk

    def mixers(self, pi, NB, TB, sample, last):
        NT = NB * TB
        A1 = self.A1
        if sample:
            self.load_sample_hist()
        def finish(grp):
            for (c, so, sk, sq, sqk, cbt, cbk) in grp:
                if sample:
                    self.tr(self.ps[6][:NT, (c % 4) * 128:(c % 4 + 1) * 128], cbt[:, 3:3 + NT], self.cf("ident"),
                            [cbk, "cstf"], ["ps6"])
                    if c % 4 == 3:
                        stg, stk = self.stage(c // 8)
                        self.cp("act", stg[:NT, (c % 8 - 3) * 128:(c % 8 + 1) * 128], self.ps[6][:NT, :], ["ps6"], [stk])
            qk = [g_ for g_ in grp if g_[0] < 16]
            sds = []
            for (c, so, sk, sq, sqk, cbt, cbk) in qk:
                b2 = 4 + c % 2 if sample else 4 + c % 4
                self.mm(self.ps[b2][:, :NT], self.cb("ones"), sq[:, :NT], [sqk, "cstb"], ["ps%d" % b2])
            for (c, so, sk, sq, sqk, cbt, cbk) in qk:
                b2 = 4 + c % 2 if sample else 4 + c % 4
                sd, sdk = self.tmpf()
                sds.append((sd, sdk))
                self.act(sd[:, :NT], self.ps[b2][:, :NT], AF.Ln, ["ps%d" % b2], [sdk], bias=L2_EPS)
            for (sd, sdk) in sds:
                self.act(sd[:, :NT], sd[:, :NT], AF.Exp, [sdk], [sdk], scale=-0.5)
            for (c, so, sk, sq, sqk, cbt, cbk), (sd, sdk) in zip(qk, sds):
                const = 128.0 ** -0.5 if c < 8 else 1.0
                self.stt(A1[:, c, :NT], so[:, :NT], const, sd[:, :NT], ALU.mult, ALU.mult, [sk, sdk], ["A1.%d" % c])

        pend = None
        for g in range(6):
            (wq,), wk = self.slab([("w_in", 0, D, g * 512, (g + 1) * 512)])
            for pr in range(2):
                cs_ = [g * 4 + pr * 2, g * 4 + pr * 2 + 1]
                for c in cs_:
                    jj = c % 4
                    bank = c % 4
                    for kc in range(8):
                        self.mm(self.ps[bank][:, :NT], wq[:, kc, jj * 128:(jj + 1) * 128], self.xT[:, kc, :NT],
                                [wk, "xT0", "xT1", "xT2", "xT3"], ["ps%d" % bank], start=(kc == 0), stop=(kc == 7))
                convs = []
                for c in cs_:
                    th = [self.hsq[:, c, j, :] for j in range(3)] if sample else None
                    wts = [self.wcq[:, c, j:j + 1] for j in range(4)]
                    convs.append(self.conv_chunk(c % 4, NT, th, wts, 4, self.histq[:, c, :], "histq%d" % c, sample))
                cur = []
                for c, (acc, ak, cbt, cbk) in zip(cs_, convs):
                    if c >= 16:
                        self.act(A1[:, c, :NT], acc[:, :NT], AF.Silu, [ak], ["A1.%d" % c])
                        cur.append((c, None, None, None, None, cbt, cbk))
                    else:
                        self.act(acc[:, :NT], acc[:, :NT], AF.Silu, [ak], [ak])
                        sq, sqk = self.tmpb()
                        if self.recording:
                            self.act(sq[:, :NT], acc[:, :NT], AF.Square, [ak], [sqk])
                        else:
                            self.tt("pool", sq[:, :NT], acc[:, :NT], acc[:, :NT], ALU.mult, [ak], [sqk])
                        cur.append((c, acc, ak, sq, sqk, cbt, cbk))
                if pend is not None:
                    finish(pend)
                pend = cur
        finish(pend)
        if sample:
            for k in range(3):
                stg, stk = self.stage(k)
                self.dma("aux", self.sqs[:, 2, k * 1024:(k + 1) * 1024], stg[:NSAMP, :], "so0", [stk], ["sqs2_%d" % k])
                self.outkeys.append("sqs2_%d" % k)
            self.dma("aux", self.sqs[:, 0:2, :], self.sq[:, 1:3, :], "so1", [], ["sqs01"])
            self.outkeys += ["sqs01"]
        elif last:
            for j in range(3):
                self.dma("aux", self.sqp[j, :].rearrange("(c p) -> p c", p=128), self.histq[:, :, j], "so0",
                         ["histq%d" % c for c in range(24)], ["sqp%d" % j], slow=True)
                self.outkeys.append("sqp%d" % j)
        if self.debug.get("mstop") == "A":
            return
        for nh in range(2):
            (wz,), wk = self.slab([("w_in", 0, D, Z0 + nh * 512, Z0 + (nh + 1) * 512)])
            for tb in range(NB):
                b = 4 + tb % 2
                for kc in range(8):
                    self.mm(self.ps[b][:TB, :], self.xT[:, kc, tb * TB:(tb + 1) * TB], wz[:, kc, :],
                            [wk, "xT0", "xT1", "xT2", "xT3"], ["ps%d" % b], start=(kc == 0), stop=(kc == 7))
                self.act(self.ztok[:TB, tb, nh * 512:(nh + 1) * 512], self.ps[b][:TB, :], AF.Silu, ["ps%d" % b], ["ztok%d" % tb])
        (wba,), wk = self.slab([("w_in", 0, D, BETA0, BETA0 + 16)])
        for tb in range(NB):
            for kc in range(8):
                self.mm(self.ps[6][:TB, 0:16], self.xT[:, kc, tb * TB:(tb + 1) * TB], wba[:, kc, :],
                        [wk, "xT0", "xT1", "xT2", "xT3"], ["ps6"], start=(kc == 0), stop=(kc == 7))
            self.act(self.beta[:TB, tb, :], self.ps[6][:TB, 0:8], AF.Sigmoid, ["ps6"], ["beta"])
            self.tt("dve", self.batok[:TB, tb, 8:16], self.ps[6][:TB, 8:16], self.smallb[:TB, 8:16], ALU.add,
                    ["ps6", "smallb"], ["batok"])
        for tb in range(NB):
            self.act(self.batok[:TB, tb, 0:8], self.batok[:TB, tb, 8:16], AF.Exp, ["batok"], ["batok"])
        for tb in range(NB):
            self.act(self.batok[:TB, tb, 0:8], self.batok[:TB, tb, 0:8], AF.Ln, ["batok"], ["batok"], bias=1.0)
            self.tt("dve", self.gtok[:TB, tb, :], self.batok[:TB, tb, 0:8], self.negA[:TB, :], ALU.mult,
                    ["batok", "negA"], ["gtok"])
        if self.debug.get("mstop") == "B":
            return
        if sample:
            self.gdn_sample()
        else:
            self.gdn_all(NB)
            if last:
                self.dma("aux", self.sgp.rearrange("h k v -> k h v"), self.S[:], "so1", ["S0", "S1"], ["sgp"])
                self.outkeys.append("sgp")
        if self.debug.get("mstop") == "C":
            return
        for c in range(8):
            (wB, wC, wH), wk = self.slab([("w_in", 0, D, B0 + c * 128, B0 + (c + 1) * 128),
                                          ("w_in", 0, D, C0 + c * 128, C0 + (c + 1) * 128),
                                          ("w_in", 0, D, H0 + c * 128, H0 + (c + 1) * 128)])
            bB, bC, bH = 0 + 3 * (c % 2), 1 + 3 * (c % 2), 2 + 3 * (c % 2)
            for (w_, b_) in ((wC, bC), (wH, bH), (wB, bB)):
                for kc in range(8):
                    self.mm(self.ps[b_][:, :NT], w_[:, kc, :], self.xT[:, kc, :NT], [wk, "xT0", "xT1", "xT2", "xT3"], ["ps%d" % b_],
                            start=(kc == 0), stop=(kc == 7))
            ct, ck = self.tmpf()
            self.cp("act", ct[:, :NT], self.ps[bC][:, :NT], ["ps%d" % bC], [ck])

            def src(dst, dk, ct=ct, ck=ck, bH=bH):
                self.tt("dve", dst, ct[:, :NT], self.ps[bH][:, :NT], ALU.mult, [ck, "ps%d" % bH], [dk])
            th = [self.hss[:, c, j, :] for j in range(2)] if sample else None
            wts = [self.wcs[:, c, j:j + 1] for j in range(3)]
            acc, ak, cbt, cbk = self.conv_chunk(None, NT, th, wts, 3, self.hists[:, c, :], "hists%d" % c, sample,
                                                src_is_psum=False, src=src)
            if sample:
                self.tr(self.ps[6][:NT, (c % 4) * 128:(c % 4 + 1) * 128], cbt[:, 2:2 + NT], self.cf("ident"),
                        [cbk, "cstf"], ["ps6"])
                if c % 4 == 3:
                    stg, stk = self.stage(0)
                    self.cp("act", stg[:NT, (c - 3) * 128:(c + 1) * 128], self.ps[6][:NT, :], ["ps6"], [stk])
            self.tt("dve", A1[:, c, :NT], acc[:, :NT], self.ps[bB][:, :NT], ALU.mult, [ak, "ps%d" % bB], ["A1.%d" % c])
        if sample:
            stg, stk = self.stage(0)
            self.dma("aux", self.sss[:, 1, :], stg[:NSAMP, 0:D], "so2", [stk], ["sss1"])
            self.dma("aux", self.sss[:, 0:1, :], self.ssc[:, 1:2, :], "so3", [], ["sss0"])
            self.outkeys += ["sss1", "sss0"]
        elif last:
            for j in range(2):
                self.dma("aux", self.ssp[j, :].rearrange("(c p) -> p c", p=128), self.hists[:, :, j], "so2",
                         ["hists%d" % c for c in range(8)], ["ssp%d" % j], slow=True)
                self.outkeys.append("ssp%d" % j)
        if self.debug.get("mstop") == "D":
            return
        for c in range(8):
            (wpg, wgg, wps, wgs), wk = self.slab([("w_p_gdn", 0, D, c * 128, (c + 1) * 128),
                                                  ("w_in", 0, D, GG0 + c * 128, GG0 + (c + 1) * 128),
                                                  ("w_p_sc", 0, D, c * 128, (c + 1) * 128),
                                                  ("w_in", 0, D, GS0 + c * 128, GS0 + (c + 1) * 128)])
            o = 4 * (c % 2)
            for (w_, b_, rhs_, rk) in ((wpg, o, A1[:, 16:24, :], ["A1.%d" % k for k in range(16, 24)]),
                                       (wgg, o + 1, self.xT, ["xT0", "xT1", "xT2", "xT3"]),
                                       (wps, o + 2, A1[:, 0:8, :], ["A1.%d" % k for k in range(8)]),
                                       (wgs, o + 3, self.xT, ["xT0", "xT1", "xT2", "xT3"])):
                for kc in range(8):
                    self.mm(self.ps[b_][:, :NT], w_[:, kc, :], rhs_[:, kc, :NT], [wk] + rk, ["ps%d" % b_],
                            start=(kc == 0), stop=(kc == 7))
            s1, s1k = self.tmpf()
            self.act(s1[:, :NT], self.ps[o + 1][:, :NT], AF.Sigmoid, ["ps%d" % (o + 1)], [s1k])
            self.tt("dve", s1[:, :NT], s1[:, :NT], self.ps[o][:, :NT], ALU.mult, [s1k, "ps%d" % o], [s1k])
            s2, s2k = self.tmpf()
            self.act(s2[:, :NT], self.ps[o + 3][:, :NT], AF.Sigmoid, ["ps%d" % (o + 3)], [s2k])
            self.tt("dve", s2[:, :NT], s2[:, :NT], self.ps[o + 2][:, :NT], ALU.mult, [s2k, "ps%d" % (o + 2)], [s2k])
            self.tt("dve", A1[:, 8 + c, :NT], s1[:, :NT], s2[:, :NT], ALU.add, [s1k, s2k], ["A1.%d" % (8 + c)])
        for nh in range(2):
            (wo,), wk = self.slab([("w_o", 0, D, nh * 512, (nh + 1) * 512)])
            for tb in range(NB):
                b = tb % 2
                for kc in range(8):
                    self.mm(self.ps[b][:TB, :], A1[:, 8 + kc, tb * TB:(tb + 1) * TB], wo[:, kc, :],
                            [wk, "A1.%d" % (8 + kc)], ["ps%d" % b], start=(kc == 0), stop=(kc == 7))
                xr = self.xres[:TB, tb, nh * 512:(nh + 1) * 512]
                self.stt(xr, self.ps[b][:TB, :], 1.0 / ALPHA, xr, ALU.mult, ALU.add, ["ps%d" % b, "xres%d" % tb], ["xres%d" % tb])

    def onorm_and_T(self, tb, TB):
        self.lockstep([self.onorm_gen(tb, TB)])

    def onorm_gen(self, tb, TB, bank=7):
        o3 = self.otok[:TB, :].rearrange("p (h d) -> p h d", h=H)
        t13 = self.t1[:TB, :].rearrange("p (h d) -> p h d", h=H)
        t23 = self.t2[:TB, :].rearrange("p (h d) -> p h d", h=H)
        st = self.stat3
        self.act(self.t1[:TB, :], self.otok[:TB, :], AF.Square, ["otok"], ["t1"])
        yield
        self.P.add("dve", lambda e: e.tensor_reduce(st[:TB, 0:8], t13, AX.X, ALU.add), ["t1"], ["stat3"])
        self.act(st[:TB, 0:8], st[:TB, 0:8], AF.Sqrt, ["stat3"], ["stat3"], bias=RMS_EPS, scale=1.0 / 128.0)
        yield
        self.P.add("dve", lambda e: e.reciprocal(st[:TB, 8:16], st[:TB, 0:8]), ["stat3"], ["stat3"])
        yield
        self.tt("dve", t13, o3, st[:TB, 8:16].unsqueeze(2).to_broadcast([TB, H, 128]), ALU.mult, ["otok", "stat3"], ["t1"])
        z3 = self.ztok[:TB, tb, :].rearrange("p (h d) -> p h d", h=H)
        self.tt("dve", t23, z3, self.wonb[:TB, :].unsqueeze(1).to_broadcast([TB, H, 128]), ALU.mult,
                ["ztok%d" % tb, "wonb"], ["t2"])
        yield
        self.tt("dve", self.xb16[:TB, :], self.t1[:TB, :], self.t2[:TB, :], ALU.mult, ["t1", "t2"], ["xb16"])
        yield
        for c in range(8):
            self.tr(self.psb[bank][:, c * TB:(c + 1) * TB], self.xb16[:TB, c * 128:(c + 1) * 128], self.cb("ident")[:TB, :TB],
                    ["xb16", "cstb"], ["ps%d" % bank])
        yield
        self.cp("act", self.A1[:, 16:24, tb * TB:(tb + 1) * TB],
                self.psb[bank][:, 0:8 * TB].rearrange("p (c t) -> p c t", c=8), ["ps%d" % bank],
                ["A1.%d" % k for k in range(16, 24)])

    def inv_chain(self, tb, hg, G, gp, pb):
        A1 = self.A1
        blk = slice(tb * 128, (tb + 1) * 128)
        g8 = self.gtok[:, tb, :]
        hs = [hg * 4 + hh for hh in range(4)]
        K = lambda nm: gp + nm
        rot = [0]

        def nb():
            x = pb[rot[0] % len(pb)]
            rot[0] += 1
            return x
        b0, b1, b2 = nb(), nb(), nb()
        f4 = lambda t: t.rearrange("p h d -> p (h d)")
        kq_r = ["A1.%d" % (8 + h) for h in hs] + ["A1.%d" % h for h in hs]
        for hh, h in enumerate(hs):
            self.ts("dve", G["Lg"][:, hh, :], self.cf("ltri"), g8[:, h:h + 1], ALU.mult, ["cstf", "gtok"], [K("Lg")])
        yield
        for hh, h in enumerate(hs):
            cs = slice(hh * 128, (hh + 1) * 128)
            self.mm(self.ps[b0][:, cs], self.cf("su"), G["Lg"][:, hh, :], ["cstf", K("Lg")], ["ps%d" % b0])
            self.mm(self.ps[b1][:, cs], A1[:, 8 + h, blk], A1[:, 8 + h, blk], kq_r, ["ps%d" % b1])
            self.mm(self.ps[b2][:, cs], A1[:, 8 + h, blk], A1[:, h, blk], kq_r, ["ps%d" % b2])
        yield
        self.act(f4(G["decTm"]), self.ps[b0][:, :], AF.Exp, ["ps%d" % b0], [K("decTm")])
        yield
        self.tt("dve", G["decTm"], G["decTm"], self.cf4("muincl"), ALU.mult, [K("decTm"), "cstf"], [K("decTm")])
        yield
        self.tt("dve", f4(G["qkTm"]), self.ps[b2][:, :], f4(G["decTm"]), ALU.mult, ["ps%d" % b2, K("decTm")], [K("qkTm")])
        self.tt("dve", f4(G["Lg"]), self.ps[b1][:, :], f4(G["decTm"]), ALU.mult, ["ps%d" % b1, K("decTm")], [K("Lg")])
        yield
        self.tt("dve", G["MT"], G["Lg"],
                self.beta[:, tb, hg * 4:hg * 4 + 4].unsqueeze(2).to_broadcast([128, 4, 128]), ALU.mult,
                [K("Lg"), "beta"], [K("MT")])
        yield
        bt = nb()
        for hh in range(4):
            self.tr(self.psb[bt][:, hh * 128:(hh + 1) * 128], G["MT"][:, hh, :], self.cb("ident"), [K("MT"), "cstb"], ["ps%d" % bt])
        yield
        self.cp("act", f4(G["M"]), self.psb[bt][:, 0:512], ["ps%d" % bt], [K("M")])
        yield
        Nn, Nt, N2, N2t = "Na", "Nb", "Nc", "Nd"
        Pn, Pt, Pn2, Pt2 = "Pa", "Pb", "Pc", "Pd"
        self.tt("dve", G[Nn], G["M"], self.cb4("mndn"), ALU.mult, [K("M"), "cstb"], [K(Nn)])
        self.tt("dve", G[Nt], G["MT"], self.cb4("mndtn"), ALU.mult, [K("MT"), "cstb"], [K(Nt)])
        yield
        self.tt("dve", G[Pn], G[Nn], self.cb4("ident"), ALU.add, [K(Nn), "cstb"], [K(Pn)])
        self.tt("dve", G[Pt], G[Nt], self.cb4("ident"), ALU.add, [K(Nt), "cstb"], [K(Pt)])
        nstep = int(np.log2(NBK)) - 1
        for s_ in range(nstep):
            ba, bb = nb(), nb()
            for hh in range(4):
                cs = slice(hh * 128, (hh + 1) * 128)
                self.mm(self.ps[ba][:, cs], G[Nt][:, hh, :], G[Nn][:, hh, :], [K(Nt), K(Nn)], ["ps%d" % ba])
                self.mm(self.ps[bb][:, cs], G[Nn][:, hh, :], G[Nt][:, hh, :], [K(Nt), K(Nn)], ["ps%d" % bb])
            yield
            self.cp("act", f4(G[N2]), self.ps[ba][:, :], ["ps%d" % ba], [K(N2)])
            self.cp("act", f4(G[N2t]), self.ps[bb][:, :], ["ps%d" % bb], [K(N2t)])
            yield
            bc_, bd = nb(), nb()
            for hh in range(4):
                cs = slice(hh * 128, (hh + 1) * 128)
                self.mm(self.ps[bc_][:, cs], G[N2t][:, hh, :], G[Pn][:, hh, :], [K(N2t), K(Pn)], ["ps%d" % bc_])
                self.mm(self.ps[bd][:, cs], G[N2][:, hh, :], G[Pt][:, hh, :], [K(N2), K(Pt)], ["ps%d" % bd])
            yield
            self.tt("dve", f4(G[Pn2]), f4(G[Pn]), self.ps[bc_][:, :], ALU.add, [K(Pn), "ps%d" % bc_], [K(Pn2)])
            self.tt("dve", f4(G[Pt2]), f4(G[Pt]), self.ps[bd][:, :], ALU.add, [K(Pt), "ps%d" % bd], [K(Pt2)])
            yield
            Nn, Nt, N2, N2t = N2, N2t, Nn, Nt
            Pn, Pt, Pn2, Pt2 = Pn2, Pt2, Pn, Pt
        T, U, T2, U2 = Pn, Pt, Pn2, Pt2
        E_, F_, X_, Y_ = Nn, Nt, N2, N2t
        b = NBK
        while b < 128:
            lastlvl = (b == 64)
            self.tt("dve", G[E_], G["M"], self.cb4("me%d" % b), ALU.mult, [K("M"), "cstb"], [K(E_)])
            if not lastlvl:
                self.tt("dve", G[F_], G["MT"], self.cb4("me%dt" % b), ALU.mult, [K("MT"), "cstb"], [K(F_)])
            yield
            ba, bb = nb(), nb()
            for hh in range(4):
                cs = slice(hh * 128, (hh + 1) * 128)
                self.mm(self.ps[ba][:, cs], G[E_][:, hh, :], G[U][:, hh, :], [K(E_), K(U)], ["ps%d" % ba])
                if not lastlvl:
                    self.mm(self.ps[bb][:, cs], G[F_][:, hh, :], G[T][:, hh, :], [K(F_), K(T)], ["ps%d" % bb])
            yield
            self.cp("act", f4(G[Y_]), self.ps[ba][:, :], ["ps%d" % ba], [K(Y_)])
            if not lastlvl:
                self.cp("act", f4(G[X_]), self.ps[bb][:, :], ["ps%d" % bb], [K(X_)])
            yield
            bc_, bd = nb(), nb()
            for hh in range(4):
                cs = slice(hh * 128, (hh + 1) * 128)
                self.mm(self.ps[bc_][:, cs], G[T][:, hh, :], G[Y_][:, hh, :], [K(T), K(Y_)], ["ps%d" % bc_])
                if not lastlvl:
                    self.mm(self.ps[bd][:, cs], G[U][:, hh, :], G[X_][:, hh, :], [K(U), K(X_)], ["ps%d" % bd])
            yield
            self.tt("dve", f4(G[U2]), f4(G[U]), self.ps[bc_][:, :], ALU.subtract, [K(U), "ps%d" % bc_], [K(U2)])
            if not lastlvl:
                self.tt("dve", f4(G[T2]), f4(G[T]), self.ps[bd][:, :], ALU.subtract, [K(T), "ps%d" % bd], [K(T2)])
            yield
            T, U, T2, U2 = T2, U2, T, U
            b *= 2
        self.cp("dve", self.Uk[hg][:], G[U], [K(U)], ["Uk%d" % hg])
        self.cp("dve", self.Qk[hg][:], G["qkTm"], [K("qkTm")], ["Qk%d" % hg])

    def scan_chain(self, tb, hg, bx, by):
        A1 = self.A1
        blk = slice(tb * 128, (tb + 1) * 128)
        pb_ = tb % 2
        sm, smk = self.gsm2[pb_], "gsm%d" % pb_
        vtok, vtk = (self.vtok, self.vtok2)[pb_], "vtok%d" % pb_
        kdec, kdk = self.kdec2[pb_], "kdec%d" % pb_
        hs = [hg * 4 + hh for hh in range(4)]
        hsl = slice(hg * 4, hg * 4 + 4)
        X, Y = self.ps[bx], self.ps[by]
        xk, yk = "ps%d" % bx, "ps%d" % by
        X3 = X[:, :].rearrange("p (h d) -> p h d", h=4)
        Y3 = Y[:, :].rearrange("p (h d) -> p h d", h=4)
        bc = lambda ap: ap.unsqueeze(2).to_broadcast([128, 4, 128])
        Sk, Sbk = "S%d" % hg, "Sbf%d" % hg
        for hh, h in enumerate(hs):
            cs = slice(hh * 128, (hh + 1) * 128)
            self.mm(X[:, cs], A1[:, 8 + h, blk], self.Sbf[:, h, :], ["A1.%d" % (8 + h), Sbk], [xk])
            self.mm(Y[:, cs], A1[:, h, blk], self.Sbf[:, h, :], ["A1.%d" % h, Sbk], [yk])
        yield
        tS, tSk = self.tmpf()
        tS3 = tS[:, 0:512].rearrange("p (h d) -> p h d", h=4)
        self.tt("dve", tS3, X3, bc(sm[:, 24 + hg * 4:28 + hg * 4]), ALU.mult, [xk, smk], [tSk])
        o1, o1k = self.tmpf()
        o13 = o1[:, 0:512].rearrange("p (h d) -> p h d", h=4)
        self.tt("dve", o13, Y3, bc(sm[:, 16 + hg * 4:20 + hg * 4]), ALU.mult, [yk, smk], [o1k])
        yield
        r, rk = self.tmpb()
        r3 = r[:, :].rearrange("p (h d) -> p h d", h=4)
        self.tt("dve", r3, tS3, vtok[:, hsl, :], ALU.add, [tSk, vtk], [rk])
        yield
        for hh in range(4):
            cs = slice(hh * 128, (hh + 1) * 128)
            self.mm(X[:, cs], self.Uk[hg][:, hh, :], r3[:, hh, :], ["Uk%d" % hg, rk], [xk])
        yield
        vn, vk = self.tmpb()
        vn3 = vn[:, :].rearrange("p (h d) -> p h d", h=4)
        self.tt("dve", vn3, X3, bc(self.beta[:, tb, hsl]), ALU.mult, [xk, "beta"], [vk])
        yield
        for hh, h in enumerate(hs):
            cs = slice(hh * 128, (hh + 1) * 128)
            self.mm(Y[:, cs], self.Qk[hg][:, hh, :], vn3[:, hh, :], ["Qk%d" % hg, vk], [yk])
            self.mm(X[:, cs], kdec[:, h, :], vn3[:, hh, :], [kdk, vk], [xk])
        yield
        self.tt("dve", self.otok[:, hg * 512:(hg + 1) * 512], o1[:, 0:512], Y[:, :], ALU.add, [o1k, yk], ["otok"])
        self.tt("dve", self.S[:, hsl, :], self.S[:, hsl, :], bc(sm[:, 40 + hg * 4:44 + hg * 4]), ALU.mult, [Sk, smk], [Sk])
        yield
        self.tt("dve", self.S[:, hsl, :], self.S[:, hsl, :], X3, ALU.add, [Sk, xk], [Sk])
        yield
        self.cp("act", self.Sbf[:, hsl, :], self.S[:, hsl, :], [Sk], [Sbk])

    def lockstep(self, gens):
        gens = list(gens)
        while gens:
            nxt = []
            for g in gens:
                try:
                    next(g)
                    nxt.append(g)
                except StopIteration:
                    pass
            gens = nxt

    def gdn_prep(self, tb):
        A1 = self.A1
        blk = slice(tb * 128, (tb + 1) * 128)
        pb_ = tb % 2
        sm, smk = self.gsm2[pb_], "gsm%d" % pb_
        vtok, vtk = (self.vtok, self.vtok2)[pb_], "vtok%d" % pb_
        kdec, kdk = self.kdec2[pb_], "kdec%d" % pb_
        g8 = self.gtok[:, tb, :]
        self.mm(self.ps[7][:, 0:8], self.cf("ltri"), g8, ["cstf", "gtok"], ["ps7"])
        self.mm(self.ps[7][:, 8:16], self.cf("ones"), g8, ["cstf", "gtok"], ["ps7"])
        self.cp("dve", sm[:, 0:16], self.ps[7][:, 0:16], ["ps7"], [smk])
        yield
        self.act(sm[:, 16:24], sm[:, 0:8], AF.Exp, [smk], [smk])
        self.tt("dve", sm[:, 32:40], sm[:, 8:16], sm[:, 0:8], ALU.subtract, [smk], [smk])
        yield
        self.ts("dve", sm[:, 24:32], sm[:, 16:24], -1.0, ALU.mult, [smk], [smk])
        self.act(sm[:, 32:40], sm[:, 32:40], AF.Exp, [smk], [smk])
        self.act(sm[:, 40:48], sm[:, 8:16], AF.Exp, [smk], [smk])
        for (dst, dk, u0, bank) in ((kdec, kdk, 8, 6), (vtok, vtk, 16, 7)):
            for h in range(H):
                self.tr(self.psb[bank][:, h * 128:(h + 1) * 128], A1[:, u0 + h, blk], self.cb("ident"),
                        ["A1.%d" % (u0 + h), "cstb"], ["ps%d" % bank])
        yield
        self.cp("act", vtok[:].rearrange("p h d -> p (h d)"), self.psb[7][:, 0:1024], ["ps7"], [vtk])
        self.tt("dve", kdec[:], self.psb[6][:, 0:1024].rearrange("p (h d) -> p h d", h=H),
                sm[:, 32:40].unsqueeze(2).to_broadcast([128, H, 128]), ALU.mult, ["ps6", smk], [kdk])

    def gdn_prep_old(self, tb):
        A1 = self.A1
        blk = slice(tb * 128, (tb + 1) * 128)
        pb_ = tb % 2
        sm, smk = self.gsm2[pb_], "gsm%d" % pb_
        vtok, vtk = (self.vtok, self.vtok2)[pb_], "vtok%d" % pb_
        kdec, kdk = self.kdec2[pb_], "kdec%d" % pb_
        g8 = self.gtok[:, tb, :]
        for (dst, dk, u0, bank) in ((self.ktok, "ktok", 8, 5), (vtok, vtk, 16, 6)):
            for h in range(H):
                self.tr(self.psb[bank][:, h * 128:(h + 1) * 128], A1[:, u0 + h, blk], self.cb("ident"),
                        ["A1.%d" % (u0 + h), "cstb"], ["ps%d" % bank])
            self.cp("act", dst[:].rearrange("p h d -> p (h d)"), self.psb[bank][:, 0:1024], ["ps%d" % bank], [dk])
        self.mm(self.ps[7][:, 0:8], self.cf("ltri"), g8, ["cstf", "gtok"], ["ps7"])
        self.mm(self.ps[7][:, 8:16], self.cf("ones"), g8, ["cstf", "gtok"], ["ps7"])
        self.cp("dve", sm[:, 0:16], self.ps[7][:, 0:16], ["ps7"], [smk])
        self.act(sm[:, 16:24], sm[:, 0:8], AF.Exp, [smk], [smk])
        self.ts("dve", sm[:, 24:32], sm[:, 16:24], -1.0, ALU.mult, [smk], [smk])
        self.tt("dve", sm[:, 32:40], sm[:, 8:16], sm[:, 0:8], ALU.subtract, [smk], [smk])
        self.act(sm[:, 32:40], sm[:, 32:40], AF.Exp, [smk], [smk])
        self.act(sm[:, 40:48], sm[:, 8:16], AF.Exp, [smk], [smk])
        self.tt("dve", kdec[:], self.ktok[:], sm[:, 32:40].unsqueeze(2).to_broadcast([128, H, 128]), ALU.mult,
                ["ktok", smk], [kdk])

    def seq_chain(self, tb, NB):
        for hg in range(2):
            for _ in self.scan_chain(tb, hg, 6, 7):
                yield
            yield
        for _ in self.onorm_gen(tb, 128, bank=6):
            yield
        if tb + 1 < NB:
            yield
            for _ in self.gdn_prep(tb + 1):
                yield

    def gdn_all(self, NB):
        if self.debug.get("mstop") == "Y":
            nbk_ = 4 if self.debug.get("gstop") == "b4" else 3
            inv = lambda tb: [self.inv_chain(tb, hg, self.gqs[hg][0], self.gqs[hg][1], [nbk_ * hg + i for i in range(nbk_)])
                              for hg in range(2)]
            for tb in range(NB):
                if self.debug.get("gstop") == "oldprep":
                    self.gdn_prep_old(tb)
                else:
                    self.lockstep([self.gdn_prep(tb)])
                self.lockstep(inv(tb))
                self.lockstep([self.scan_chain(tb, 0, 6, 7)])
                self.lockstep([self.scan_chain(tb, 1, 6, 7)])
                self.lockstep([self.onorm_gen(tb, 128, bank=6)])
            return
        self.lockstep([self.gdn_prep(0)])
        inv = lambda tb: [self.inv_chain(tb, hg, self.gqs[hg][0], self.gqs[hg][1], [3 * hg + i for i in range(3)])
                          for hg in range(2)]
        self.lockstep(inv(0))
        for tb in range(NB):
            gens = [self.seq_chain(tb, NB)]
            if tb + 1 < NB:
                gens = inv(tb + 1) + gens
            self.lockstep(gens)

    def stage(self, k):
        return [(self.t1, "t1"), (self.t2, "t2"), (self.otok, "otok")][k]

    def load_sample_hist(self):
        for (srcd, dst, nch, nj) in ((self.sq, self.hsq, 24, 3), (self.ssc, self.hss, 8, 2)):
            for j in range(nj):
                for k in range(nch // 8):
                    t, tk = self.stage(k)
                    self.dma("aux", t[:NSAMP, :], srcd[:, j, k * 1024:(k + 1) * 1024], "hl", [], [tk])
                    for cc in range(8):
                        self.tr(self.ps[6][:, cc * NSAMP:(cc + 1) * NSAMP], t[:NSAMP, cc * 128:(cc + 1) * 128],
                                self.cf("ident")[:NSAMP, :NSAMP], [tk, "cstf"], ["ps6"])
                    self.cp("dve", dst[:, k * 8:(k + 1) * 8, j, :],
                            self.ps[6][:, 0:8 * NSAMP].rearrange("p (c b) -> p c b", c=8), ["ps6"], ["hsamp"])

    def gdn_sample(self):
        A1 = self.A1
        NS = NSAMP
        sm = self.gsm
        st = self.stat
        for (dst, dk, u0, bank) in ((self.qtok[:NS, :], "qtok", 0, 4), (self.ktok[:NS].rearrange("p h d -> p (h d)"), "ktok", 8, 5),
                                    (self.vtok[:NS].rearrange("p h d -> p (h d)"), "vtok", 16, 6)):
            for h in range(H):
                self.tr(self.psb[bank][:NS, h * 128:(h + 1) * 128], A1[:, u0 + h, 0:NS], self.cb("ident"),
                        ["A1.%d" % (u0 + h), "cstb"], ["ps%d" % bank])
            self.cp("act", dst, self.psb[bank][:NS, 0:1024], ["ps%d" % bank], [dk])
        a = sm[:NS, 0:8]
        self.act(a, self.gtok[:NS, 0, :], AF.Exp, ["gtok"], ["gsm"])
        q3 = self.qtok[:NS, :].rearrange("p (h d) -> p h d", h=H)
        t13 = self.t1[:NS, :].rearrange("p (h d) -> p h d", h=H)
        t23 = self.t2[:NS, :].rearrange("p (h d) -> p h d", h=H)
        o3 = self.otok[:NS, :].rearrange("p (h d) -> p h d", h=H)
        self.tt("dve", t13, q3, self.ktok[:NS], ALU.mult, ["qtok", "ktok"], ["t1"])
        self.P.add("dve", lambda e: e.tensor_reduce(sm[:NS, 8:16], t13, AX.X, ALU.add), ["t1"], ["gsm"])
        i16 = self.i16b[:].rearrange("p (a b) -> p a b", a=NS)
        for h in range(H):
            self.tt("dve", self.kTm[:, h, :, :], A1[:, 8 + h:9 + h, 0:NS].to_broadcast([128, NS, NS]), i16, ALU.mult,
                    ["A1.%d" % (8 + h), "i16b"], ["kTm"])
            self.tt("dve", self.qTm[:, h, :, :], A1[:, h:h + 1, 0:NS].to_broadcast([128, NS, NS]), i16, ALU.mult,
                    ["A1.%d" % h, "i16b"], ["qTm"])
        for b in range(NS):
            i3, i2 = b % 3, b % 2
            self.dma("sp", self.Sin[i3], self.sg[b].rearrange("h k v -> k h v"), "sin%d" % i3, [], ["Sin%d" % i3])
            self.cp("act" if b % 2 == 0 else "dve", self.Sinb[i2], self.Sin[i3], ["Sin%d" % i3], ["Sinb%d" % i2])
            for h in range(H):
                bk, bq = h // 4, 2 + h // 4
                cs = slice((h % 4) * 128, (h % 4 + 1) * 128)
                first = (b == 0 and h % 4 == 0)
                self.mm(self.ps[bk][:NS, cs], self.kTm[:, h, b, :], self.Sinb[i2][:, h, :], ["kTm", "Sinb%d" % i2],
                        ["ps%d" % bk], start=first, stop=(b == NS - 1))
                self.mm(self.ps[bq][:NS, cs], self.qTm[:, h, b, :], self.Sinb[i2][:, h, :], ["qTm", "Sinb%d" % i2],
                        ["ps%d" % bq], start=first, stop=(b == NS - 1))
        a_b = a.unsqueeze(2).to_broadcast([NS, H, 128])
        for half in range(2):
            hsl = slice(half * 4, half * 4 + 4)
            k3 = self.ps[half][:NS, :].rearrange("p (h d) -> p h d", h=4)
            qs3 = self.ps[2 + half][:NS, :].rearrange("p (h d) -> p h d", h=4)
            ab = a[:, hsl].unsqueeze(2).to_broadcast([NS, 4, 128])
            self.tt("dve", t13[:, hsl, :], k3, ab, ALU.mult, ["ps%d" % half, "gsm"], ["t1"])
            self.tt("dve", t13[:, hsl, :], self.vtok[:NS, hsl, :], t13[:, hsl, :], ALU.subtract, ["vtok", "t1"], ["t1"])
            self.tt("dve", t13[:, hsl, :], t13[:, hsl, :],
                    self.beta[:NS, 0, hsl].unsqueeze(2).to_broadcast([NS, 4, 128]), ALU.mult, ["t1", "beta"], ["t1"])
            self.tt("dve", t23[:, hsl, :], qs3, ab, ALU.mult, ["ps%d" % (2 + half), "gsm"], ["t2"])
            self.tt("dve", o3[:, hsl, :], t13[:, hsl, :], sm[:NS, 8 + half * 4:12 + half * 4].unsqueeze(2).to_broadcast([NS, 4, 128]),
                    ALU.mult, ["t1", "gsm"], ["otok"])
            self.tt("dve", o3[:, hsl, :], o3[:, hsl, :], t23[:, hsl, :], ALU.add, ["otok", "t2"], ["otok"])
        dbf = self.xb16
        self.cp("act", dbf[:NS, :], self.t1[:NS, :], ["t1"], ["xb16"])
        ad = self.t2[:NS, 0:128].rearrange("p (b h) -> p b h", b=NS)
        idr = self.cf("ident")[:NS, 0:NS].unsqueeze(2).to_broadcast([NS, NS, H])
        self.tt("dve", ad, a.unsqueeze(1).to_broadcast([NS, NS, H]), idr, ALU.mult, ["gsm", "cstf"], ["t2"])
        self.mm(self.ps[4][:, 0:128], self.cf("ones")[:NS, :], self.t2[:NS, 0:128], ["cstf", "t2"], ["ps4"])
        self.cp("dve", self.abc[:], self.ps[4][:, 0:128], ["ps4"], ["abc"])
        kflat = self.ktok[:NS].rearrange("p h d -> p (h d)")
        for b in range(NS):
            i2 = b % 2
            i3 = (b + 1) % 3
            self.dma("sp", self.Sin[i3], self.sg[b].rearrange("h k v -> k h v"), "sin%d" % i3, [], ["Sin%d" % i3])
            self.ts("dve", self.kmask[i2][:NS, :], kflat, self.cf("ident")[:NS, b:b + 1], ALU.mult,
                    ["ktok", "cstf"], ["kmask%d" % i2])
            for h in range(H):
                pb_ = 5 + h // 4
                cs = slice((h % 4) * 128, (h % 4 + 1) * 128)
                self.mm(self.ps[pb_][:, cs], self.kmask[i2][:NS, h * 128:(h + 1) * 128], dbf[:NS, h * 128:(h + 1) * 128],
                        ["kmask%d" % i2, "xb16"], ["ps%d" % pb_])
                self.stt(self.Sin[i3][:, h, :], self.Sin[i3][:, h, :], self.abc[:, b * 8 + h:b * 8 + h + 1],
                         self.ps[pb_][:, cs], ALU.mult, ALU.add, ["Sin%d" % i3, "abc", "ps%d" % pb_], ["Sin%d" % i3])
            self.dma("sp", self.sgs[b].rearrange("h k v -> k h v"), self.Sin[i3], "sout%d" % i3, ["Sin%d" % i3], ["sgs%d" % b])
            self.outkeys.append("sgs%d" % b)
        self.onorm_and_T(0, NS)


_CACHE = {}


WBIG_LEN = 2 * (3 * D * HID) + D * IN_W + 4 * D * D + 256 * D


def pack_wbig(weights, specs, offs, tot):
    out = np.empty((tot,), np.float32)
    for spec, (off, n) in zip(specs, offs):
        parts = []
        for (name, r0, r1, c0, c1) in spec:
            w = weights[name][r0:r1, c0:c1]
            kc = (r1 - r0) // 128
            parts.append(w.reshape(kc, 128, c1 - c0).transpose(1, 0, 2).reshape(128, kc * (c1 - c0)))
        out[off:off + 128 * n] = np.concatenate(parts, axis=1).reshape(-1)
    return out


def kernel(x_prompt, x_sample, p_prompt, p_sample, state_gdn, state_qkv_conv, state_sc_conv,
           ffn1_w_gate, ffn1_w_up, ffn1_w_down, ln1_g, ln1_b,
           w_in, w_conv_qkv, A_log, dt_bias, w_onorm, w_p_gdn, w_conv_sc, w_p_sc, w_o, ln2_g, ln2_b,
           ffn2_w_gate, ffn2_w_up, ffn2_w_down, ln3_g, ln3_b,
           w_ple_gate, w_ple_proj, ln4_g, ln4_b, _debug=None):
    f = lambda a: np.ascontiguousarray(np.asarray(a, dtype=np.float32))
    weights = {"ffn1_w_gate": f(ffn1_w_gate)[0], "ffn1_w_up": f(ffn1_w_up)[0], "ffn1_w_down": f(ffn1_w_down)[0],
               "w_in": f(w_in)[0], "w_p_gdn": f(w_p_gdn)[0], "w_p_sc": f(w_p_sc)[0], "w_o": f(w_o)[0],
               "ffn2_w_gate": f(ffn2_w_gate)[0], "ffn2_w_up": f(ffn2_w_up)[0], "ffn2_w_down": f(ffn2_w_down)[0],
               "w_ple_gate": f(w_ple_gate)[0], "w_ple_proj": f(w_ple_proj)[0]}
    bld = Builder(debug=_debug)
    bld.wbig_len = WBIG_LEN
    nc = bld.build()
    assert bld.slab_tot == WBIG_LEN or _debug, (bld.slab_tot, WBIG_LEN)
    assert bld.slab_tot <= WBIG_LEN
    wbig = np.zeros((WBIG_LEN,), np.float32)
    wbig[:bld.slab_tot] = pack_wbig(weights, bld.slab_specs, bld.slab_off, bld.slab_tot)
    lnp = np.stack([f(ln1_g)[0], f(ln1_b)[0], f(ln2_g)[0], f(ln2_b)[0], f(ln3_g)[0], f(ln3_b)[0], f(ln4_g)[0], f(ln4_b)[0]])
    wcq = np.ascontiguousarray(f(w_conv_qkv)[0].reshape(4, 24, 128).transpose(2, 1, 0).reshape(128, 96))
    wcs = np.ascontiguousarray(f(w_conv_sc)[0].reshape(3, 8, 128).transpose(2, 1, 0).reshape(128, 24))
    smallp = np.stack([f(A_log)[0], f(dt_bias)[0]])
    cst, cst2 = make_consts()
    cst2 = np.ascontiguousarray(cst2.reshape(128, -1))
    i16 = np.ascontiguousarray(np.broadcast_to(np.eye(16, dtype=np.float32).reshape(1, 256), (128, 256)))
    xp = f(x_prompt)
    xsm = f(x_sample)[:, 0, :]
    ppr = f(p_prompt)[0]
    psm = f(p_sample)[0, :, 0, :]
    sg = f(state_gdn)[0]
    sq = f(state_qkv_conv)[0]
    ssc = f(state_sc_conv)[0]
    in_maps = []
    for c in range(8):
        sl = slice(c * NSAMP, (c + 1) * NSAMP)
        in_maps.append({"x": xp[c], "pp": ppr[c], "xs": xsm[sl], "psm": psm[sl], "sg": sg[sl], "sq": sq[sl], "ssc": ssc[sl],
                        "wbig": wbig, "lnp": lnp, "wcq": wcq, "wcs": wcs, "smallp": smallp, "won": f(w_onorm)[0],
                        "cst": cst, "cst2": cst2, "i16": i16})
    ncores = (_debug or {}).get("ncores", 8)
    res = run_bass_kernel_spmd(nc, in_maps[:ncores], core_ids=list(range(ncores)))
    R = list(res.results)
    while len(R) < 8:
        R.append({k: np.zeros_like(v) for k, v in R[0].items()})
    y = np.stack([R[c]["y"] for c in range(8)])
    ys = np.concatenate([R[c]["ys"] for c in range(8)])[:, None, :]
    sgp = np.stack([R[c]["sgp"] for c in range(8)])[None]
    sqp = np.stack([R[c]["sqp"] for c in range(8)])[None]
    ssp = np.stack([R[c]["ssp"] for c in range(8)])[None]
    sgs = np.concatenate([R[c]["sgs"] for c in range(8)])[None]
    sqs = np.concatenate([R[c]["sqs"] for c in range(8)])[None]
    sss = np.concatenate([R[c]["sss"] for c in range(8)])[None]
    return (y.astype(np.float32), ys.astype(np.float32), sgp.astype(np.float32), sqp.astype(np.float32),
            ssp.astype(np.float32), sgs.astype(np.float32), sqs.astype(np.float32), sss.astype(np.float32))
```

```python
import contextlib
import numpy as np
import concourse.bass as bass
import concourse.mybir as mybir
from concourse.bass_utils import run_bass_kernel_spmd

F32 = mybir.dt.float32
BF16 = mybir.dt.bfloat16
AF = mybir.ActivationFunctionType
ALU = mybir.AluOpType
AX = mybir.AxisListType

D = 1024
SEQ = 2048
NSAMP = 16
HID = 2816
NJ = HID // 128
H = 8
QKV_W = 3072
IN_W = 9232
Z0, BETA0, A0, B0, C0, H0, GG0, GS0 = 3072, 4096, 4104, 4112, 5136, 6160, 7184, 8208
ALPHA = 2.0 ** 0.25
LN_EPS = 1e-5
RMS_EPS = 1e-6
L2_EPS = 1e-6
NTP = 512
SLOT = 4096
NSLOT = 4
NBK = 16

COMPUTE = ("pe", "act", "dve", "pool")


class _Op:
    __slots__ = ("eng", "fn", "r", "w", "key", "eidx", "kn", "waits", "done", "inc")

    def __init__(self, eng, fn, r, w, key):
        self.eng = eng
        self.fn = fn
        self.r = r
        self.w = w
        self.key = key
        self.eidx = -1
        self.kn = 0
        self.waits = []
        self.done = None
        self.inc = False


class Prog:
    def __init__(self, nc):
        self.nc = nc
        self.ops = []

    def add(self, eng, fn, r=(), w=(), key=None):
        self.ops.append(_Op(eng, fn, tuple(r), tuple(w), key))

    def finalize(self):
        last_w = {}
        readers = {}
        issue = {e: {} for e in ("pe", "act", "dve", "pool", "sp")}
        ecount = {e: 0 for e in issue}
        kcount = {}
        kops = {}
        eops = {e: [] for e in issue}
        for op in self.ops:
            e = op.eng
            deps = set()
            for res in op.r:
                lw = last_w.get(res)
                if lw is not None:
                    deps.add(lw)
            for res in op.w:
                lw = last_w.get(res)
                if lw is not None:
                    deps.add(lw)
                for rd in readers.get(res, ()):
                    deps.add(rd)
            deps.discard(op)
            clock = issue[e]
            if op.key is None:
                op.eidx = ecount[e]
                ecount[e] += 1
                eops[e].append(op)
            else:
                n = kcount.get(op.key, 0) + 1
                kcount[op.key] = n
                op.kn = n
                kops.setdefault(op.key, []).append(op)
                if n > 1:
                    deps.add(kops[op.key][n - 2])
            best = {}
            dma_deps = []
            for d in deps:
                if d.key is None:
                    b = best.get(d.eng)
                    if b is None or d.eidx > b.eidx:
                        best[d.eng] = d
                else:
                    dma_deps.append(d)
            newclock = None
            for f, d in best.items():
                if clock.get(f, -1) >= d.eidx:
                    continue
                if f == e and op.key is None:
                    if e == "pe":
                        continue
                    if e != "pool" and (op.eidx - d.eidx) > 12:
                        continue
                op.waits.append(("c", f, d.eidx))
                d.inc = True
                if newclock is None:
                    newclock = dict(clock)
                for k, v in d.done.items():
                    if newclock.get(k, -1) < v:
                        newclock[k] = v
            for d in dma_deps:
                kk = ("dma", d.key)
                cur = clock if newclock is None else newclock
                if cur.get(kk, 0) >= d.kn:
                    continue
                op.waits.append(("d", d.key, d.kn))
                if newclock is None:
                    newclock = dict(clock)
                for k, v in d.done.items():
                    if newclock.get(k, -1) < v:
                        newclock[k] = v
            if newclock is not None:
                issue[e] = newclock
                clock = newclock
            done = dict(clock)
            if op.key is None:
                done[e] = op.eidx
            else:
                done[("dma", op.key)] = op.kn
            op.done = done
            for res in op.r:
                readers.setdefault(res, []).append(op)
            for res in op.w:
                last_w[res] = op
                readers[res] = []
        self.rank = {}
        for e, lst in eops.items():
            k = 0
            for op in lst:
                if op.inc:
                    k += 1
                    self.rank[(e, op.eidx)] = k
        self.keys = list(kcount.keys())
        for op in self.ops:
            op.done = None
            if len(op.waits) > 1:
                m = {}
                for t, a, b in op.waits:
                    if (t, a) not in m or m[(t, a)] < b:
                        m[(t, a)] = b
                op.waits = [(t, a, b) for (t, a), b in m.items()]

    def emit(self, es):
        nc = self.nc
        sems = {}
        for e in COMPUTE:
            sems[e] = es.enter_context(nc.semaphore("s_" + e))
        ksem = {}
        for k in self.keys:
            ksem[k] = es.enter_context(nc.semaphore("k_" + str(k)))
        block = es.enter_context(nc.Block())
        rank = self.rank

        def run(ename, eng):
            for op in self.ops:
                if op.eng != ename:
                    continue
                for t, a, b in op.waits:
                    if t == "c":
                        eng.wait_ge(sems[a], rank[(a, b)])
                    else:
                        eng.wait_ge(ksem[a], 16 * b)
                if op.fn is None:
                    continue
                ins = op.fn(eng)
                if op.key is not None:
                    ins.then_inc(ksem[op.key], 16)
                elif op.inc:
                    ins.then_inc(sems[ename], 1)

        @block.tensor
        def _(eng):
            run("pe", eng)

        @block.scalar
        def _(eng):
            run("act", eng)

        @block.vector
        def _(eng):
            run("dve", eng)

        @block.gpsimd
        def _(eng):
            run("pool", eng)

        @block.sync
        def _(eng):
            run("sp", eng)


CSTF_NAMES = ["ident", "ltri", "su", "muincl", "ones"]
CSTB_NAMES = ["ident", "ones", "mndn", "mndtn", "me16", "me16t", "me32", "me32t", "me64", "me64t"]


def make_consts():
    i = np.arange(128)[:, None]
    j = np.arange(128)[None, :]
    c = {}
    c["ident"] = (i == j)
    c["ltri"] = (i <= j)
    c["su"] = (i > j)
    c["muincl"] = (j >= i)
    c["ones"] = np.ones((128, 128), bool)
    nd = (i // NBK == j // NBK) & (i > j)
    c["mndn"] = -1.0 * nd
    c["mndtn"] = -1.0 * nd.T
    for b in (16, 32, 64):
        e = (i // (2 * b) == j // (2 * b)) & ((i % (2 * b)) >= b) & ((j % (2 * b)) < b)
        c["me%d" % b] = e
        c["me%dt" % b] = e.T
    arrf = np.stack([np.asarray(c[n], np.float32) for n in CSTF_NAMES], axis=1)
    arrb = np.stack([np.asarray(c[n], np.float32) for n in CSTB_NAMES], axis=1)
    return np.ascontiguousarray(arrf), np.ascontiguousarray(arrb)


class Builder:
    def __init__(self, debug=None):
        self.debug = debug or {}
        self.slab_specs = []
        self.slab_off = []
        self.slab_tot = 0
        self.nslab_pass = None

    def mm(self, out, lhsT, rhs, r, w, start=True, stop=True):
        self.P.add("pe", lambda e: e.matmul(out, lhsT, rhs, start=start, stop=stop), r, w)

    def tr(self, out, in_, ident, r, w):
        self.P.add("pe", lambda e: e.transpose(out, in_, ident), r, w)

    def act(self, out, in_, func, r, w, bias=None, scale=None):
        kw = {}
        if bias is not None:
            kw["bias"] = bias
        if scale is not None:
            kw["scale"] = scale
        self.P.add("act", lambda e: e.activation(out, in_, func, **kw), r, w)

    def tt(self, eng, out, in0, in1, op, r, w):
        self.P.add(eng, lambda e: e.tensor_tensor(out, in0, in1, op), r, w)

    def ts(self, eng, out, in0, s1, op0, r, w, s2=None, op1=None):
        if op1 is None:
            self.P.add(eng, lambda e: e.tensor_scalar(out, in0, s1, None, op0), r, w)
        else:
            self.P.add(eng, lambda e: e.tensor_scalar(out, in0, s1, s2, op0, op1), r, w)

    def stt(self, out, in0, scalar, in1, op0, op1, r, w):
        self.P.add("dve", lambda e: e.scalar_tensor_tensor(out, in0, scalar, in1, op0, op1), r, w)

    def cp(self, eng, out, in_, r, w):
        if eng == "act":
            self.P.add("act", lambda e: e.activation(out, in_, AF.Copy), r, w)
        else:
            self.P.add(eng, lambda e: e.tensor_copy(out, in_), r, w)

    def dq(self):
        return "sp" if self.recording else "pool"

    def dma(self, eng, out, in_, key, r, w, slow=False):
        if eng == "aux":
            eng = self.dq()
        if slow:
            self.P.add(eng, lambda e: e.dma_start(out=out, in_=in_, allow_slow_non_contiguous=True), r, w, key=key)
        else:
            self.P.add(eng, lambda e: e.dma_start(out=out, in_=in_), r, w, key=key)

    def slab(self, spec):
        if self.recording:
            self.slab_specs.append(spec)
            n = sum(((r1 - r0) // 128) * (c1 - c0) for (_, r0, r1, c0, c1) in spec)
            assert n <= SLOT, n
            self.slab_off.append((self.slab_tot, n))
            self.slab_tot += 128 * n
        si = self.slab_i % self.nslab_pass if self.nslab_pass else self.slab_i
        off, n = self.slab_off[si]
        slot = self.slab_i % NSLOT
        self.slab_i += 1
        t = self.wring[slot]
        key = "w%d" % slot
        scr = self.wscr[off:off + 128 * n].rearrange("(p n) -> p n", p=128)
        if self.recording:
            src = self.wbig[off:off + 128 * n].rearrange("(p n) -> p n", p=128)
            self.dma("pool", t[:, 0:n], src, key, r=[], w=[key])
            if self.debug.get("npass", 4) > 1 or not self.debug.get("nosample", False):
                self.dma("sp", scr, t[:, 0:n], "wb%d" % slot, r=[key], w=["wscr%d" % si])
        else:
            self.dma("sp", t[:, 0:n], scr, key, r=["wscr%d" % si], w=[key])
        views = []
        o = 0
        for (_, r0, r1, c0, c1) in spec:
            kc = (r1 - r0) // 128
            nc_ = c1 - c0
            views.append(t[:, o:o + kc * nc_].rearrange("p (k n) -> p k n", k=kc))
            o += kc * nc_
        return views, key

    def build(self):
        nc = bass.Bass("TRN2", target_bir_lowering=False)
        self.nc = nc
        self.es = contextlib.ExitStack()
        with self.es:
            self._build_inner()
        return nc

    def dram_in(self, name, shape, dt=F32):
        return self.nc.dram_tensor(name, list(shape), dt, kind="ExternalInput").ap()

    def dram_out(self, name, shape, dt=F32):
        return self.nc.dram_tensor(name, list(shape), dt, kind="ExternalOutput").ap()

    def sb(self, name, shape, dt):
        return self.es.enter_context(self.nc.sbuf_tensor(name, list(shape), dt))

    def _build_inner(self):
        nc = self.nc
        self.P = Prog(nc)
        P = self.P
        self.x = self.dram_in("x", [SEQ, D])
        self.pp = self.dram_in("pp", [SEQ, 256])
        self.xs = self.dram_in("xs", [NSAMP, D])
        self.psm = self.dram_in("psm", [NSAMP, 256])
        self.sg = self.dram_in("sg", [NSAMP, H, 128, 128])
        self.sq = self.dram_in("sq", [NSAMP, 3, QKV_W])
        self.ssc = self.dram_in("ssc", [NSAMP, 2, D])
        self.wbig = self.dram_in("wbig", [self.wbig_len])
        self.wscr = self.nc.dram_tensor("wscr", [self.wbig_len], BF16, kind="Internal").ap()
        self.lnp = self.dram_in("lnp", [8, D])
        self.wcq_d = self.dram_in("wcq", [128, 24 * 4])
        self.wcs_d = self.dram_in("wcs", [128, 8 * 3])
        self.smallp = self.dram_in("smallp", [2, 8])
        self.won_d = self.dram_in("won", [128])
        self.cst_d = self.dram_in("cst", [128, len(CSTF_NAMES), 128])
        self.cst2_d = self.dram_in("cst2", [128, len(CSTB_NAMES) * 128])
        self.i16_d = self.dram_in("i16", [128, 256])
        self.y = self.dram_out("y", [SEQ, D])
        self.ys = self.dram_out("ys", [NSAMP, D])
        self.sgp = self.dram_out("sgp", [H, 128, 128])
        self.sqp = self.dram_out("sqp", [3, QKV_W])
        self.ssp = self.dram_out("ssp", [2, D])
        self.sgs = self.dram_out("sgs", [NSAMP, H, 128, 128])
        self.sqs = self.dram_out("sqs", [NSAMP, 3, QKV_W])
        self.sss = self.dram_out("sss", [NSAMP, 2, D])
        self.outkeys = []

        sb = self.sb
        self.wring = [sb("wr%d" % i, [128, SLOT], BF16) for i in range(NSLOT)]
        self.xres = sb("xres", [128, 4, D], F32)
        self.xT = sb("xT", [128, 8, NTP], BF16)
        self.gbt = sb("gbt", [128, 2, D], F32)
        self.cstf = sb("cstf", [128, len(CSTF_NAMES), 128], F32)
        self.cstb = sb("cstb", [128, len(CSTB_NAMES), 128], BF16)
        self.i16b = sb("i16b", [128, 256], BF16)
        self.wcq = sb("wcq_s", [128, 24, 4], F32)
        self.wcs = sb("wcs_s", [128, 8, 3], F32)
        self.wonb = sb("wonb", [128, 128], F32)
        self.smallb = sb("smallb", [128, 16], F32)
        self.negA = sb("negA", [128, 8], F32)
        self.histq = sb("histq", [128, 24, 3], F32)
        self.hists = sb("hists", [128, 8, 2], F32)
        self.S = sb("S", [128, H, 128], F32)
        self.Sbf = sb("Sbf", [128, H, 128], BF16)
        self.A1 = sb("A1", [128, 24, NTP], BF16)
        self.ztok = sb("ztok", [128, 4, D], BF16)
        self.ktok = sb("ktok", [128, H, 128], BF16)
        self.vtok = sb("vtok", [128, H, 128], BF16)
        self.batok = sb("batok", [128, 4, 16], F32)
        self.beta = sb("beta", [128, 4, 8], F32)
        self.gtok = sb("gtok", [128, 4, 8], F32)
        self.tf = [sb("tf%d" % i, [128, NTP + 4], F32) for i in range(11)]
        self.tfi = 0
        self.tb16 = [sb("tb%d" % i, [128, NTP], BF16) for i in range(4)]
        self.tbi = 0
        self.t1 = sb("t1", [128, D], F32)
        self.t2 = sb("t2", [128, D], F32)
        self.otok = sb("otok", [128, D], F32)
        self.xb16 = sb("xb16", [128, D], BF16)
        self.stat = sb("stat", [128, 4, 16], F32)
        self.stat3 = sb("stat3", [128, 16], F32)
        self.xb16b = sb("xb16b", [128, D], BF16)
        self.pT = sb("pT", [128, 2, NTP], BF16)
        self.pb = sb("pb", [128, 256], BF16)
        GQ = [("decTm", F32), ("Lg", F32), ("qkTm", BF16), ("MT", BF16), ("M", BF16),
              ("Na", BF16), ("Nb", BF16), ("Nc", BF16), ("Nd", BF16), ("Pa", BF16), ("Pb", BF16),
              ("Pc", BF16), ("Pd", BF16)]
        self.gqs = []
        self.arenaA = sb("arenaA", [128, 15 * 512], BF16)
        self.arenaB = sb("arenaB", [128, 15 * 512], BF16)
        self.gq_names = [nm for nm, _ in GQ]
        for ar, pfx in ((self.arenaA, "gA_"), (self.arenaB, "gB_")):
            gX = {}
            o = 0
            for nm, dt in GQ:
                n = 1024 if dt == F32 else 512
                v = ar[:, o:o + n]
                if dt == F32:
                    v = v.bitcast(F32)
                gX[nm] = v.rearrange("p (h d) -> p h d", h=4)
                o += n
            self.gqs.append((gX, pfx))
        self.kdec2 = [sb("kdec%d" % i, [128, H, 128], BF16) for i in range(2)]
        self.gsm2 = [sb("gsm%d" % i, [128, 64], F32) for i in range(2)]
        self.vtok2 = sb("vtok2", [128, H, 128], BF16)
        self.Uk = [sb("Uk%d" % i, [128, 4, 128], BF16) for i in range(2)]
        self.Qk = [sb("Qk%d" % i, [128, 4, 128], BF16) for i in range(2)]
        self.gsm = self.gsm2[0]
        self.kdec = self.kdec2[0]
        fA = lambda o, n: self.arenaA[:, o:o + n]
        fB = lambda o, n: self.arenaB[:, o:o + n]
        self.Sin = [fA(k * 2048, 2048).bitcast(F32).rearrange("p (h d) -> p h d", h=H) for k in range(3)]
        self.hss = fA(6144, 512).bitcast(F32).rearrange("p (c j b) -> p c j b", c=8, j=2)
        self.Sinb = [fA(6656, 1024).rearrange("p (h d) -> p h d", h=H), fB(6400, 1024).rearrange("p (h d) -> p h d", h=H)]
        self.kTm = fB(0, 2048).rearrange("p (h a b) -> p h a b", h=H, a=NSAMP)
        self.qTm = fB(2048, 2048).rearrange("p (h a b) -> p h a b", h=H, a=NSAMP)
        self.hsq = fB(4096, 2304).bitcast(F32).rearrange("p (c j b) -> p c j b", c=24, j=3)
        self.kmask = [sb("kmask%d" % i, [NSAMP, D], BF16) for i in range(2)]
        self.qtok = sb("qtok", [NSAMP, D], BF16)
        self.abc = sb("abc", [128, 128], F32)
        self.ps = [self.es.enter_context(nc.psum_tensor("ps%d" % i, [128, 512], F32)) for i in range(8)]
        self.psb = [p.bitcast(BF16) for p in self.ps]

        self.recording = True
        self.slab_i = 0
        self.setup()
        npass = self.debug.get("npass", 4)
        self.prologue_done = False
        do_sample = not self.debug.get("nosample", False)
        for pi in range(npass):
            if pi + 1 < npass:
                nsrc = (self.x[(pi + 1) * NTP:(pi + 2) * NTP, :], 4, 128)
            elif do_sample:
                nsrc = (self.xs, 1, NSAMP)
            else:
                nsrc = None
            if self.debug.get("stop"):
                nsrc = None
            self.layer_pass(pi, NTP, sample=False, last=(pi == npass - 1), next_src=nsrc)
            if pi == 0:
                self.recording = False
                self.nslab_pass = len(self.slab_off)
        if not self.debug.get("nosample", False):
            gbk = [p + nm for p in ("gA_", "gB_") for nm in self.gq_names]
            gbk += ["gsm0", "gsm1", "vtok0", "vtok1", "kdec0", "kdec1"]
            P.add("dve", lambda e: e.memset(self.gsm[:, 60:64], 0.0), gbk,
                  gbk + ["kTm", "qTm", "hsamp", "Sin0", "Sin1", "Sin2", "Sinb0", "Sinb1", "gsm", "vtok"])
            self.layer_pass(0, NSAMP, sample=True, last=True)
        P.add("sp", None, r=self.outkeys)
        P.finalize()
        P.emit(self.es)

    def cf(self, name):
        return self.cstf[:, CSTF_NAMES.index(name), :]

    def cb(self, name):
        return self.cstb[:, CSTB_NAMES.index(name), :]

    def cb4(self, name):
        i = CSTB_NAMES.index(name)
        return self.cstb[:, i:i + 1, :].to_broadcast([128, 4, 128])

    def cf4(self, name):
        i = CSTF_NAMES.index(name)
        return self.cstf[:, i:i + 1, :].to_broadcast([128, 4, 128])

    def tmpf(self):
        i = self.tfi % len(self.tf)
        self.tfi += 1
        return self.tf[i], "tf%d" % i

    def tmpb(self):
        i = self.tbi % len(self.tb16)
        self.tbi += 1
        return self.tb16[i], "tb%d" % i

    def setup(self):
        d = self.dma
        d("sp", self.cstf[:], self.cst_d, "c0", [], ["cstf"])
        nb = len(CSTB_NAMES) * 128
        for k in range(0, nb, 1024):
            n = min(1024, nb - k)
            d("sp", self.t1[:, 0:n], self.cst2_d[:, k:k + n], "c1", [], ["t1"])
            self.cp("dve", self.cstb[:].rearrange("p c d -> p (c d)")[:, k:k + n], self.t1[:, 0:n], ["t1"], ["cstb"])
        d("sp", self.t2[:, 0:256], self.i16_d, "c1", [], ["t2"])
        self.cp("dve", self.i16b[:], self.t2[:, 0:256], ["t2"], ["i16b"])
        d("sp", self.wcq[:].rearrange("p c j -> p (c j)"), self.wcq_d, "c2", [], ["wcq"])
        d("sp", self.wcs[:].rearrange("p c j -> p (c j)"), self.wcs_d, "c3", [], ["wcs"])
        d("sp", self.wonb[:], self.won_d.partition_broadcast(128), "c4", [], ["wonb"])
        d("sp", self.smallb[:], self.smallp.rearrange("a b -> (a b)").partition_broadcast(128), "c5", [], ["smallb"])
        self.act(self.negA[:], self.smallb[:, 0:8], AF.Exp, ["smallb"], ["negA"])
        self.ts("dve", self.negA[:], self.negA[:], -1.0, ALU.mult, ["negA"], ["negA"])
        self.P.add("dve", lambda e: e.memset(self.S[:], 0.0), [], ["S0", "S1"])
        self.P.add("dve", lambda e: e.memset(self.Sbf[:], 0.0), [], ["Sbf0", "Sbf1"])
        self.P.add("dve", lambda e: e.memset(self.histq[:], 0.0), [], ["histq"])
        self.P.add("dve", lambda e: e.memset(self.hists[:], 0.0), [], ["hists"])

    def make_xT(self, tb, TB, bank, staged=False):
        ps, psk = self.psb[bank], "ps%d" % bank
        xb, xbk = ((self.xb16, "xb16"), (self.xb16b, "xb16b"))[tb % 2]
        if not staged:
            self.cp("act", xb[:TB, :], self.xres[:TB, tb, :], ["xres%d" % tb], [xbk])
        for c in range(8):
            self.tr(ps[:, c * TB:(c + 1) * TB], xb[:TB, c * 128:(c + 1) * 128], self.cb("ident")[:TB, :TB],
                    [xbk, "cstb"], [psk])
        self.cp("dve", self.xT[:, :, tb * TB:(tb + 1) * TB],
                ps[:, 0:8 * TB].rearrange("p (c t) -> p c t", c=8), [psk], ["xT%d" % tb])

    def layer_norm(self, idx, NB, TB, final_out=None):
        self.dma("aux", self.gbt[:, 0, :], self.lnp[2 * idx, :].partition_broadcast(128), "gb0", [], ["gbt0"])
        self.dma("aux", self.gbt[:, 1, :], self.lnp[2 * idx + 1, :].partition_broadcast(128), "gb1", [], ["gbt1"])
        eps = LN_EPS / (ALPHA * ALPHA)
        st = self.stat
        for tb in range(NB):
            xr = self.xres[:TB, tb, :]
            xk = "xres%d" % tb
            self.P.add("dve", lambda e, xr=xr, tb=tb: e.bn_stats(st[:TB, tb, 0:6], xr[:, 0:512]), [xk], ["stat"])
            self.P.add("dve", lambda e, xr=xr, tb=tb: e.bn_stats(st[:TB, tb, 6:12], xr[:, 512:1024]), [xk], ["stat"])
            self.P.add("dve", lambda e, tb=tb: e.bn_aggr(st[:TB, tb, 12:14], st[:TB, tb, 0:12]), ["stat"], ["stat"])
        self.act(st[:TB, 0:NB, 14], st[:TB, 0:NB, 13], AF.Sqrt, ["stat"], ["stat2"], bias=eps)
        self.P.add("dve", lambda e: e.reciprocal(st[:TB, 0:NB, 15], st[:TB, 0:NB, 14]), ["stat2"], ["stat2"])
        for tb in range(NB):
            xr = self.xres[:TB, tb, :]
            xk = "xres%d" % tb
            self.stt(self.t1[:TB, :], xr, st[:TB, tb, 12:13], self.gbt[:TB, 0, :], ALU.subtract, ALU.mult,
                     [xk, "stat", "gbt0"], ["t1"])
            self.stt(xr, self.t1[:TB, :], st[:TB, tb, 15:16], self.gbt[:TB, 1, :], ALU.mult, ALU.add,
                     ["t1", "stat2", "gbt1"], [xk])
            if final_out is not None:
                ok = final_out[1] + str(tb)
                self.dma("aux", final_out[0][tb * TB:(tb + 1) * TB, :], xr, "yo%d" % tb, [xk], [ok])
                self.outkeys.append(ok)
            else:
                self.make_xT(tb, TB, (2 * tb) % 8)

    def ffn(self, pfx, NB, TB):
        NT = NB * TB
        for j0 in range(0, NJ, 2):
            (wg, wu), wk = self.slab([(pfx + "_w_gate", 0, D, j0 * 128, j0 * 128 + 256),
                                      (pfx + "_w_up", 0, D, j0 * 128, j0 * 128 + 256)])
            for jj in range(2):
                j = j0 + jj
                bg, bu = 2 * (j % 2), 2 * (j % 2) + 1
                for kc in range(8):
                    self.mm(self.ps[bg][:, :NT], wg[:, kc, jj * 128:(jj + 1) * 128], self.xT[:, kc, :NT],
                            [wk, "xT0", "xT1", "xT2", "xT3"], ["ps%d" % bg], start=(kc == 0), stop=(kc == 7))
                for kc in range(8):
                    self.mm(self.ps[bu][:, :NT], wu[:, kc, jj * 128:(jj + 1) * 128], self.xT[:, kc, :NT],
                            [wk, "xT0", "xT1", "xT2", "xT3"], ["ps%d" % bu], start=(kc == 0), stop=(kc == 7))
                t, tk = self.tmpf()
                self.act(t[:, :NT], self.ps[bg][:, :NT], AF.Silu, ["ps%d" % bg], [tk])
                self.tt("dve", self.A1[:, j, :NT], t[:, :NT], self.ps[bu][:, :NT], ALU.mult,
                        [tk, "ps%d" % bu], ["A1.%d" % j])
        for j0 in range(0, NJ, 4):
            j1 = min(NJ, j0 + 4)
            (wd,), wk = self.slab([(pfx + "_w_down", j0 * 128, j1 * 128, 0, D)])
            for jj in range(j1 - j0):
                j = j0 + jj
                for tb in range(NB):
                    for nh in range(2):
                        b = tb * 2 + nh
                        self.mm(self.ps[b][:TB, :], self.A1[:, j, tb * TB:(tb + 1) * TB], wd[:, jj, nh * 512:(nh + 1) * 512],
                                [wk, "A1.%d" % j], ["ps%d" % b], start=(j == 0), stop=(j == NJ - 1))
        c = 0.5 / ALPHA
        for tb in range(NB):
            for nh in range(2):
                b = tb * 2 + nh
                xr = self.xres[:TB, tb, nh * 512:(nh + 1) * 512]
                self.stt(xr, self.ps[b][:TB, :], c, xr, ALU.mult, ALU.add, ["ps%d" % b, "xres%d" % tb], ["xres%d" % tb])

    def prologue(self, src, NB, TB):
        for tb in range(NB):
            xb, xbk = ((self.xb16, "xb16"), (self.xb16b, "xb16b"))[tb % 2]
            self.dma("pool", xb[:TB, :], src[tb * TB:(tb + 1) * TB, :], "xc%d" % (tb % 2), [], [xbk])
            self.make_xT(tb, TB, tb % 8, staged=True)

    def layer_pass(self, pi, NT, sample, last, next_src=None):
        TB = min(128, NT)
        NB = NT // TB
        self.slab_i = 0 if self.recording else self.slab_i
        if not sample:
            src, psrc = self.x[pi * NT:(pi + 1) * NT, :], self.pp[pi * NT:(pi + 1) * NT, :]
            yout = (self.y[pi * NT:(pi + 1) * NT, :], "y%d_" % pi)
        else:
            src, psrc = self.xs, self.psm
            yout = (self.ys, "ys_")
        if not self.prologue_done:
            self.prologue(src, NB, TB)
        self.prologue_done = False
        for tb in range(NB):
            self.dma("aux", self.xres[:TB, tb, :], src[tb * TB:(tb + 1) * TB, :], "x%d" % tb, [], ["xres%d" % tb])
        stop = self.debug.get("stop")
        self.ffn("ffn1", NB, TB)
        if stop == "ffn1":
            return self.dump(yout, NB, TB)
        self.layer_norm(0, NB, TB)
        if stop == "ln1":
            return self.dump(yout, NB, TB)
        self.mixers(pi, NB, TB, sample, last)
        if stop == "mix":
            return self.dump(yout, NB, TB)
        self.layer_norm(1, NB, TB)
        self.ffn("ffn2", NB, TB)
        self.layer_norm(2, NB, TB)
        if stop == "ln3":
            return self.dump(yout, NB, TB)
        self.ple(psrc, NB, TB)
        if next_src is not None:
            self.prologue(*next_src)
            self.prologue_done = True
        self.layer_norm(3, NB, TB, final_out=yout)

    def dump(self, yout, NB, TB):
        for tb in range(NB):
            ok = yout[1] + str(tb)
            self.dma("aux", yout[0][tb * TB:(tb + 1) * TB, :], self.xres[:TB, tb, :], "yo%d" % tb, ["xres%d" % tb], [ok])
            self.outkeys.append(ok)

    def ple(self, psrc, NB, TB):
        NT = NB * TB
        for tb in range(NB):
            pf, pfk = self.tmpf()
            self.dma("aux", pf[:TB, 0:256], psrc[tb * TB:(tb + 1) * TB, :], "pf", [], [pfk])
            self.cp("act", self.pb[:TB, :], pf[:TB, 0:256], [pfk], ["pb"])
            for c in range(2):
                self.tr(self.psb[7][:, c * TB:(c + 1) * TB], self.pb[:TB, c * 128:(c + 1) * 128],
                        self.cb("ident")[:TB, :TB], ["pb", "cstb"], ["ps7"])
            self.cp("dve", self.pT[:, :, tb * TB:(tb + 1) * TB],
                    self.psb[7][:, 0:2 * TB].rearrange("p (c t) -> p c t", c=2), ["ps7"], ["pT"])
        for nh in range(2):
            (wg,), wgk = self.slab([("w_ple_gate", 0, D, nh * 512, (nh + 1) * 512)])
            (wp,), wpk = self.slab([("w_ple_proj", 0, 256, nh * 512, (nh + 1) * 512)])
            for tb in range(NB):
                bg, bp = 2 * (tb % 2), 2 * (tb % 2) + 1
                for kc in range(8):
                    self.mm(self.ps[bg][:TB, :], self.xT[:, kc, tb * TB:(tb + 1) * TB], wg[:, kc, :],
                            [wgk, "xT0", "xT1", "xT2", "xT3"], ["ps%d" % bg], start=(kc == 0), stop=(kc == 7))
                for kc in range(2):
                    self.mm(self.ps[bp][:TB, :], self.pT[:, kc, tb * TB:(tb + 1) * TB], wp[:, kc, :],
                            [wpk, "pT"], ["ps%d" % bp], start=(kc == 0), stop=(kc == 1))
                t, tk = self.tmpf()
                self.act(t[:TB, :512], self.ps[bg][:TB, :], AF.Sigmoid, ["ps%d" % bg], [tk])
                self.tt("dve", t[:TB, :512], t[:TB, :512], self.ps[bp][:TB, :], ALU.mult, [tk, "ps%d" % bp], [tk])
                xr = self.xres[:TB, tb, nh * 512:(nh + 1) * 512]
                self.stt(xr, t[:TB, :512], 1.0 / ALPHA, xr, ALU.mult, ALU.add, [tk, "xres%d" % tb], ["xres%d" % tb])

    def conv_chunk(self, psbank, NT, taps_hist, wts, ntap, hist_tile, hist_key, sample, src_is_psum=True, src=None):
        H_ = ntap - 1
        cbt, cbk = self.tmpf()
        if src_is_psum:
            self.cp("act", cbt[:, H_:H_ + NT], self.ps[psbank][:, :NT], ["ps%d" % psbank], [cbk])
        else:
            src(cbt[:, H_:H_ + NT], cbk)
        if not sample:
            self.cp("dve", cbt[:, 0:H_], hist_tile, [hist_key], [cbk])
            self.cp("dve", hist_tile, cbt[:, NT:NT + H_], [cbk], [hist_key])
            taps = [cbt[:, j:j + NT] for j in range(ntap)]
            tr_ = [cbk]
        else:
            taps = [taps_hist[j] for j in range(H_)] + [cbt[:, H_:H_ + NT]]
            tr_ = [cbk, "hsamp"]
        acc, ak = self.tmpf()
        self.ts("dve", acc[:, :NT], taps[0], wts[0], ALU.mult, tr_ + ["wc"], [ak])
        for j in range(1, ntap):
            self.stt(acc[:, :NT], taps[j], wts[j], acc[:, :NT], ALU.mult, ALU.add, tr_ + ["wc", ak], [ak])
        return acc, ak, cbt, cbk

    def mixers(self, pi, NB, TB, sample, last):
        NT = NB * TB
        A1 = self.A1
        if sample:
            self.load_sample_hist()
        def finish(grp):
            for (c, so, sk, sq, sqk, cbt, cbk) in grp:
                if sample:
                    self.tr(self.ps[6][:NT, (c % 4) * 128:(c % 4 + 1) * 128], cbt[:, 3:3 + NT], self.cf("ident"),
                            [cbk, "cstf"], ["ps6"])
                    if c % 4 == 3:
                        stg, stk = self.stage(c // 8)
                        self.cp("act", stg[:NT, (c % 8 - 3) * 128:(c % 8 + 1) * 128], self.ps[6][:NT, :], ["ps6"], [stk])
            qk = [g_ for g_ in grp if g_[0] < 16]
            sds = []
            for (c, so, sk, sq, sqk, cbt, cbk) in qk:
                b2 = 4 + c % 2 if sample else 4 + c % 4
                self.mm(self.ps[b2][:, :NT], self.cb("ones"), sq[:, :NT], [sqk, "cstb"], ["ps%d" % b2])
            for (c, so, sk, sq, sqk, cbt, cbk) in qk:
                b2 = 4 + c % 2 if sample else 4 + c % 4
                sd, sdk = self.tmpf()
                sds.append((sd, sdk))
                self.act(sd[:, :NT], self.ps[b2][:, :NT], AF.Ln, ["ps%d" % b2], [sdk], bias=L2_EPS)
            for (sd, sdk) in sds:
                self.act(sd[:, :NT], sd[:, :NT], AF.Exp, [sdk], [sdk], scale=-0.5)
            for (c, so, sk, sq, sqk, cbt, cbk), (sd, sdk) in zip(qk, sds):
                const = 128.0 ** -0.5 if c < 8 else 1.0
                self.stt(A1[:, c, :NT], so[:, :NT], const, sd[:, :NT], ALU.mult, ALU.mult, [sk, sdk], ["A1.%d" % c])

        pend = None
        for g in range(6):
            (wq,), wk = self.slab([("w_in", 0, D, g * 512, (g + 1) * 512)])
            for pr in range(2):
                cs_ = [g * 4 + pr * 2, g * 4 + pr * 2 + 1]
                for c in cs_:
                    jj = c % 4
                    bank = c % 4
                    for kc in range(8):
                        self.mm(self.ps[bank][:, :NT], wq[:, kc, jj * 128:(jj + 1) * 128], self.xT[:, kc, :NT],
                                [wk, "xT0", "xT1", "xT2", "xT3"], ["ps%d" % bank], start=(kc == 0), stop=(kc == 7))
                convs = []
                for c in cs_:
                    th = [self.hsq[:, c, j, :] for j in range(3)] if sample else None
                    wts = [self.wcq[:, c, j:j + 1] for j in range(4)]
                    convs.append(self.conv_chunk(c % 4, NT, th, wts, 4, self.histq[:, c, :], "histq%d" % c, sample))
                if pend is not None:
                    finish(pend)
                cur = []
                for c, (acc, ak, cbt, cbk) in zip(cs_, convs):
                    if c >= 16:
                        self.act(A1[:, c, :NT], acc[:, :NT], AF.Silu, [ak], ["A1.%d" % c])
                        cur.append((c, None, None, None, None, cbt, cbk))
                    else:
                        self.act(acc[:, :NT], acc[:, :NT], AF.Silu, [ak], [ak])
                        sq, sqk = self.tmpb()
                        if self.recording:
                            self.act(sq[:, :NT], acc[:, :NT], AF.Square, [ak], [sqk])
                        else:
                            self.tt("pool", sq[:, :NT], acc[:, :NT], acc[:, :NT], ALU.mult, [ak], [sqk])
                        cur.append((c, acc, ak, sq, sqk, cbt, cbk))
                pend = cur
        finish(pend)
        if sample:
            for k in range(3):
                stg, stk = self.stage(k)
                self.dma("aux", self.sqs[:, 2, k * 1024:(k + 1) * 1024], stg[:NSAMP, :], "so0", [stk], ["sqs2_%d" % k])
                self.outkeys.append("sqs2_%d" % k)
            self.dma("aux", self.sqs[:, 0:2, :], self.sq[:, 1:3, :], "so1", [], ["sqs01"])
            self.outkeys += ["sqs01"]
        elif last:
            for j in range(3):
                self.dma("aux", self.sqp[j, :].rearrange("(c p) -> p c", p=128), self.histq[:, :, j], "so0",
                         ["histq%d" % c for c in range(24)], ["sqp%d" % j], slow=True)
                self.outkeys.append("sqp%d" % j)
        if self.debug.get("mstop") == "A":
            return
        for nh in range(2):
            (wz,), wk = self.slab([("w_in", 0, D, Z0 + nh * 512, Z0 + (nh + 1) * 512)])
            for tb in range(NB):
                b = 4 + tb % 2
                for kc in range(8):
                    self.mm(self.ps[b][:TB, :], self.xT[:, kc, tb * TB:(tb + 1) * TB], wz[:, kc, :],
                            [wk, "xT0", "xT1", "xT2", "xT3"], ["ps%d" % b], start=(kc == 0), stop=(kc == 7))
                self.act(self.ztok[:TB, tb, nh * 512:(nh + 1) * 512], self.ps[b][:TB, :], AF.Silu, ["ps%d" % b], ["ztok%d" % tb])
        (wba,), wk = self.slab([("w_in", 0, D, BETA0, BETA0 + 16)])
        for tb in range(NB):
            for kc in range(8):
                self.mm(self.ps[6][:TB, 0:16], self.xT[:, kc, tb * TB:(tb + 1) * TB], wba[:, kc, :],
                        [wk, "xT0", "xT1", "xT2", "xT3"], ["ps6"], start=(kc == 0), stop=(kc == 7))
            self.act(self.beta[:TB, tb, :], self.ps[6][:TB, 0:8], AF.Sigmoid, ["ps6"], ["beta"])
            self.tt("dve", self.batok[:TB, tb, 8:16], self.ps[6][:TB, 8:16], self.smallb[:TB, 8:16], ALU.add,
                    ["ps6", "smallb"], ["batok"])
        for tb in range(NB):
            self.act(self.batok[:TB, tb, 0:8], self.batok[:TB, tb, 8:16], AF.Exp, ["batok"], ["batok"])
        for tb in range(NB):
            self.act(self.batok[:TB, tb, 0:8], self.batok[:TB, tb, 0:8], AF.Ln, ["batok"], ["batok"], bias=1.0)
            self.tt("dve", self.gtok[:TB, tb, :], self.batok[:TB, tb, 0:8], self.negA[:TB, :], ALU.mult,
                    ["batok", "negA"], ["gtok"])
        if self.debug.get("mstop") == "B":
            return
        if sample:
            self.gdn_sample()
        else:
            self.gdn_all(NB)
            if last:
                self.dma("aux", self.sgp.rearrange("h k v -> k h v"), self.S[:], "so1", ["S0", "S1"], ["sgp"])
                self.outkeys.append("sgp")
        if self.debug.get("mstop") == "C":
            return
        for c in range(8):
            (wB, wC, wH), wk = self.slab([("w_in", 0, D, B0 + c * 128, B0 + (c + 1) * 128),
                                          ("w_in", 0, D, C0 + c * 128, C0 + (c + 1) * 128),
                                          ("w_in", 0, D, H0 + c * 128, H0 + (c + 1) * 128)])
            bB, bC, bH = 0 + 3 * (c % 2), 1 + 3 * (c % 2), 2 + 3 * (c % 2)
            for (w_, b_) in ((wC, bC), (wH, bH), (wB, bB)):
                for kc in range(8):
                    self.mm(self.ps[b_][:, :NT], w_[:, kc, :], self.xT[:, kc, :NT], [wk, "xT0", "xT1", "xT2", "xT3"], ["ps%d" % b_],
                            start=(kc == 0), stop=(kc == 7))
            ct, ck = self.tmpf()
            self.cp("act", ct[:, :NT], self.ps[bC][:, :NT], ["ps%d" % bC], [ck])

            def src(dst, dk, ct=ct, ck=ck, bH=bH):
                self.tt("dve", dst, ct[:, :NT], self.ps[bH][:, :NT], ALU.mult, [ck, "ps%d" % bH], [dk])
            th = [self.hss[:, c, j, :] for j in range(2)] if sample else None
            wts = [self.wcs[:, c, j:j + 1] for j in range(3)]
            acc, ak, cbt, cbk = self.conv_chunk(None, NT, th, wts, 3, self.hists[:, c, :], "hists%d" % c, sample,
                                                src_is_psum=False, src=src)
            if sample:
                self.tr(self.ps[6][:NT, (c % 4) * 128:(c % 4 + 1) * 128], cbt[:, 2:2 + NT], self.cf("ident"),
                        [cbk, "cstf"], ["ps6"])
                if c % 4 == 3:
                    stg, stk = self.stage(0)
                    self.cp("act", stg[:NT, (c - 3) * 128:(c + 1) * 128], self.ps[6][:NT, :], ["ps6"], [stk])
            self.tt("dve", A1[:, c, :NT], acc[:, :NT], self.ps[bB][:, :NT], ALU.mult, [ak, "ps%d" % bB], ["A1.%d" % c])
        if sample:
            stg, stk = self.stage(0)
            self.dma("aux", self.sss[:, 1, :], stg[:NSAMP, 0:D], "so2", [stk], ["sss1"])
            self.dma("aux", self.sss[:, 0:1, :], self.ssc[:, 1:2, :], "so3", [], ["sss0"])
            self.outkeys += ["sss1", "sss0"]
        elif last:
            for j in range(2):
                self.dma("aux", self.ssp[j, :].rearrange("(c p) -> p c", p=128), self.hists[:, :, j], "so2",
                         ["hists%d" % c for c in range(8)], ["ssp%d" % j], slow=True)
                self.outkeys.append("ssp%d" % j)
        if self.debug.get("mstop") == "D":
            return
        for c in range(8):
            (wpg, wgg, wps, wgs), wk = self.slab([("w_p_gdn", 0, D, c * 128, (c + 1) * 128),
                                                  ("w_in", 0, D, GG0 + c * 128, GG0 + (c + 1) * 128),
                                                  ("w_p_sc", 0, D, c * 128, (c + 1) * 128),
                                                  ("w_in", 0, D, GS0 + c * 128, GS0 + (c + 1) * 128)])
            o = 4 * (c % 2)
            for (w_, b_, rhs_, rk) in ((wpg, o, A1[:, 16:24, :], ["A1.%d" % k for k in range(16, 24)]),
                                       (wgg, o + 1, self.xT, ["xT0", "xT1", "xT2", "xT3"]),
                                       (wps, o + 2, A1[:, 0:8, :], ["A1.%d" % k for k in range(8)]),
                                       (wgs, o + 3, self.xT, ["xT0", "xT1", "xT2", "xT3"])):
                for kc in range(8):
                    self.mm(self.ps[b_][:, :NT], w_[:, kc, :], rhs_[:, kc, :NT], [wk] + rk, ["ps%d" % b_],
                            start=(kc == 0), stop=(kc == 7))
            s1, s1k = self.tmpf()
            self.act(s1[:, :NT], self.ps[o + 1][:, :NT], AF.Sigmoid, ["ps%d" % (o + 1)], [s1k])
            self.tt("dve", s1[:, :NT], s1[:, :NT], self.ps[o][:, :NT], ALU.mult, [s1k, "ps%d" % o], [s1k])
            s2, s2k = self.tmpf()
            self.act(s2[:, :NT], self.ps[o + 3][:, :NT], AF.Sigmoid, ["ps%d" % (o + 3)], [s2k])
            self.tt("dve", s2[:, :NT], s2[:, :NT], self.ps[o + 2][:, :NT], ALU.mult, [s2k, "ps%d" % (o + 2)], [s2k])
            self.tt("dve", A1[:, 8 + c, :NT], s1[:, :NT], s2[:, :NT], ALU.add, [s1k, s2k], ["A1.%d" % (8 + c)])
        for nh in range(2):
            (wo,), wk = self.slab([("w_o", 0, D, nh * 512, (nh + 1) * 512)])
            for tb in range(NB):
                b = tb % 2
                for kc in range(8):
                    self.mm(self.ps[b][:TB, :], A1[:, 8 + kc, tb * TB:(tb + 1) * TB], wo[:, kc, :],
                            [wk, "A1.%d" % (8 + kc)], ["ps%d" % b], start=(kc == 0), stop=(kc == 7))
                xr = self.xres[:TB, tb, nh * 512:(nh + 1) * 512]
                self.stt(xr, self.ps[b][:TB, :], 1.0 / ALPHA, xr, ALU.mult, ALU.add, ["ps%d" % b, "xres%d" % tb], ["xres%d" % tb])

    def onorm_and_T(self, tb, TB):
        self.lockstep([self.onorm_gen(tb, TB)])

    def onorm_gen(self, tb, TB, bank=7):
        o3 = self.otok[:TB, :].rearrange("p (h d) -> p h d", h=H)
        t13 = self.t1[:TB, :].rearrange("p (h d) -> p h d", h=H)
        t23 = self.t2[:TB, :].rearrange("p (h d) -> p h d", h=H)
        st = self.stat3
        self.act(self.t1[:TB, :], self.otok[:TB, :], AF.Square, ["otok"], ["t1"])
        yield
        self.P.add("dve", lambda e: e.tensor_reduce(st[:TB, 0:8], t13, AX.X, ALU.add), ["t1"], ["stat3"])
        self.act(st[:TB, 0:8], st[:TB, 0:8], AF.Sqrt, ["stat3"], ["stat3"], bias=RMS_EPS, scale=1.0 / 128.0)
        yield
        self.P.add("dve", lambda e: e.reciprocal(st[:TB, 8:16], st[:TB, 0:8]), ["stat3"], ["stat3"])
        yield
        self.tt("dve", t13, o3, st[:TB, 8:16].unsqueeze(2).to_broadcast([TB, H, 128]), ALU.mult, ["otok", "stat3"], ["t1"])
        z3 = self.ztok[:TB, tb, :].rearrange("p (h d) -> p h d", h=H)
        self.tt("dve", t23, z3, self.wonb[:TB, :].unsqueeze(1).to_broadcast([TB, H, 128]), ALU.mult,
                ["ztok%d" % tb, "wonb"], ["t2"])
        yield
        self.tt("dve", self.xb16[:TB, :], self.t1[:TB, :], self.t2[:TB, :], ALU.mult, ["t1", "t2"], ["xb16"])
        yield
        for c in range(8):
            self.tr(self.psb[bank][:, c * TB:(c + 1) * TB], self.xb16[:TB, c * 128:(c + 1) * 128], self.cb("ident")[:TB, :TB],
                    ["xb16", "cstb"], ["ps%d" % bank])
        yield
        self.cp("act", self.A1[:, 16:24, tb * TB:(tb + 1) * TB],
                self.psb[bank][:, 0:8 * TB].rearrange("p (c t) -> p c t", c=8), ["ps%d" % bank],
                ["A1.%d" % k for k in range(16, 24)])

    def inv_chain(self, tb, hg, G, gp, pb):
        A1 = self.A1
        blk = slice(tb * 128, (tb + 1) * 128)
        g8 = self.gtok[:, tb, :]
        hs = [hg * 4 + hh for hh in range(4)]
        K = lambda nm: gp + nm
        rot = [0]

        def nb():
            x = pb[rot[0] % len(pb)]
            rot[0] += 1
            return x
        b0, b1, b2 = nb(), nb(), nb()
        f4 = lambda t: t.rearrange("p h d -> p (h d)")
        kq_r = ["A1.%d" % (8 + h) for h in hs] + ["A1.%d" % h for h in hs]
        for hh, h in enumerate(hs):
            self.ts("dve", G["Lg"][:, hh, :], self.cf("ltri"), g8[:, h:h + 1], ALU.mult, ["cstf", "gtok"], [K("Lg")])
        yield
        for hh, h in enumerate(hs):
            cs = slice(hh * 128, (hh + 1) * 128)
            self.mm(self.ps[b0][:, cs], self.cf("su"), G["Lg"][:, hh, :], ["cstf", K("Lg")], ["ps%d" % b0])
            self.mm(self.ps[b1][:, cs], A1[:, 8 + h, blk], A1[:, 8 + h, blk], kq_r, ["ps%d" % b1])
            self.mm(self.ps[b2][:, cs], A1[:, 8 + h, blk], A1[:, h, blk], kq_r, ["ps%d" % b2])
        yield
        self.act(f4(G["decTm"]), self.ps[b0][:, :], AF.Exp, ["ps%d" % b0], [K("decTm")])
        yield
        self.tt("dve", G["decTm"], G["decTm"], self.cf4("muincl"), ALU.mult, [K("decTm"), "cstf"], [K("decTm")])
        yield
        self.tt("dve", f4(G["qkTm"]), self.ps[b2][:, :], f4(G["decTm"]), ALU.mult, ["ps%d" % b2, K("decTm")], [K("qkTm")])
        self.tt("dve", f4(G["Lg"]), self.ps[b1][:, :], f4(G["decTm"]), ALU.mult, ["ps%d" % b1, K("decTm")], [K("Lg")])
        yield
        self.tt("dve", G["MT"], G["Lg"],
                self.beta[:, tb, hg * 4:hg * 4 + 4].unsqueeze(2).to_broadcast([128, 4, 128]), ALU.mult,
                [K("Lg"), "beta"], [K("MT")])
        yield
        bt = nb()
        for hh in range(4):
            self.tr(self.psb[bt][:, hh * 128:(hh + 1) * 128], G["MT"][:, hh, :], self.cb("ident"), [K("MT"), "cstb"], ["ps%d" % bt])
        yield
        self.cp("act", f4(G["M"]), self.psb[bt][:, 0:512], ["ps%d" % bt], [K("M")])
        yield
        Nn, Nt, N2, N2t = "Na", "Nb", "Nc", "Nd"
        Pn, Pt, Pn2, Pt2 = "Pa", "Pb", "Pc", "Pd"
        self.tt("dve", G[Nn], G["M"], self.cb4("mndn"), ALU.mult, [K("M"), "cstb"], [K(Nn)])
        self.tt("dve", G[Nt], G["MT"], self.cb4("mndtn"), ALU.mult, [K("MT"), "cstb"], [K(Nt)])
        yield
        self.tt("dve", G[Pn], G[Nn], self.cb4("ident"), ALU.add, [K(Nn), "cstb"], [K(Pn)])
        self.tt("dve", G[Pt], G[Nt], self.cb4("ident"), ALU.add, [K(Nt), "cstb"], [K(Pt)])
        nstep = int(np.log2(NBK)) - 1
        for s_ in range(nstep):
            ba, bb = nb(), nb()
            for hh in range(4):
                cs = slice(hh * 128, (hh + 1) * 128)
                self.mm(self.ps[ba][:, cs], G[Nt][:, hh, :], G[Nn][:, hh, :], [K(Nt), K(Nn)], ["ps%d" % ba])
                self.mm(self.ps[bb][:, cs], G[Nn][:, hh, :], G[Nt][:, hh, :], [K(Nt), K(Nn)], ["ps%d" % bb])
            yield
            self.cp("act", f4(G[N2]), self.ps[ba][:, :], ["ps%d" % ba], [K(N2)])
            self.cp("act", f4(G[N2t]), self.ps[bb][:, :], ["ps%d" % bb], [K(N2t)])
            yield
            bc_, bd = nb(), nb()
            for hh in range(4):
                cs = slice(hh * 128, (hh + 1) * 128)
                self.mm(self.ps[bc_][:, cs], G[N2t][:, hh, :], G[Pn][:, hh, :], [K(N2t), K(Pn)], ["ps%d" % bc_])
                self.mm(self.ps[bd][:, cs], G[N2][:, hh, :], G[Pt][:, hh, :], [K(N2), K(Pt)], ["ps%d" % bd])
            yield
            self.tt("dve", f4(G[Pn2]), f4(G[Pn]), self.ps[bc_][:, :], ALU.add, [K(Pn), "ps%d" % bc_], [K(Pn2)])
            self.tt("dve", f4(G[Pt2]), f4(G[Pt]), self.ps[bd][:, :], ALU.add, [K(Pt), "ps%d" % bd], [K(Pt2)])
            yield
            Nn, Nt, N2, N2t = N2, N2t, Nn, Nt
            Pn, Pt, Pn2, Pt2 = Pn2, Pt2, Pn, Pt
        T, U, T2, U2 = Pn, Pt, Pn2, Pt2
        E_, F_, X_, Y_ = Nn, Nt, N2, N2t
        b = NBK
        while b < 128:
            lastlvl = (b == 64)
            self.tt("dve", G[E_], G["M"], self.cb4("me%d" % b), ALU.mult, [K("M"), "cstb"], [K(E_)])
            if not lastlvl:
                self.tt("dve", G[F_], G["MT"], self.cb4("me%dt" % b), ALU.mult, [K("MT"), "cstb"], [K(F_)])
            yield
            ba, bb = nb(), nb()
            for hh in range(4):
                cs = slice(hh * 128, (hh + 1) * 128)
                self.mm(self.ps[ba][:, cs], G[E_][:, hh, :], G[U][:, hh, :], [K(E_), K(U)], ["ps%d" % ba])
                if not lastlvl:
                    self.mm(self.ps[bb][:, cs], G[F_][:, hh, :], G[T][:, hh, :], [K(F_), K(T)], ["ps%d" % bb])
            yield
            self.cp("act", f4(G[Y_]), self.ps[ba][:, :], ["ps%d" % ba], [K(Y_)])
            if not lastlvl:
                self.cp("act", f4(G[X_]), self.ps[bb][:, :], ["ps%d" % bb], [K(X_)])
            yield
            bc_, bd = nb(), nb()
            for hh in range(4):
                cs = slice(hh * 128, (hh + 1) * 128)
                self.mm(self.ps[bc_][:, cs], G[T][:, hh, :], G[Y_][:, hh, :], [K(T), K(Y_)], ["ps%d" % bc_])
                if not lastlvl:
                    self.mm(self.ps[bd][:, cs], G[U][:, hh, :], G[X_][:, hh, :], [K(U), K(X_)], ["ps%d" % bd])
            yield
            self.tt("dve", f4(G[U2]), f4(G[U]), self.ps[bc_][:, :], ALU.subtract, [K(U), "ps%d" % bc_], [K(U2)])
            if not lastlvl:
                self.tt("dve", f4(G[T2]), f4(G[T]), self.ps[bd][:, :], ALU.subtract, [K(T), "ps%d" % bd], [K(T2)])
            yield
            T, U, T2, U2 = T2, U2, T, U
            b *= 2
        self.cp("dve", self.Uk[hg][:], G[U], [K(U)], ["Uk%d" % hg])
        self.cp("dve", self.Qk[hg][:], G["qkTm"], [K("qkTm")], ["Qk%d" % hg])

    def scan_chain(self, tb, hg, bx, by):
        A1 = self.A1
        blk = slice(tb * 128, (tb + 1) * 128)
        pb_ = tb % 2
        sm, smk = self.gsm2[pb_], "gsm%d" % pb_
        vtok, vtk = (self.vtok, self.vtok2)[pb_], "vtok%d" % pb_
        kdec, kdk = self.kdec2[pb_], "kdec%d" % pb_
        hs = [hg * 4 + hh for hh in range(4)]
        hsl = slice(hg * 4, hg * 4 + 4)
        X, Y = self.ps[bx], self.ps[by]
        xk, yk = "ps%d" % bx, "ps%d" % by
        X3 = X[:, :].rearrange("p (h d) -> p h d", h=4)
        Y3 = Y[:, :].rearrange("p (h d) -> p h d", h=4)
        bc = lambda ap: ap.unsqueeze(2).to_broadcast([128, 4, 128])
        Sk, Sbk = "S%d" % hg, "Sbf%d" % hg
        for hh, h in enumerate(hs):
            cs = slice(hh * 128, (hh + 1) * 128)
            self.mm(X[:, cs], A1[:, 8 + h, blk], self.Sbf[:, h, :], ["A1.%d" % (8 + h), Sbk], [xk])
            self.mm(Y[:, cs], A1[:, h, blk], self.Sbf[:, h, :], ["A1.%d" % h, Sbk], [yk])
        yield
        tS, tSk = self.tmpf()
        tS3 = tS[:, 0:512].rearrange("p (h d) -> p h d", h=4)
        self.tt("dve", tS3, X3, bc(sm[:, 24 + hg * 4:28 + hg * 4]), ALU.mult, [xk, smk], [tSk])
        o1, o1k = self.tmpf()
        o13 = o1[:, 0:512].rearrange("p (h d) -> p h d", h=4)
        self.tt("dve", o13, Y3, bc(sm[:, 16 + hg * 4:20 + hg * 4]), ALU.mult, [yk, smk], [o1k])
        yield
        r, rk = self.tmpb()
        r3 = r[:, :].rearrange("p (h d) -> p h d", h=4)
        self.tt("dve", r3, tS3, vtok[:, hsl, :], ALU.add, [tSk, vtk], [rk])
        yield
        for hh in range(4):
            cs = slice(hh * 128, (hh + 1) * 128)
            self.mm(X[:, cs], self.Uk[hg][:, hh, :], r3[:, hh, :], ["Uk%d" % hg, rk], [xk])
        yield
        vn, vk = self.tmpb()
        vn3 = vn[:, :].rearrange("p (h d) -> p h d", h=4)
        self.tt("dve", vn3, X3, bc(self.beta[:, tb, hsl]), ALU.mult, [xk, "beta"], [vk])
        yield
        for hh, h in enumerate(hs):
            cs = slice(hh * 128, (hh + 1) * 128)
            self.mm(Y[:, cs], self.Qk[hg][:, hh, :], vn3[:, hh, :], ["Qk%d" % hg, vk], [yk])
            self.mm(X[:, cs], kdec[:, h, :], vn3[:, hh, :], [kdk, vk], [xk])
        yield
        self.tt("dve", self.otok[:, hg * 512:(hg + 1) * 512], o1[:, 0:512], Y[:, :], ALU.add, [o1k, yk], ["otok"])
        self.tt("dve", self.S[:, hsl, :], self.S[:, hsl, :], bc(sm[:, 40 + hg * 4:44 + hg * 4]), ALU.mult, [Sk, smk], [Sk])
        yield
        self.tt("dve", self.S[:, hsl, :], self.S[:, hsl, :], X3, ALU.add, [Sk, xk], [Sk])
        yield
        self.cp("act", self.Sbf[:, hsl, :], self.S[:, hsl, :], [Sk], [Sbk])

    def lockstep(self, gens):
        gens = list(gens)
        while gens:
            nxt = []
            for g in gens:
                try:
                    next(g)
                    nxt.append(g)
                except StopIteration:
                    pass
            gens = nxt

    def gdn_prep(self, tb):
        A1 = self.A1
        blk = slice(tb * 128, (tb + 1) * 128)
        pb_ = tb % 2
        sm, smk = self.gsm2[pb_], "gsm%d" % pb_
        vtok, vtk = (self.vtok, self.vtok2)[pb_], "vtok%d" % pb_
        kdec, kdk = self.kdec2[pb_], "kdec%d" % pb_
        g8 = self.gtok[:, tb, :]
        self.mm(self.ps[7][:, 0:8], self.cf("ltri"), g8, ["cstf", "gtok"], ["ps7"])
        self.mm(self.ps[7][:, 8:16], self.cf("ones"), g8, ["cstf", "gtok"], ["ps7"])
        self.cp("dve", sm[:, 0:16], self.ps[7][:, 0:16], ["ps7"], [smk])
        yield
        self.act(sm[:, 16:24], sm[:, 0:8], AF.Exp, [smk], [smk])
        self.tt("dve", sm[:, 32:40], sm[:, 8:16], sm[:, 0:8], ALU.subtract, [smk], [smk])
        yield
        self.ts("dve", sm[:, 24:32], sm[:, 16:24], -1.0, ALU.mult, [smk], [smk])
        self.act(sm[:, 32:40], sm[:, 32:40], AF.Exp, [smk], [smk])
        self.act(sm[:, 40:48], sm[:, 8:16], AF.Exp, [smk], [smk])
        for (dst, dk, u0, bank) in ((kdec, kdk, 8, 6), (vtok, vtk, 16, 7)):
            for h in range(H):
                self.tr(self.psb[bank][:, h * 128:(h + 1) * 128], A1[:, u0 + h, blk], self.cb("ident"),
                        ["A1.%d" % (u0 + h), "cstb"], ["ps%d" % bank])
        yield
        self.cp("act", vtok[:].rearrange("p h d -> p (h d)"), self.psb[7][:, 0:1024], ["ps7"], [vtk])
        self.tt("dve", kdec[:], self.psb[6][:, 0:1024].rearrange("p (h d) -> p h d", h=H),
                sm[:, 32:40].unsqueeze(2).to_broadcast([128, H, 128]), ALU.mult, ["ps6", smk], [kdk])

    def gdn_prep_old(self, tb):
        A1 = self.A1
        blk = slice(tb * 128, (tb + 1) * 128)
        pb_ = tb % 2
        sm, smk = self.gsm2[pb_], "gsm%d" % pb_
        vtok, vtk = (self.vtok, self.vtok2)[pb_], "vtok%d" % pb_
        kdec, kdk = self.kdec2[pb_], "kdec%d" % pb_
        g8 = self.gtok[:, tb, :]
        for (dst, dk, u0, bank) in ((self.ktok, "ktok", 8, 5), (vtok, vtk, 16, 6)):
            for h in range(H):
                self.tr(self.psb[bank][:, h * 128:(h + 1) * 128], A1[:, u0 + h, blk], self.cb("ident"),
                        ["A1.%d" % (u0 + h), "cstb"], ["ps%d" % bank])
            self.cp("act", dst[:].rearrange("p h d -> p (h d)"), self.psb[bank][:, 0:1024], ["ps%d" % bank], [dk])
        self.mm(self.ps[7][:, 0:8], self.cf("ltri"), g8, ["cstf", "gtok"], ["ps7"])
        self.mm(self.ps[7][:, 8:16], self.cf("ones"), g8, ["cstf", "gtok"], ["ps7"])
        self.cp("dve", sm[:, 0:16], self.ps[7][:, 0:16], ["ps7"], [smk])
        self.act(sm[:, 16:24], sm[:, 0:8], AF.Exp, [smk], [smk])
        self.ts("dve", sm[:, 24:32], sm[:, 16:24], -1.0, ALU.mult, [smk], [smk])
        self.tt("dve", sm[:, 32:40], sm[:, 8:16], sm[:, 0:8], ALU.subtract, [smk], [smk])
        self.act(sm[:, 32:40], sm[:, 32:40], AF.Exp, [smk], [smk])
        self.act(sm[:, 40:48], sm[:, 8:16], AF.Exp, [smk], [smk])
        self.tt("dve", kdec[:], self.ktok[:], sm[:, 32:40].unsqueeze(2).to_broadcast([128, H, 128]), ALU.mult,
                ["ktok", smk], [kdk])

    def seq_chain(self, tb, NB):
        for hg in range(2):
            for _ in self.scan_chain(tb, hg, 6, 7):
                yield
            yield
        for _ in self.onorm_gen(tb, 128, bank=6):
            yield
        if tb + 1 < NB:
            yield
            for _ in self.gdn_prep(tb + 1):
                yield

    def gdn_all(self, NB):
        if self.debug.get("mstop") == "Y":
            nbk_ = 4 if self.debug.get("gstop") == "b4" else 3
            inv = lambda tb: [self.inv_chain(tb, hg, self.gqs[hg][0], self.gqs[hg][1], [nbk_ * hg + i for i in range(nbk_)])
                              for hg in range(2)]
            for tb in range(NB):
                if self.debug.get("gstop") == "oldprep":
                    self.gdn_prep_old(tb)
                else:
                    self.lockstep([self.gdn_prep(tb)])
                self.lockstep(inv(tb))
                self.lockstep([self.scan_chain(tb, 0, 6, 7)])
                self.lockstep([self.scan_chain(tb, 1, 6, 7)])
                self.lockstep([self.onorm_gen(tb, 128, bank=6)])
            return
        self.lockstep([self.gdn_prep(0)])
        inv = lambda tb: [self.inv_chain(tb, hg, self.gqs[hg][0], self.gqs[hg][1], [3 * hg + i for i in range(3)])
                          for hg in range(2)]
        self.lockstep(inv(0))
        for tb in range(NB):
            gens = [self.seq_chain(tb, NB)]
            if tb + 1 < NB:
                gens = inv(tb + 1) + gens
            self.lockstep(gens)

    def stage(self, k):
        return [(self.t1, "t1"), (self.t2, "t2"), (self.otok, "otok")][k]

    def load_sample_hist(self):
        for (srcd, dst, nch, nj) in ((self.sq, self.hsq, 24, 3), (self.ssc, self.hss, 8, 2)):
            for j in range(nj):
                for k in range(nch // 8):
                    t, tk = self.stage(k)
                    self.dma("aux", t[:NSAMP, :], srcd[:, j, k * 1024:(k + 1) * 1024], "hl", [], [tk])
                    for cc in range(8):
                        self.tr(self.ps[6][:, cc * NSAMP:(cc + 1) * NSAMP], t[:NSAMP, cc * 128:(cc + 1) * 128],
                                self.cf("ident")[:NSAMP, :NSAMP], [tk, "cstf"], ["ps6"])
                    self.cp("dve", dst[:, k * 8:(k + 1) * 8, j, :],
                            self.ps[6][:, 0:8 * NSAMP].rearrange("p (c b) -> p c b", c=8), ["ps6"], ["hsamp"])

    def gdn_sample(self):
        A1 = self.A1
        NS = NSAMP
        sm = self.gsm
        st = self.stat
        for (dst, dk, u0, bank) in ((self.qtok[:NS, :], "qtok", 0, 4), (self.ktok[:NS].rearrange("p h d -> p (h d)"), "ktok", 8, 5),
                                    (self.vtok[:NS].rearrange("p h d -> p (h d)"), "vtok", 16, 6)):
            for h in range(H):
                self.tr(self.psb[bank][:NS, h * 128:(h + 1) * 128], A1[:, u0 + h, 0:NS], self.cb("ident"),
                        ["A1.%d" % (u0 + h), "cstb"], ["ps%d" % bank])
            self.cp("act", dst, self.psb[bank][:NS, 0:1024], ["ps%d" % bank], [dk])
        a = sm[:NS, 0:8]
        self.act(a, self.gtok[:NS, 0, :], AF.Exp, ["gtok"], ["gsm"])
        q3 = self.qtok[:NS, :].rearrange("p (h d) -> p h d", h=H)
        t13 = self.t1[:NS, :].rearrange("p (h d) -> p h d", h=H)
        t23 = self.t2[:NS, :].rearrange("p (h d) -> p h d", h=H)
        o3 = self.otok[:NS, :].rearrange("p (h d) -> p h d", h=H)
        self.tt("dve", t13, q3, self.ktok[:NS], ALU.mult, ["qtok", "ktok"], ["t1"])
        self.P.add("dve", lambda e: e.tensor_reduce(sm[:NS, 8:16], t13, AX.X, ALU.add), ["t1"], ["gsm"])
        i16 = self.i16b[:].rearrange("p (a b) -> p a b", a=NS)
        for h in range(H):
            self.tt("dve", self.kTm[:, h, :, :], A1[:, 8 + h:9 + h, 0:NS].to_broadcast([128, NS, NS]), i16, ALU.mult,
                    ["A1.%d" % (8 + h), "i16b"], ["kTm"])
            self.tt("dve", self.qTm[:, h, :, :], A1[:, h:h + 1, 0:NS].to_broadcast([128, NS, NS]), i16, ALU.mult,
                    ["A1.%d" % h, "i16b"], ["qTm"])
        for b in range(NS):
            i3, i2 = b % 3, b % 2
            self.dma("sp", self.Sin[i3], self.sg[b].rearrange("h k v -> k h v"), "sin%d" % i3, [], ["Sin%d" % i3])
            self.cp("act" if b % 2 == 0 else "dve", self.Sinb[i2], self.Sin[i3], ["Sin%d" % i3], ["Sinb%d" % i2])
            for h in range(H):
                bk, bq = h // 4, 2 + h // 4
                cs = slice((h % 4) * 128, (h % 4 + 1) * 128)
                first = (b == 0 and h % 4 == 0)
                self.mm(self.ps[bk][:NS, cs], self.kTm[:, h, b, :], self.Sinb[i2][:, h, :], ["kTm", "Sinb%d" % i2],
                        ["ps%d" % bk], start=first, stop=(b == NS - 1))
                self.mm(self.ps[bq][:NS, cs], self.qTm[:, h, b, :], self.Sinb[i2][:, h, :], ["qTm", "Sinb%d" % i2],
                        ["ps%d" % bq], start=first, stop=(b == NS - 1))
        a_b = a.unsqueeze(2).to_broadcast([NS, H, 128])
        for half in range(2):
            hsl = slice(half * 4, half * 4 + 4)
            k3 = self.ps[half][:NS, :].rearrange("p (h d) -> p h d", h=4)
            qs3 = self.ps[2 + half][:NS, :].rearrange("p (h d) -> p h d", h=4)
            ab = a[:, hsl].unsqueeze(2).to_broadcast([NS, 4, 128])
            self.tt("dve", t13[:, hsl, :], k3, ab, ALU.mult, ["ps%d" % half, "gsm"], ["t1"])
            self.tt("dve", t13[:, hsl, :], self.vtok[:NS, hsl, :], t13[:, hsl, :], ALU.subtract, ["vtok", "t1"], ["t1"])
            self.tt("dve", t13[:, hsl, :], t13[:, hsl, :],
                    self.beta[:NS, 0, hsl].unsqueeze(2).to_broadcast([NS, 4, 128]), ALU.mult, ["t1", "beta"], ["t1"])
            self.tt("dve", t23[:, hsl, :], qs3, ab, ALU.mult, ["ps%d" % (2 + half), "gsm"], ["t2"])
            self.tt("dve", o3[:, hsl, :], t13[:, hsl, :], sm[:NS, 8 + half * 4:12 + half * 4].unsqueeze(2).to_broadcast([NS, 4, 128]),
                    ALU.mult, ["t1", "gsm"], ["otok"])
            self.tt("dve", o3[:, hsl, :], o3[:, hsl, :], t23[:, hsl, :], ALU.add, ["otok", "t2"], ["otok"])
        dbf = self.xb16
        self.cp("act", dbf[:NS, :], self.t1[:NS, :], ["t1"], ["xb16"])
        ad = self.t2[:NS, 0:128].rearrange("p (b h) -> p b h", b=NS)
        idr = self.cf("ident")[:NS, 0:NS].unsqueeze(2).to_broadcast([NS, NS, H])
        self.tt("dve", ad, a.unsqueeze(1).to_broadcast([NS, NS, H]), idr, ALU.mult, ["gsm", "cstf"], ["t2"])
        self.mm(self.ps[4][:, 0:128], self.cf("ones")[:NS, :], self.t2[:NS, 0:128], ["cstf", "t2"], ["ps4"])
        self.cp("dve", self.abc[:], self.ps[4][:, 0:128], ["ps4"], ["abc"])
        kflat = self.ktok[:NS].rearrange("p h d -> p (h d)")
        for b in range(NS):
            i2 = b % 2
            i3 = (b + 1) % 3
            self.dma("sp", self.Sin[i3], self.sg[b].rearrange("h k v -> k h v"), "sin%d" % i3, [], ["Sin%d" % i3])
            self.ts("dve", self.kmask[i2][:NS, :], kflat, self.cf("ident")[:NS, b:b + 1], ALU.mult,
                    ["ktok", "cstf"], ["kmask%d" % i2])
            for h in range(H):
                pb_ = 5 + h // 4
                cs = slice((h % 4) * 128, (h % 4 + 1) * 128)
                self.mm(self.ps[pb_][:, cs], self.kmask[i2][:NS, h * 128:(h + 1) * 128], dbf[:NS, h * 128:(h + 1) * 128],
                        ["kmask%d" % i2, "xb16"], ["ps%d" % pb_])
                self.stt(self.Sin[i3][:, h, :], self.Sin[i3][:, h, :], self.abc[:, b * 8 + h:b * 8 + h + 1],
                         self.ps[pb_][:, cs], ALU.mult, ALU.add, ["Sin%d" % i3, "abc", "ps%d" % pb_], ["Sin%d" % i3])
            self.dma("sp", self.sgs[b].rearrange("h k v -> k h v"), self.Sin[i3], "sout%d" % i3, ["Sin%d" % i3], ["sgs%d" % b])
            self.outkeys.append("sgs%d" % b)
        self.onorm_and_T(0, NS)


_CACHE = {}


WBIG_LEN = 2 * (3 * D * HID) + D * IN_W + 4 * D * D + 256 * D


def pack_wbig(weights, specs, offs, tot):
    out = np.empty((tot,), np.float32)
    for spec, (off, n) in zip(specs, offs):
        parts = []
        for (name, r0, r1, c0, c1) in spec:
            w = weights[name][r0:r1, c0:c1]
            kc = (r1 - r0) // 128
            parts.append(w.reshape(kc, 128, c1 - c0).transpose(1, 0, 2).reshape(128, kc * (c1 - c0)))
        out[off:off + 128 * n] = np.concatenate(parts, axis=1).reshape(-1)
    return out


def kernel(x_prompt, x_sample, p_prompt, p_sample, state_gdn, state_qkv_conv, state_sc_conv,
           ffn1_w_gate, ffn1_w_up, ffn1_w_down, ln1_g, ln1_b,
           w_in, w_conv_qkv, A_log, dt_bias, w_onorm, w_p_gdn, w_conv_sc, w_p_sc, w_o, ln2_g, ln2_b,
           ffn2_w_gate, ffn2_w_up, ffn2_w_down, ln3_g, ln3_b,
           w_ple_gate, w_ple_proj, ln4_g, ln4_b, _debug=None):
    f = lambda a: np.ascontiguousarray(np.asarray(a, dtype=np.float32))
    weights = {"ffn1_w_gate": f(ffn1_w_gate)[0], "ffn1_w_up": f(ffn1_w_up)[0], "ffn1_w_down": f(ffn1_w_down)[0],
               "w_in": f(w_in)[0], "w_p_gdn": f(w_p_gdn)[0], "w_p_sc": f(w_p_sc)[0], "w_o": f(w_o)[0],
               "ffn2_w_gate": f(ffn2_w_gate)[0], "ffn2_w_up": f(ffn2_w_up)[0], "ffn2_w_down": f(ffn2_w_down)[0],
               "w_ple_gate": f(w_ple_gate)[0], "w_ple_proj": f(w_ple_proj)[0]}
    bld = Builder(debug=_debug)
    bld.wbig_len = WBIG_LEN
    nc = bld.build()
    assert bld.slab_tot == WBIG_LEN or _debug, (bld.slab_tot, WBIG_LEN)
    assert bld.slab_tot <= WBIG_LEN
    wbig = np.zeros((WBIG_LEN,), np.float32)
    wbig[:bld.slab_tot] = pack_wbig(weights, bld.slab_specs, bld.slab_off, bld.slab_tot)
    lnp = np.stack([f(ln1_g)[0], f(ln1_b)[0], f(ln2_g)[0], f(ln2_b)[0], f(ln3_g)[0], f(ln3_b)[0], f(ln4_g)[0], f(ln4_b)[0]])
    wcq = np.ascontiguousarray(f(w_conv_qkv)[0].reshape(4, 24, 128).transpose(2, 1, 0).reshape(128, 96))
    wcs = np.ascontiguousarray(f(w_conv_sc)[0].reshape(3, 8, 128).transpose(2, 1, 0).reshape(128, 24))
    smallp = np.stack([f(A_log)[0], f(dt_bias)[0]])
    cst, cst2 = make_consts()
    cst2 = np.ascontiguousarray(cst2.reshape(128, -1))
    i16 = np.ascontiguousarray(np.broadcast_to(np.eye(16, dtype=np.float32).reshape(1, 256), (128, 256)))
    xp = f(x_prompt)
    xsm = f(x_sample)[:, 0, :]
    ppr = f(p_prompt)[0]
    psm = f(p_sample)[0, :, 0, :]
    sg = f(state_gdn)[0]
    sq = f(state_qkv_conv)[0]
    ssc = f(state_sc_conv)[0]
    in_maps = []
    for c in range(8):
        sl = slice(c * NSAMP, (c + 1) * NSAMP)
        in_maps.append({"x": xp[c], "pp": ppr[c], "xs": xsm[sl], "psm": psm[sl], "sg": sg[sl], "sq": sq[sl], "ssc": ssc[sl],
                        "wbig": wbig, "lnp": lnp, "wcq": wcq, "wcs": wcs, "smallp": smallp, "won": f(w_onorm)[0],
                        "cst": cst, "cst2": cst2, "i16": i16})
    ncores = (_debug or {}).get("ncores", 8)
    res = run_bass_kernel_spmd(nc, in_maps[:ncores], core_ids=list(range(ncores)))
    R = list(res.results)
    while len(R) < 8:
        R.append({k: np.zeros_like(v) for k, v in R[0].items()})
    y = np.stack([R[c]["y"] for c in range(8)])
    ys = np.concatenate([R[c]["ys"] for c in range(8)])[:, None, :]
    sgp = np.stack([R[c]["sgp"] for c in range(8)])[None]
    sqp = np.stack([R[c]["sqp"] for c in range(8)])[None]
    ssp = np.stack([R[c]["ssp"] for c in range(8)])[None]
    sgs = np.concatenate([R[c]["sgs"] for c in range(8)])[None]
    sqs = np.concatenate([R[c]["sqs"] for c in range(8)])[None]
    sss = np.concatenate([R[c]["sss"] for c in range(8)])[None]
    return (y.astype(np.float32), ys.astype(np.float32), sgp.astype(np.float32), sqp.astype(np.float32),
            ssp.astype(np.float32), sgs.astype(np.float32), sqs.astype(np.float32), sss.astype(np.float32))
```

```python
import contextlib
import numpy as np
import concourse.bass as bass
import concourse.mybir as mybir
from concourse.bass_utils import run_bass_kernel_spmd

F32 = mybir.dt.float32
BF16 = mybir.dt.bfloat16
AF = mybir.ActivationFunctionType
ALU = mybir.AluOpType
AX = mybir.AxisListType

D = 1024
SEQ = 2048
NSAMP = 16
HID = 2816
NJ = HID // 128
H = 8
QKV_W = 3072
IN_W = 9232
Z0, BETA0, A0, B0, C0, H0, GG0, GS0 = 3072, 4096, 4104, 4112, 5136, 6160, 7184, 8208
ALPHA = 2.0 ** 0.25
LN_EPS = 1e-5
RMS_EPS = 1e-6
L2_EPS = 1e-6
NTP = 512
SLOT = 4096
NSLOT = 4
NBK = 16

COMPUTE = ("pe", "act", "dve", "pool")


class _Op:
    __slots__ = ("eng", "fn", "r", "w", "key", "eidx", "kn", "waits", "done", "inc")

    def __init__(self, eng, fn, r, w, key):
        self.eng = eng
        self.fn = fn
        self.r = r
        self.w = w
        self.key = key
        self.eidx = -1
        self.kn = 0
        self.waits = []
        self.done = None
        self.inc = False


class Prog:
    def __init__(self, nc):
        self.nc = nc
        self.ops = []

    def add(self, eng, fn, r=(), w=(), key=None):
        self.ops.append(_Op(eng, fn, tuple(r), tuple(w), key))

    def finalize(self):
        last_w = {}
        readers = {}
        issue = {e: {} for e in ("pe", "act", "dve", "pool", "sp")}
        ecount = {e: 0 for e in issue}
        kcount = {}
        kops = {}
        eops = {e: [] for e in issue}
        for op in self.ops:
            e = op.eng
            deps = set()
            for res in op.r:
                lw = last_w.get(res)
                if lw is not None:
                    deps.add(lw)
            for res in op.w:
                lw = last_w.get(res)
                if lw is not None:
                    deps.add(lw)
                for rd in readers.get(res, ()):
                    deps.add(rd)
            deps.discard(op)
            clock = issue[e]
            if op.key is None:
                op.eidx = ecount[e]
                ecount[e] += 1
                eops[e].append(op)
            else:
                n = kcount.get(op.key, 0) + 1
                kcount[op.key] = n
                op.kn = n
                kops.setdefault(op.key, []).append(op)
                if n > 1:
                    deps.add(kops[op.key][n - 2])
            best = {}
            dma_deps = []
            for d in deps:
                if d.key is None:
                    b = best.get(d.eng)
                    if b is None or d.eidx > b.eidx:
                        best[d.eng] = d
                else:
                    dma_deps.append(d)
            newclock = None
            for f, d in best.items():
                if clock.get(f, -1) >= d.eidx:
                    continue
                if f == e and op.key is None:
                    if e == "pe":
                        continue
                    if e != "pool" and (op.eidx - d.eidx) > 12:
                        continue
                op.waits.append(("c", f, d.eidx))
                d.inc = True
                if newclock is None:
                    newclock = dict(clock)
                for k, v in d.done.items():
                    if newclock.get(k, -1) < v:
                        newclock[k] = v
            for d in dma_deps:
                kk = ("dma", d.key)
                cur = clock if newclock is None else newclock
                if cur.get(kk, 0) >= d.kn:
                    continue
                op.waits.append(("d", d.key, d.kn))
                if newclock is None:
                    newclock = dict(clock)
                for k, v in d.done.items():
                    if newclock.get(k, -1) < v:
                        newclock[k] = v
            if newclock is not None:
                issue[e] = newclock
                clock = newclock
            done = dict(clock)
            if op.key is None:
                done[e] = op.eidx
            else:
                done[("dma", op.key)] = op.kn
            op.done = done
            for res in op.r:
                readers.setdefault(res, []).append(op)
            for res in op.w:
                last_w[res] = op
                readers[res] = []
        self.rank = {}
        for e, lst in eops.items():
            k = 0
            for op in lst:
                if op.inc:
                    k += 1
                    self.rank[(e, op.eidx)] = k
        self.keys = list(kcount.keys())
        for op in self.ops:
            op.done = None
            if len(op.waits) > 1:
                m = {}
                for t, a, b in op.waits:
                    if (t, a) not in m or m[(t, a)] < b:
                        m[(t, a)] = b
                op.waits = [(t, a, b) for (t, a), b in m.items()]

    def emit(self, es):
        nc = self.nc
        sems = {}
        for e in COMPUTE:
            sems[e] = es.enter_context(nc.semaphore("s_" + e))
        ksem = {}
        for k in self.keys:
            ksem[k] = es.enter_context(nc.semaphore("k_" + str(k)))
        block = es.enter_context(nc.Block())
        rank = self.rank

        def run(ename, eng):
            for op in self.ops:
                if op.eng != ename:
                    continue
                for t, a, b in op.waits:
                    if t == "c":
                        eng.wait_ge(sems[a], rank[(a, b)])
                    else:
                        eng.wait_ge(ksem[a], 16 * b)
                if op.fn is None:
                    continue
                ins = op.fn(eng)
                if op.key is not None:
                    ins.then_inc(ksem[op.key], 16)
                elif op.inc:
                    ins.then_inc(sems[ename], 1)

        @block.tensor
        def _(eng):
            run("pe", eng)

        @block.scalar
        def _(eng):
            run("act", eng)

        @block.vector
        def _(eng):
            run("dve", eng)

        @block.gpsimd
        def _(eng):
            run("pool", eng)

        @block.sync
        def _(eng):
            run("sp", eng)


CSTF_NAMES = ["ident", "ltri", "su", "muincl", "ones"]
CSTB_NAMES = ["ident", "ones", "mndn", "mndtn", "me16", "me16t", "me32", "me32t", "me64", "me64t"]


def make_consts():
    i = np.arange(128)[:, None]
    j = np.arange(128)[None, :]
    c = {}
    c["ident"] = (i == j)
    c["ltri"] = (i <= j)
    c["su"] = (i > j)
    c["muincl"] = (j >= i)
    c["ones"] = np.ones((128, 128), bool)
    nd = (i // NBK == j // NBK) & (i > j)
    c["mndn"] = -1.0 * nd
    c["mndtn"] = -1.0 * nd.T
    for b in (16, 32, 64):
        e = (i // (2 * b) == j // (2 * b)) & ((i % (2 * b)) >= b) & ((j % (2 * b)) < b)
        c["me%d" % b] = e
        c["me%dt" % b] = e.T
    arrf = np.stack([np.asarray(c[n], np.float32) for n in CSTF_NAMES], axis=1)
    arrb = np.stack([np.asarray(c[n], np.float32) for n in CSTB_NAMES], axis=1)
    return np.ascontiguousarray(arrf), np.ascontiguousarray(arrb)


class Builder:
    def __init__(self, debug=None):
        self.debug = debug or {}
        self.slab_specs = []
        self.slab_off = []
        self.slab_tot = 0
        self.nslab_pass = None

    def mm(self, out, lhsT, rhs, r, w, start=True, stop=True):
        self.P.add("pe", lambda e: e.matmul(out, lhsT, rhs, start=start, stop=stop), r, w)

    def tr(self, out, in_, ident, r, w):
        self.P.add("pe", lambda e: e.transpose(out, in_, ident), r, w)

    def act(self, out, in_, func, r, w, bias=None, scale=None):
        kw = {}
        if bias is not None:
            kw["bias"] = bias
        if scale is not None:
            kw["scale"] = scale
        self.P.add("act", lambda e: e.activation(out, in_, func, **kw), r, w)

    def tt(self, eng, out, in0, in1, op, r, w):
        self.P.add(eng, lambda e: e.tensor_tensor(out, in0, in1, op), r, w)

    def ts(self, eng, out, in0, s1, op0, r, w, s2=None, op1=None):
        if op1 is None:
            self.P.add(eng, lambda e: e.tensor_scalar(out, in0, s1, None, op0), r, w)
        else:
            self.P.add(eng, lambda e: e.tensor_scalar(out, in0, s1, s2, op0, op1), r, w)

    def stt(self, out, in0, scalar, in1, op0, op1, r, w):
        self.P.add("dve", lambda e: e.scalar_tensor_tensor(out, in0, scalar, in1, op0, op1), r, w)

    def cp(self, eng, out, in_, r, w):
        if eng == "act":
            self.P.add("act", lambda e: e.activation(out, in_, AF.Copy), r, w)
        else:
            self.P.add(eng, lambda e: e.tensor_copy(out, in_), r, w)

    def dq(self):
        return "sp" if self.recording else "pool"

    def dma(self, eng, out, in_, key, r, w, slow=False):
        if eng == "aux":
            eng = self.dq()
        if slow:
            self.P.add(eng, lambda e: e.dma_start(out=out, in_=in_, allow_slow_non_contiguous=True), r, w, key=key)
        else:
            self.P.add(eng, lambda e: e.dma_start(out=out, in_=in_), r, w, key=key)

    def slab(self, spec):
        if self.recording:
            self.slab_specs.append(spec)
            n = sum(((r1 - r0) // 128) * (c1 - c0) for (_, r0, r1, c0, c1) in spec)
            assert n <= SLOT, n
            self.slab_off.append((self.slab_tot, n))
            self.slab_tot += 128 * n
        si = self.slab_i % self.nslab_pass if self.nslab_pass else self.slab_i
        off, n = self.slab_off[si]
        slot = self.slab_i % NSLOT
        self.slab_i += 1
        t = self.wring[slot]
        key = "w%d" % slot
        scr = self.wscr[off:off + 128 * n].rearrange("(p n) -> p n", p=128)
        if self.recording:
            src = self.wbig[off:off + 128 * n].rearrange("(p n) -> p n", p=128)
            self.dma("pool", t[:, 0:n], src, key, r=[], w=[key])
            if self.debug.get("npass", 4) > 1 or not self.debug.get("nosample", False):
                self.dma("sp", scr, t[:, 0:n], "wb%d" % slot, r=[key], w=["wscr%d" % si])
        else:
            self.dma("sp", t[:, 0:n], scr, key, r=["wscr%d" % si], w=[key])
        views = []
        o = 0
        for (_, r0, r1, c0, c1) in spec:
            kc = (r1 - r0) // 128
            nc_ = c1 - c0
            views.append(t[:, o:o + kc * nc_].rearrange("p (k n) -> p k n", k=kc))
            o += kc * nc_
        return views, key

    def build(self):
        nc = bass.Bass("TRN2", target_bir_lowering=False)
        self.nc = nc
        self.es = contextlib.ExitStack()
        with self.es:
            self._build_inner()
        return nc

    def dram_in(self, name, shape, dt=F32):
        return self.nc.dram_tensor(name, list(shape), dt, kind="ExternalInput").ap()

    def dram_out(self, name, shape, dt=F32):
        return self.nc.dram_tensor(name, list(shape), dt, kind="ExternalOutput").ap()

    def sb(self, name, shape, dt):
        return self.es.enter_context(self.nc.sbuf_tensor(name, list(shape), dt))

    def _build_inner(self):
        nc = self.nc
        self.P = Prog(nc)
        P = self.P
        self.x = self.dram_in("x", [SEQ, D])
        self.pp = self.dram_in("pp", [SEQ, 256])
        self.xs = self.dram_in("xs", [NSAMP, D])
        self.psm = self.dram_in("psm", [NSAMP, 256])
        self.sg = self.dram_in("sg", [NSAMP, H, 128, 128])
        self.sq = self.dram_in("sq", [NSAMP, 3, QKV_W])
        self.ssc = self.dram_in("ssc", [NSAMP, 2, D])
        self.wbig = self.dram_in("wbig", [self.wbig_len])
        self.wscr = self.nc.dram_tensor("wscr", [self.wbig_len], BF16, kind="Internal").ap()
        self.lnp = self.dram_in("lnp", [8, D])
        self.wcq_d = self.dram_in("wcq", [128, 24 * 4])
        self.wcs_d = self.dram_in("wcs", [128, 8 * 3])
        self.smallp = self.dram_in("smallp", [2, 8])
        self.won_d = self.dram_in("won", [128])
        self.cst_d = self.dram_in("cst", [128, len(CSTF_NAMES), 128])
        self.cst2_d = self.dram_in("cst2", [128, len(CSTB_NAMES) * 128])
        self.i16_d = self.dram_in("i16", [128, 256])
        self.y = self.dram_out("y", [SEQ, D])
        self.ys = self.dram_out("ys", [NSAMP, D])
        self.sgp = self.dram_out("sgp", [H, 128, 128])
        self.sqp = self.dram_out("sqp", [3, QKV_W])
        self.ssp = self.dram_out("ssp", [2, D])
        self.sgs = self.dram_out("sgs", [NSAMP, H, 128, 128])
        self.sqs = self.dram_out("sqs", [NSAMP, 3, QKV_W])
        self.sss = self.dram_out("sss", [NSAMP, 2, D])
        self.outkeys = []

        sb = self.sb
        self.wring = [sb("wr%d" % i, [128, SLOT], BF16) for i in range(NSLOT)]
        self.xres = sb("xres", [128, 4, D], F32)
        self.xT = sb("xT", [128, 8, NTP], BF16)
        self.gbt = sb("gbt", [128, 2, D], F32)
        self.cstf = sb("cstf", [128, len(CSTF_NAMES), 128], F32)
        self.cstb = sb("cstb", [128, len(CSTB_NAMES), 128], BF16)
        self.i16b = sb("i16b", [128, 256], BF16)
        self.wcq = sb("wcq_s", [128, 24, 4], F32)
        self.wcs = sb("wcs_s", [128, 8, 3], F32)
        self.wonb = sb("wonb", [128, 128], F32)
        self.smallb = sb("smallb", [128, 16], F32)
        self.negA = sb("negA", [128, 8], F32)
        self.histq = sb("histq", [128, 24, 3], F32)
        self.hists = sb("hists", [128, 8, 2], F32)
        self.S = sb("S", [128, H, 128], F32)
        self.Sbf = sb("Sbf", [128, H, 128], BF16)
        self.A1 = sb("A1", [128, 24, NTP], BF16)
        self.ztok = sb("ztok", [128, 4, D], BF16)
        self.ktok = sb("ktok", [128, H, 128], BF16)
        self.vtok = sb("vtok", [128, H, 128], BF16)
        self.batok = sb("batok", [128, 4, 16], F32)
        self.beta = sb("beta", [128, 4, 8], F32)
        self.gtok = sb("gtok", [128, 4, 8], F32)
        self.tf = [sb("tf%d" % i, [128, NTP + 4], F32) for i in range(11)]
        self.tfi = 0
        self.tb16 = [sb("tb%d" % i, [128, NTP], BF16) for i in range(4)]
        self.tbi = 0
        self.t1 = sb("t1", [128, D], F32)
        self.t2 = sb("t2", [128, D], F32)
        self.otok = sb("otok", [128, D], F32)
        self.xb16 = sb("xb16", [128, D], BF16)
        self.stat = sb("stat", [128, 4, 16], F32)
        self.stat3 = sb("stat3", [128, 16], F32)
        self.xb16b = sb("xb16b", [128, D], BF16)
        self.pT = sb("pT", [128, 2, NTP], BF16)
        self.pb = sb("pb", [128, 256], BF16)
        GQ = [("decTm", F32), ("Lg", F32), ("qkTm", BF16), ("MT", BF16), ("M", BF16),
              ("Na", BF16), ("Nb", BF16), ("Nc", BF16), ("Nd", BF16), ("Pa", BF16), ("Pb", BF16),
              ("Pc", BF16), ("Pd", BF16)]
        self.gqs = []
        self.arenaA = sb("arenaA", [128, 15 * 512], BF16)
        self.arenaB = sb("arenaB", [128, 15 * 512], BF16)
        self.gq_names = [nm for nm, _ in GQ]
        for ar, pfx in ((self.arenaA, "gA_"), (self.arenaB, "gB_")):
            gX = {}
            o = 0
            for nm, dt in GQ:
                n = 1024 if dt == F32 else 512
                v = ar[:, o:o + n]
                if dt == F32:
                    v = v.bitcast(F32)
                gX[nm] = v.rearrange("p (h d) -> p h d", h=4)
                o += n
            self.gqs.append((gX, pfx))
        self.kdec2 = [sb("kdec%d" % i, [128, H, 128], BF16) for i in range(2)]
        self.gsm2 = [sb("gsm%d" % i, [128, 64], F32) for i in range(2)]
        self.vtok2 = sb("vtok2", [128, H, 128], BF16)
        self.Uk = [sb("Uk%d" % i, [128, 4, 128], BF16) for i in range(2)]
        self.Qk = [sb("Qk%d" % i, [128, 4, 128], BF16) for i in range(2)]
        self.gsm = self.gsm2[0]
        self.kdec = self.kdec2[0]
        fA = lambda o, n: self.arenaA[:, o:o + n]
        fB = lambda o, n: self.arenaB[:, o:o + n]
        self.Sin = [fA(k * 2048, 2048).bitcast(F32).rearrange("p (h d) -> p h d", h=H) for k in range(3)]
        self.hss = fA(6144, 512).bitcast(F32).rearrange("p (c j b) -> p c j b", c=8, j=2)
        self.Sinb = [fA(6656, 1024).rearrange("p (h d) -> p h d", h=H), fB(6400, 1024).rearrange("p (h d) -> p h d", h=H)]
        self.kTm = fB(0, 2048).rearrange("p (h a b) -> p h a b", h=H, a=NSAMP)
        self.qTm = fB(2048, 2048).rearrange("p (h a b) -> p h a b", h=H, a=NSAMP)
        self.hsq = fB(4096, 2304).bitcast(F32).rearrange("p (c j b) -> p c j b", c=24, j=3)
        self.kmask = [sb("kmask%d" % i, [NSAMP, D], BF16) for i in range(2)]
        self.qtok = sb("qtok", [NSAMP, D], BF16)
        self.abc = sb("abc", [128, 128], F32)
        self.ps = [self.es.enter_context(nc.psum_tensor("ps%d" % i, [128, 512], F32)) for i in range(8)]
        self.psb = [p.bitcast(BF16) for p in self.ps]

        self.recording = True
        self.slab_i = 0
        self.setup()
        npass = self.debug.get("npass", 4)
        self.prologue_done = False
        do_sample = not self.debug.get("nosample", False)
        for pi in range(npass):
            if pi + 1 < npass:
                nsrc = (self.x[(pi + 1) * NTP:(pi + 2) * NTP, :], 4, 128)
            elif do_sample:
                nsrc = (self.xs, 1, NSAMP)
            else:
                nsrc = None
            if self.debug.get("stop"):
                nsrc = None
            self.layer_pass(pi, NTP, sample=False, last=(pi == npass - 1), next_src=nsrc)
            if pi == 0:
                self.recording = False
                self.nslab_pass = len(self.slab_off)
        if not self.debug.get("nosample", False):
            gbk = [p + nm for p in ("gA_", "gB_") for nm in self.gq_names]
            gbk += ["gsm0", "gsm1", "vtok0", "vtok1", "kdec0", "kdec1"]
            P.add("dve", lambda e: e.memset(self.gsm[:, 60:64], 0.0), gbk,
                  gbk + ["kTm", "qTm", "hsamp", "Sin0", "Sin1", "Sin2", "Sinb0", "Sinb1", "gsm", "vtok"])
            self.layer_pass(0, NSAMP, sample=True, last=True)
        P.add("sp", None, r=self.outkeys)
        P.finalize()
        P.emit(self.es)

    def cf(self, name):
        return self.cstf[:, CSTF_NAMES.index(name), :]

    def cb(self, name):
        return self.cstb[:, CSTB_NAMES.index(name), :]

    def cb4(self, name):
        i = CSTB_NAMES.index(name)
        return self.cstb[:, i:i + 1, :].to_broadcast([128, 4, 128])

    def cf4(self, name):
        i = CSTF_NAMES.index(name)
        return self.cstf[:, i:i + 1, :].to_broadcast([128, 4, 128])

    def tmpf(self):
        i = self.tfi % len(self.tf)
        self.tfi += 1
        return self.tf[i], "tf%d" % i

    def tmpb(self):
        i = self.tbi % len(self.tb16)
        self.tbi += 1
        return self.tb16[i], "tb%d" % i

    def setup(self):
        d = self.dma
        d("sp", self.cstf[:], self.cst_d, "c0", [], ["cstf"])
        nb = len(CSTB_NAMES) * 128
        for k in range(0, nb, 1024):
            n = min(1024, nb - k)
            d("sp", self.t1[:, 0:n], self.cst2_d[:, k:k + n], "c1", [], ["t1"])
            self.cp("dve", self.cstb[:].rearrange("p c d -> p (c d)")[:, k:k + n], self.t1[:, 0:n], ["t1"], ["cstb"])
        d("sp", self.t2[:, 0:256], self.i16_d, "c1", [], ["t2"])
        self.cp("dve", self.i16b[:], self.t2[:, 0:256], ["t2"], ["i16b"])
        d("sp", self.wcq[:].rearrange("p c j -> p (c j)"), self.wcq_d, "c2", [], ["wcq"])
        d("sp", self.wcs[:].rearrange("p c j -> p (c j)"), self.wcs_d, "c3", [], ["wcs"])
        d("sp", self.wonb[:], self.won_d.partition_broadcast(128), "c4", [], ["wonb"])
        d("sp", self.smallb[:], self.smallp.rearrange("a b -> (a b)").partition_broadcast(128), "c5", [], ["smallb"])
        self.act(self.negA[:], self.smallb[:, 0:8], AF.Exp, ["smallb"], ["negA"])
        self.ts("dve", self.negA[:], self.negA[:], -1.0, ALU.mult, ["negA"], ["negA"])
        self.P.add("dve", lambda e: e.memset(self.S[:], 0.0), [], ["S0", "S1"])
        self.P.add("dve", lambda e: e.memset(self.Sbf[:], 0.0), [], ["Sbf0", "Sbf1"])
        self.P.add("dve", lambda e: e.memset(self.histq[:], 0.0), [], ["histq"])
        self.P.add("dve", lambda e: e.memset(self.hists[:], 0.0), [], ["hists"])

    def make_xT(self, tb, TB, bank, staged=False):
        ps, psk = self.psb[bank], "ps%d" % bank
        xb, xbk = ((self.xb16, "xb16"), (self.xb16b, "xb16b"))[tb % 2]
        if not staged:
            self.cp("act", xb[:TB, :], self.xres[:TB, tb, :], ["xres%d" % tb], [xbk])
        for c in range(8):
            self.tr(ps[:, c * TB:(c + 1) * TB], xb[:TB, c * 128:(c + 1) * 128], self.cb("ident")[:TB, :TB],
                    [xbk, "cstb"], [psk])
        self.cp("dve", self.xT[:, :, tb * TB:(tb + 1) * TB],
                ps[:, 0:8 * TB].rearrange("p (c t) -> p c t", c=8), [psk], ["xT%d" % tb])

    def layer_norm(self, idx, NB, TB, final_out=None):
        self.dma("aux", self.gbt[:, 0, :], self.lnp[2 * idx, :].partition_broadcast(128), "gb0", [], ["gbt0"])
        self.dma("aux", self.gbt[:, 1, :], self.lnp[2 * idx + 1, :].partition_broadcast(128), "gb1", [], ["gbt1"])
        eps = LN_EPS / (ALPHA * ALPHA)
        st = self.stat
        for tb in range(NB):
            xr = self.xres[:TB, tb, :]
            xk = "xres%d" % tb
            self.P.add("dve", lambda e, xr=xr, tb=tb: e.bn_stats(st[:TB, tb, 0:6], xr[:, 0:512]), [xk], ["stat"])
            self.P.add("dve", lambda e, xr=xr, tb=tb: e.bn_stats(st[:TB, tb, 6:12], xr[:, 512:1024]), [xk], ["stat"])
            self.P.add("dve", lambda e, tb=tb: e.bn_aggr(st[:TB, tb, 12:14], st[:TB, tb, 0:12]), ["stat"], ["stat"])
        self.act(st[:TB, 0:NB, 14], st[:TB, 0:NB, 13], AF.Sqrt, ["stat"], ["stat2"], bias=eps)
        self.P.add("dve", lambda e: e.reciprocal(st[:TB, 0:NB, 15], st[:TB, 0:NB, 14]), ["stat2"], ["stat2"])
        for tb in range(NB):
            xr = self.xres[:TB, tb, :]
            xk = "xres%d" % tb
            self.stt(self.t1[:TB, :], xr, st[:TB, tb, 12:13], self.gbt[:TB, 0, :], ALU.subtract, ALU.mult,
                     [xk, "stat", "gbt0"], ["t1"])
            self.stt(xr, self.t1[:TB, :], st[:TB, tb, 15:16], self.gbt[:TB, 1, :], ALU.mult, ALU.add,
                     ["t1", "stat2", "gbt1"], [xk])
            if final_out is not None:
                ok = final_out[1] + str(tb)
                self.dma("aux", final_out[0][tb * TB:(tb + 1) * TB, :], xr, "yo%d" % tb, [xk], [ok])
                self.outkeys.append(ok)
            else:
                self.make_xT(tb, TB, (2 * tb) % 8)

    def ffn(self, pfx, NB, TB):
        NT = NB * TB
        for j0 in range(0, NJ, 2):
            (wg, wu), wk = self.slab([(pfx + "_w_gate", 0, D, j0 * 128, j0 * 128 + 256),
                                      (pfx + "_w_up", 0, D, j0 * 128, j0 * 128 + 256)])
            for jj in range(2):
                j = j0 + jj
                bg, bu = 2 * (j % 2), 2 * (j % 2) + 1
                for kc in range(8):
                    self.mm(self.ps[bg][:, :NT], wg[:, kc, jj * 128:(jj + 1) * 128], self.xT[:, kc, :NT],
                            [wk, "xT0", "xT1", "xT2", "xT3"], ["ps%d" % bg], start=(kc == 0), stop=(kc == 7))
                for kc in range(8):
                    self.mm(self.ps[bu][:, :NT], wu[:, kc, jj * 128:(jj + 1) * 128], self.xT[:, kc, :NT],
                            [wk, "xT0", "xT1", "xT2", "xT3"], ["ps%d" % bu], start=(kc == 0), stop=(kc == 7))
                t, tk = self.tmpf()
                self.act(t[:, :NT], self.ps[bg][:, :NT], AF.Silu, ["ps%d" % bg], [tk])
                self.tt("dve", self.A1[:, j, :NT], t[:, :NT], self.ps[bu][:, :NT], ALU.mult,
                        [tk, "ps%d" % bu], ["A1.%d" % j])
        for j0 in range(0, NJ, 4):
            j1 = min(NJ, j0 + 4)
            (wd,), wk = self.slab([(pfx + "_w_down", j0 * 128, j1 * 128, 0, D)])
            for jj in range(j1 - j0):
                j = j0 + jj
                for tb in range(NB):
                    for nh in range(2):
                        b = tb * 2 + nh
                        self.mm(self.ps[b][:TB, :], self.A1[:, j, tb * TB:(tb + 1) * TB], wd[:, jj, nh * 512:(nh + 1) * 512],
                                [wk, "A1.%d" % j], ["ps%d" % b], start=(j == 0), stop=(j == NJ - 1))
        c = 0.5 / ALPHA
        for tb in range(NB):
            for nh in range(2):
                b = tb * 2 + nh
                xr = self.xres[:TB, tb, nh * 512:(nh + 1) * 512]
                self.stt(xr, self.ps[b][:TB, :], c, xr, ALU.mult, ALU.add, ["ps%d" % b, "xres%d" % tb], ["xres%d" % tb])

    def prologue(self, src, NB, TB):
        for tb in range(NB):
            xb, xbk = ((self.xb16, "xb16"), (self.xb16b, "xb16b"))[tb % 2]
            self.dma("pool", xb[:TB, :], src[tb * TB:(tb + 1) * TB, :], "xc%d" % (tb % 2), [], [xbk])
            self.make_xT(tb, TB, tb % 8, staged=True)

    def layer_pass(self, pi, NT, sample, last, next_src=None):
        TB = min(128, NT)
        NB = NT // TB
        self.slab_i = 0 if self.recording else self.slab_i
        if not sample:
            src, psrc = self.x[pi * NT:(pi + 1) * NT, :], self.pp[pi * NT:(pi + 1) * NT, :]
            yout = (self.y[pi * NT:(pi + 1) * NT, :], "y%d_" % pi)
        else:
            src, psrc = self.xs, self.psm
            yout = (self.ys, "ys_")
        if not self.prologue_done:
            self.prologue(src, NB, TB)
        self.prologue_done = False
        for tb in range(NB):
            self.dma("aux", self.xres[:TB, tb, :], src[tb * TB:(tb + 1) * TB, :], "x%d" % tb, [], ["xres%d" % tb])
        stop = self.debug.get("stop")
        self.ffn("ffn1", NB, TB)
        if stop == "ffn1":
            return self.dump(yout, NB, TB)
        self.layer_norm(0, NB, TB)
        if stop == "ln1":
            return self.dump(yout, NB, TB)
        self.mixers(pi, NB, TB, sample, last)
        if stop == "mix":
            return self.dump(yout, NB, TB)
        self.layer_norm(1, NB, TB)
        self.ffn("ffn2", NB, TB)
        self.layer_norm(2, NB, TB)
        if stop == "ln3":
            return self.dump(yout, NB, TB)
        self.ple(psrc, NB, TB)
        if next_src is not None:
            self.prologue(*next_src)
            self.prologue_done = True
        self.layer_norm(3, NB, TB, final_out=yout)

    def dump(self, yout, NB, TB):
        for tb in range(NB):
            ok = yout[1] + str(tb)
            self.dma("aux", yout[0][tb * TB:(tb + 1) * TB, :], self.xres[:TB, tb, :], "yo%d" % tb, ["xres%d" % tb], [ok])
            self.outkeys.append(ok)

    def ple(self, psrc, NB, TB):
        NT = NB * TB
        for tb in range(NB):
            pf, pfk = self.tmpf()
            self.dma("aux", pf[:TB, 0:256], psrc[tb * TB:(tb + 1) * TB, :], "pf", [], [pfk])
            self.cp("act", self.pb[:TB, :], pf[:TB, 0:256], [pfk], ["pb"])
            for c in range(2):
                self.tr(self.psb[7][:, c * TB:(c + 1) * TB], self.pb[:TB, c * 128:(c + 1) * 128],
                        self.cb("ident")[:TB, :TB], ["pb", "cstb"], ["ps7"])
            self.cp("dve", self.pT[:, :, tb * TB:(tb + 1) * TB],
                    self.psb[7][:, 0:2 * TB].rearrange("p (c t) -> p c t", c=2), ["ps7"], ["pT"])
        for nh in range(2):
            (wg,), wgk = self.slab([("w_ple_gate", 0, D, nh * 512, (nh + 1) * 512)])
            (wp,), wpk = self.slab([("w_ple_proj", 0, 256, nh * 512, (nh + 1) * 512)])
            for tb in range(NB):
                bg, bp = 2 * (tb % 2), 2 * (tb % 2) + 1
                for kc in range(8):
                    self.mm(self.ps[bg][:TB, :], self.xT[:, kc, tb * TB:(tb + 1) * TB], wg[:, kc, :],
                            [wgk, "xT0", "xT1", "xT2", "xT3"], ["ps%d" % bg], start=(kc == 0), stop=(kc == 7))
                for kc in range(2):
                    self.mm(self.ps[bp][:TB, :], self.pT[:, kc, tb * TB:(tb + 1) * TB], wp[:, kc, :],
                            [wpk, "pT"], ["ps%d" % bp], start=(kc == 0), stop=(kc == 1))
                t, tk = self.tmpf()
                self.act(t[:TB, :512], self.ps[bg][:TB, :], AF.Sigmoid, ["ps%d" % bg], [tk])
                self.tt("dve", t[:TB, :512], t[:TB, :512], self.ps[bp][:TB, :], ALU.mult, [tk, "ps%d" % bp], [tk])
                xr = self.xres[:TB, tb, nh * 512:(nh + 1) * 512]
                self.stt(xr, t[:TB, :512], 1.0 / ALPHA, xr, ALU.mult, ALU.add, [tk, "xres%d" % tb], ["xres%d" % tb])

    def conv_chunk(self, psbank, NT, taps_hist, wts, ntap, hist_tile, hist_key, sample, src_is_psum=True, src=None):
        H_ = ntap - 1
        cbt, cbk = self.tmpf()
        if src_is_psum:
            self.cp("act", cbt[:, H_:H_ + NT], self.ps[psbank][:, :NT], ["ps%d" % psbank], [cbk])
        else:
            src(cbt[:, H_:H_ + NT], cbk)
        if not sample:
            self.cp("dve", cbt[:, 0:H_], hist_tile, [hist_key], [cbk])
            self.cp("dve", hist_tile, cbt[:, NT:NT + H_], [cbk], [hist_key])
            taps = [cbt[:, j:j + NT] for j in range(ntap)]
            tr_ = [cbk]
        else:
            taps = [taps_hist[j] for j in range(H_)] + [cbt[:, H_:H_ + NT]]
            tr_ = [cbk, "hsamp"]
        acc, ak = self.tmpf()
        self.ts("dve", acc[:, :NT], taps[0], wts[0], ALU.mult, tr_ + ["wc"], [ak])
        for j in range(1, ntap):
            self.stt(acc[:, :NT], taps[j], wts[j], acc[:, :NT], ALU.mult, ALU.add, tr_ + ["wc", ak], [ak])
        return acc, ak, cbt, cbk

    def mixers(self, pi, NB, TB, sample, last):
        NT = NB * TB
        A1 = self.A1
        if sample:
            self.load_sample_hist()
        def finish(grp):
            for (c, so, sk, sq, sqk, cbt, cbk) in grp:
                if sample:
                    self.tr(self.ps[6][:NT, (c % 4) * 128:(c % 4 + 1) * 128], cbt[:, 3:3 + NT], self.cf("ident"),
                            [cbk, "cstf"], ["ps6"])
                    if c % 4 == 3:
                        stg, stk = self.stage(c // 8)
                        self.cp("act", stg[:NT, (c % 8 - 3) * 128:(c % 8 + 1) * 128], self.ps[6][:NT, :], ["ps6"], [stk])
            qk = [g_ for g_ in grp if g_[0] < 16]
            sds = []
            for (c, so, sk, sq, sqk, cbt, cbk) in qk:
                b2 = 4 + c % 2 if sample else 4 + c % 4
                self.mm(self.ps[b2][:, :NT], self.cb("ones"), sq[:, :NT], [sqk, "cstb"], ["ps%d" % b2])
            for (c, so, sk, sq, sqk, cbt, cbk) in qk:
                b2 = 4 + c % 2 if sample else 4 + c % 4
                sd, sdk = self.tmpf()
                sds.append((sd, sdk))
                self.act(sd[:, :NT], self.ps[b2][:, :NT], AF.Ln, ["ps%d" % b2], [sdk], bias=L2_EPS)
            for (sd, sdk) in sds:
                self.act(sd[:, :NT], sd[:, :NT], AF.Exp, [sdk], [sdk], scale=-0.5)
            for (c, so, sk, sq, sqk, cbt, cbk), (sd, sdk) in zip(qk, sds):
                const = 128.0 ** -0.5 if c < 8 else 1.0
                self.stt(A1[:, c, :NT], so[:, :NT], const, sd[:, :NT], ALU.mult, ALU.mult, [sk, sdk], ["A1.%d" % c])

        pend = None
        for g in range(6):
            (wq,), wk = self.slab([("w_in", 0, D, g * 512, (g + 1) * 512)])
            for pr in range(2):
                cs_ = [g * 4 + pr * 2, g * 4 + pr * 2 + 1]
                for c in cs_:
                    jj = c % 4
                    bank = c % 4
                    for kc in range(8):
                        self.mm(self.ps[bank][:, :NT], wq[:, kc, jj * 128:(jj + 1) * 128], self.xT[:, kc, :NT],
                                [wk, "xT0", "xT1", "xT2", "xT3"], ["ps%d" % bank], start=(kc == 0), stop=(kc == 7))
                convs = []
                for c in cs_:
                    th = [self.hsq[:, c, j, :] for j in range(3)] if sample else None
                    wts = [self.wcq[:, c, j:j + 1] for j in range(4)]
                    convs.append(self.conv_chunk(c % 4, NT, th, wts, 4, self.histq[:, c, :], "histq%d" % c, sample))
                if pend is not None:
                    finish(pend)
                cur = []
                for c, (acc, ak, cbt, cbk) in zip(cs_, convs):
                    if c >= 16:
                        self.act(A1[:, c, :NT], acc[:, :NT], AF.Silu, [ak], ["A1.%d" % c])
                        cur.append((c, None, None, None, None, cbt, cbk))
                    else:
                        self.act(acc[:, :NT], acc[:, :NT], AF.Silu, [ak], [ak])
                        sq, sqk = self.tmpb()
                        if self.recording:
                            self.act(sq[:, :NT], acc[:, :NT], AF.Square, [ak], [sqk])
                        else:
                            self.tt("pool", sq[:, :NT], acc[:, :NT], acc[:, :NT], ALU.mult, [ak], [sqk])
                        cur.append((c, acc, ak, sq, sqk, cbt, cbk))
                pend = cur
        finish(pend)
        if sample:
            for k in range(3):
                stg, stk = self.stage(k)
                self.dma("aux", self.sqs[:, 2, k * 1024:(k + 1) * 1024], stg[:NSAMP, :], "so0", [stk], ["sqs2_%d" % k])
                self.outkeys.append("sqs2_%d" % k)
            self.dma("aux", self.sqs[:, 0:2, :], self.sq[:, 1:3, :], "so1", [], ["sqs01"])
            self.outkeys += ["sqs01"]
        elif last:
            for j in range(3):
                self.dma("aux", self.sqp[j, :].rearrange("(c p) -> p c", p=128), self.histq[:, :, j], "so0",
                         ["histq%d" % c for c in range(24)], ["sqp%d" % j], slow=True)
                self.outkeys.append("sqp%d" % j)
        if self.debug.get("mstop") == "A":
            return
        for nh in range(2):
            (wz,), wk = self.slab([("w_in", 0, D, Z0 + nh * 512, Z0 + (nh + 1) * 512)])
            for tb in range(NB):
                b = 4 + tb % 2
                for kc in range(8):
                    self.mm(self.ps[b][:TB, :], self.xT[:, kc, tb * TB:(tb + 1) * TB], wz[:, kc, :],
                            [wk, "xT0", "xT1", "xT2", "xT3"], ["ps%d" % b], start=(kc == 0), stop=(kc == 7))
                self.act(self.ztok[:TB, tb, nh * 512:(nh + 1) * 512], self.ps[b][:TB, :], AF.Silu, ["ps%d" % b], ["ztok%d" % tb])
        (wba,), wk = self.slab([("w_in", 0, D, BETA0, BETA0 + 16)])
        for tb in range(NB):
            for kc in range(8):
                self.mm(self.ps[6][:TB, 0:16], self.xT[:, kc, tb * TB:(tb + 1) * TB], wba[:, kc, :],
                        [wk, "xT0", "xT1", "xT2", "xT3"], ["ps6"], start=(kc == 0), stop=(kc == 7))
            self.act(self.beta[:TB, tb, :], self.ps[6][:TB, 0:8], AF.Sigmoid, ["ps6"], ["beta"])
            self.tt("dve", self.batok[:TB, tb, 8:16], self.ps[6][:TB, 8:16], self.smallb[:TB, 8:16], ALU.add,
                    ["ps6", "smallb"], ["batok"])
        for tb in range(NB):
            self.act(self.batok[:TB, tb, 0:8], self.batok[:TB, tb, 8:16], AF.Exp, ["batok"], ["batok"])
        for tb in range(NB):
            self.act(self.batok[:TB, tb, 0:8], self.batok[:TB, tb, 0:8], AF.Ln, ["batok"], ["batok"], bias=1.0)
            self.tt("dve", self.gtok[:TB, tb, :], self.batok[:TB, tb, 0:8], self.negA[:TB, :], ALU.mult,
                    ["batok", "negA"], ["gtok"])
        if self.debug.get("mstop") == "B":
            return
        if sample:
            self.gdn_sample()
        else:
            self.gdn_all(NB)
            if last:
                self.dma("aux", self.sgp.rearrange("h k v -> k h v"), self.S[:], "so1", ["S0", "S1"], ["sgp"])
                self.outkeys.append("sgp")
        if self.debug.get("mstop") == "C":
            return
        for c in range(8):
            (wB, wC, wH), wk = self.slab([("w_in", 0, D, B0 + c * 128, B0 + (c + 1) * 128),
                                          ("w_in", 0, D, C0 + c * 128, C0 + (c + 1) * 128),
                                          ("w_in", 0, D, H0 + c * 128, H0 + (c + 1) * 128)])
            bB, bC, bH = 0 + 3 * (c % 2), 1 + 3 * (c % 2), 2 + 3 * (c % 2)
            for (w_, b_) in ((wC, bC), (wH, bH), (wB, bB)):
                for kc in range(8):
                    self.mm(self.ps[b_][:, :NT], w_[:, kc, :], self.xT[:, kc, :NT], [wk, "xT0", "xT1", "xT2", "xT3"], ["ps%d" % b_],
                            start=(kc == 0), stop=(kc == 7))
            ct, ck = self.tmpf()
            self.cp("act", ct[:, :NT], self.ps[bC][:, :NT], ["ps%d" % bC], [ck])

            def src(dst, dk, ct=ct, ck=ck, bH=bH):
                self.tt("dve", dst, ct[:, :NT], self.ps[bH][:, :NT], ALU.mult, [ck, "ps%d" % bH], [dk])
            th = [self.hss[:, c, j, :] for j in range(2)] if sample else None
            wts = [self.wcs[:, c, j:j + 1] for j in range(3)]
            acc, ak, cbt, cbk = self.conv_chunk(None, NT, th, wts, 3, self.hists[:, c, :], "hists%d" % c, sample,
                                                src_is_psum=False, src=src)
            if sample:
                self.tr(self.ps[6][:NT, (c % 4) * 128:(c % 4 + 1) * 128], cbt[:, 2:2 + NT], self.cf("ident"),
                        [cbk, "cstf"], ["ps6"])
                if c % 4 == 3:
                    stg, stk = self.stage(0)
                    self.cp("act", stg[:NT, (c - 3) * 128:(c + 1) * 128], self.ps[6][:NT, :], ["ps6"], [stk])
            self.tt("dve", A1[:, c, :NT], acc[:, :NT], self.ps[bB][:, :NT], ALU.mult, [ak, "ps%d" % bB], ["A1.%d" % c])
        if sample:
            stg, stk = self.stage(0)
            self.dma("aux", self.sss[:, 1, :], stg[:NSAMP, 0:D], "so2", [stk], ["sss1"])
            self.dma("aux", self.sss[:, 0:1, :], self.ssc[:, 1:2, :], "so3", [], ["sss0"])
            self.outkeys += ["sss1", "sss0"]
        elif last:
            for j in range(2):
                self.dma("aux", self.ssp[j, :].rearrange("(c p) -> p c", p=128), self.hists[:, :, j], "so2",
                         ["hists%d" % c for c in range(8)], ["ssp%d" % j], slow=True)
                self.outkeys.append("ssp%d" % j)
        if self.debug.get("mstop") == "D":
            return
        for c in range(8):
            (wpg, wgg, wps, wgs), wk = self.slab([("w_p_gdn", 0, D, c * 128, (c + 1) * 128),
                                                  ("w_in", 0, D, GG0 + c * 128, GG0 + (c + 1) * 128),
                                                  ("w_p_sc", 0, D, c * 128, (c + 1) * 128),
                                                  ("w_in", 0, D, GS0 + c * 128, GS0 + (c + 1) * 128)])
            o = 4 * (c % 2)
            for (w_, b_, rhs_, rk) in ((wpg, o, A1[:, 16:24, :], ["A1.%d" % k for k in range(16, 24)]),
                                       (wgg, o + 1, self.xT, ["xT0", "xT1", "xT2", "xT3"]),
                                       (wps, o + 2, A1[:, 0:8, :], ["A1.%d" % k for k in range(8)]),
                                       (wgs, o + 3, self.xT, ["xT0", "xT1", "xT2", "xT3"])):
                for kc in range(8):
                    self.mm(self.ps[b_][:, :NT], w_[:, kc, :], rhs_[:, kc, :NT], [wk] + rk, ["ps%d" % b_],
                            start=(kc == 0), stop=(kc == 7))
            s1, s1k = self.tmpf()
            self.act(s1[:, :NT], self.ps[o + 1][:, :NT], AF.Sigmoid, ["ps%d" % (o + 1)], [s1k])
            self.tt("dve", s1[:, :NT], s1[:, :NT], self.ps[o][:, :NT], ALU.mult, [s1k, "ps%d" % o], [s1k])
            s2, s2k = self.tmpf()
            self.act(s2[:, :NT], self.ps[o + 3][:, :NT], AF.Sigmoid, ["ps%d" % (o + 3)], [s2k])
            self.tt("dve", s2[:, :NT], s2[:, :NT], self.ps[o + 2][:, :NT], ALU.mult, [s2k, "ps%d" % (o + 2)], [s2k])
            self.tt("dve", A1[:, 8 + c, :NT], s1[:, :NT], s2[:, :NT], ALU.add, [s1k, s2k], ["A1.%d" % (8 + c)])
        for nh in range(2):
            (wo,), wk = self.slab([("w_o", 0, D, nh * 512, (nh + 1) * 512)])
            for tb in range(NB):
                b = tb % 2
                for kc in range(8):
                    self.mm(self.ps[b][:TB, :], A1[:, 8 + kc, tb * TB:(tb + 1) * TB], wo[:, kc, :],
                            [wk, "A1.%d" % (8 + kc)], ["ps%d" % b], start=(kc == 0), stop=(kc == 7))
                xr = self.xres[:TB, tb, nh * 512:(nh + 1) * 512]
                self.stt(xr, self.ps[b][:TB, :], 1.0 / ALPHA, xr, ALU.mult, ALU.add, ["ps%d" % b, "xres%d" % tb], ["xres%d" % tb])

    def onorm_and_T(self, tb, TB):
        self.lockstep([self.onorm_gen(tb, TB)])

    def onorm_gen(self, tb, TB, bank=7):
        o3 = self.otok[:TB, :].rearrange("p (h d) -> p h d", h=H)
        t13 = self.t1[:TB, :].rearrange("p (h d) -> p h d", h=H)
        t23 = self.t2[:TB, :].rearrange("p (h d) -> p h d", h=H)
        st = self.stat3
        self.act(self.t1[:TB, :], self.otok[:TB, :], AF.Square, ["otok"], ["t1"])
        yield
        self.P.add("dve", lambda e: e.tensor_reduce(st[:TB, 0:8], t13, AX.X, ALU.add), ["t1"], ["stat3"])
        self.act(st[:TB, 0:8], st[:TB, 0:8], AF.Sqrt, ["stat3"], ["stat3"], bias=RMS_EPS, scale=1.0 / 128.0)
        yield
        self.P.add("dve", lambda e: e.reciprocal(st[:TB, 8:16], st[:TB, 0:8]), ["stat3"], ["stat3"])
        yield
        self.tt("dve", t13, o3, st[:TB, 8:16].unsqueeze(2).to_broadcast([TB, H, 128]), ALU.mult, ["otok", "stat3"], ["t1"])
        z3 = self.ztok[:TB, tb, :].rearrange("p (h d) -> p h d", h=H)
        self.tt("dve", t23, z3, self.wonb[:TB, :].unsqueeze(1).to_broadcast([TB, H, 128]), ALU.mult,
                ["ztok%d" % tb, "wonb"], ["t2"])
        yield
        self.tt("dve", self.xb16[:TB, :], self.t1[:TB, :], self.t2[:TB, :], ALU.mult, ["t1", "t2"], ["xb16"])
        yield
        for c in range(8):
            self.tr(self.psb[bank][:, c * TB:(c + 1) * TB], self.xb16[:TB, c * 128:(c + 1) * 128], self.cb("ident")[:TB, :TB],
                    ["xb16", "cstb"], ["ps%d" % bank])
        yield
        self.cp("act", self.A1[:, 16:24, tb * TB:(tb + 1) * TB],
                self.psb[bank][:, 0:8 * TB].rearrange("p (c t) -> p c t", c=8), ["ps%d" % bank],
                ["A1.%d" % k for k in range(16, 24)])

    def inv_chain(self, tb, hg, G, gp, pb):
        A1 = self.A1
        blk = slice(tb * 128, (tb + 1) * 128)
        g8 = self.gtok[:, tb, :]
        hs = [hg * 4 + hh for hh in range(4)]
        K = lambda nm: gp + nm
        rot = [0]

        def nb():
            x = pb[rot[0] % len(pb)]
            rot[0] += 1
            return x
        b0, b1, b2 = nb(), nb(), nb()
        f4 = lambda t: t.rearrange("p h d -> p (h d)")
        kq_r = ["A1.%d" % (8 + h) for h in hs] + ["A1.%d" % h for h in hs]
        for hh, h in enumerate(hs):
            self.ts("dve", G["Lg"][:, hh, :], self.cf("ltri"), g8[:, h:h + 1], ALU.mult, ["cstf", "gtok"], [K("Lg")])
        yield
        for hh, h in enumerate(hs):
            cs = slice(hh * 128, (hh + 1) * 128)
            self.mm(self.ps[b0][:, cs], self.cf("su"), G["Lg"][:, hh, :], ["cstf", K("Lg")], ["ps%d" % b0])
            self.mm(self.ps[b1][:, cs], A1[:, 8 + h, blk], A1[:, 8 + h, blk], kq_r, ["ps%d" % b1])
            self.mm(self.ps[b2][:, cs], A1[:, 8 + h, blk], A1[:, h, blk], kq_r, ["ps%d" % b2])
        yield
        self.act(f4(G["decTm"]), self.ps[b0][:, :], AF.Exp, ["ps%d" % b0], [K("decTm")])
        yield
        self.tt("dve", G["decTm"], G["decTm"], self.cf4("muincl"), ALU.mult, [K("decTm"), "cstf"], [K("decTm")])
        yield
        self.tt("dve", f4(G["qkTm"]), self.ps[b2][:, :], f4(G["decTm"]), ALU.mult, ["ps%d" % b2, K("decTm")], [K("qkTm")])
        self.tt("dve", f4(G["Lg"]), self.ps[b1][:, :], f4(G["decTm"]), ALU.mult, ["ps%d" % b1, K("decTm")], [K("Lg")])
        yield
        self.tt("dve", G["MT"], G["Lg"],
                self.beta[:, tb, hg * 4:hg * 4 + 4].unsqueeze(2).to_broadcast([128, 4, 128]), ALU.mult,
                [K("Lg"), "beta"], [K("MT")])
        yield
        bt = nb()
        for hh in range(4):
            self.tr(self.psb[bt][:, hh * 128:(hh + 1) * 128], G["MT"][:, hh, :], self.cb("ident"), [K("MT"), "cstb"], ["ps%d" % bt])
        yield
        self.cp("act", f4(G["M"]), self.psb[bt][:, 0:512], ["ps%d" % bt], [K("M")])
        yield
        Nn, Nt, N2, N2t = "Na", "Nb", "Nc", "Nd"
        Pn, Pt, Pn2, Pt2 = "Pa", "Pb", "Pc", "Pd"
        self.tt("dve", G[Nn], G["M"], self.cb4("mndn"), ALU.mult, [K("M"), "cstb"], [K(Nn)])
        self.tt("dve", G[Nt], G["MT"], self.cb4("mndtn"), ALU.mult, [K("MT"), "cstb"], [K(Nt)])
        yield
        self.tt("dve", G[Pn], G[Nn], self.cb4("ident"), ALU.add, [K(Nn), "cstb"], [K(Pn)])
        self.tt("dve", G[Pt], G[Nt], self.cb4("ident"), ALU.add, [K(Nt), "cstb"], [K(Pt)])
        nstep = int(np.log2(NBK)) - 1
        for s_ in range(nstep):
            ba, bb = nb(), nb()
            for hh in range(4):
                cs = slice(hh * 128, (hh + 1) * 128)
                self.mm(self.ps[ba][:, cs], G[Nt][:, hh, :], G[Nn][:, hh, :], [K(Nt), K(Nn)], ["ps%d" % ba])
                self.mm(self.ps[bb][:, cs], G[Nn][:, hh, :], G[Nt][:, hh, :], [K(Nt), K(Nn)], ["ps%d" % bb])
            yield
            self.cp("act", f4(G[N2]), self.ps[ba][:, :], ["ps%d" % ba], [K(N2)])
            self.cp("act", f4(G[N2t]), self.ps[bb][:, :], ["ps%d" % bb], [K(N2t)])
            yield
            bc_, bd = nb(), nb()
            for hh in range(4):
                cs = slice(hh * 128, (hh + 1) * 128)
                self.mm(self.ps[bc_][:, cs], G[N2t][:, hh, :], G[Pn][:, hh, :], [K(N2t), K(Pn)], ["ps%d" % bc_])
                self.mm(self.ps[bd][:, cs], G[N2][:, hh, :], G[Pt][:, hh, :], [K(N2), K(Pt)], ["ps%d" % bd])
            yield
            self.tt("dve", f4(G[Pn2]), f4(G[Pn]), self.ps[bc_][:, :], ALU.add, [K(Pn), "ps%d" % bc_], [K(Pn2)])
            self.tt("dve", f4(G[Pt2]), f4(G[Pt]), self.ps[bd][:, :], ALU.add, [K(Pt), "ps%d" % bd], [K(Pt2)])
            yield
            Nn, Nt, N2, N2t = N2, N2t, Nn, Nt
            Pn, Pt, Pn2, Pt2 = Pn2, Pt2, Pn, Pt
        T, U, T2, U2 = Pn, Pt, Pn2, Pt2
        E_, F_, X_, Y_ = Nn, Nt, N2, N2t
        b = NBK
        while b < 128:
            lastlvl = (b == 64)
            self.tt("dve", G[E_], G["M"], self.cb4("me%d" % b), ALU.mult, [K("M"), "cstb"], [K(E_)])
            if not lastlvl:
                self.tt("dve", G[F_], G["MT"], self.cb4("me%dt" % b), ALU.mult, [K("MT"), "cstb"], [K(F_)])
            yield
            ba, bb = nb(), nb()
            for hh in range(4):
                cs = slice(hh * 128, (hh + 1) * 128)
                self.mm(self.ps[ba][:, cs], G[E_][:, hh, :], G[U][:, hh, :], [K(E_), K(U)], ["ps%d" % ba])
                if not lastlvl:
                    self.mm(self.ps[bb][:, cs], G[F_][:, hh, :], G[T][:, hh, :], [K(F_), K(T)], ["ps%d" % bb])
            yield
            self.cp("act", f4(G[Y_]), self.ps[ba][:, :], ["ps%d" % ba], [K(Y_)])
            if not lastlvl:
                self.cp("act", f4(G[X_]), self.ps[bb][:, :], ["ps%d" % bb], [K(X_)])
            yield
            bc_, bd = nb(), nb()
            for hh in range(4):
                cs = slice(hh * 128, (hh + 1) * 128)
                self.mm(self.ps[bc_][:, cs], G[T][:, hh, :], G[Y_][:, hh, :], [K(T), K(Y_)], ["ps%d" % bc_])
                if not lastlvl:
                    self.mm(self.ps[bd][:, cs], G[U][:, hh, :], G[X_][:, hh, :], [K(U), K(X_)], ["ps%d" % bd])
            yield
            self.tt("dve", f4(G[U2]), f4(G[U]), self.ps[bc_][:, :], ALU.subtract, [K(U), "ps%d" % bc_], [K(U2)])
            if not lastlvl:
                self.tt("dve", f4(G[T2]), f4(G[T]), self.ps[bd][:, :], ALU.subtract, [K(T), "ps%d" % bd], [K(T2)])
            yield
            T, U, T2, U2 = T2, U2, T, U
            b *= 2
        self.cp("dve", self.Uk[hg][:], G[U], [K(U)], ["Uk%d" % hg])
        self.cp("dve", self.Qk[hg][:], G["qkTm"], [K("qkTm")], ["Qk%d" % hg])

    def scan_chain(self, tb, hg, bx, by):
        A1 = self.A1
        blk = slice(tb * 128, (tb + 1) * 128)
        pb_ = tb % 2
        sm, smk = self.gsm2[pb_], "gsm%d" % pb_
        vtok, vtk = (self.vtok, self.vtok2)[pb_], "vtok%d" % pb_
        kdec, kdk = self.kdec2[pb_], "kdec%d" % pb_
        hs = [hg * 4 + hh for hh in range(4)]
        hsl = slice(hg * 4, hg * 4 + 4)
        X, Y = self.ps[bx], self.ps[by]
        xk, yk = "ps%d" % bx, "ps%d" % by
        X3 = X[:, :].rearrange("p (h d) -> p h d", h=4)
        Y3 = Y[:, :].rearrange("p (h d) -> p h d", h=4)
        bc = lambda ap: ap.unsqueeze(2).to_broadcast([128, 4, 128])
        Sk, Sbk = "S%d" % hg, "Sbf%d" % hg
        for hh, h in enumerate(hs):
            cs = slice(hh * 128, (hh + 1) * 128)
            self.mm(X[:, cs], A1[:, 8 + h, blk], self.Sbf[:, h, :], ["A1.%d" % (8 + h), Sbk], [xk])
            self.mm(Y[:, cs], A1[:, h, blk], self.Sbf[:, h, :], ["A1.%d" % h, Sbk], [yk])
        yield
        tS, tSk = self.tmpf()
        tS3 = tS[:, 0:512].rearrange("p (h d) -> p h d", h=4)
        self.tt("dve", tS3, X3, bc(sm[:, 24 + hg * 4:28 + hg * 4]), ALU.mult, [xk, smk], [tSk])
        o1, o1k = self.tmpf()
        o13 = o1[:, 0:512].rearrange("p (h d) -> p h d", h=4)
        self.tt("dve", o13, Y3, bc(sm[:, 16 + hg * 4:20 + hg * 4]), ALU.mult, [yk, smk], [o1k])
        yield
        r, rk = self.tmpb()
        r3 = r[:, :].rearrange("p (h d) -> p h d", h=4)
        self.tt("dve", r3, tS3, vtok[:, hsl, :], ALU.add, [tSk, vtk], [rk])
        yield
        for hh in range(4):
            cs = slice(hh * 128, (hh + 1) * 128)
            self.mm(X[:, cs], self.Uk[hg][:, hh, :], r3[:, hh, :], ["Uk%d" % hg, rk], [xk])
        yield
        vn, vk = self.tmpb()
        vn3 = vn[:, :].rearrange("p (h d) -> p h d", h=4)
        self.tt("dve", vn3, X3, bc(self.beta[:, tb, hsl]), ALU.mult, [xk, "beta"], [vk])
        yield
        for hh, h in enumerate(hs):
            cs = slice(hh * 128, (hh + 1) * 128)
            self.mm(Y[:, cs], self.Qk[hg][:, hh, :], vn3[:, hh, :], ["Qk%d" % hg, vk], [yk])
            self.mm(X[:, cs], kdec[:, h, :], vn3[:, hh, :], [kdk, vk], [xk])
        yield
        self.tt("dve", self.otok[:, hg * 512:(hg + 1) * 512], o1[:, 0:512], Y[:, :], ALU.add, [o1k, yk], ["otok"])
        self.tt("dve", self.S[:, hsl, :], self.S[:, hsl, :], bc(sm[:, 40 + hg * 4:44 + hg * 4]), ALU.mult, [Sk, smk], [Sk])
        yield
        self.tt("dve", self.S[:, hsl, :], self.S[:, hsl, :], X3, ALU.add, [Sk, xk], [Sk])
        yield
        self.cp("act", self.Sbf[:, hsl, :], self.S[:, hsl, :], [Sk], [Sbk])

    def lockstep(self, gens):
        gens = list(gens)
        while gens:
            nxt = []
            for g in gens:
                try:
                    next(g)
                    nxt.append(g)
                except StopIteration:
                    pass
            gens = nxt

    def gdn_prep(self, tb):
        A1 = self.A1
        blk = slice(tb * 128, (tb + 1) * 128)
        pb_ = tb % 2
        sm, smk = self.gsm2[pb_], "gsm%d" % pb_
        vtok, vtk = (self.vtok, self.vtok2)[pb_], "vtok%d" % pb_
        kdec, kdk = self.kdec2[pb_], "kdec%d" % pb_
        g8 = self.gtok[:, tb, :]
        self.mm(self.ps[7][:, 0:8], self.cf("ltri"), g8, ["cstf", "gtok"], ["ps7"])
        self.mm(self.ps[7][:, 8:16], self.cf("ones"), g8, ["cstf", "gtok"], ["ps7"])
        self.cp("dve", sm[:, 0:16], self.ps[7][:, 0:16], ["ps7"], [smk])
        yield
        self.act(sm[:, 16:24], sm[:, 0:8], AF.Exp, [smk], [smk])
        self.tt("dve", sm[:, 32:40], sm[:, 8:16], sm[:, 0:8], ALU.subtract, [smk], [smk])
        yield
        self.ts("dve", sm[:, 24:32], sm[:, 16:24], -1.0, ALU.mult, [smk], [smk])
        self.act(sm[:, 32:40], sm[:, 32:40], AF.Exp, [smk], [smk])
        self.act(sm[:, 40:48], sm[:, 8:16], AF.Exp, [smk], [smk])
        for (dst, dk, u0, bank) in ((kdec, kdk, 8, 6), (vtok, vtk, 16, 7)):
            for h in range(H):
                self.tr(self.psb[bank][:, h * 128:(h + 1) * 128], A1[:, u0 + h, blk], self.cb("ident"),
                        ["A1.%d" % (u0 + h), "cstb"], ["ps%d" % bank])
        yield
        self.cp("act", vtok[:].rearrange("p h d -> p (h d)"), self.psb[7][:, 0:1024], ["ps7"], [vtk])
        self.tt("dve", kdec[:], self.psb[6][:, 0:1024].rearrange("p (h d) -> p h d", h=H),
                sm[:, 32:40].unsqueeze(2).to_broadcast([128, H, 128]), ALU.mult, ["ps6", smk], [kdk])

    def gdn_prep_old(self, tb):
        A1 = self.A1
        blk = slice(tb * 128, (tb + 1) * 128)
        pb_ = tb % 2
        sm, smk = self.gsm2[pb_], "gsm%d" % pb_
        vtok, vtk = (self.vtok, self.vtok2)[pb_], "vtok%d" % pb_
        kdec, kdk = self.kdec2[pb_], "kdec%d" % pb_
        g8 = self.gtok[:, tb, :]
        for (dst, dk, u0, bank) in ((self.ktok, "ktok", 8, 5), (vtok, vtk, 16, 6)):
            for h in range(H):
                self.tr(self.psb[bank][:, h * 128:(h + 1) * 128], A1[:, u0 + h, blk], self.cb("ident"),
                        ["A1.%d" % (u0 + h), "cstb"], ["ps%d" % bank])
            self.cp("act", dst[:].rearrange("p h d -> p (h d)"), self.psb[bank][:, 0:1024], ["ps%d" % bank], [dk])
        self.mm(self.ps[7][:, 0:8], self.cf("ltri"), g8, ["cstf", "gtok"], ["ps7"])
        self.mm(self.ps[7][:, 8:16], self.cf("ones"), g8, ["cstf", "gtok"], ["ps7"])
        self.cp("dve", sm[:, 0:16], self.ps[7][:, 0:16], ["ps7"], [smk])
        self.act(sm[:, 16:24], sm[:, 0:8], AF.Exp, [smk], [smk])
        self.ts("dve", sm[:, 24:32], sm[:, 16:24], -1.0, ALU.mult, [smk], [smk])
        self.tt("dve", sm[:, 32:40], sm[:, 8:16], sm[:, 0:8], ALU.subtract, [smk], [smk])
        self.act(sm[:, 32:40], sm[:, 32:40], AF.Exp, [smk], [smk])
        self.act(sm[:, 40:48], sm[:, 8:16], AF.Exp, [smk], [smk])
        self.tt("dve", kdec[:], self.ktok[:], sm[:, 32:40].unsqueeze(2).to_broadcast([128, H, 128]), ALU.mult,
                ["ktok", smk], [kdk])

    def seq_chain(self, tb, NB):
        for hg in range(2):
            for _ in self.scan_chain(tb, hg, 6, 7):
                yield
            yield
        for _ in self.onorm_gen(tb, 128, bank=6):
            yield
        if tb + 1 < NB:
            yield
            for _ in self.gdn_prep(tb + 1):
                yield

    def gdn_all(self, NB):
        if self.debug.get("mstop") == "Y":
            nbk_ = 4 if self.debug.get("gstop") == "b4" else 3
            inv = lambda tb: [self.inv_chain(tb, hg, self.gqs[hg][0], self.gqs[hg][1], [nbk_ * hg + i for i in range(nbk_)])
                              for hg in range(2)]
            for tb in range(NB):
                if self.debug.get("gstop") == "oldprep":
                    self.gdn_prep_old(tb)
                else:
                    self.lockstep([self.gdn_prep(tb)])
                self.lockstep(inv(tb))
                self.lockstep([self.scan_chain(tb, 0, 6, 7)])
                self.lockstep([self.scan_chain(tb, 1, 6, 7)])
                self.lockstep([self.onorm_gen(tb, 128, bank=6)])
            return
        self.lockstep([self.gdn_prep(0)])
        inv = lambda tb: [self.inv_chain(tb, hg, self.gqs[hg][0], self.gqs[hg][1], [3 * hg + i for i in range(3)])
                          for hg in range(2)]
        self.lockstep(inv(0))
        for tb in range(NB):
            gens = [self.seq_chain(tb, NB)]
            if tb + 1 < NB:
                gens = inv(tb + 1) + gens
            self.lockstep(gens)

    def stage(self, k):
        return [(self.t1, "t1"), (self.t2, "t2"), (self.otok, "otok")][k]

    def load_sample_hist(self):
        for (srcd, dst, nch, nj) in ((self.sq, self.hsq, 24, 3), (self.ssc, self.hss, 8, 2)):
            for j in range(nj):
                for k in range(nch // 8):
                    t, tk = self.stage(k)
                    self.dma("aux", t[:NSAMP, :], srcd[:, j, k * 1024:(k + 1) * 1024], "hl", [], [tk])
                    for cc in range(8):
                        self.tr(self.ps[6][:, cc * NSAMP:(cc + 1) * NSAMP], t[:NSAMP, cc * 128:(cc + 1) * 128],
                                self.cf("ident")[:NSAMP, :NSAMP], [tk, "cstf"], ["ps6"])
                    self.cp("dve", dst[:, k * 8:(k + 1) * 8, j, :],
                            self.ps[6][:, 0:8 * NSAMP].rearrange("p (c b) -> p c b", c=8), ["ps6"], ["hsamp"])

    def gdn_sample(self):
        A1 = self.A1
        NS = NSAMP
        sm = self.gsm
        st = self.stat
        for (dst, dk, u0, bank) in ((self.qtok[:NS, :], "qtok", 0, 4), (self.ktok[:NS].rearrange("p h d -> p (h d)"), "ktok", 8, 5),
                                    (self.vtok[:NS].rearrange("p h d -> p (h d)"), "vtok", 16, 6)):
            for h in range(H):
                self.tr(self.psb[bank][:NS, h * 128:(h + 1) * 128], A1[:, u0 + h, 0:NS], self.cb("ident"),
                        ["A1.%d" % (u0 + h), "cstb"], ["ps%d" % bank])
            self.cp("act", dst, self.psb[bank][:NS, 0:1024], ["ps%d" % bank], [dk])
        a = sm[:NS, 0:8]
        self.act(a, self.gtok[:NS, 0, :], AF.Exp, ["gtok"], ["gsm"])
        q3 = self.qtok[:NS, :].rearrange("p (h d) -> p h d", h=H)
        t13 = self.t1[:NS, :].rearrange("p (h d) -> p h d", h=H)
        t23 = self.t2[:NS, :].rearrange("p (h d) -> p h d", h=H)
        o3 = self.otok[:NS, :].rearrange("p (h d) -> p h d", h=H)
        self.tt("dve", t13, q3, self.ktok[:NS], ALU.mult, ["qtok", "ktok"], ["t1"])
        self.P.add("dve", lambda e: e.tensor_reduce(sm[:NS, 8:16], t13, AX.X, ALU.add), ["t1"], ["gsm"])
        i16 = self.i16b[:].rearrange("p (a b) -> p a b", a=NS)
        for h in range(H):
            self.tt("dve", self.kTm[:, h, :, :], A1[:, 8 + h:9 + h, 0:NS].to_broadcast([128, NS, NS]), i16, ALU.mult,
                    ["A1.%d" % (8 + h), "i16b"], ["kTm"])
            self.tt("dve", self.qTm[:, h, :, :], A1[:, h:h + 1, 0:NS].to_broadcast([128, NS, NS]), i16, ALU.mult,
                    ["A1.%d" % h, "i16b"], ["qTm"])
        for b in range(NS):
            i3, i2 = b % 3, b % 2
            self.dma("sp", self.Sin[i3], self.sg[b].rearrange("h k v -> k h v"), "sin%d" % i3, [], ["Sin%d" % i3])
            self.cp("act" if b % 2 == 0 else "dve", self.Sinb[i2], self.Sin[i3], ["Sin%d" % i3], ["Sinb%d" % i2])
            for h in range(H):
                bk, bq = h // 4, 2 + h // 4
                cs = slice((h % 4) * 128, (h % 4 + 1) * 128)
                first = (b == 0 and h % 4 == 0)
                self.mm(self.ps[bk][:NS, cs], self.kTm[:, h, b, :], self.Sinb[i2][:, h, :], ["kTm", "Sinb%d" % i2],
                        ["ps%d" % bk], start=first, stop=(b == NS - 1))
                self.mm(self.ps[bq][:NS, cs], self.qTm[:, h, b, :], self.Sinb[i2][:, h, :], ["qTm", "Sinb%d" % i2],
                        ["ps%d" % bq], start=first, stop=(b == NS - 1))
        a_b = a.unsqueeze(2).to_broadcast([NS, H, 128])
        for half in range(2):
            hsl = slice(half * 4, half * 4 + 4)
            k3 = self.ps[half][:NS, :].rearrange("p (h d) -> p h d", h=4)
            qs3 = self.ps[2 + half][:NS, :].rearrange("p (h d) -> p h d", h=4)
            ab = a[:, hsl].unsqueeze(2).to_broadcast([NS, 4, 128])
            self.tt("dve", t13[:, hsl, :], k3, ab, ALU.mult, ["ps%d" % half, "gsm"], ["t1"])
            self.tt("dve", t13[:, hsl, :], self.vtok[:NS, hsl, :], t13[:, hsl, :], ALU.subtract, ["vtok", "t1"], ["t1"])
            self.tt("dve", t13[:, hsl, :], t13[:, hsl, :],
                    self.beta[:NS, 0, hsl].unsqueeze(2).to_broadcast([NS, 4, 128]), ALU.mult, ["t1", "beta"], ["t1"])
            self.tt("dve", t23[:, hsl, :], qs3, ab, ALU.mult, ["ps%d" % (2 + half), "gsm"], ["t2"])
            self.tt("dve", o3[:, hsl, :], t13[:, hsl, :], sm[:NS, 8 + half * 4:12 + half * 4].unsqueeze(2).to_broadcast([NS, 4, 128]),
                    ALU.mult, ["t1", "gsm"], ["otok"])
            self.tt("dve", o3[:, hsl, :], o3[:, hsl, :], t23[:, hsl, :], ALU.add, ["otok", "t2"], ["otok"])
        dbf = self.xb16
        self.cp("act", dbf[:NS, :], self.t1[:NS, :], ["t1"], ["xb16"])
        ad = self.t2[:NS, 0:128].rearrange("p (b h) -> p b h", b=NS)
        idr = self.cf("ident")[:NS, 0:NS].unsqueeze(2).to_broadcast([NS, NS, H])
        self.tt("dve", ad, a.unsqueeze(1).to_broadcast([NS, NS, H]), idr, ALU.mult, ["gsm", "cstf"], ["t2"])
        self.mm(self.ps[4][:, 0:128], self.cf("ones")[:NS, :], self.t2[:NS, 0:128], ["cstf", "t2"], ["ps4"])
        self.cp("dve", self.abc[:], self.ps[4][:, 0:128], ["ps4"], ["abc"])
        kflat = self.ktok[:NS].rearrange("p h d -> p (h d)")

        def load2(b):
            i3 = (b + 1) % 3
            self.dma("sp", self.Sin[i3], self.sg[b].rearrange("h k v -> k h v"), "sin%d" % i3, [], ["Sin%d" % i3])
        load2(0)
        load2(1)
        for b in range(NS):
            i2 = b % 2
            i3 = (b + 1) % 3
            if b + 2 < NS:
                load2(b + 2)
            self.ts("dve", self.kmask[i2][:NS, :], kflat, self.cf("ident")[:NS, b:b + 1], ALU.mult,
                    ["ktok", "cstf"], ["kmask%d" % i2])
            for h in range(H):
                self.mm(self.ps[h][:, 0:128], self.kmask[i2][:NS, h * 128:(h + 1) * 128], dbf[:NS, h * 128:(h + 1) * 128],
                        ["kmask%d" % i2, "xb16"], ["ps%d" % h])
                self.stt(self.Sin[i3][:, h, :], self.Sin[i3][:, h, :], self.abc[:, b * 8 + h:b * 8 + h + 1],
                         self.ps[h][:, 0:128], ALU.mult, ALU.add, ["Sin%d" % i3, "abc", "ps%d" % h], ["Sin%d" % i3])
            self.dma("pool", self.sgs[b].rearrange("h k v -> k h v"), self.Sin[i3], "sout%d" % i3, ["Sin%d" % i3], ["sgs%d" % b])
            self.outkeys.append("sgs%d" % b)
        self.onorm_and_T(0, NS)


_CACHE = {}


WBIG_LEN = 2 * (3 * D * HID) + D * IN_W + 4 * D * D + 256 * D


def pack_wbig(weights, specs, offs, tot):
    out = np.empty((tot,), np.float32)
    for spec, (off, n) in zip(specs, offs):
        parts = []
        for (name, r0, r1, c0, c1) in spec:
            w = weights[name][r0:r1, c0:c1]
            kc = (r1 - r0) // 128
            parts.append(w.reshape(kc, 128, c1 - c0).transpose(1, 0, 2).reshape(128, kc * (c1 - c0)))
        out[off:off + 128 * n] = np.concatenate(parts, axis=1).reshape(-1)
    return out


def kernel(x_prompt, x_sample, p_prompt, p_sample, state_gdn, state_qkv_conv, state_sc_conv,
           ffn1_w_gate, ffn1_w_up, ffn1_w_down, ln1_g, ln1_b,
           w_in, w_conv_qkv, A_log, dt_bias, w_onorm, w_p_gdn, w_conv_sc, w_p_sc, w_o, ln2_g, ln2_b,
           ffn2_w_gate, ffn2_w_up, ffn2_w_down, ln3_g, ln3_b,
           w_ple_gate, w_ple_proj, ln4_g, ln4_b, _debug=None):
    f = lambda a: np.ascontiguousarray(np.asarray(a, dtype=np.float32))
    weights = {"ffn1_w_gate": f(ffn1_w_gate)[0], "ffn1_w_up": f(ffn1_w_up)[0], "ffn1_w_down": f(ffn1_w_down)[0],
               "w_in": f(w_in)[0], "w_p_gdn": f(w_p_gdn)[0], "w_p_sc": f(w_p_sc)[0], "w_o": f(w_o)[0],
               "ffn2_w_gate": f(ffn2_w_gate)[0], "ffn2_w_up": f(ffn2_w_up)[0], "ffn2_w_down": f(ffn2_w_down)[0],
               "w_ple_gate": f(w_ple_gate)[0], "w_ple_proj": f(w_ple_proj)[0]}
    bld = Builder(debug=_debug)
    bld.wbig_len = WBIG_LEN
    nc = bld.build()
    assert bld.slab_tot == WBIG_LEN or _debug, (bld.slab_tot, WBIG_LEN)
    assert bld.slab_tot <= WBIG_LEN
    wbig = np.zeros((WBIG_LEN,), np.float32)
    wbig[:bld.slab_tot] = pack_wbig(weights, bld.slab_specs, bld.slab_off, bld.slab_tot)
    lnp = np.stack([f(ln1_g)[0], f(ln1_b)[0], f(ln2_g)[0], f(ln2_b)[0], f(ln3_g)[0], f(ln3_b)[0], f(ln4_g)[0], f(ln4_b)[0]])
    wcq = np.ascontiguousarray(f(w_conv_qkv)[0].reshape(4, 24, 128).transpose(2, 1, 0).reshape(128, 96))
    wcs = np.ascontiguousarray(f(w_conv_sc)[0].reshape(3, 8, 128).transpose(2, 1, 0).reshape(128, 24))
    smallp = np.stack([f(A_log)[0], f(dt_bias)[0]])
    cst, cst2 = make_consts()
    cst2 = np.ascontiguousarray(cst2.reshape(128, -1))
    i16 = np.ascontiguousarray(np.broadcast_to(np.eye(16, dtype=np.float32).reshape(1, 256), (128, 256)))
    xp = f(x_prompt)
    xsm = f(x_sample)[:, 0, :]
    ppr = f(p_prompt)[0]
    psm = f(p_sample)[0, :, 0, :]
    sg = f(state_gdn)[0]
    sq = f(state_qkv_conv)[0]
    ssc = f(state_sc_conv)[0]
    in_maps = []
    for c in range(8):
        sl = slice(c * NSAMP, (c + 1) * NSAMP)
        in_maps.append({"x": xp[c], "pp": ppr[c], "xs": xsm[sl], "psm": psm[sl], "sg": sg[sl], "sq": sq[sl], "ssc": ssc[sl],
                        "wbig": wbig, "lnp": lnp, "wcq": wcq, "wcs": wcs, "smallp": smallp, "won": f(w_onorm)[0],
                        "cst": cst, "cst2": cst2, "i16": i16})
    ncores = (_debug or {}).get("ncores", 8)
    res = run_bass_kernel_spmd(nc, in_maps[:ncores], core_ids=list(range(ncores)))
    R = list(res.results)
    while len(R) < 8:
        R.append({k: np.zeros_like(v) for k, v in R[0].items()})
    y = np.stack([R[c]["y"] for c in range(8)])
    ys = np.concatenate([R[c]["ys"] for c in range(8)])[:, None, :]
    sgp = np.stack([R[c]["sgp"] for c in range(8)])[None]
    sqp = np.stack([R[c]["sqp"] for c in range(8)])[None]
    ssp = np.stack([R[c]["ssp"] for c in range(8)])[None]
    sgs = np.concatenate([R[c]["sgs"] for c in range(8)])[None]
    sqs = np.concatenate([R[c]["sqs"] for c in range(8)])[None]
    sss = np.concatenate([R[c]["sss"] for c in range(8)])[None]
    return (y.astype(np.float32), ys.astype(np.float32), sgp.astype(np.float32), sqp.astype(np.float32),
            ssp.astype(np.float32), sgs.astype(np.float32), sqs.astype(np.float32), sss.astype(np.float32))
```

```python
import contextlib
import numpy as np
import concourse.bass as bass
import concourse.mybir as mybir
from concourse.bass_utils import run_bass_kernel_spmd

F32 = mybir.dt.float32
BF16 = mybir.dt.bfloat16
AF = mybir.ActivationFunctionType
ALU = mybir.AluOpType
AX = mybir.AxisListType

D = 1024
SEQ = 2048
NSAMP = 16
HID = 2816
NJ = HID // 128
H = 8
QKV_W = 3072
IN_W = 9232
Z0, BETA0, A0, B0, C0, H0, GG0, GS0 = 3072, 4096, 4104, 4112, 5136, 6160, 7184, 8208
ALPHA = 2.0 ** 0.25
LN_EPS = 1e-5
RMS_EPS = 1e-6
L2_EPS = 1e-6
NTP = 512
SLOT = 4096
NSLOT = 4
NBK = 16

COMPUTE = ("pe", "act", "dve", "pool")


class _Op:
    __slots__ = ("eng", "fn", "r", "w", "key", "eidx", "kn", "waits", "done", "inc")

    def __init__(self, eng, fn, r, w, key):
        self.eng = eng
        self.fn = fn
        self.r = r
        self.w = w
        self.key = key
        self.eidx = -1
        self.kn = 0
        self.waits = []
        self.done = None
        self.inc = False


class Prog:
    def __init__(self, nc):
        self.nc = nc
        self.ops = []

    def add(self, eng, fn, r=(), w=(), key=None):
        self.ops.append(_Op(eng, fn, tuple(r), tuple(w), key))

    def finalize(self):
        last_w = {}
        readers = {}
        issue = {e: {} for e in ("pe", "act", "dve", "pool", "sp")}
        ecount = {e: 0 for e in issue}
        kcount = {}
        kops = {}
        eops = {e: [] for e in issue}
        for op in self.ops:
            e = op.eng
            deps = set()
            for res in op.r:
                lw = last_w.get(res)
                if lw is not None:
                    deps.add(lw)
            for res in op.w:
                lw = last_w.get(res)
                if lw is not None:
                    deps.add(lw)
                for rd in readers.get(res, ()):
                    deps.add(rd)
            deps.discard(op)
            clock = issue[e]
            if op.key is None:
                op.eidx = ecount[e]
                ecount[e] += 1
                eops[e].append(op)
            else:
                n = kcount.get(op.key, 0) + 1
                kcount[op.key] = n
                op.kn = n
                kops.setdefault(op.key, []).append(op)
                if n > 1:
                    deps.add(kops[op.key][n - 2])
            best = {}
            dma_deps = []
            for d in deps:
                if d.key is None:
                    b = best.get(d.eng)
                    if b is None or d.eidx > b.eidx:
                        best[d.eng] = d
                else:
                    dma_deps.append(d)
            newclock = None
            for f, d in best.items():
                if clock.get(f, -1) >= d.eidx:
                    continue
                if f == e and op.key is None:
                    if e == "pe":
                        continue
                    if e != "pool" and (op.eidx - d.eidx) > 12:
                        continue
                op.waits.append(("c", f, d.eidx))
                d.inc = True
                if newclock is None:
                    newclock = dict(clock)
                for k, v in d.done.items():
                    if newclock.get(k, -1) < v:
                        newclock[k] = v
            for d in dma_deps:
                kk = ("dma", d.key)
                cur = clock if newclock is None else newclock
                if cur.get(kk, 0) >= d.kn:
                    continue
                op.waits.append(("d", d.key, d.kn))
                if newclock is None:
                    newclock = dict(clock)
                for k, v in d.done.items():
                    if newclock.get(k, -1) < v:
                        newclock[k] = v
            if newclock is not None:
                issue[e] = newclock
                clock = newclock
            done = dict(clock)
            if op.key is None:
                done[e] = op.eidx
            else:
                done[("dma", op.key)] = op.kn
            op.done = done
            for res in op.r:
                readers.setdefault(res, []).append(op)
            for res in op.w:
                last_w[res] = op
                readers[res] = []
        self.rank = {}
        for e, lst in eops.items():
            k = 0
            for op in lst:
                if op.inc:
                    k += 1
                    self.rank[(e, op.eidx)] = k
        self.keys = list(kcount.keys())
        for op in self.ops:
            op.done = None
            if len(op.waits) > 1:
                m = {}
                for t, a, b in op.waits:
                    if (t, a) not in m or m[(t, a)] < b:
                        m[(t, a)] = b
                op.waits = [(t, a, b) for (t, a), b in m.items()]

    def emit(self, es):
        nc = self.nc
        sems = {}
        for e in COMPUTE:
            sems[e] = es.enter_context(nc.semaphore("s_" + e))
        ksem = {}
        for k in self.keys:
            ksem[k] = es.enter_context(nc.semaphore("k_" + str(k)))
        block = es.enter_context(nc.Block())
        rank = self.rank

        def run(ename, eng):
            for op in self.ops:
                if op.eng != ename:
                    continue
                for t, a, b in op.waits:
                    if t == "c":
                        eng.wait_ge(sems[a], rank[(a, b)])
                    else:
                        eng.wait_ge(ksem[a], 16 * b)
                if op.fn is None:
                    continue
                ins = op.fn(eng)
                if op.key is not None:
                    ins.then_inc(ksem[op.key], 16)
                elif op.inc:
                    ins.then_inc(sems[ename], 1)

        @block.tensor
        def _(eng):
            run("pe", eng)

        @block.scalar
        def _(eng):
            run("act", eng)

        @block.vector
        def _(eng):
            run("dve", eng)

        @block.gpsimd
        def _(eng):
            run("pool", eng)

        @block.sync
        def _(eng):
            run("sp", eng)


CSTF_NAMES = ["ident", "ltri", "su", "muincl", "ones"]
CSTB_NAMES = ["ident", "ones", "mndn", "mndtn", "me16", "me16t", "me32", "me32t", "me64", "me64t"]


def make_consts():
    i = np.arange(128)[:, None]
    j = np.arange(128)[None, :]
    c = {}
    c["ident"] = (i == j)
    c["ltri"] = (i <= j)
    c["su"] = (i > j)
    c["muincl"] = (j >= i)
    c["ones"] = np.ones((128, 128), bool)
    nd = (i // NBK == j // NBK) & (i > j)
    c["mndn"] = -1.0 * nd
    c["mndtn"] = -1.0 * nd.T
    for b in (16, 32, 64):
        e = (i // (2 * b) == j // (2 * b)) & ((i % (2 * b)) >= b) & ((j % (2 * b)) < b)
        c["me%d" % b] = e
        c["me%dt" % b] = e.T
    arrf = np.stack([np.asarray(c[n], np.float32) for n in CSTF_NAMES], axis=1)
    arrb = np.stack([np.asarray(c[n], np.float32) for n in CSTB_NAMES], axis=1)
    return np.ascontiguousarray(arrf), np.ascontiguousarray(arrb)


class Builder:
    def __init__(self, debug=None):
        self.debug = debug or {}
        self.slab_specs = []
        self.slab_off = []
        self.slab_tot = 0
        self.nslab_pass = None

    def mm(self, out, lhsT, rhs, r, w, start=True, stop=True):
        self.P.add("pe", lambda e: e.matmul(out, lhsT, rhs, start=start, stop=stop), r, w)

    def tr(self, out, in_, ident, r, w):
        self.P.add("pe", lambda e: e.transpose(out, in_, ident), r, w)

    def act(self, out, in_, func, r, w, bias=None, scale=None):
        kw = {}
        if bias is not None:
            kw["bias"] = bias
        if scale is not None:
            kw["scale"] = scale
        self.P.add("act", lambda e: e.activation(out, in_, func, **kw), r, w)

    def tt(self, eng, out, in0, in1, op, r, w):
        self.P.add(eng, lambda e: e.tensor_tensor(out, in0, in1, op), r, w)

    def ts(self, eng, out, in0, s1, op0, r, w, s2=None, op1=None):
        if op1 is None:
            self.P.add(eng, lambda e: e.tensor_scalar(out, in0, s1, None, op0), r, w)
        else:
            self.P.add(eng, lambda e: e.tensor_scalar(out, in0, s1, s2, op0, op1), r, w)

    def stt(self, out, in0, scalar, in1, op0, op1, r, w):
        self.P.add("dve", lambda e: e.scalar_tensor_tensor(out, in0, scalar, in1, op0, op1), r, w)

    def cp(self, eng, out, in_, r, w):
        if eng == "act":
            self.P.add("act", lambda e: e.activation(out, in_, AF.Copy), r, w)
        else:
            self.P.add(eng, lambda e: e.tensor_copy(out, in_), r, w)

    def dq(self):
        return "sp" if self.recording else "pool"

    def dma(self, eng, out, in_, key, r, w, slow=False):
        if eng == "aux":
            eng = self.dq()
        if slow:
            self.P.add(eng, lambda e: e.dma_start(out=out, in_=in_, allow_slow_non_contiguous=True), r, w, key=key)
        else:
            self.P.add(eng, lambda e: e.dma_start(out=out, in_=in_), r, w, key=key)

    def slab(self, spec):
        if self.recording:
            self.slab_specs.append(spec)
            n = sum(((r1 - r0) // 128) * (c1 - c0) for (_, r0, r1, c0, c1) in spec)
            assert n <= SLOT, n
            self.slab_off.append((self.slab_tot, n))
            self.slab_tot += 128 * n
        si = self.slab_i % self.nslab_pass if self.nslab_pass else self.slab_i
        off, n = self.slab_off[si]
        slot = self.slab_i % NSLOT
        self.slab_i += 1
        t = self.wring[slot]
        key = "w%d" % slot
        scr = self.wscr[off:off + 128 * n].rearrange("(p n) -> p n", p=128)
        if self.recording:
            src = self.wbig[off:off + 128 * n].rearrange("(p n) -> p n", p=128)
            self.dma("pool", t[:, 0:n], src, key, r=[], w=[key])
            if self.debug.get("npass", 4) > 1 or not self.debug.get("nosample", False):
                self.dma("sp", scr, t[:, 0:n], "wb%d" % slot, r=[key], w=["wscr%d" % si])
        else:
            self.dma("sp", t[:, 0:n], scr, key, r=["wscr%d" % si], w=[key])
        views = []
        o = 0
        for (_, r0, r1, c0, c1) in spec:
            kc = (r1 - r0) // 128
            nc_ = c1 - c0
            views.append(t[:, o:o + kc * nc_].rearrange("p (k n) -> p k n", k=kc))
            o += kc * nc_
        return views, key

    def build(self):
        nc = bass.Bass("TRN2", target_bir_lowering=False)
        self.nc = nc
        self.es = contextlib.ExitStack()
        with self.es:
            self._build_inner()
        return nc

    def dram_in(self, name, shape, dt=F32):
        return self.nc.dram_tensor(name, list(shape), dt, kind="ExternalInput").ap()

    def dram_out(self, name, shape, dt=F32):
        return self.nc.dram_tensor(name, list(shape), dt, kind="ExternalOutput").ap()

    def sb(self, name, shape, dt):
        return self.es.enter_context(self.nc.sbuf_tensor(name, list(shape), dt))

    def _build_inner(self):
        nc = self.nc
        self.P = Prog(nc)
        P = self.P
        self.x = self.dram_in("x", [SEQ, D])
        self.pp = self.dram_in("pp", [SEQ, 256])
        self.xs = self.dram_in("xs", [NSAMP, D])
        self.psm = self.dram_in("psm", [NSAMP, 256])
        self.sg = self.dram_in("sg", [NSAMP, H, 128, 128])
        self.sq = self.dram_in("sq", [NSAMP, 3, QKV_W])
        self.ssc = self.dram_in("ssc", [NSAMP, 2, D])
        self.wbig = self.dram_in("wbig", [self.wbig_len])
        self.wscr = self.nc.dram_tensor("wscr", [self.wbig_len], BF16, kind="Internal").ap()
        self.lnp = self.dram_in("lnp", [8, D])
        self.wcq_d = self.dram_in("wcq", [128, 24 * 4])
        self.wcs_d = self.dram_in("wcs", [128, 8 * 3])
        self.smallp = self.dram_in("smallp", [2, 8])
        self.won_d = self.dram_in("won", [128])
        self.cst_d = self.dram_in("cst", [128, len(CSTF_NAMES), 128])
        self.cst2_d = self.dram_in("cst2", [128, len(CSTB_NAMES) * 128])
        self.i16_d = self.dram_in("i16", [128, 256])
        self.y = self.dram_out("y", [SEQ, D])
        self.ys = self.dram_out("ys", [NSAMP, D])
        self.sgp = self.dram_out("sgp", [H, 128, 128])
        self.sqp = self.dram_out("sqp", [3, QKV_W])
        self.ssp = self.dram_out("ssp", [2, D])
        self.sgs = self.dram_out("sgs", [NSAMP, H, 128, 128])
        self.sqs = self.dram_out("sqs", [NSAMP, 3, QKV_W])
        self.sss = self.dram_out("sss", [NSAMP, 2, D])
        self.outkeys = []

        sb = self.sb
        self.wring = [sb("wr%d" % i, [128, SLOT], BF16) for i in range(NSLOT)]
        self.xres = sb("xres", [128, 4, D], F32)
        self.xT = sb("xT", [128, 8, NTP], BF16)
        self.gbt = sb("gbt", [128, 2, D], F32)
        self.cstf = sb("cstf", [128, len(CSTF_NAMES), 128], F32)
        self.cstb = sb("cstb", [128, len(CSTB_NAMES), 128], BF16)
        self.i16b = sb("i16b", [128, 256], BF16)
        self.wcq = sb("wcq_s", [128, 24, 4], F32)
        self.wcs = sb("wcs_s", [128, 8, 3], F32)
        self.wonb = sb("wonb", [128, 128], F32)
        self.smallb = sb("smallb", [128, 16], F32)
        self.negA = sb("negA", [128, 8], F32)
        self.histq = sb("histq", [128, 24, 3], F32)
        self.hists = sb("hists", [128, 8, 2], F32)
        self.S = sb("S", [128, H, 128], F32)
        self.Sbf = sb("Sbf", [128, H, 128], BF16)
        self.A1 = sb("A1", [128, 24, NTP], BF16)
        self.ztok = sb("ztok", [128, 4, D], BF16)
        self.ktok = sb("ktok", [128, H, 128], BF16)
        self.vtok = sb("vtok", [128, H, 128], BF16)
        self.batok = sb("batok", [128, 4, 16], F32)
        self.beta = sb("beta", [128, 4, 8], F32)
        self.gtok = sb("gtok", [128, 4, 8], F32)
        self.tf = [sb("tf%d" % i, [128, NTP + 4], F32) for i in range(11)]
        self.tfi = 0
        self.tb16 = [sb("tb%d" % i, [128, NTP], BF16) for i in range(4)]
        self.tbi = 0
        self.t1 = sb("t1", [128, D], F32)
        self.t2 = sb("t2", [128, D], F32)
        self.otok = sb("otok", [128, D], F32)
        self.xb16 = sb("xb16", [128, D], BF16)
        self.stat = sb("stat", [128, 4, 16], F32)
        self.stat3 = sb("stat3", [128, 16], F32)
        self.xb16b = sb("xb16b", [128, D], BF16)
        self.pT = sb("pT", [128, 2, NTP], BF16)
        self.pb = sb("pb", [128, 256], BF16)
        GQ = [("decTm", F32), ("Lg", F32), ("qkTm", BF16), ("MT", BF16), ("M", BF16),
              ("Na", BF16), ("Nb", BF16), ("Nc", BF16), ("Nd", BF16), ("Pa", BF16), ("Pb", BF16),
              ("Pc", BF16), ("Pd", BF16)]
        self.gqs = []
        self.arenaA = sb("arenaA", [128, 15 * 512], BF16)
        self.arenaB = sb("arenaB", [128, 15 * 512], BF16)
        self.gq_names = [nm for nm, _ in GQ]
        for ar, pfx in ((self.arenaA, "gA_"), (self.arenaB, "gB_")):
            gX = {}
            o = 0
            for nm, dt in GQ:
                n = 1024 if dt == F32 else 512
                v = ar[:, o:o + n]
                if dt == F32:
                    v = v.bitcast(F32)
                gX[nm] = v.rearrange("p (h d) -> p h d", h=4)
                o += n
            self.gqs.append((gX, pfx))
        self.kdec2 = [sb("kdec%d" % i, [128, H, 128], BF16) for i in range(2)]
        self.gsm2 = [sb("gsm%d" % i, [128, 64], F32) for i in range(2)]
        self.vtok2 = sb("vtok2", [128, H, 128], BF16)
        self.Uk = [sb("Uk%d" % i, [128, 4, 128], BF16) for i in range(2)]
        self.Qk = [sb("Qk%d" % i, [128, 4, 128], BF16) for i in range(2)]
        self.gsm = self.gsm2[0]
        self.kdec = self.kdec2[0]
        fA = lambda o, n: self.arenaA[:, o:o + n]
        fB = lambda o, n: self.arenaB[:, o:o + n]
        self.Sin = [fA(k * 2048, 2048).bitcast(F32).rearrange("p (h d) -> p h d", h=H) for k in range(3)]
        self.hss = fA(6144, 512).bitcast(F32).rearrange("p (c j b) -> p c j b", c=8, j=2)
        self.Sinb = [fA(6656, 1024).rearrange("p (h d) -> p h d", h=H), fB(6400, 1024).rearrange("p (h d) -> p h d", h=H)]
        self.kTm = fB(0, 2048).rearrange("p (h a b) -> p h a b", h=H, a=NSAMP)
        self.qTm = fB(2048, 2048).rearrange("p (h a b) -> p h a b", h=H, a=NSAMP)
        self.hsq = fB(4096, 2304).bitcast(F32).rearrange("p (c j b) -> p c j b", c=24, j=3)
        self.kmask = [sb("kmask%d" % i, [NSAMP, D], BF16) for i in range(2)]
        self.qtok = sb("qtok", [NSAMP, D], BF16)
        self.abc = sb("abc", [128, 128], F32)
        self.ps = [self.es.enter_context(nc.psum_tensor("ps%d" % i, [128, 512], F32)) for i in range(8)]
        self.psb = [p.bitcast(BF16) for p in self.ps]

        self.recording = True
        self.slab_i = 0
        self.setup()
        npass = self.debug.get("npass", 4)
        self.prologue_done = False
        do_sample = not self.debug.get("nosample", False)
        for pi in range(npass):
            if pi + 1 < npass:
                nsrc = (self.x[(pi + 1) * NTP:(pi + 2) * NTP, :], 4, 128)
            elif do_sample:
                nsrc = (self.xs, 1, NSAMP)
            else:
                nsrc = None
            if self.debug.get("stop"):
                nsrc = None
            self.layer_pass(pi, NTP, sample=False, last=(pi == npass - 1), next_src=nsrc)
            if pi == 0:
                self.recording = False
                self.nslab_pass = len(self.slab_off)
        if not self.debug.get("nosample", False):
            gbk = [p + nm for p in ("gA_", "gB_") for nm in self.gq_names]
            gbk += ["gsm0", "gsm1", "vtok0", "vtok1", "kdec0", "kdec1"]
            P.add("dve", lambda e: e.memset(self.gsm[:, 60:64], 0.0), gbk,
                  gbk + ["kTm", "qTm", "hsamp", "Sin0", "Sin1", "Sin2", "Sinb0", "Sinb1", "gsm", "vtok"])
            self.layer_pass(0, NSAMP, sample=True, last=True)
        P.add("sp", None, r=self.outkeys)
        P.finalize()
        P.emit(self.es)

    def cf(self, name):
        return self.cstf[:, CSTF_NAMES.index(name), :]

    def cb(self, name):
        return self.cstb[:, CSTB_NAMES.index(name), :]

    def cb4(self, name):
        i = CSTB_NAMES.index(name)
        return self.cstb[:, i:i + 1, :].to_broadcast([128, 4, 128])

    def cf4(self, name):
        i = CSTF_NAMES.index(name)
        return self.cstf[:, i:i + 1, :].to_broadcast([128, 4, 128])

    def tmpf(self):
        i = self.tfi % len(self.tf)
        self.tfi += 1
        return self.tf[i], "tf%d" % i

    def tmpb(self):
        i = self.tbi % len(self.tb16)
        self.tbi += 1
        return self.tb16[i], "tb%d" % i

    def setup(self):
        d = self.dma
        d("sp", self.cstf[:], self.cst_d, "c0", [], ["cstf"])
        nb = len(CSTB_NAMES) * 128
        for k in range(0, nb, 1024):
            n = min(1024, nb - k)
            d("sp", self.t1[:, 0:n], self.cst2_d[:, k:k + n], "c1", [], ["t1"])
            self.cp("dve", self.cstb[:].rearrange("p c d -> p (c d)")[:, k:k + n], self.t1[:, 0:n], ["t1"], ["cstb"])
        d("sp", self.t2[:, 0:256], self.i16_d, "c1", [], ["t2"])
        self.cp("dve", self.i16b[:], self.t2[:, 0:256], ["t2"], ["i16b"])
        d("sp", self.wcq[:].rearrange("p c j -> p (c j)"), self.wcq_d, "c2", [], ["wcq"])
        d("sp", self.wcs[:].rearrange("p c j -> p (c j)"), self.wcs_d, "c3", [], ["wcs"])
        d("sp", self.wonb[:], self.won_d.partition_broadcast(128), "c4", [], ["wonb"])
        d("sp", self.smallb[:], self.smallp.rearrange("a b -> (a b)").partition_broadcast(128), "c5", [], ["smallb"])
        self.act(self.negA[:], self.smallb[:, 0:8], AF.Exp, ["smallb"], ["negA"])
        self.ts("dve", self.negA[:], self.negA[:], -1.0, ALU.mult, ["negA"], ["negA"])
        self.P.add("dve", lambda e: e.memset(self.S[:], 0.0), [], ["S0", "S1"])
        self.P.add("dve", lambda e: e.memset(self.Sbf[:], 0.0), [], ["Sbf0", "Sbf1"])
        self.P.add("dve", lambda e: e.memset(self.histq[:], 0.0), [], ["histq"])
        self.P.add("dve", lambda e: e.memset(self.hists[:], 0.0), [], ["hists"])

    def make_xT(self, tb, TB, bank, staged=False):
        ps, psk = self.psb[bank], "ps%d" % bank
        xb, xbk = ((self.xb16, "xb16"), (self.xb16b, "xb16b"))[tb % 2]
        if not staged:
            self.cp("act", xb[:TB, :], self.xres[:TB, tb, :], ["xres%d" % tb], [xbk])
        for c in range(8):
            self.tr(ps[:, c * TB:(c + 1) * TB], xb[:TB, c * 128:(c + 1) * 128], self.cb("ident")[:TB, :TB],
                    [xbk, "cstb"], [psk])
        self.cp("dve", self.xT[:, :, tb * TB:(tb + 1) * TB],
                ps[:, 0:8 * TB].rearrange("p (c t) -> p c t", c=8), [psk], ["xT%d" % tb])

    def ln_load(self, idx):
        self.dma("aux", self.gbt[:, 0, :], self.lnp[2 * idx, :].partition_broadcast(128), "gb0", [], ["gbt0"])
        self.dma("aux", self.gbt[:, 1, :], self.lnp[2 * idx + 1, :].partition_broadcast(128), "gb1", [], ["gbt1"])

    def ln_norm(self, tbs, TB, stage=False):
        eps = LN_EPS / (ALPHA * ALPHA)
        st = self.stat
        t0, t1_ = tbs[0], tbs[-1] + 1
        for tb in tbs:
            xr = self.xres[:TB, tb, :]
            xk = "xres%d" % tb
            self.P.add("dve", lambda e, xr=xr, tb=tb: e.bn_stats(st[:TB, tb, 0:6], xr[:, 0:512]), [xk], ["stat"])
            self.P.add("dve", lambda e, xr=xr, tb=tb: e.bn_stats(st[:TB, tb, 6:12], xr[:, 512:1024]), [xk], ["stat"])
            self.P.add("dve", lambda e, tb=tb: e.bn_aggr(st[:TB, tb, 12:14], st[:TB, tb, 0:12]), ["stat"], ["stat"])
        self.act(st[:TB, t0:t1_, 14], st[:TB, t0:t1_, 13], AF.Sqrt, ["stat"], ["stat2"], bias=eps)
        self.P.add("dve", lambda e: e.reciprocal(st[:TB, t0:t1_, 15], st[:TB, t0:t1_, 14]), ["stat2"], ["stat2"])
        for tb in tbs:
            xr = self.xres[:TB, tb, :]
            xk = "xres%d" % tb
            self.stt(self.t1[:TB, :], xr, st[:TB, tb, 12:13], self.gbt[:TB, 0, :], ALU.subtract, ALU.mult,
                     [xk, "stat", "gbt0"], ["t1"])
            self.stt(xr, self.t1[:TB, :], st[:TB, tb, 15:16], self.gbt[:TB, 1, :], ALU.mult, ALU.add,
                     ["t1", "stat2", "gbt1"], [xk])
            if stage:
                xb, xbk = ((self.xb16, "xb16"), (self.xb16b, "xb16b"))[tb % 2]
                self.cp("act", xb[:TB, :], self.xres[:TB, tb, :], [xk], [xbk])

    def layer_norm(self, idx, NB, TB, final_out=None):
        self.ln_load(idx)
        self.ln_norm(list(range(NB)), TB)
        for tb in range(NB):
            xr = self.xres[:TB, tb, :]
            xk = "xres%d" % tb
            if final_out is not None:
                ok = final_out[1] + str(tb)
                self.dma("aux", final_out[0][tb * TB:(tb + 1) * TB, :], xr, "yo%d" % tb, [xk], [ok])
                self.outkeys.append(ok)
            else:
                self.make_xT(tb, TB, (2 * tb) % 8)

    def ffn(self, pfx, NB, TB, ln_idx):
        NT = NB * TB
        for j0 in range(0, NJ, 2):
            (wg, wu), wk = self.slab([(pfx + "_w_gate", 0, D, j0 * 128, j0 * 128 + 256),
                                      (pfx + "_w_up", 0, D, j0 * 128, j0 * 128 + 256)])
            for jj in range(2):
                j = j0 + jj
                bg, bu = 2 * (j % 2), 2 * (j % 2) + 1
                for kc in range(8):
                    self.mm(self.ps[bg][:, :NT], wg[:, kc, jj * 128:(jj + 1) * 128], self.xT[:, kc, :NT],
                            [wk, "xT0", "xT1", "xT2", "xT3"], ["ps%d" % bg], start=(kc == 0), stop=(kc == 7))
                for kc in range(8):
                    self.mm(self.ps[bu][:, :NT], wu[:, kc, jj * 128:(jj + 1) * 128], self.xT[:, kc, :NT],
                            [wk, "xT0", "xT1", "xT2", "xT3"], ["ps%d" % bu], start=(kc == 0), stop=(kc == 7))
                t, tk = self.tmpf()
                self.act(t[:, :NT], self.ps[bg][:, :NT], AF.Silu, ["ps%d" % bg], [tk])
                self.tt("dve", self.A1[:, j, :NT], t[:, :NT], self.ps[bu][:, :NT], ALU.mult,
                        [tk, "ps%d" % bu], ["A1.%d" % j])
        c = 0.5 / ALPHA
        self.ln_load(ln_idx)
        halves = [[0, 1], [2, 3]] if NB == 4 else [list(range(NB))]
        for hi, tbs in enumerate(halves):
            for j0 in range(0, NJ, 4):
                j1 = min(NJ, j0 + 4)
                (wd,), wk = self.slab([(pfx + "_w_down", j0 * 128, j1 * 128, 0, D)])
                for jj in range(j1 - j0):
                    j = j0 + jj
                    for tb in tbs:
                        for nh in range(2):
                            b = tb * 2 + nh
                            self.mm(self.ps[b][:TB, :], self.A1[:, j, tb * TB:(tb + 1) * TB], wd[:, jj, nh * 512:(nh + 1) * 512],
                                    [wk, "A1.%d" % j], ["ps%d" % b], start=(j == 0), stop=(j == NJ - 1))
            for tb in tbs:
                for nh in range(2):
                    b = tb * 2 + nh
                    xr = self.xres[:TB, tb, nh * 512:(nh + 1) * 512]
                    self.stt(xr, self.ps[b][:TB, :], c, xr, ALU.mult, ALU.add, ["ps%d" % b, "xres%d" % tb], ["xres%d" % tb])
            if hi == 0 and len(halves) == 2:
                self.ln_norm(tbs, TB, stage=True)
        if len(halves) == 1:
            self.slab_i += (NJ + 3) // 4
            self.ln_norm(halves[0], TB)
            for tb in halves[0]:
                self.make_xT(tb, TB, (2 * tb) % 8)
        else:
            for tb in halves[0]:
                self.make_xT(tb, TB, (2 * tb) % 8, staged=True)
            self.ln_norm(halves[1], TB)
            for tb in halves[1]:
                self.make_xT(tb, TB, (2 * tb) % 8)

    def prologue(self, src, NB, TB):
        for tb in range(NB):
            xb, xbk = ((self.xb16, "xb16"), (self.xb16b, "xb16b"))[tb % 2]
            self.dma("pool", xb[:TB, :], src[tb * TB:(tb + 1) * TB, :], "xc%d" % (tb % 2), [], [xbk])
            self.make_xT(tb, TB, tb % 8, staged=True)

    def layer_pass(self, pi, NT, sample, last, next_src=None):
        TB = min(128, NT)
        NB = NT // TB
        self.slab_i = 0 if self.recording else self.slab_i
        if not sample:
            src, psrc = self.x[pi * NT:(pi + 1) * NT, :], self.pp[pi * NT:(pi + 1) * NT, :]
            yout = (self.y[pi * NT:(pi + 1) * NT, :], "y%d_" % pi)
        else:
            src, psrc = self.xs, self.psm
            yout = (self.ys, "ys_")
        if not self.prologue_done:
            self.prologue(src, NB, TB)
        self.prologue_done = False
        for tb in range(NB):
            self.dma("aux", self.xres[:TB, tb, :], src[tb * TB:(tb + 1) * TB, :], "x%d" % tb, [], ["xres%d" % tb])
        stop = self.debug.get("stop")
        self.ffn("ffn1", NB, TB, 0)
        if stop == "ln1":
            return self.dump(yout, NB, TB)
        self.mixers(pi, NB, TB, sample, last)
        if stop == "mix":
            return self.dump(yout, NB, TB)
        self.layer_norm(1, NB, TB)
        self.ffn("ffn2", NB, TB, 2)
        if stop == "ln3":
            return self.dump(yout, NB, TB)
        self.ple(psrc, NB, TB)
        if next_src is not None:
            self.prologue(*next_src)
            self.prologue_done = True
        self.layer_norm(3, NB, TB, final_out=yout)

    def dump(self, yout, NB, TB):
        for tb in range(NB):
            ok = yout[1] + str(tb)
            self.dma("aux", yout[0][tb * TB:(tb + 1) * TB, :], self.xres[:TB, tb, :], "yo%d" % tb, ["xres%d" % tb], [ok])
            self.outkeys.append(ok)

    def ple(self, psrc, NB, TB):
        NT = NB * TB
        for tb in range(NB):
            pf, pfk = self.tmpf()
            self.dma("aux", pf[:TB, 0:256], psrc[tb * TB:(tb + 1) * TB, :], "pf", [], [pfk])
            self.cp("act", self.pb[:TB, :], pf[:TB, 0:256], [pfk], ["pb"])
            for c in range(2):
                self.tr(self.psb[7][:, c * TB:(c + 1) * TB], self.pb[:TB, c * 128:(c + 1) * 128],
                        self.cb("ident")[:TB, :TB], ["pb", "cstb"], ["ps7"])
            self.cp("dve", self.pT[:, :, tb * TB:(tb + 1) * TB],
                    self.psb[7][:, 0:2 * TB].rearrange("p (c t) -> p c t", c=2), ["ps7"], ["pT"])
        for nh in range(2):
            (wg,), wgk = self.slab([("w_ple_gate", 0, D, nh * 512, (nh + 1) * 512)])
            (wp,), wpk = self.slab([("w_ple_proj", 0, 256, nh * 512, (nh + 1) * 512)])
            for tb in range(NB):
                bg, bp = 2 * (tb % 2), 2 * (tb % 2) + 1
                for kc in range(8):
                    self.mm(self.ps[bg][:TB, :], self.xT[:, kc, tb * TB:(tb + 1) * TB], wg[:, kc, :],
                            [wgk, "xT0", "xT1", "xT2", "xT3"], ["ps%d" % bg], start=(kc == 0), stop=(kc == 7))
                for kc in range(2):
                    self.mm(self.ps[bp][:TB, :], self.pT[:, kc, tb * TB:(tb + 1) * TB], wp[:, kc, :],
                            [wpk, "pT"], ["ps%d" % bp], start=(kc == 0), stop=(kc == 1))
                t, tk = self.tmpf()
                self.act(t[:TB, :512], self.ps[bg][:TB, :], AF.Sigmoid, ["ps%d" % bg], [tk])
                self.tt("dve", t[:TB, :512], t[:TB, :512], self.ps[bp][:TB, :], ALU.mult, [tk, "ps%d" % bp], [tk])
                xr = self.xres[:TB, tb, nh * 512:(nh + 1) * 512]
                self.stt(xr, t[:TB, :512], 1.0 / ALPHA, xr, ALU.mult, ALU.add, [tk, "xres%d" % tb], ["xres%d" % tb])

    def conv_chunk(self, psbank, NT, taps_hist, wts, ntap, hist_tile, hist_key, sample, src_is_psum=True, src=None):
        H_ = ntap - 1
        cbt, cbk = self.tmpf()
        if src_is_psum:
            self.cp("act", cbt[:, H_:H_ + NT], self.ps[psbank][:, :NT], ["ps%d" % psbank], [cbk])
        else:
            src(cbt[:, H_:H_ + NT], cbk)
        if not sample:
            self.cp("dve", cbt[:, 0:H_], hist_tile, [hist_key], [cbk])
            self.cp("dve", hist_tile, cbt[:, NT:NT + H_], [cbk], [hist_key])
            taps = [cbt[:, j:j + NT] for j in range(ntap)]
            tr_ = [cbk]
        else:
            taps = [taps_hist[j] for j in range(H_)] + [cbt[:, H_:H_ + NT]]
            tr_ = [cbk, "hsamp"]
        acc, ak = self.tmpf()
        self.ts("dve", acc[:, :NT], taps[0], wts[0], ALU.mult, tr_ + ["wc"], [ak])
        for j in range(1, ntap):
            self.stt(acc[:, :NT], taps[j], wts[j], acc[:, :NT], ALU.mult, ALU.add, tr_ + ["wc", ak], [ak])
        return acc, ak, cbt, cbk

    def mixers(self, pi, NB, TB, sample, last):
        NT = NB * TB
        A1 = self.A1
        if sample:
            self.load_sample_hist()
        def finish(grp):
            for (c, so, sk, sq, sqk, cbt, cbk) in grp:
                if sample:
                    self.tr(self.ps[6][:NT, (c % 4) * 128:(c % 4 + 1) * 128], cbt[:, 3:3 + NT], self.cf("ident"),
                            [cbk, "cstf"], ["ps6"])
                    if c % 4 == 3:
                        stg, stk = self.stage(c // 8)
                        self.cp("act", stg[:NT, (c % 8 - 3) * 128:(c % 8 + 1) * 128], self.ps[6][:NT, :], ["ps6"], [stk])
            qk = [g_ for g_ in grp if g_[0] < 16]
            sds = []
            for (c, so, sk, sq, sqk, cbt, cbk) in qk:
                b2 = 4 + c % 2 if sample else 4 + c % 4
                self.mm(self.ps[b2][:, :NT], self.cb("ones"), sq[:, :NT], [sqk, "cstb"], ["ps%d" % b2])
            for (c, so, sk, sq, sqk, cbt, cbk) in qk:
                b2 = 4 + c % 2 if sample else 4 + c % 4
                sd, sdk = self.tmpf()
                sds.append((sd, sdk))
                self.act(sd[:, :NT], self.ps[b2][:, :NT], AF.Ln, ["ps%d" % b2], [sdk], bias=L2_EPS)
            for (sd, sdk) in sds:
                self.act(sd[:, :NT], sd[:, :NT], AF.Exp, [sdk], [sdk], scale=-0.5)
            for (c, so, sk, sq, sqk, cbt, cbk), (sd, sdk) in zip(qk, sds):
                const = 128.0 ** -0.5 if c < 8 else 1.0
                self.stt(A1[:, c, :NT], so[:, :NT], const, sd[:, :NT], ALU.mult, ALU.mult, [sk, sdk], ["A1.%d" % c])

        pend = None
        for g in range(6):
            (wq,), wk = self.slab([("w_in", 0, D, g * 512, (g + 1) * 512)])
            for pr in range(2):
                cs_ = [g * 4 + pr * 2, g * 4 + pr * 2 + 1]
                for c in cs_:
                    jj = c % 4
                    bank = c % 4
                    for kc in range(8):
                        self.mm(self.ps[bank][:, :NT], wq[:, kc, jj * 128:(jj + 1) * 128], self.xT[:, kc, :NT],
                                [wk, "xT0", "xT1", "xT2", "xT3"], ["ps%d" % bank], start=(kc == 0), stop=(kc == 7))
                convs = []
                for c in cs_:
                    th = [self.hsq[:, c, j, :] for j in range(3)] if sample else None
                    wts = [self.wcq[:, c, j:j + 1] for j in range(4)]
                    convs.append(self.conv_chunk(c % 4, NT, th, wts, 4, self.histq[:, c, :], "histq%d" % c, sample))
                if pend is not None:
                    finish(pend)
                cur = []
                for c, (acc, ak, cbt, cbk) in zip(cs_, convs):
                    if c >= 16:
                        self.act(A1[:, c, :NT], acc[:, :NT], AF.Silu, [ak], ["A1.%d" % c])
                        cur.append((c, None, None, None, None, cbt, cbk))
                    else:
                        self.act(acc[:, :NT], acc[:, :NT], AF.Silu, [ak], [ak])
                        sq, sqk = self.tmpb()
                        if self.recording:
                            self.act(sq[:, :NT], acc[:, :NT], AF.Square, [ak], [sqk])
                        else:
                            self.tt("pool", sq[:, :NT], acc[:, :NT], acc[:, :NT], ALU.mult, [ak], [sqk])
                        cur.append((c, acc, ak, sq, sqk, cbt, cbk))
                pend = cur
        finish(pend)
        if sample:
            for k in range(3):
                stg, stk = self.stage(k)
                self.dma("aux", self.sqs[:, 2, k * 1024:(k + 1) * 1024], stg[:NSAMP, :], "so0", [stk], ["sqs2_%d" % k])
                self.outkeys.append("sqs2_%d" % k)
            self.dma("aux", self.sqs[:, 0:2, :], self.sq[:, 1:3, :], "so1", [], ["sqs01"])
            self.outkeys += ["sqs01"]
        elif last:
            for j in range(3):
                self.dma("aux", self.sqp[j, :].rearrange("(c p) -> p c", p=128), self.histq[:, :, j], "so0",
                         ["histq%d" % c for c in range(24)], ["sqp%d" % j], slow=True)
                self.outkeys.append("sqp%d" % j)
        if self.debug.get("mstop") == "A":
            return
        for nh in range(2):
            (wz,), wk = self.slab([("w_in", 0, D, Z0 + nh * 512, Z0 + (nh + 1) * 512)])
            for tb in range(NB):
                b = 4 + tb % 2
                for kc in range(8):
                    self.mm(self.ps[b][:TB, :], self.xT[:, kc, tb * TB:(tb + 1) * TB], wz[:, kc, :],
                            [wk, "xT0", "xT1", "xT2", "xT3"], ["ps%d" % b], start=(kc == 0), stop=(kc == 7))
                self.act(self.ztok[:TB, tb, nh * 512:(nh + 1) * 512], self.ps[b][:TB, :], AF.Silu, ["ps%d" % b], ["ztok%d" % tb])
        (wba,), wk = self.slab([("w_in", 0, D, BETA0, BETA0 + 16)])
        for tb in range(NB):
            for kc in range(8):
                self.mm(self.ps[6][:TB, 0:16], self.xT[:, kc, tb * TB:(tb + 1) * TB], wba[:, kc, :],
                        [wk, "xT0", "xT1", "xT2", "xT3"], ["ps6"], start=(kc == 0), stop=(kc == 7))
            self.act(self.beta[:TB, tb, :], self.ps[6][:TB, 0:8], AF.Sigmoid, ["ps6"], ["beta"])
            self.tt("dve", self.batok[:TB, tb, 8:16], self.ps[6][:TB, 8:16], self.smallb[:TB, 8:16], ALU.add,
                    ["ps6", "smallb"], ["batok"])
        for tb in range(NB):
            self.act(self.batok[:TB, tb, 0:8], self.batok[:TB, tb, 8:16], AF.Exp, ["batok"], ["batok"])
        for tb in range(NB):
            self.act(self.batok[:TB, tb, 0:8], self.batok[:TB, tb, 0:8], AF.Ln, ["batok"], ["batok"], bias=1.0)
            self.tt("dve", self.gtok[:TB, tb, :], self.batok[:TB, tb, 0:8], self.negA[:TB, :], ALU.mult,
                    ["batok", "negA"], ["gtok"])
        if self.debug.get("mstop") == "B":
            return
        if sample:
            self.gdn_sample()
        else:
            self.gdn_all(NB)
            if last:
                self.dma("aux", self.sgp.rearrange("h k v -> k h v"), self.S[:], "so1", ["S0", "S1"], ["sgp"])
                self.outkeys.append("sgp")
        if self.debug.get("mstop") == "C":
            return
        for c in range(8):
            (wB, wC, wH), wk = self.slab([("w_in", 0, D, B0 + c * 128, B0 + (c + 1) * 128),
                                          ("w_in", 0, D, C0 + c * 128, C0 + (c + 1) * 128),
                                          ("w_in", 0, D, H0 + c * 128, H0 + (c + 1) * 128)])
            bB, bC, bH = 0 + 3 * (c % 2), 1 + 3 * (c % 2), 2 + 3 * (c % 2)
            for (w_, b_) in ((wC, bC), (wH, bH), (wB, bB)):
                for kc in range(8):
                    self.mm(self.ps[b_][:, :NT], w_[:, kc, :], self.xT[:, kc, :NT], [wk, "xT0", "xT1", "xT2", "xT3"], ["ps%d" % b_],
                            start=(kc == 0), stop=(kc == 7))
            ct, ck = self.tmpf()
            self.cp("act", ct[:, :NT], self.ps[bC][:, :NT], ["ps%d" % bC], [ck])

            def src(dst, dk, ct=ct, ck=ck, bH=bH):
                self.tt("dve", dst, ct[:, :NT], self.ps[bH][:, :NT], ALU.mult, [ck, "ps%d" % bH], [dk])
            th = [self.hss[:, c, j, :] for j in range(2)] if sample else None
            wts = [self.wcs[:, c, j:j + 1] for j in range(3)]
            acc, ak, cbt, cbk = self.conv_chunk(None, NT, th, wts, 3, self.hists[:, c, :], "hists%d" % c, sample,
                                                src_is_psum=False, src=src)
            if sample:
                self.tr(self.ps[6][:NT, (c % 4) * 128:(c % 4 + 1) * 128], cbt[:, 2:2 + NT], self.cf("ident"),
                        [cbk, "cstf"], ["ps6"])
                if c % 4 == 3:
                    stg, stk = self.stage(0)
                    self.cp("act", stg[:NT, (c - 3) * 128:(c + 1) * 128], self.ps[6][:NT, :], ["ps6"], [stk])
            self.tt("dve", A1[:, c, :NT], acc[:, :NT], self.ps[bB][:, :NT], ALU.mult, [ak, "ps%d" % bB], ["A1.%d" % c])
        if sample:
            stg, stk = self.stage(0)
            self.dma("aux", self.sss[:, 1, :], stg[:NSAMP, 0:D], "so2", [stk], ["sss1"])
            self.dma("aux", self.sss[:, 0:1, :], self.ssc[:, 1:2, :], "so3", [], ["sss0"])
            self.outkeys += ["sss1", "sss0"]
        elif last:
            for j in range(2):
                self.dma("aux", self.ssp[j, :].rearrange("(c p) -> p c", p=128), self.hists[:, :, j], "so2",
                         ["hists%d" % c for c in range(8)], ["ssp%d" % j], slow=True)
                self.outkeys.append("ssp%d" % j)
        if self.debug.get("mstop") == "D":
            return
        for c in range(8):
            (wpg, wgg, wps, wgs), wk = self.slab([("w_p_gdn", 0, D, c * 128, (c + 1) * 128),
                                                  ("w_in", 0, D, GG0 + c * 128, GG0 + (c + 1) * 128),
                                                  ("w_p_sc", 0, D, c * 128, (c + 1) * 128),
                                                  ("w_in", 0, D, GS0 + c * 128, GS0 + (c + 1) * 128)])
            o = 4 * (c % 2)
            for (w_, b_, rhs_, rk) in ((wpg, o, A1[:, 16:24, :], ["A1.%d" % k for k in range(16, 24)]),
                                       (wgg, o + 1, self.xT, ["xT0", "xT1", "xT2", "xT3"]),
                                       (wps, o + 2, A1[:, 0:8, :], ["A1.%d" % k for k in range(8)]),
                                       (wgs, o + 3, self.xT, ["xT0", "xT1", "xT2", "xT3"])):
                for kc in range(8):
                    self.mm(self.ps[b_][:, :NT], w_[:, kc, :], rhs_[:, kc, :NT], [wk] + rk, ["ps%d" % b_],
                            start=(kc == 0), stop=(kc == 7))
            s1, s1k = self.tmpf()
            self.act(s1[:, :NT], self.ps[o + 1][:, :NT], AF.Sigmoid, ["ps%d" % (o + 1)], [s1k])
            self.tt("dve", s1[:, :NT], s1[:, :NT], self.ps[o][:, :NT], ALU.mult, [s1k, "ps%d" % o], [s1k])
            s2, s2k = self.tmpf()
            self.act(s2[:, :NT], self.ps[o + 3][:, :NT], AF.Sigmoid, ["ps%d" % (o + 3)], [s2k])
            self.tt("dve", s2[:, :NT], s2[:, :NT], self.ps[o + 2][:, :NT], ALU.mult, [s2k, "ps%d" % (o + 2)], [s2k])
            self.tt("dve", A1[:, 8 + c, :NT], s1[:, :NT], s2[:, :NT], ALU.add, [s1k, s2k], ["A1.%d" % (8 + c)])
        for nh in range(2):
            (wo,), wk = self.slab([("w_o", 0, D, nh * 512, (nh + 1) * 512)])
            for tb in range(NB):
                b = tb % 2
                for kc in range(8):
                    self.mm(self.ps[b][:TB, :], A1[:, 8 + kc, tb * TB:(tb + 1) * TB], wo[:, kc, :],
                            [wk, "A1.%d" % (8 + kc)], ["ps%d" % b], start=(kc == 0), stop=(kc == 7))
                xr = self.xres[:TB, tb, nh * 512:(nh + 1) * 512]
                self.stt(xr, self.ps[b][:TB, :], 1.0 / ALPHA, xr, ALU.mult, ALU.add, ["ps%d" % b, "xres%d" % tb], ["xres%d" % tb])

    def onorm_and_T(self, tb, TB):
        self.lockstep([self.onorm_gen(tb, TB)])

    def onorm_gen(self, tb, TB, bank=7):
        o3 = self.otok[:TB, :].rearrange("p (h d) -> p h d", h=H)
        t13 = self.t1[:TB, :].rearrange("p (h d) -> p h d", h=H)
        t23 = self.t2[:TB, :].rearrange("p (h d) -> p h d", h=H)
        st = self.stat3
        self.act(self.t1[:TB, :], self.otok[:TB, :], AF.Square, ["otok"], ["t1"])
        yield
        self.P.add("dve", lambda e: e.tensor_reduce(st[:TB, 0:8], t13, AX.X, ALU.add), ["t1"], ["stat3"])
        self.act(st[:TB, 0:8], st[:TB, 0:8], AF.Sqrt, ["stat3"], ["stat3"], bias=RMS_EPS, scale=1.0 / 128.0)
        yield
        self.P.add("dve", lambda e: e.reciprocal(st[:TB, 8:16], st[:TB, 0:8]), ["stat3"], ["stat3"])
        yield
        self.tt("dve", t13, o3, st[:TB, 8:16].unsqueeze(2).to_broadcast([TB, H, 128]), ALU.mult, ["otok", "stat3"], ["t1"])
        z3 = self.ztok[:TB, tb, :].rearrange("p (h d) -> p h d", h=H)
        self.tt("dve", t23, z3, self.wonb[:TB, :].unsqueeze(1).to_broadcast([TB, H, 128]), ALU.mult,
                ["ztok%d" % tb, "wonb"], ["t2"])
        yield
        self.tt("dve", self.xb16[:TB, :], self.t1[:TB, :], self.t2[:TB, :], ALU.mult, ["t1", "t2"], ["xb16"])
        yield
        for c in range(8):
            self.tr(self.psb[bank][:, c * TB:(c + 1) * TB], self.xb16[:TB, c * 128:(c + 1) * 128], self.cb("ident")[:TB, :TB],
                    ["xb16", "cstb"], ["ps%d" % bank])
        yield
        self.cp("act", self.A1[:, 16:24, tb * TB:(tb + 1) * TB],
                self.psb[bank][:, 0:8 * TB].rearrange("p (c t) -> p c t", c=8), ["ps%d" % bank],
                ["A1.%d" % k for k in range(16, 24)])

    def inv_chain(self, tb, hg, G, gp, pb):
        A1 = self.A1
        blk = slice(tb * 128, (tb + 1) * 128)
        g8 = self.gtok[:, tb, :]
        hs = [hg * 4 + hh for hh in range(4)]
        K = lambda nm: gp + nm
        rot = [0]

        def nb():
            x = pb[rot[0] % len(pb)]
            rot[0] += 1
            return x
        b0, b1, b2 = nb(), nb(), nb()
        f4 = lambda t: t.rearrange("p h d -> p (h d)")
        kq_r = ["A1.%d" % (8 + h) for h in hs] + ["A1.%d" % h for h in hs]
        for hh, h in enumerate(hs):
            self.ts("dve", G["Lg"][:, hh, :], self.cf("ltri"), g8[:, h:h + 1], ALU.mult, ["cstf", "gtok"], [K("Lg")])
        yield
        for hh, h in enumerate(hs):
            cs = slice(hh * 128, (hh + 1) * 128)
            self.mm(self.ps[b0][:, cs], self.cf("su"), G["Lg"][:, hh, :], ["cstf", K("Lg")], ["ps%d" % b0])
            self.mm(self.ps[b1][:, cs], A1[:, 8 + h, blk], A1[:, 8 + h, blk], kq_r, ["ps%d" % b1])
            self.mm(self.ps[b2][:, cs], A1[:, 8 + h, blk], A1[:, h, blk], kq_r, ["ps%d" % b2])
        yield
        self.act(f4(G["decTm"]), self.ps[b0][:, :], AF.Exp, ["ps%d" % b0], [K("decTm")])
        yield
        self.tt("dve", G["decTm"], G["decTm"], self.cf4("muincl"), ALU.mult, [K("decTm"), "cstf"], [K("decTm")])
        yield
        self.tt("dve", f4(G["qkTm"]), self.ps[b2][:, :], f4(G["decTm"]), ALU.mult, ["ps%d" % b2, K("decTm")], [K("qkTm")])
        self.tt("dve", f4(G["Lg"]), self.ps[b1][:, :], f4(G["decTm"]), ALU.mult, ["ps%d" % b1, K("decTm")], [K("Lg")])
        yield
        self.tt("dve", G["MT"], G["Lg"],
                self.beta[:, tb, hg * 4:hg * 4 + 4].unsqueeze(2).to_broadcast([128, 4, 128]), ALU.mult,
                [K("Lg"), "beta"], [K("MT")])
        yield
        bt = nb()
        for hh in range(4):
            self.tr(self.psb[bt][:, hh * 128:(hh + 1) * 128], G["MT"][:, hh, :], self.cb("ident"), [K("MT"), "cstb"], ["ps%d" % bt])
        yield
        self.cp("act", f4(G["M"]), self.psb[bt][:, 0:512], ["ps%d" % bt], [K("M")])
        yield
        Nn, Nt, N2, N2t = "Na", "Nb", "Nc", "Nd"
        Pn, Pt, Pn2, Pt2 = "Pa", "Pb", "Pc", "Pd"
        self.tt("dve", G[Nn], G["M"], self.cb4("mndn"), ALU.mult, [K("M"), "cstb"], [K(Nn)])
        self.tt("dve", G[Nt], G["MT"], self.cb4("mndtn"), ALU.mult, [K("MT"), "cstb"], [K(Nt)])
        yield
        self.tt("dve", G[Pn], G[Nn], self.cb4("ident"), ALU.add, [K(Nn), "cstb"], [K(Pn)])
        self.tt("dve", G[Pt], G[Nt], self.cb4("ident"), ALU.add, [K(Nt), "cstb"], [K(Pt)])
        nstep = int(np.log2(NBK)) - 1
        for s_ in range(nstep):
            ba, bb = nb(), nb()
            for hh in range(4):
                cs = slice(hh * 128, (hh + 1) * 128)
                self.mm(self.ps[ba][:, cs], G[Nt][:, hh, :], G[Nn][:, hh, :], [K(Nt), K(Nn)], ["ps%d" % ba])
                self.mm(self.ps[bb][:, cs], G[Nn][:, hh, :], G[Nt][:, hh, :], [K(Nt), K(Nn)], ["ps%d" % bb])
            yield
            self.cp("act", f4(G[N2]), self.ps[ba][:, :], ["ps%d" % ba], [K(N2)])
            self.cp("act", f4(G[N2t]), self.ps[bb][:, :], ["ps%d" % bb], [K(N2t)])
            yield
            bc_, bd = nb(), nb()
            for hh in range(4):
                cs = slice(hh * 128, (hh + 1) * 128)
                self.mm(self.ps[bc_][:, cs], G[N2t][:, hh, :], G[Pn][:, hh, :], [K(N2t), K(Pn)], ["ps%d" % bc_])
                self.mm(self.ps[bd][:, cs], G[N2][:, hh, :], G[Pt][:, hh, :], [K(N2), K(Pt)], ["ps%d" % bd])
            yield
            self.tt("dve", f4(G[Pn2]), f4(G[Pn]), self.ps[bc_][:, :], ALU.add, [K(Pn), "ps%d" % bc_], [K(Pn2)])
            self.tt("dve", f4(G[Pt2]), f4(G[Pt]), self.ps[bd][:, :], ALU.add, [K(Pt), "ps%d" % bd], [K(Pt2)])
            yield
            Nn, Nt, N2, N2t = N2, N2t, Nn, Nt
            Pn, Pt, Pn2, Pt2 = Pn2, Pt2, Pn, Pt
        T, U, T2, U2 = Pn, Pt, Pn2, Pt2
        E_, F_, X_, Y_ = Nn, Nt, N2, N2t
        b = NBK
        while b < 128:
            lastlvl = (b == 64)
            self.tt("dve", G[E_], G["M"], self.cb4("me%d" % b), ALU.mult, [K("M"), "cstb"], [K(E_)])
            if not lastlvl:
                self.tt("dve", G[F_], G["MT"], self.cb4("me%dt" % b), ALU.mult, [K("MT"), "cstb"], [K(F_)])
            yield
            ba, bb = nb(), nb()
            for hh in range(4):
                cs = slice(hh * 128, (hh + 1) * 128)
                self.mm(self.ps[ba][:, cs], G[E_][:, hh, :], G[U][:, hh, :], [K(E_), K(U)], ["ps%d" % ba])
                if not lastlvl:
                    self.mm(self.ps[bb][:, cs], G[F_][:, hh, :], G[T][:, hh, :], [K(F_), K(T)], ["ps%d" % bb])
            yield
            self.cp("act", f4(G[Y_]), self.ps[ba][:, :], ["ps%d" % ba], [K(Y_)])
            if not lastlvl:
                self.cp("act", f4(G[X_]), self.ps[bb][:, :], ["ps%d" % bb], [K(X_)])
            yield
            bc_, bd = nb(), nb()
            for hh in range(4):
                cs = slice(hh * 128, (hh + 1) * 128)
                self.mm(self.ps[bc_][:, cs], G[T][:, hh, :], G[Y_][:, hh, :], [K(T), K(Y_)], ["ps%d" % bc_])
                if not lastlvl:
                    self.mm(self.ps[bd][:, cs], G[U][:, hh, :], G[X_][:, hh, :], [K(U), K(X_)], ["ps%d" % bd])
            yield
            self.tt("dve", f4(G[U2]), f4(G[U]), self.ps[bc_][:, :], ALU.subtract, [K(U), "ps%d" % bc_], [K(U2)])
            if not lastlvl:
                self.tt("dve", f4(G[T2]), f4(G[T]), self.ps[bd][:, :], ALU.subtract, [K(T), "ps%d" % bd], [K(T2)])
            yield
            T, U, T2, U2 = T2, U2, T, U
            b *= 2
        self.cp("dve", self.Uk[hg][:], G[U], [K(U)], ["Uk%d" % hg])
        self.cp("dve", self.Qk[hg][:], G["qkTm"], [K("qkTm")], ["Qk%d" % hg])

    def scan_chain(self, tb, hg, bx, by):
        A1 = self.A1
        blk = slice(tb * 128, (tb + 1) * 128)
        pb_ = tb % 2
        sm, smk = self.gsm2[pb_], "gsm%d" % pb_
        vtok, vtk = (self.vtok, self.vtok2)[pb_], "vtok%d" % pb_
        kdec, kdk = self.kdec2[pb_], "kdec%d" % pb_
        hs = [hg * 4 + hh for hh in range(4)]
        hsl = slice(hg * 4, hg * 4 + 4)
        X, Y = self.ps[bx], self.ps[by]
        xk, yk = "ps%d" % bx, "ps%d" % by
        X3 = X[:, :].rearrange("p (h d) -> p h d", h=4)
        Y3 = Y[:, :].rearrange("p (h d) -> p h d", h=4)
        bc = lambda ap: ap.unsqueeze(2).to_broadcast([128, 4, 128])
        Sk, Sbk = "S%d" % hg, "Sbf%d" % hg
        for hh, h in enumerate(hs):
            cs = slice(hh * 128, (hh + 1) * 128)
            self.mm(X[:, cs], A1[:, 8 + h, blk], self.Sbf[:, h, :], ["A1.%d" % (8 + h), Sbk], [xk])
            self.mm(Y[:, cs], A1[:, h, blk], self.Sbf[:, h, :], ["A1.%d" % h, Sbk], [yk])
        yield
        tS, tSk = self.tmpf()
        tS3 = tS[:, 0:512].rearrange("p (h d) -> p h d", h=4)
        self.tt("dve", tS3, X3, bc(sm[:, 24 + hg * 4:28 + hg * 4]), ALU.mult, [xk, smk], [tSk])
        o1, o1k = self.tmpf()
        o13 = o1[:, 0:512].rearrange("p (h d) -> p h d", h=4)
        self.tt("dve", o13, Y3, bc(sm[:, 16 + hg * 4:20 + hg * 4]), ALU.mult, [yk, smk], [o1k])
        yield
        r, rk = self.tmpb()
        r3 = r[:, :].rearrange("p (h d) -> p h d", h=4)
        self.tt("dve", r3, tS3, vtok[:, hsl, :], ALU.add, [tSk, vtk], [rk])
        yield
        for hh in range(4):
            cs = slice(hh * 128, (hh + 1) * 128)
            self.mm(X[:, cs], self.Uk[hg][:, hh, :], r3[:, hh, :], ["Uk%d" % hg, rk], [xk])
        yield
        vn, vk = self.tmpb()
        vn3 = vn[:, :].rearrange("p (h d) -> p h d", h=4)
        self.tt("dve", vn3, X3, bc(self.beta[:, tb, hsl]), ALU.mult, [xk, "beta"], [vk])
        yield
        for hh, h in enumerate(hs):
            cs = slice(hh * 128, (hh + 1) * 128)
            self.mm(Y[:, cs], self.Qk[hg][:, hh, :], vn3[:, hh, :], ["Qk%d" % hg, vk], [yk])
            self.mm(X[:, cs], kdec[:, h, :], vn3[:, hh, :], [kdk, vk], [xk])
        yield
        self.tt("dve", self.otok[:, hg * 512:(hg + 1) * 512], o1[:, 0:512], Y[:, :], ALU.add, [o1k, yk], ["otok"])
        self.tt("dve", self.S[:, hsl, :], self.S[:, hsl, :], bc(sm[:, 40 + hg * 4:44 + hg * 4]), ALU.mult, [Sk, smk], [Sk])
        yield
        self.tt("dve", self.S[:, hsl, :], self.S[:, hsl, :], X3, ALU.add, [Sk, xk], [Sk])
        yield
        self.cp("act", self.Sbf[:, hsl, :], self.S[:, hsl, :], [Sk], [Sbk])

    def lockstep(self, gens):
        gens = list(gens)
        while gens:
            nxt = []
            for g in gens:
                try:
                    next(g)
                    nxt.append(g)
                except StopIteration:
                    pass
            gens = nxt

    def gdn_prep(self, tb):
        A1 = self.A1
        blk = slice(tb * 128, (tb + 1) * 128)
        pb_ = tb % 2
        sm, smk = self.gsm2[pb_], "gsm%d" % pb_
        vtok, vtk = (self.vtok, self.vtok2)[pb_], "vtok%d" % pb_
        kdec, kdk = self.kdec2[pb_], "kdec%d" % pb_
        g8 = self.gtok[:, tb, :]
        self.mm(self.ps[7][:, 0:8], self.cf("ltri"), g8, ["cstf", "gtok"], ["ps7"])
        self.mm(self.ps[7][:, 8:16], self.cf("ones"), g8, ["cstf", "gtok"], ["ps7"])
        self.cp("dve", sm[:, 0:16], self.ps[7][:, 0:16], ["ps7"], [smk])
        yield
        self.act(sm[:, 16:24], sm[:, 0:8], AF.Exp, [smk], [smk])
        self.tt("dve", sm[:, 32:40], sm[:, 8:16], sm[:, 0:8], ALU.subtract, [smk], [smk])
        yield
        self.ts("dve", sm[:, 24:32], sm[:, 16:24], -1.0, ALU.mult, [smk], [smk])
        self.act(sm[:, 32:40], sm[:, 32:40], AF.Exp, [smk], [smk])
        self.act(sm[:, 40:48], sm[:, 8:16], AF.Exp, [smk], [smk])
        for (dst, dk, u0, bank) in ((kdec, kdk, 8, 6), (vtok, vtk, 16, 7)):
            for h in range(H):
                self.tr(self.psb[bank][:, h * 128:(h + 1) * 128], A1[:, u0 + h, blk], self.cb("ident"),
                        ["A1.%d" % (u0 + h), "cstb"], ["ps%d" % bank])
        yield
        self.cp("act", vtok[:].rearrange("p h d -> p (h d)"), self.psb[7][:, 0:1024], ["ps7"], [vtk])
        self.tt("dve", kdec[:], self.psb[6][:, 0:1024].rearrange("p (h d) -> p h d", h=H),
                sm[:, 32:40].unsqueeze(2).to_broadcast([128, H, 128]), ALU.mult, ["ps6", smk], [kdk])

    def gdn_prep_old(self, tb):
        A1 = self.A1
        blk = slice(tb * 128, (tb + 1) * 128)
        pb_ = tb % 2
        sm, smk = self.gsm2[pb_], "gsm%d" % pb_
        vtok, vtk = (self.vtok, self.vtok2)[pb_], "vtok%d" % pb_
        kdec, kdk = self.kdec2[pb_], "kdec%d" % pb_
        g8 = self.gtok[:, tb, :]
        for (dst, dk, u0, bank) in ((self.ktok, "ktok", 8, 5), (vtok, vtk, 16, 6)):
            for h in range(H):
                self.tr(self.psb[bank][:, h * 128:(h + 1) * 128], A1[:, u0 + h, blk], self.cb("ident"),
                        ["A1.%d" % (u0 + h), "cstb"], ["ps%d" % bank])
            self.cp("act", dst[:].rearrange("p h d -> p (h d)"), self.psb[bank][:, 0:1024], ["ps%d" % bank], [dk])
        self.mm(self.ps[7][:, 0:8], self.cf("ltri"), g8, ["cstf", "gtok"], ["ps7"])
        self.mm(self.ps[7][:, 8:16], self.cf("ones"), g8, ["cstf", "gtok"], ["ps7"])
        self.cp("dve", sm[:, 0:16], self.ps[7][:, 0:16], ["ps7"], [smk])
        self.act(sm[:, 16:24], sm[:, 0:8], AF.Exp, [smk], [smk])
        self.ts("dve", sm[:, 24:32], sm[:, 16:24], -1.0, ALU.mult, [smk], [smk])
        self.tt("dve", sm[:, 32:40], sm[:, 8:16], sm[:, 0:8], ALU.subtract, [smk], [smk])
        self.act(sm[:, 32:40], sm[:, 32:40], AF.Exp, [smk], [smk])
        self.act(sm[:, 40:48], sm[:, 8:16], AF.Exp, [smk], [smk])
        self.tt("dve", kdec[:], self.ktok[:], sm[:, 32:40].unsqueeze(2).to_broadcast([128, H, 128]), ALU.mult,
                ["ktok", smk], [kdk])

    def seq_chain(self, tb, NB):
        for hg in range(2):
            for _ in self.scan_chain(tb, hg, 6, 7):
                yield
            yield
        for _ in self.onorm_gen(tb, 128, bank=6):
            yield
        if tb + 1 < NB:
            yield
            for _ in self.gdn_prep(tb + 1):
                yield

    def gdn_all(self, NB):
        if self.debug.get("mstop") == "Y":
            nbk_ = 4 if self.debug.get("gstop") == "b4" else 3
            inv = lambda tb: [self.inv_chain(tb, hg, self.gqs[hg][0], self.gqs[hg][1], [nbk_ * hg + i for i in range(nbk_)])
                              for hg in range(2)]
            for tb in range(NB):
                if self.debug.get("gstop") == "oldprep":
                    self.gdn_prep_old(tb)
                else:
                    self.lockstep([self.gdn_prep(tb)])
                self.lockstep(inv(tb))
                self.lockstep([self.scan_chain(tb, 0, 6, 7)])
                self.lockstep([self.scan_chain(tb, 1, 6, 7)])
                self.lockstep([self.onorm_gen(tb, 128, bank=6)])
            return
        self.lockstep([self.gdn_prep(0)])
        inv = lambda tb: [self.inv_chain(tb, hg, self.gqs[hg][0], self.gqs[hg][1], [3 * hg + i for i in range(3)])
                          for hg in range(2)]
        self.lockstep(inv(0))
        for tb in range(NB):
            gens = [self.seq_chain(tb, NB)]
            if tb + 1 < NB:
                gens = inv(tb + 1) + gens
            self.lockstep(gens)

    def stage(self, k):
        return [(self.t1, "t1"), (self.t2, "t2"), (self.otok, "otok")][k]

    def load_sample_hist(self):
        for (srcd, dst, nch, nj) in ((self.sq, self.hsq, 24, 3), (self.ssc, self.hss, 8, 2)):
            for j in range(nj):
                for k in range(nch // 8):
                    t, tk = self.stage(k)
                    self.dma("aux", t[:NSAMP, :], srcd[:, j, k * 1024:(k + 1) * 1024], "hl", [], [tk])
                    for cc in range(8):
                        self.tr(self.ps[6][:, cc * NSAMP:(cc + 1) * NSAMP], t[:NSAMP, cc * 128:(cc + 1) * 128],
                                self.cf("ident")[:NSAMP, :NSAMP], [tk, "cstf"], ["ps6"])
                    self.cp("dve", dst[:, k * 8:(k + 1) * 8, j, :],
                            self.ps[6][:, 0:8 * NSAMP].rearrange("p (c b) -> p c b", c=8), ["ps6"], ["hsamp"])

    def gdn_sample(self):
        A1 = self.A1
        NS = NSAMP
        sm = self.gsm
        st = self.stat
        for (dst, dk, u0, bank) in ((self.qtok[:NS, :], "qtok", 0, 4), (self.ktok[:NS].rearrange("p h d -> p (h d)"), "ktok", 8, 5),
                                    (self.vtok[:NS].rearrange("p h d -> p (h d)"), "vtok", 16, 6)):
            for h in range(H):
                self.tr(self.psb[bank][:NS, h * 128:(h + 1) * 128], A1[:, u0 + h, 0:NS], self.cb("ident"),
                        ["A1.%d" % (u0 + h), "cstb"], ["ps%d" % bank])
            self.cp("act", dst, self.psb[bank][:NS, 0:1024], ["ps%d" % bank], [dk])
        a = sm[:NS, 0:8]
        self.act(a, self.gtok[:NS, 0, :], AF.Exp, ["gtok"], ["gsm"])
        q3 = self.qtok[:NS, :].rearrange("p (h d) -> p h d", h=H)
        t13 = self.t1[:NS, :].rearrange("p (h d) -> p h d", h=H)
        t23 = self.t2[:NS, :].rearrange("p (h d) -> p h d", h=H)
        o3 = self.otok[:NS, :].rearrange("p (h d) -> p h d", h=H)
        self.tt("dve", t13, q3, self.ktok[:NS], ALU.mult, ["qtok", "ktok"], ["t1"])
        self.P.add("dve", lambda e: e.tensor_reduce(sm[:NS, 8:16], t13, AX.X, ALU.add), ["t1"], ["gsm"])
        i16 = self.i16b[:].rearrange("p (a b) -> p a b", a=NS)
        for h in range(H):
            self.tt("dve", self.kTm[:, h, :, :], A1[:, 8 + h:9 + h, 0:NS].to_broadcast([128, NS, NS]), i16, ALU.mult,
                    ["A1.%d" % (8 + h), "i16b"], ["kTm"])
            self.tt("dve", self.qTm[:, h, :, :], A1[:, h:h + 1, 0:NS].to_broadcast([128, NS, NS]), i16, ALU.mult,
                    ["A1.%d" % h, "i16b"], ["qTm"])
        for b in range(NS):
            i3, i2 = b % 3, b % 2
            self.dma("sp", self.Sin[i3], self.sg[b].rearrange("h k v -> k h v"), "sin%d" % i3, [], ["Sin%d" % i3])
            self.cp("act" if b % 2 == 0 else "dve", self.Sinb[i2], self.Sin[i3], ["Sin%d" % i3], ["Sinb%d" % i2])
            for h in range(H):
                bk, bq = h // 4, 2 + h // 4
                cs = slice((h % 4) * 128, (h % 4 + 1) * 128)
                first = (b == 0 and h % 4 == 0)
                self.mm(self.ps[bk][:NS, cs], self.kTm[:, h, b, :], self.Sinb[i2][:, h, :], ["kTm", "Sinb%d" % i2],
                        ["ps%d" % bk], start=first, stop=(b == NS - 1))
                self.mm(self.ps[bq][:NS, cs], self.qTm[:, h, b, :], self.Sinb[i2][:, h, :], ["qTm", "Sinb%d" % i2],
                        ["ps%d" % bq], start=first, stop=(b == NS - 1))
        a_b = a.unsqueeze(2).to_broadcast([NS, H, 128])
        for half in range(2):
            hsl = slice(half * 4, half * 4 + 4)
            k3 = self.ps[half][:NS, :].rearrange("p (h d) -> p h d", h=4)
            qs3 = self.ps[2 + half][:NS, :].rearrange("p (h d) -> p h d", h=4)
            ab = a[:, hsl].unsqueeze(2).to_broadcast([NS, 4, 128])
            self.tt("dve", t13[:, hsl, :], k3, ab, ALU.mult, ["ps%d" % half, "gsm"], ["t1"])
            self.tt("dve", t13[:, hsl, :], self.vtok[:NS, hsl, :], t13[:, hsl, :], ALU.subtract, ["vtok", "t1"], ["t1"])
            self.tt("dve", t13[:, hsl, :], t13[:, hsl, :],
                    self.beta[:NS, 0, hsl].unsqueeze(2).to_broadcast([NS, 4, 128]), ALU.mult, ["t1", "beta"], ["t1"])
            self.tt("dve", t23[:, hsl, :], qs3, ab, ALU.mult, ["ps%d" % (2 + half), "gsm"], ["t2"])
            self.tt("dve", o3[:, hsl, :], t13[:, hsl, :], sm[:NS, 8 + half * 4:12 + half * 4].unsqueeze(2).to_broadcast([NS, 4, 128]),
                    ALU.mult, ["t1", "gsm"], ["otok"])
            self.tt("dve", o3[:, hsl, :], o3[:, hsl, :], t23[:, hsl, :], ALU.add, ["otok", "t2"], ["otok"])
        dbf = self.xb16
        self.cp("act", dbf[:NS, :], self.t1[:NS, :], ["t1"], ["xb16"])
        ad = self.t2[:NS, 0:128].rearrange("p (b h) -> p b h", b=NS)
        idr = self.cf("ident")[:NS, 0:NS].unsqueeze(2).to_broadcast([NS, NS, H])
        self.tt("dve", ad, a.unsqueeze(1).to_broadcast([NS, NS, H]), idr, ALU.mult, ["gsm", "cstf"], ["t2"])
        self.mm(self.ps[4][:, 0:128], self.cf("ones")[:NS, :], self.t2[:NS, 0:128], ["cstf", "t2"], ["ps4"])
        self.cp("dve", self.abc[:], self.ps[4][:, 0:128], ["ps4"], ["abc"])
        kflat = self.ktok[:NS].rearrange("p h d -> p (h d)")

        def load2(b):
            i3 = (b + 1) % 3
            self.dma("sp", self.Sin[i3], self.sg[b].rearrange("h k v -> k h v"), "sin%d" % i3, [], ["Sin%d" % i3])
        load2(0)
        load2(1)
        for b in range(NS):
            i2 = b % 2
            i3 = (b + 1) % 3
            if b + 2 < NS:
                load2(b + 2)
            self.ts("dve", self.kmask[i2][:NS, :], kflat, self.cf("ident")[:NS, b:b + 1], ALU.mult,
                    ["ktok", "cstf"], ["kmask%d" % i2])
            for h in range(H):
                self.mm(self.ps[h][:, 0:128], self.kmask[i2][:NS, h * 128:(h + 1) * 128], dbf[:NS, h * 128:(h + 1) * 128],
                        ["kmask%d" % i2, "xb16"], ["ps%d" % h])
                self.stt(self.Sin[i3][:, h, :], self.Sin[i3][:, h, :], self.abc[:, b * 8 + h:b * 8 + h + 1],
                         self.ps[h][:, 0:128], ALU.mult, ALU.add, ["Sin%d" % i3, "abc", "ps%d" % h], ["Sin%d" % i3])
            self.dma("pool", self.sgs[b].rearrange("h k v -> k h v"), self.Sin[i3], "sout%d" % i3, ["Sin%d" % i3], ["sgs%d" % b])
            self.outkeys.append("sgs%d" % b)
        self.onorm_and_T(0, NS)


_CACHE = {}


WBIG_LEN = 2 * (4 * D * HID) + D * IN_W + 4 * D * D + 256 * D


def pack_wbig(weights, specs, offs, tot):
    out = np.empty((tot,), np.float32)
    for spec, (off, n) in zip(specs, offs):
        parts = []
        for (name, r0, r1, c0, c1) in spec:
            w = weights[name][r0:r1, c0:c1]
            kc = (r1 - r0) // 128
            parts.append(w.reshape(kc, 128, c1 - c0).transpose(1, 0, 2).reshape(128, kc * (c1 - c0)))
        out[off:off + 128 * n] = np.concatenate(parts, axis=1).reshape(-1)
    return out


def kernel(x_prompt, x_sample, p_prompt, p_sample, state_gdn, state_qkv_conv, state_sc_conv,
           ffn1_w_gate, ffn1_w_up, ffn1_w_down, ln1_g, ln1_b,
           w_in, w_conv_qkv, A_log, dt_bias, w_onorm, w_p_gdn, w_conv_sc, w_p_sc, w_o, ln2_g, ln2_b,
           ffn2_w_gate, ffn2_w_up, ffn2_w_down, ln3_g, ln3_b,
           w_ple_gate, w_ple_proj, ln4_g, ln4_b, _debug=None):
    f = lambda a: np.ascontiguousarray(np.asarray(a, dtype=np.float32))
    weights = {"ffn1_w_gate": f(ffn1_w_gate)[0], "ffn1_w_up": f(ffn1_w_up)[0], "ffn1_w_down": f(ffn1_w_down)[0],
               "w_in": f(w_in)[0], "w_p_gdn": f(w_p_gdn)[0], "w_p_sc": f(w_p_sc)[0], "w_o": f(w_o)[0],
               "ffn2_w_gate": f(ffn2_w_gate)[0], "ffn2_w_up": f(ffn2_w_up)[0], "ffn2_w_down": f(ffn2_w_down)[0],
               "w_ple_gate": f(w_ple_gate)[0], "w_ple_proj": f(w_ple_proj)[0]}
    bld = Builder(debug=_debug)
    bld.wbig_len = WBIG_LEN
    nc = bld.build()
    assert bld.slab_tot == WBIG_LEN or _debug, (bld.slab_tot, WBIG_LEN)
    assert bld.slab_tot <= WBIG_LEN
    wbig = np.zeros((WBIG_LEN,), np.float32)
    wbig[:bld.slab_tot] = pack_wbig(weights, bld.slab_specs, bld.slab_off, bld.slab_tot)
    lnp = np.stack([f(ln1_g)[0], f(ln1_b)[0], f(ln2_g)[0], f(ln2_b)[0], f(ln3_g)[0], f(ln3_b)[0], f(ln4_g)[0], f(ln4_b)[0]])
    wcq = np.ascontiguousarray(f(w_conv_qkv)[0].reshape(4, 24, 128).transpose(2, 1, 0).reshape(128, 96))
    wcs = np.ascontiguousarray(f(w_conv_sc)[0].reshape(3, 8, 128).transpose(2, 1, 0).reshape(128, 24))
    smallp = np.stack([f(A_log)[0], f(dt_bias)[0]])
    cst, cst2 = make_consts()
    cst2 = np.ascontiguousarray(cst2.reshape(128, -1))
    i16 = np.ascontiguousarray(np.broadcast_to(np.eye(16, dtype=np.float32).reshape(1, 256), (128, 256)))
    xp = f(x_prompt)
    xsm = f(x_sample)[:, 0, :]
    ppr = f(p_prompt)[0]
    psm = f(p_sample)[0, :, 0, :]
    sg = f(state_gdn)[0]
    sq = f(state_qkv_conv)[0]
    ssc = f(state_sc_conv)[0]
    in_maps = []
    for c in range(8):
        sl = slice(c * NSAMP, (c + 1) * NSAMP)
        in_maps.append({"x": xp[c], "pp": ppr[c], "xs": xsm[sl], "psm": psm[sl], "sg": sg[sl], "sq": sq[sl], "ssc": ssc[sl],
                        "wbig": wbig, "lnp": lnp, "wcq": wcq, "wcs": wcs, "smallp": smallp, "won": f(w_onorm)[0],
                        "cst": cst, "cst2": cst2, "i16": i16})
    ncores = (_debug or {}).get("ncores", 8)
    res = run_bass_kernel_spmd(nc, in_maps[:ncores], core_ids=list(range(ncores)))
    R = list(res.results)
    while len(R) < 8:
        R.append({k: np.zeros_like(v) for k, v in R[0].items()})
    y = np.stack([R[c]["y"] for c in range(8)])
    ys = np.concatenate([R[c]["ys"] for c in range(8)])[:, None, :]
    sgp = np.stack([R[c]["sgp"] for c in range(8)])[None]
    sqp = np.stack([R[c]["sqp"] for c in range(8)])[None]
    ssp = np.stack([R[c]["ssp"] for c in range(8)])[None]
    sgs = np.concatenate([R[c]["sgs"] for c in range(8)])[None]
    sqs = np.concatenate([R[c]["sqs"] for c in range(8)])[None]
    sss = np.concatenate([R[c]["sss"] for c in range(8)])[None]
    return (y.astype(np.float32), ys.astype(np.float32), sgp.astype(np.float32), sqp.astype(np.float32),
            ssp.astype(np.float32), sgs.astype(np.float32), sqs.astype(np.float32), sss.astype(np.float32))
```
